# Optimizing a Trainium2 kernel written in Bass

```python
import math
import jax, jax.numpy as jnp
from jax import lax
import numpy as np

D_MODEL = 1024
BATCH = 2
SEQ = 8192
DEPTH = 2

NORM_EPS = 1e-6
MASK_VALUE = -1e30
D_FF = 2816
N_BRANCHES = 4
BRANCH_WIDTH = 512

HG_HEADS = 4
HG_KDIM = 128
HG_VDIM = BRANCH_WIDTH // HG_HEADS
HG_CHUNK = 64

S5_GROUP = 16
S5_GROUPS = BRANCH_WIDTH // S5_GROUP
S5_STATE = 64
S5_DT_MIN = 1e-3
S5_DT_MAX = 1e-1

CONV_CH = BRANCH_WIDTH
CONV_WIDTH = 31

ATT_HEAD_DIM = 128
ATT_CONFIGS = ((128, 1), (512, 4), (2048, 16))
ATT_GROUPS = len(ATT_CONFIGS)
ATT_HEADS_PER_GROUP = BRANCH_WIDTH // ATT_HEAD_DIM
ATT_HEADS = ATT_GROUPS * ATT_HEADS_PER_GROUP
ATT_QBLOCK = 128
ROPE_THETA = 500000.0
ROPE_DIM = ATT_HEAD_DIM // 4

IN_SPLITS = (HG_HEADS * HG_KDIM, HG_HEADS * HG_KDIM, HG_HEADS * HG_VDIM, HG_HEADS * HG_VDIM,
             BRANCH_WIDTH, 2 * CONV_CH, 3 * ATT_HEADS * ATT_HEAD_DIM, N_BRANCHES * D_MODEL)
IN_WIDTH = int(sum(IN_SPLITS))
IN_SPLIT_POINTS = tuple(int(v) for v in np.cumsum(IN_SPLITS)[:-1])

kernel_name = 'hybrid_gated_parallel_mixers'


def _rmsnorm(x, gain):
    xf = x.astype(jnp.float32)
    y = xf * lax.rsqrt(jnp.mean(xf * xf, axis=-1, keepdims=True) + NORM_EPS)
    return (y * gain.astype(jnp.float32)).astype(x.dtype)


def _layernorm(x, gain, bias):
    xf = x.astype(jnp.float32)
    xc = xf - jnp.mean(xf, axis=-1, keepdims=True)
    var = jnp.mean(xc * xc, axis=-1, keepdims=True)
    y = xc * lax.rsqrt(var + NORM_EPS) * gain.astype(jnp.float32) + bias.astype(jnp.float32)
    return y.astype(x.dtype)


def _swiglu(x, w_gate, w_up, w_down):
    return (jax.nn.silu(x @ w_gate) * (x @ w_up)) @ w_down


def _hgrn2(q, f, i, g, lb, gnorm):
    f32 = jnp.float32
    Bsz, L, _ = q.shape
    n_chunks = L // HG_CHUNK
    f = f.astype(f32)
    lb = lb.astype(f32)
    qf = jax.nn.silu(q.astype(f32)) * (HG_KDIM ** -0.5)
    kf = (1.0 - lb) * jax.nn.sigmoid(-f)
    logf = jnp.log(lb + (1.0 - lb) * jax.nn.sigmoid(f))

    def heads(t, d):
        return t.reshape(Bsz, n_chunks, HG_CHUNK, HG_HEADS, d).transpose(1, 0, 3, 2, 4)

    qs, ks, gs = heads(qf, HG_KDIM), heads(kf, HG_KDIM), heads(logf, HG_KDIM)
    vs = heads(i.astype(f32), HG_VDIM)
    causal = jnp.tril(jnp.ones((HG_CHUNK, HG_CHUNK), dtype=bool))[:, :, None]

    def chunk_step(state, inp):
        qc, kc, vc, gc = inp
        b = jnp.cumsum(gc, axis=2)
        o_inter = jnp.einsum('bhck,bhkv->bhcv', qc * jnp.exp(b), state)
        rel = b[:, :, :, None, :] - b[:, :, None, :, :]
        decay = jnp.where(causal, jnp.exp(jnp.minimum(rel, 0.0)), 0.0)
        att = jnp.einsum('bhtk,bhsk,bhtsk->bhts', qc, kc, decay)
        o_intra = jnp.einsum('bhts,bhsv->bhtv', att, vc)
        b_last = b[:, :, -1:, :]
        new_state = jnp.exp(b_last[:, :, 0, :])[..., None] * state + jnp.einsum(
            'bhsk,bhsv->bhkv', kc * jnp.exp(b_last - b), vc)
        return new_state, o_inter + o_intra

    s0 = jnp.zeros((Bsz, HG_HEADS, HG_KDIM, HG_VDIM), f32)
    _, o = lax.scan(chunk_step, s0, (qs, ks, vs, gs))
    o = o.transpose(1, 0, 3, 2, 4).reshape(Bsz, L, HG_HEADS, HG_VDIM)
    o = o * lax.rsqrt(jnp.mean(o * o, axis=-1, keepdims=True) + NORM_EPS)
    o = o * gnorm.astype(f32).reshape(HG_HEADS, HG_VDIM)
    o = o.reshape(Bsz, L, HG_HEADS * HG_VDIM) * jax.nn.silu(g.astype(f32))
    return o.astype(q.dtype)


def _s5(u, a_re, a_im, log_dt, b_re, b_im, c_re, c_im, d_skip, w_glu):
    f32 = jnp.float32
    Bsz, L, _ = u.shape
    uf = u.astype(f32)
    a_re = a_re.astype(f32)
    a_im = a_im.astype(f32)
    b_re, b_im, c_re, c_im = (t.astype(f32) for t in (b_re, b_im, c_re, c_im))
    dt = jnp.exp(log_dt.astype(f32))[:, None]
    mag = jnp.exp(a_re * dt)
    ab_re = mag * jnp.cos(a_im * dt)
    ab_im = mag * jnp.sin(a_im * dt)
    den = a_re * a_re + a_im * a_im
    zr = ((ab_re - 1.0) * a_re + ab_im * a_im) / den
    zi = (ab_im * a_re - (ab_re - 1.0) * a_im) / den
    bb_re = zr[..., None] * b_re - zi[..., None] * b_im
    bb_im = zr[..., None] * b_im + zi[..., None] * b_re
    ug = uf.reshape(Bsz, L, S5_GROUPS, S5_GROUP)
    bu_re = jnp.einsum('blgc,gpc->blgp', ug, bb_re)
    bu_im = jnp.einsum('blgc,gpc->blgp', ug, bb_im)
    abar_re = jnp.broadcast_to(ab_re, bu_re.shape)
    abar_im = jnp.broadcast_to(ab_im, bu_im.shape)

    def combine(e1, e2):
        a1r, a1i, b1r, b1i = e1
        a2r, a2i, b2r, b2i = e2
        return (a2r * a1r - a2i * a1i, a2r * a1i + a2i * a1r,
                a2r * b1r - a2i * b1i + b2r, a2r * b1i + a2i * b1r + b2i)

    _, _, xr, xi = lax.associative_scan(combine, (abar_re, abar_im, bu_re, bu_im), axis=1)
    y = jnp.einsum('blgp,gcp->blgc', xr, c_re) - jnp.einsum('blgp,gcp->blgc', xi, c_im)
    y = y.reshape(Bsz, L, BRANCH_WIDTH) + d_skip.astype(f32) * uf
    z = jax.nn.gelu(y)
    za, zg = jnp.split(z @ w_glu.astype(f32), 2, axis=-1)
    return (za * jax.nn.sigmoid(zg)).astype(u.dtype)


def _conformer_conv(u, conv_w, conv_b, ln_g, ln_b):
    a, b = jnp.split(u, 2, axis=-1)
    z = a * jax.nn.sigmoid(b)
    z = lax.conv_general_dilated(
        z, conv_w[:, None, :].astype(z.dtype), window_strides=(1,),
        padding=((CONV_WIDTH - 1, 0),), dimension_numbers=('NWC', 'WIO', 'NWC'),
        feature_group_count=CONV_CH) + conv_b
    z = _layernorm(z, ln_g, ln_b)
    return jax.nn.silu(z)


def _partial_rope(x, positions):
    half = ROPE_DIM // 2
    inv_freq = ROPE_THETA ** (-jnp.arange(half, dtype=jnp.float32) / half)
    ang = positions.astype(jnp.float32)[..., None] * inv_freq
    cos = jnp.cos(ang)[:, :, None, :]
    sin = jnp.sin(ang)[:, :, None, :]
    xr = x[..., :ROPE_DIM].astype(jnp.float32)
    x1, x2 = xr[..., :half], xr[..., half:]
    rot = jnp.concatenate([x1 * cos - x2 * sin, x2 * cos + x1 * sin], axis=-1)
    return jnp.concatenate([rot.astype(x.dtype), x[..., ROPE_DIM:]], axis=-1)


def _strided_window_attention(q, k, v, window, dilation):
    f32 = jnp.float32
    Bsz, L, H, Dh = q.shape
    span = window // dilation
    n_sub = L // dilation
    nb = -(-n_sub // ATT_QBLOCK)
    pad = nb * ATT_QBLOCK - n_sub

    def gather(t):
        t = t.reshape(Bsz, n_sub, dilation, H, Dh).transpose(0, 2, 3, 1, 4)
        t = jnp.pad(t, ((0, 0), (0, 0), (0, 0), (0, pad), (0, 0)))
        return t.reshape(Bsz, dilation, H, nb, ATT_QBLOCK, Dh)

    def with_prev(t):
        prev = jnp.pad(t, ((0, 0), (0, 0), (0, 0), (1, 0), (0, 0), (0, 0)))[:, :, :, :-1]
        return jnp.concatenate([prev, t], axis=4)

    qb = gather(q)
    kk = with_prev(gather(k))
    vv = with_prev(gather(v))
    s = jnp.einsum('brhnqe,brhnke->brhnqk', qb, kk,
                   preferred_element_type=f32) * (Dh ** -0.5)
    qi = jnp.arange(ATT_QBLOCK)[:, None]
    kj = jnp.arange(2 * ATT_QBLOCK)[None, :]
    rel = ATT_QBLOCK + qi - kj
    band = (rel >= 0) & (rel <= span)
    not_first = jnp.arange(nb)[:, None, None] > 0
    valid = band[None] & (not_first | (kj[None] >= ATT_QBLOCK))
    s = jnp.where(valid, s, MASK_VALUE)
    lse = jax.nn.logsumexp(s, axis=-1)
    p = jnp.where(valid, jnp.exp(s - lse[..., None]), 0.0)
    o = jnp.einsum('brhnqk,brhnke->brhnqe', p, vv.astype(f32))
    o = o.reshape(Bsz, dilation, H, nb * ATT_QBLOCK, Dh)[:, :, :, :n_sub]
    o = o.transpose(0, 3, 1, 2, 4).reshape(Bsz, L, H, Dh)
    lse = lse.reshape(Bsz, dilation, H, nb * ATT_QBLOCK)[:, :, :, :n_sub]
    lse = lse.transpose(0, 3, 1, 2).reshape(Bsz, L, H)
    return o, lse


def _dilated_attention(qkv, positions):
    Bsz, L, _ = qkv.shape
    qkv = qkv.reshape(Bsz, L, 3, ATT_HEADS, ATT_HEAD_DIM)
    q = _partial_rope(qkv[:, :, 0], positions)
    k = _partial_rope(qkv[:, :, 1], positions)
    v = qkv[:, :, 2]
    outs, lses = [], []
    for gi, (window, dilation) in enumerate(ATT_CONFIGS):
        sl = slice(gi * ATT_HEADS_PER_GROUP, (gi + 1) * ATT_HEADS_PER_GROUP)
        o, lse = _strided_window_attention(q[:, :, sl], k[:, :, sl], v[:, :, sl], window, dilation)
        outs.append(o)
        lses.append(lse)
    w = jax.nn.softmax(jnp.stack(lses, axis=0), axis=0)
    out = jnp.sum(w[..., None] * jnp.stack(outs, axis=0), axis=0)
    return out.reshape(Bsz, L, ATT_HEADS_PER_GROUP * ATT_HEAD_DIM).astype(qkv.dtype)


def setup_inputs(seed: int = 0) -> dict:
    key = jax.random.key(seed)
    ks = iter(jax.random.split(key, 48))
    f32 = jnp.float32

    def nrm(shape, scale):
        return scale * jax.random.normal(next(ks), shape, f32)

    def gain(shape):
        return 1.0 + nrm(shape, 0.01)

    x = jax.random.normal(next(ks), (BATCH, SEQ, D_MODEL), f32)
    offsets = jax.random.randint(next(ks), (BATCH, 1), 0, 4096)
    positions = (offsets + jnp.arange(SEQ)[None, :]).astype(jnp.int32)
    a_im_base = jnp.pi * jnp.arange(S5_STATE, dtype=f32)
    return {
        'x': x,
        'positions': positions,
        'ffn1_norm': gain((DEPTH, D_MODEL)),
        'ffn1_w_gate': nrm((DEPTH, D_MODEL, D_FF), D_MODEL ** -0.5),
        'ffn1_w_up': nrm((DEPTH, D_MODEL, D_FF), D_MODEL ** -0.5),
        'ffn1_w_down': nrm((DEPTH, D_FF, D_MODEL), D_FF ** -0.5),
        'mix_norm': gain((DEPTH, D_MODEL)),
        'w_in': nrm((DEPTH, D_MODEL, IN_WIDTH), D_MODEL ** -0.5),
        'hg_lb_logits': nrm((DEPTH, HG_HEADS * HG_KDIM), 0.1),
        'hg_gnorm': gain((DEPTH, HG_HEADS * HG_VDIM)),
        's5_a_re': -0.5 + nrm((DEPTH, S5_GROUPS, S5_STATE), 0.01),
        's5_a_im': a_im_base + nrm((DEPTH, S5_GROUPS, S5_STATE), 0.01),
        's5_log_dt': jax.random.uniform(next(ks), (DEPTH, S5_GROUPS), f32,
                                        math.log(S5_DT_MIN), math.log(S5_DT_MAX)),
        's5_b_re': nrm((DEPTH, S5_GROUPS, S5_STATE, S5_GROUP), (2 * S5_GROUP) ** -0.5),
        's5_b_im': nrm((DEPTH, S5_GROUPS, S5_STATE, S5_GROUP), (2 * S5_GROUP) ** -0.5),
        's5_c_re': nrm((DEPTH, S5_GROUPS, S5_GROUP, S5_STATE), (2 * S5_STATE) ** -0.5),
        's5_c_im': nrm((DEPTH, S5_GROUPS, S5_GROUP, S5_STATE), (2 * S5_STATE) ** -0.5),
        's5_d': nrm((DEPTH, BRANCH_WIDTH), 1.0),
        's5_w_glu': nrm((DEPTH, BRANCH_WIDTH, 2 * BRANCH_WIDTH), BRANCH_WIDTH ** -0.5),
        'conv_w': nrm((DEPTH, CONV_WIDTH, CONV_CH), CONV_WIDTH ** -0.5),
        'conv_b': nrm((DEPTH, CONV_CH), 0.01),
        'conv_ln_g': gain((DEPTH, CONV_CH)),
        'conv_ln_b': nrm((DEPTH, CONV_CH), 0.01),
        'w_branch': nrm((DEPTH, N_BRANCHES, BRANCH_WIDTH, D_MODEL), BRANCH_WIDTH ** -0.5),
        'w_out': nrm((DEPTH, D_MODEL, D_MODEL), D_MODEL ** -0.5),
        'ffn2_norm': gain((DEPTH, D_MODEL)),
        'ffn2_w_gate': nrm((DEPTH, D_MODEL, D_FF), D_MODEL ** -0.5),
        'ffn2_w_up': nrm((DEPTH, D_MODEL, D_FF), D_MODEL ** -0.5),
        'ffn2_w_down': nrm((DEPTH, D_FF, D_MODEL), D_FF ** -0.5),
        'final_norm': gain((D_MODEL,)),
    }


def reference(x, positions, ffn1_norm, ffn1_w_gate, ffn1_w_up, ffn1_w_down, mix_norm, w_in,
              hg_lb_logits, hg_gnorm, s5_a_re, s5_a_im, s5_log_dt, s5_b_re, s5_b_im,
              s5_c_re, s5_c_im, s5_d, s5_w_glu, conv_w, conv_b, conv_ln_g, conv_ln_b,
              w_branch, w_out, ffn2_norm, ffn2_w_gate, ffn2_w_up, ffn2_w_down, final_norm):
    Bsz, L, _ = x.shape
    lb_soft = jax.nn.softmax(hg_lb_logits.astype(jnp.float32), axis=0)
    lb_all = jnp.cumsum(lb_soft, axis=0) - lb_soft[0]
    for l in range(DEPTH):
        h = _rmsnorm(x, ffn1_norm[l])
        x = x + 0.5 * _swiglu(h, ffn1_w_gate[l], ffn1_w_up[l], ffn1_w_down[l])

        h = _rmsnorm(x, mix_norm[l])
        hq, hf, hi, hg, s5_in, conv_in, att_in, gate_in = jnp.split(
            h @ w_in[l], IN_SPLIT_POINTS, axis=-1)
        y_a = _hgrn2(hq, hf, hi, hg, lb_all[l], hg_gnorm[l])
        y_b = _s5(s5_in, s5_a_re[l], s5_a_im[l], s5_log_dt[l], s5_b_re[l], s5_b_im[l],
                  s5_c_re[l], s5_c_im[l], s5_d[l], s5_w_glu[l])
        y_c = _conformer_conv(conv_in, conv_w[l], conv_b[l], conv_ln_g[l], conv_ln_b[l])
        y_d = _dilated_attention(att_in, positions)
        gates = jax.nn.sigmoid(gate_in).reshape(Bsz, L, N_BRANCHES, D_MODEL)
        merged = gates[:, :, 0] * (y_a @ w_branch[l, 0])
        merged = merged + gates[:, :, 1] * (y_b @ w_branch[l, 1])
        merged = merged + gates[:, :, 2] * (y_c @ w_branch[l, 2])
        merged = merged + gates[:, :, 3] * (y_d @ w_branch[l, 3])
        x = x + merged @ w_out[l]

        h = _rmsnorm(x, ffn2_norm[l])
        x = x + 0.5 * _swiglu(h, ffn2_w_gate[l], ffn2_w_up[l], ffn2_w_down[l])
    return _rmsnorm(x, final_norm)
```

```python
import contextlib
import math

import numpy as np
import concourse.bass as bass
import concourse.mybir as mybir
from concourse.bass_utils import run_bass_kernel_spmd

F32 = mybir.dt.float32
BF16 = mybir.dt.bfloat16
I32 = mybir.dt.int32
AF = mybir.ActivationFunctionType
ALU = mybir.AluOpType

NCORES = 8
D = 1024
DFF = 2816
B = 2
L = 8192
DEPTH = 2
TOK = 2048
TT = 512
EPS = 1e-6
NMIX = 8192

ENGINES = ("sync", "scalar", "gpsimd", "vector", "tensor")


class _Op:
    __slots__ = ("eng", "fn", "deps", "is_dma", "sem_key", "ticket", "signals", "idx", "tiny")


class Prog:
    def __init__(self, nc):
        self.nc = nc
        self.ops = []
        self.last_writer = {}
        self.readers = {}
        self.tiny_mode = False

    def _add(self, eng, fn, reads, writes, is_dma=False, sem_key=None):
        op = _Op()
        op.eng, op.fn, op.is_dma, op.sem_key = eng, fn, is_dma, sem_key
        op.signals, op.ticket, op.idx = False, None, len(self.ops)
        op.tiny = self.tiny_mode and not is_dma and eng != "tensor"
        deps = set()
        for r in reads:
            w = self.last_writer.get(r)
            if w is not None:
                deps.add(w)
        for w_ in writes:
            w = self.last_writer.get(w_)
            if w is not None:
                deps.add(w)
            deps.update(self.readers.get(w_, ()))
        deps.discard(op.idx)
        op.deps = deps
        for r in reads:
            self.readers.setdefault(r, []).append(op.idx)
        for w_ in writes:
            self.last_writer[w_] = op.idx
            self.readers[w_] = []
        self.ops.append(op)
        return op

    def op(self, eng, fn, reads=(), writes=()):
        return self._add(eng, fn, tuple(reads), tuple(writes))

    def dma(self, eng, fn, reads=(), writes=(), key=None):
        return self._add(eng, fn, tuple(reads), tuple(writes), is_dma=True, sem_key=key)

    def V(self, fn, reads=(), writes=()):
        return self.op("vector", fn, reads, writes)

    def A(self, fn, reads=(), writes=()):
        return self.op("scalar", fn, reads, writes)

    def G(self, fn, reads=(), writes=()):
        return self.op("gpsimd", fn, reads, writes)

    def T(self, fn, reads=(), writes=()):
        return self.op("tensor", fn, reads, writes)

    def emit(self):
        nc, ops = self.nc, self.ops
        for op in ops:
            for d in op.deps:
                dop = ops[d]
                if dop.is_dma or dop.eng != op.eng or dop.tiny:
                    dop.signals = True
        for op in ops:
            if op.is_dma:
                op.signals = True
        eng_cnt = {e: 0 for e in ENGINES}
        key_cnt = {}
        for op in ops:
            if not op.signals:
                continue
            if op.is_dma:
                key_cnt[op.sem_key] = key_cnt.get(op.sem_key, 0) + 16
                op.ticket = key_cnt[op.sem_key]
            else:
                eng_cnt[op.eng] += 1
                op.ticket = eng_cnt[op.eng]
        with contextlib.ExitStack() as es:
            eng_sem = {e: es.enter_context(nc.semaphore("es_" + e)) for e in ENGINES}
            key_sem = {k: es.enter_context(nc.semaphore("ds_%d" % i)) for i, k in enumerate(key_cnt)}
            block = es.enter_context(nc.Block())

            def semval(dop):
                if dop.is_dma:
                    return key_sem[dop.sem_key], ("k", dop.sem_key), dop.ticket
                return eng_sem[dop.eng], ("e", dop.eng), dop.ticket

            def make_body(ename):
                def body(eng):
                    waited = {}
                    for op in ops:
                        if op.eng != ename:
                            continue
                        need = {}
                        for d in op.deps:
                            dop = ops[d]
                            if (not dop.is_dma) and dop.eng == ename and not dop.tiny:
                                continue
                            sem, skey, val = semval(dop)
                            if waited.get(skey, 0) >= val:
                                continue
                            if need.get(skey, (None, 0))[1] < val:
                                need[skey] = (sem, val)
                        for skey, (sem, val) in need.items():
                            eng.wait_ge(sem, val)
                            waited[skey] = val
                        inst = op.fn(eng)
                        if op.signals:
                            sem, skey, val = semval(op)
                            inst.then_inc(sem, 16 if op.is_dma else 1)
                    if ename == "sync":
                        for k, c in key_cnt.items():
                            if waited.get(("k", k), 0) < c:
                                eng.wait_ge(key_sem[k], c)
                        for e2, c in eng_cnt.items():
                            if c > 0 and e2 != "sync":
                                eng.wait_ge(eng_sem[e2], c)
                return body

            block.sync(make_body("sync"))
            block.scalar(make_body("scalar"))
            block.gpsimd(make_body("gpsimd"))
            block.vector(make_body("vector"))
            block.tensor(make_body("tensor"))


class KB:
    def __init__(self):
        self.nc = bass.Bass("TRN2", target_bir_lowering=False)
        self.es = contextlib.ExitStack()
        self.P = Prog(self.nc)
        self._rr = {}
        self._uid = 0

    def din(self, name, shape, dt=F32):
        return self.nc.dram_tensor(name, list(shape), dt, kind="ExternalInput").ap()

    def dout(self, name, shape, dt=F32):
        return self.nc.dram_tensor(name, list(shape), dt, kind="ExternalOutput").ap()

    def sb(self, name, shape, dt=F32):
        return self.es.enter_context(self.nc.sbuf_tensor("sb_" + name, list(shape), dt))

    def ps(self, name, shape, dt=F32):
        return self.es.enter_context(self.nc.psum_tensor("ps_" + name, list(shape), dt))

    def rr(self, name, n):
        i = self._rr.get(name, 0)
        self._rr[name] = i + 1
        return i % n

    def uid(self):
        self._uid += 1
        return self._uid

    def finish(self):
        self.P.emit()
        self.es.close()
        return self.nc

    def mm_group(self, out_ap, out_key, pairs, reads):
        n = len(pairs)
        for i, (l, r) in enumerate(pairs):
            self.P.T(lambda e, l=l, r=r, i=i: e.matmul(out_ap, lhsT=l, rhs=r, start=(i == 0), stop=(i == n - 1)),
                     reads=reads, writes=[out_key])


class TokenPhase:
    def __init__(self, kb, G):
        self.kb = kb
        self.G = G
        self.NT = G // TT
        kb_ = kb
        self.xT = kb_.sb("xT", [128, 8 * G], F32)
        self.hT = kb_.sb("hT", [128, 8 * G], BF16)
        self.hid = kb_.sb("hid", [128, 22 * G], BF16)
        self.sq = kb_.sb("sq", [128, 8 * TT], BF16)
        self.rs = kb_.sb("rs", [128, TT], F32)
        self.ones = kb_.sb("ones", [128, 128], BF16)
        self.sg = [kb_.sb("sg%d" % i, [128, TT], F32) for i in range(2)]
        self.wslot = [kb_.sb("wslot%d" % i, [128, 22 * 256], BF16) for i in range(3)]
        self.banks = [kb_.ps("bank%d" % i, [128, TT], F32) for i in range(8)]
        kb.P.V(lambda e: e.memset(self.ones[:], 1.0), writes=["ones"])

    def x(self, c, tt):
        return self.xT[:, c * self.G + tt * TT: c * self.G + (tt + 1) * TT]

    def h(self, c, tt):
        return self.hT[:, c * self.G + tt * TT: c * self.G + (tt + 1) * TT]

    def hd(self, j, tt):
        return self.hid[:, j * self.G + tt * TT: j * self.G + (tt + 1) * TT]

    def bank(self, purpose):
        groups = {"gate": (0, 1), "up": (2, 3), "acc": (4, 5, 6), "stat": (7,)}[purpose]
        i = groups[self.kb.rr("bank_" + purpose, len(groups))]
        return self.banks[i], ("bank", i)

    def wload(self, src_ap, nk, ncols):
        kb = self.kb
        si = kb.rr("wslot", 3)
        slot = self.wslot[si]
        view = slot[:, 0:nk * ncols].rearrange("p (c n) -> p c n", n=ncols)
        kb.P.dma("gpsimd", lambda e: e.dma_start(out=view, in_=src_ap), writes=[("wslot", si)], key=("wslot", si))
        return view, ("wslot", si)

    def rmsnorm(self, gain_ap, tt, out_f32=False):
        kb, P = self.kb, self.kb.P
        for c in range(8):
            P.A(lambda e, c=c: e.activation(out=self.sq[:, c * TT:(c + 1) * TT], in_=self.x(c, tt), func=AF.Square),
                reads=[("x", c, tt)], writes=[("sq", c)])
        bk, bkey = self.bank("stat")
        kb.mm_group(bk[:], bkey, [(self.ones[:], self.sq[:, c * TT:(c + 1) * TT]) for c in range(8)],
                    reads=["ones"] + [("sq", c) for c in range(8)])
        P.A(lambda e: e.activation(out=self.rs[:], in_=bk[:], func=AF.Sqrt, bias=EPS, scale=1.0 / D),
            reads=[bkey], writes=["rs"])
        P.V(lambda e: e.reciprocal(out=self.rs[:], in_=self.rs[:]), reads=["rs"], writes=["rs"])
        if out_f32:
            return
        for c in range(8):
            P.V(lambda e, c=c: e.scalar_tensor_tensor(out=self.h(c, tt), in0=self.x(c, tt), scalar=gain_ap[:, c:c + 1],
                                                      in1=self.rs[:], op0=ALU.mult, op1=ALU.mult),
                reads=[("x", c, tt), "rs", "gains"], writes=[("h", c, tt)])

    def ffn(self, wg, wu, wd):
        kb, P, NT = self.kb, self.kb.P, self.NT
        wg_v = wg.rearrange("(c p) n -> p c n", p=128)
        wu_v = wu.rearrange("(c p) n -> p c n", p=128)
        wd_v = wd.rearrange("(c p) n -> p c n", p=128)
        for blk in range(DFF // 256):
            gv, gk = self.wload(wg_v[:, :, blk * 256:(blk + 1) * 256], 8, 256)
            uv, uk = self.wload(wu_v[:, :, blk * 256:(blk + 1) * 256], 8, 256)
            for sub in range(2):
                j = blk * 2 + sub
                for tt in range(NT):
                    hreads = [("h", c, tt) for c in range(8)]
                    bg, bgk = self.bank("gate")
                    kb.mm_group(bg[:], bgk, [(gv[:, c, sub * 128:(sub + 1) * 128], self.h(c, tt)) for c in range(8)],
                                reads=[gk] + hreads)
                    bu, buk = self.bank("up")
                    kb.mm_group(bu[:], buk, [(uv[:, c, sub * 128:(sub + 1) * 128], self.h(c, tt)) for c in range(8)],
                                reads=[uk] + hreads)
                    si = kb.rr("sg", 2)
                    sg = self.sg[si]
                    P.A(lambda e, sg=sg, bg=bg: e.activation(out=sg[:], in_=bg[:], func=AF.Silu),
                        reads=[bgk], writes=[("sg", si)])
                    P.V(lambda e, sg=sg, bu=bu, j=j, tt=tt: e.tensor_tensor(out=self.hd(j, tt), in0=bu[:], in1=sg[:], op=ALU.mult),
                        reads=[buk, ("sg", si)], writes=[("hid", j, tt)])
        for mblk in range(D // 256):
            dv, dk = self.wload(wd_v[:, :, mblk * 256:(mblk + 1) * 256], 22, 256)
            for sub in range(2):
                m = mblk * 2 + sub
                for tt in range(NT):
                    ba, bak = self.bank("acc")
                    kb.mm_group(ba[:], bak, [(dv[:, j, sub * 128:(sub + 1) * 128], self.hd(j, tt)) for j in range(22)],
                                reads=[dk] + [("hid", j, tt) for j in range(22)])
                    P.V(lambda e, ba=ba, m=m, tt=tt: e.scalar_tensor_tensor(out=self.x(m, tt), in0=ba[:], scalar=0.5, in1=self.x(m, tt),
                                                                             op0=ALU.mult, op1=ALU.add),
                        reads=[bak, ("x", m, tt)], writes=[("x", m, tt)])


def build_A():
    kb = KB()
    P = kb.P
    xT_d = kb.din("xT", [D, TOK])
    g1_d = kb.din("g_ffn1", [128, 8])
    g2_d = kb.din("g_mix", [128, 8])
    wg_d = kb.din("w_gate", [D, DFF])
    wu_d = kb.din("w_up", [D, DFF])
    wd_d = kb.din("w_down", [DFF, D])
    win_d = kb.din("w_in", [D, NMIX])
    x1_o = kb.dout("x1T", [D, TOK])
    pj_o = kb.dout("projT", [NMIX, TOK], BF16)
    G = 1024
    tp = TokenPhase(kb, G)
    NT = tp.NT
    g1 = kb.sb("g1", [128, 8])
    g2 = kb.sb("g2", [128, 8])
    stage = [kb.sb("stage%d" % i, [128, TT], BF16) for i in range(4)]
    P.dma("sync", lambda e: e.dma_start(out=g1[:], in_=g1_d), writes=["gains"], key="gains")
    P.dma("sync", lambda e: e.dma_start(out=g2[:], in_=g2_d), writes=["gains"], key="gains")
    xT_v = xT_d.rearrange("(c p) t -> p c t", p=128)
    x1_v = x1_o.rearrange("(c p) t -> p c t", p=128)
    win_v = win_d.rearrange("(c p) n -> p c n", p=128)
    xkeys = [("x", c, tt) for c in range(8) for tt in range(NT)]
    for grp in range(TOK // G):
        t0 = grp * G
        xsb = tp.xT[:].rearrange("p (c t) -> p c t", t=G)
        P.dma("sync", lambda e, t0=t0: e.dma_start(out=xsb, in_=xT_v[:, :, t0:t0 + G]), writes=xkeys, key="xload")
        for tt in range(NT):
            tp.rmsnorm(g1, tt)
        tp.ffn(wg_d, wu_d, wd_d)
        P.dma("sync", lambda e, t0=t0: e.dma_start(out=x1_v[:, :, t0:t0 + G], in_=xsb), reads=xkeys, key="x1store")
        for tt in range(NT):
            tp.rmsnorm(g2, tt)
        for blk in range(NMIX // 256):
            wv, wk = tp.wload(win_v[:, :, blk * 256:(blk + 1) * 256], 8, 256)
            for sub in range(2):
                col = blk * 256 + sub * 128
                for tt in range(NT):
                    ba, bak = tp.bank("acc")
                    kb.mm_group(ba[:], bak, [(wv[:, c, sub * 128:(sub + 1) * 128], tp.h(c, tt)) for c in range(8)],
                                reads=[wk] + [("h", c, tt) for c in range(8)])
                    si = kb.rr("stage", 4)
                    st = stage[si]
                    if si % 2 == 0:
                        P.A(lambda e, st=st, ba=ba: e.activation(out=st[:], in_=ba[:], func=AF.Copy), reads=[bak], writes=[("stage", si)])
                    else:
                        P.V(lambda e, st=st, ba=ba: e.tensor_copy(out=st[:], in_=ba[:]), reads=[bak], writes=[("stage", si)])
                    P.dma("sync", lambda e, st=st, col=col, tt=tt, t0=t0: e.dma_start(
                        out=pj_o[col:col + 128, t0 + tt * TT:t0 + (tt + 1) * TT], in_=st[:]),
                        reads=[("stage", si)], key=("stage", si))
    return kb.finish()


_DBG = 0
NT_L = L // TT
HGC = 64
TWO_PI = 2.0 * math.pi


def build_B(parts=("h", "s", "a")):
    kb = KB()
    P = kb.P
    hg_d = kb.din("hg4", [4, 128, L], BF16) if "h" in parts else None
    s5u_d = kb.din("s5u", [128, L], BF16) if "s" in parts else None
    att_d = kb.din("att9", [9, 128, L], BF16) if "a" in parts else None
    pos_d = kb.din("pos", [1, L], I32)
    cmask_d = kb.din("cmask", [128, TT])
    amask_d = kb.din("amask", [64, TT])
    pmask_d = kb.din("pmask", [2, 128, TT])
    ident_d = kb.din("ident", [128, 128], BF16)
    psw_d = kb.din("psw", [32, 32], BF16)
    ropec_d = kb.din("ropec", [32, 2])
    tau_d = kb.din("tau", [128, TT])
    hgp_d = kb.din("hgp", [128, 4])
    s5p_d = kb.din("s5p", [128, 4 * 67])
    s5d_d = kb.din("s5d", [128, 1])
    ya_o = kb.dout("ya", [128, L], BF16) if "h" in parts else None
    zb_o = kb.dout("zb", [128, L], BF16) if "s" in parts else None
    yd_o = kb.dout("yd", [128, L], BF16) if "a" in parts else None

    NS = 11
    S = [kb.sb("scr%d" % i, [128, TT], F32) for i in range(NS)]
    SK = [("scr", i) for i in range(NS)]
    NSB = 6
    SB = [kb.sb("scb%d" % i, [128, TT], BF16) for i in range(NSB)]
    SBK = [("scb", i) for i in range(NSB)]
    banks = [kb.ps("bank%d" % i, [128, TT], F32) for i in range(8)]
    BK = [("bank", i) for i in range(8)]
    cmask = kb.sb("cmask", [128, TT]); amask = kb.sb("amask", [64, TT])
    pmask = kb.sb("pmask", [128, 2 * TT], BF16)
    ident = kb.sb("ident", [128, 128], BF16); psw = kb.sb("psw", [32, 32], BF16)
    ropec = kb.sb("ropec", [32, 2]); tau = kb.sb("tau", [128, TT])
    hgp = kb.sb("hgp", [128, 4]); s5p = kb.sb("s5p", [128, 4 * 67]); s5d = kb.sb("s5d", [128, 1])
    ones = kb.sb("ones", [128, 128], BF16)
    P.V(lambda e: e.memset(ones[:], 1.0), writes=["ones"])
    for nm, dst, src in (("cmask", cmask, cmask_d), ("amask", amask, amask_d), ("ident", ident, ident_d), ("psw", psw, psw_d),
                         ("ropec", ropec, ropec_d), ("tau", tau, tau_d), ("hgp", hgp, hgp_d), ("s5p", s5p, s5p_d), ("s5d", s5d, s5d_d)):
        P.dma("sync", lambda e, dst=dst, src=src: e.dma_start(out=dst[:], in_=src), writes=[nm], key="c_" + nm)
    P.dma("gpsimd", lambda e: e.dma_start(out=pmask[:].rearrange("p (a t) -> p a t", a=2), in_=pmask_d.rearrange("a p t -> p a t")),
          writes=["pmask"], key="c_pmask")

    def sin_table(out_ap, ang_ap, shape, shift, tmpf, tmpi, reads, writes, scale_ap=None, tkey="sin_tmp2"):
        P.V(lambda e: e.tensor_scalar(out=tmpf, in0=ang_ap, scalar1=1.0 / TWO_PI, scalar2=shift / TWO_PI, op0=ALU.mult, op1=ALU.add),
            reads=reads, writes=["sin_tmp", tkey])
        P.V(lambda e: e.tensor_copy(out=tmpi, in_=tmpf), reads=["sin_tmp"], writes=["sin_tmpi"])
        P.V(lambda e: e.tensor_copy(out=tmpf, in_=tmpi), reads=["sin_tmpi"], writes=["sin_tmp", tkey])
        P.V(lambda e: e.scalar_tensor_tensor(out=tmpf, in0=tmpf, scalar=-TWO_PI, in1=ang_ap, op0=ALU.mult, op1=ALU.add),
            reads=["sin_tmp"] + list(reads), writes=["sin_tmp", tkey])
        P.V(lambda e: e.tensor_scalar(out=tmpf, in0=tmpf, scalar1=-math.pi - shift, scalar2=math.pi - shift, op0=ALU.max, op1=ALU.min),
            reads=["sin_tmp"], writes=["sin_tmp", tkey])
        if scale_ap is None:
            P.A(lambda e: e.activation(out=out_ap, in_=tmpf, func=AF.Sin, bias=shiftc[shift][:shape[0], 0:1]), reads=["sin_tmp", "shiftc", tkey], writes=writes)
        else:
            assert shift == 0.0
            P.A(lambda e: e.activation(out=out_ap, in_=tmpf, func=AF.Sin, scale=scale_ap), reads=["sin_tmp", tkey], writes=writes)

    shiftc = {0.0: kb.sb("shift0", [128, 1]), math.pi / 2: kb.sb("shift1", [128, 1])}
    P.V(lambda e: e.memset(shiftc[0.0][:], 0.0), writes=["shiftc"])
    P.V(lambda e: e.memset(shiftc[math.pi / 2][:], math.pi / 2), writes=["shiftc"])
    tmpi = kb.sb("tmpi", [128, TT], I32)

    lb = kb.sb("lb", [128, 1]); oml = kb.sb("oml", [128, 1])
    P.tiny_mode = True
    P.V(lambda e: e.tensor_tensor(out=lb[:], in0=hgp[:, 1:2], in1=hgp[:, 0:1], op=ALU.subtract), reads=["hgp"], writes=["lb"])
    P.A(lambda e: e.activation(out=lb[:], in_=lb[:], func=AF.Sigmoid), reads=["lb"], writes=["lb"])
    P.V(lambda e: e.tensor_tensor(out=lb[:], in0=lb[:], in1=hgp[:, 3:4], op=ALU.mult), reads=["lb", "hgp"], writes=["lb"])
    P.V(lambda e: e.tensor_scalar(out=oml[:], in0=lb[:], scalar1=-1.0, scalar2=1.0, op0=ALU.mult, op1=ALU.add), reads=["lb"], writes=["oml"])
    P.tiny_mode = False
    Sst = kb.sb("Sst", [128, 128]); Sb = kb.sb("Sb", [128, 128], BF16)
    P.V(lambda e: e.memset(Sst[:], 0.0), writes=["Sst"])
    esc = kb.sb("esc", [128, 32])
    hin = [[kb.sb("hin%d_%d" % (a, i), [128, TT], BF16) for i in range(4)] for a in range(2)]
    KT = kb.sb("KTtok", [64, 8 * 128], BF16); VT = kb.sb("VTtok", [64, 8 * 128], BF16)
    pT = [kb.ps("pT%d" % i, [128, 1024], BF16) for i in range(0)]
    QSC = float(128 ** -0.5)
    for t in (range(NT_L) if "h" in parts else ()):
        a = t % 2
        tsl = slice(t * TT, (t + 1) * TT)
        for i in range(4):
            P.dma("sync", lambda e, i=i, a=a, tsl=tsl: e.dma_start(out=hin[a][i][:], in_=hg_d[i, :, tsl]),
                  writes=[("hin", a, i)], key=("hin", a, i))
        q_t, f_t, i_t, g_t = hin[a]
        sg, t1, lf, bb, bq, eq, ek, qs, osb, rst, sgl = (S[k] for k in range(11))
        Qt, Kt, attm, osq = SB[0], SB[1], SB[2], SB[3]
        P.A(lambda e, f_t=f_t: e.activation(out=sg[:], in_=f_t[:], func=AF.Sigmoid), reads=[("hin", a, 1)], writes=[SK[0]])
        P.V(lambda e: e.tensor_scalar(out=t1[:], in0=sg[:], scalar1=oml[:, 0:1], scalar2=lb[:, 0:1], op0=ALU.mult, op1=ALU.add),
            reads=[SK[0], "lb", "oml"], writes=[SK[1]])
        P.A(lambda e: e.activation(out=lf[:], in_=t1[:], func=AF.Ln), reads=[SK[1]], writes=[SK[2]])
        P.V(lambda e: e.tensor_tensor_scan(out=bb[:], data0=cmask[:], data1=lf[:], initial=0.0, op0=ALU.mult, op1=ALU.add),
            reads=[SK[2], "cmask"], writes=[SK[3]])
        b3 = bb[:].rearrange("p (c t) -> p c t", t=HGC)
        P.V(lambda e, b3=b3: e.tensor_tensor(out=bq[:].rearrange("p (c t) -> p c t", t=HGC), in0=b3,
                                             in1=b3[:, :, 32:33].to_broadcast([128, 8, HGC]), op=ALU.subtract),
            reads=[SK[3]], writes=[SK[4]])
        P.A(lambda e: e.activation(out=eq[:], in_=bq[:], func=AF.Exp), reads=[SK[4]], writes=[SK[5]])
        P.A(lambda e: e.activation(out=ek[:], in_=bq[:], func=AF.Exp, scale=-1.0), reads=[SK[4]], writes=[SK[6]])
        P.A(lambda e, q_t=q_t: e.activation(out=qs[:], in_=q_t[:], func=AF.Silu), reads=[("hin", a, 0)], writes=[SK[7]])
        P.V(lambda e: e.scalar_tensor_tensor(out=Qt[:], in0=qs[:], scalar=QSC, in1=eq[:], op0=ALU.mult, op1=ALU.mult),
            reads=[SK[7], SK[5]], writes=[SBK[0]])
        P.V(lambda e: e.tensor_scalar(out=t1[:], in0=t1[:], scalar1=-1.0, scalar2=1.0, op0=ALU.mult, op1=ALU.add), reads=[SK[1]], writes=[SK[1]])
        P.V(lambda e: e.tensor_tensor(out=Kt[:], in0=t1[:], in1=ek[:], op=ALU.mult), reads=[SK[1], SK[6]], writes=[SBK[1]])
        if _DBG == 1:
            continue
        P.tiny_mode = True
        P.V(lambda e, b3=b3: e.tensor_copy(out=esc[:, 0:8], in_=b3[:, :, 32]), reads=[SK[3]], writes=["esc"])
        P.A(lambda e: e.activation(out=esc[:, 8:16], in_=esc[:, 0:8], func=AF.Exp), reads=["esc"], writes=["esc"])
        P.A(lambda e, b3=b3: e.activation(out=esc[:, 16:24], in_=b3[:, :, 63], func=AF.Exp), reads=[SK[3], "esc"], writes=["esc"])
        P.V(lambda e, b3=b3: e.tensor_tensor(out=esc[:, 24:32], in0=b3[:, :, 63], in1=esc[:, 0:8], op=ALU.subtract), reads=[SK[3], "esc"], writes=["esc"])
        P.A(lambda e: e.activation(out=esc[:, 24:32], in_=esc[:, 24:32], func=AF.Exp), reads=["esc"], writes=["esc"])
        if _DBG == 2:
            continue
        P.tiny_mode = False
        kt_ps = banks[0][:].bitcast(BF16)
        vt_ps = banks[1][:].bitcast(BF16)
        for n in range(8):
            P.T(lambda e, n=n: e.transpose(out=kt_ps[0:64, n * 128:(n + 1) * 128], in_=Kt[:, n * 64:(n + 1) * 64], identity=ident[:]),
                reads=[SBK[1], "ident"], writes=[BK[0]])
        for n in range(8):
            P.T(lambda e, n=n, i_t=i_t: e.transpose(out=vt_ps[0:64, n * 128:(n + 1) * 128], in_=i_t[:, n * 64:(n + 1) * 64], identity=ident[:]),
                reads=[("hin", a, 2), "ident"], writes=[BK[1]])
        P.A(lambda e: e.activation(out=KT[:], in_=kt_ps[0:64, :], func=AF.Copy), reads=[BK[0]], writes=["KT"])
        P.V(lambda e: e.tensor_copy(out=VT[:], in_=vt_ps[0:64, :]), reads=[BK[1]], writes=["VT"])
        if _DBG == 3:
            continue
        for n in range(8):
            P.T(lambda e, n=n: e.matmul(banks[2][0:64, n * 64:(n + 1) * 64], lhsT=Kt[:, n * 64:(n + 1) * 64], rhs=Qt[:, n * 64:(n + 1) * 64],
                                        start=True, stop=True), reads=[SBK[0], SBK[1]], writes=[BK[2]])
        P.V(lambda e: e.tensor_tensor(out=attm[0:64, :], in0=banks[2][0:64, :], in1=amask[:], op=ALU.mult), reads=[BK[2], "amask"], writes=[SBK[2]])
        if _DBG == 4:
            continue
        for n in range(8):
            bi = 3 + n // 4
            P.T(lambda e, n=n, bi=bi: e.matmul(banks[bi][:, (n % 4) * 128:(n % 4 + 1) * 128], lhsT=KT[:, n * 128:(n + 1) * 128],
                                               rhs=VT[:, n * 128:(n + 1) * 128], start=True, stop=True), reads=["KT", "VT"], writes=[BK[bi]])
        P.tiny_mode = True
        for n in range(8):
            bi = 3 + n // 4
            P.V(lambda e, n=n: e.tensor_scalar(out=Sb[:], in0=Sst[:], scalar1=esc[:, 8 + n:9 + n], scalar2=None, op0=ALU.mult),
                reads=["Sst", "esc"], writes=["Sb"])
            P.T(lambda e, n=n: e.matmul(banks[5][:, n * 64:(n + 1) * 64], lhsT=Sb[:], rhs=Qt[:, n * 64:(n + 1) * 64], start=True, stop=False),
                reads=["Sb", SBK[0]], writes=[BK[5]])
            P.T(lambda e, n=n: e.matmul(banks[5][:, n * 64:(n + 1) * 64], lhsT=VT[:, n * 128:(n + 1) * 128], rhs=attm[0:64, n * 64:(n + 1) * 64],
                                        start=False, stop=True), reads=["VT", SBK[2]], writes=[BK[5]])
            P.V(lambda e, n=n: e.tensor_scalar(out=Sst[:], in0=Sst[:], scalar1=esc[:, 16 + n:17 + n], scalar2=None, op0=ALU.mult),
                reads=["Sst", "esc"], writes=["Sst"])
            P.V(lambda e, n=n, bi=bi: e.scalar_tensor_tensor(out=Sst[:], in0=banks[bi][:, (n % 4) * 128:(n % 4 + 1) * 128], scalar=esc[:, 24 + n:25 + n],
                                                             in1=Sst[:], op0=ALU.mult, op1=ALU.add), reads=[BK[bi], "Sst", "esc"], writes=["Sst"])
        P.tiny_mode = False
        if _DBG == 5:
            continue
        if _DBG != 11:
            pass
        if _DBG != 10:
            P.V(lambda e: e.tensor_copy(out=osb[:], in_=banks[5][:]), reads=[BK[5]], writes=[SK[8]])
        if _DBG == 20 and t == 0:
            P.dma("sync", lambda e: e.dma_start(out=zb_o[:, 0:512], in_=Qt[:]), reads=[SBK[0]], key="dbg0")
            P.dma("sync", lambda e: e.dma_start(out=zb_o[:, 512:1024], in_=Kt[:]), reads=[SBK[1]], key="dbg1")
            P.dma("sync", lambda e: e.dma_start(out=zb_o[0:64, 1024:1536], in_=attm[0:64, :]), reads=[SBK[2]], key="dbg2")
            P.dma("gpsimd", lambda e: e.dma_start(out=zb_o[:, 1536:2048], in_=bb[:]), reads=[SK[3]], key="dbg3")
            P.dma("sync", lambda e: e.dma_start(out=yd_o[0:64, 0:1024], in_=KT[:]), reads=["KT"], key="dbg4")
            P.dma("sync", lambda e: e.dma_start(out=yd_o[0:64, 1024:2048], in_=VT[:]), reads=["VT"], key="dbg5")
            P.dma("gpsimd", lambda e: e.dma_start(out=yd_o[:, 2048:2560], in_=osb[:]), reads=[SK[8]], key="dbg6")
            P.dma("gpsimd", lambda e: e.dma_start(out=yd_o[:, 2560:2688], in_=Sst[:]), reads=["Sst"], key="dbg7")
            P.dma("gpsimd", lambda e: e.dma_start(out=yd_o[:, 2688:2720], in_=esc[:]), reads=["esc"], key="dbg8")
        P.A(lambda e: e.activation(out=osq[:], in_=osb[:], func=AF.Square), reads=[SK[8]], writes=[SBK[3]])
        if _DBG in (10, 11):
            continue
        if _DBG == 6:
            continue
        P.T(lambda e: e.matmul(banks[6][:], lhsT=ones[:], rhs=osq[:], start=True, stop=True), reads=["ones", SBK[3]], writes=[BK[6]])
        P.A(lambda e: e.activation(out=rst[:], in_=banks[6][:], func=AF.Sqrt, bias=EPS, scale=1.0 / 128), reads=[BK[6]], writes=[SK[9]])
        P.V(lambda e: e.reciprocal(out=rst[:], in_=rst[:]), reads=[SK[9]], writes=[SK[9]])
        if _DBG == 7:
            continue
        P.A(lambda e, g_t=g_t: e.activation(out=sgl[:], in_=g_t[:], func=AF.Silu), reads=[("hin", a, 3)], writes=[SK[10]])
        P.V(lambda e: e.scalar_tensor_tensor(out=osb[:], in0=osb[:], scalar=hgp[:, 2:3], in1=rst[:], op0=ALU.mult, op1=ALU.mult),
            reads=[SK[8], SK[9], "hgp"], writes=[SK[8]])
        if _DBG == 8:
            continue
        yo = SB[4 + a]
        P.V(lambda e, yo=yo: e.tensor_tensor(out=yo[:], in0=osb[:], in1=sgl[:], op=ALU.mult), reads=[SK[8], SK[10]], writes=[SBK[4 + a]])
        if _DBG == 9:
            continue
        P.dma("sync", lambda e, yo=yo, tsl=tsl: e.dma_start(out=ya_o[:, tsl], in_=yo[:]), reads=[SBK[4 + a]], key=("yo", a))

    s5v = s5p[:].rearrange("p (j k) -> p j k", k=67)
    a_re, a_im, ldt = s5v[:, :, 0], s5v[:, :, 1], s5v[:, :, 2]
    sm = kb.sb("s5small", [128, 64])
    def col(i):
        return sm[:, 4 * i:4 * i + 4]
    dt_, adt, mag, th, cth, sth, abr, abi, den, m1, zr, zi, tA, tB, c512, s512 = (col(i) for i in range(16))
    smi = kb.sb("s5smalli", [128, 4], I32)
    P.tiny_mode = True
    P.A(lambda e: e.activation(out=dt_, in_=ldt, func=AF.Exp), reads=["s5p"], writes=["sm_dt"])
    P.V(lambda e: e.tensor_tensor(out=adt, in0=a_re, in1=dt_, op=ALU.mult), reads=["s5p", "sm_dt"], writes=["sm_adt"])
    P.A(lambda e: e.activation(out=mag, in_=adt, func=AF.Exp), reads=["sm_adt"], writes=["sm_mag"])
    P.V(lambda e: e.tensor_tensor(out=th, in0=a_im, in1=dt_, op=ALU.mult), reads=["s5p", "sm_dt"], writes=["sm_th"])
    sin_table(sth, th, [128, 4], 0.0, tA, smi[:], ["sm_th"], ["sm_sth"])
    sin_table(cth, th, [128, 4], math.pi / 2, tA, smi[:], ["sm_th"], ["sm_cth"])
    P.V(lambda e: e.tensor_scalar(out=tB, in0=th, scalar1=float(TT), scalar2=None, op0=ALU.mult), reads=["sm_th"], writes=["sm_tB"])
    sin_table(s512, tB, [128, 4], 0.0, tA, smi[:], ["sm_tB"], ["sm_s512"])
    sin_table(c512, tB, [128, 4], math.pi / 2, tA, smi[:], ["sm_tB"], ["sm_c512"])
    ns512 = kb.sb("ns512", [128, 4])
    P.V(lambda e: e.tensor_scalar(out=ns512[:], in0=s512, scalar1=-1.0, scalar2=None, op0=ALU.mult), reads=["sm_s512"], writes=["ns512"])
    P.V(lambda e: e.tensor_tensor(out=abr, in0=mag, in1=cth, op=ALU.mult), reads=["sm_mag", "sm_cth"], writes=["sm_abr"])
    P.V(lambda e: e.tensor_tensor(out=abi, in0=mag, in1=sth, op=ALU.mult), reads=["sm_mag", "sm_sth"], writes=["sm_abi"])
    P.V(lambda e: e.tensor_tensor(out=den, in0=a_re, in1=a_re, op=ALU.mult), reads=["s5p"], writes=["sm_den"])
    P.V(lambda e: e.tensor_tensor(out=tA, in0=a_im, in1=a_im, op=ALU.mult), reads=["s5p", "sm_c512"], writes=["sin_tmp"])
    P.V(lambda e: e.tensor_tensor(out=den, in0=den, in1=tA, op=ALU.add), reads=["sm_den", "sin_tmp"], writes=["sm_den"])
    P.V(lambda e: e.reciprocal(out=den, in_=den), reads=["sm_den"], writes=["sm_den"])
    P.V(lambda e: e.tensor_scalar(out=m1, in0=abr, scalar1=-1.0, scalar2=None, op0=ALU.add), reads=["sm_abr"], writes=["sm_m1"])
    P.V(lambda e: e.tensor_tensor(out=zr, in0=m1, in1=a_re, op=ALU.mult), reads=["sm_m1", "s5p"], writes=["sm_zr"])
    P.V(lambda e: e.tensor_tensor(out=tA, in0=abi, in1=a_im, op=ALU.mult), reads=["sm_abi", "s5p"], writes=["sin_tmp"])
    P.V(lambda e: e.tensor_tensor(out=zr, in0=zr, in1=tA, op=ALU.add), reads=["sm_zr", "sin_tmp"], writes=["sm_zr"])
    P.V(lambda e: e.tensor_tensor(out=zr, in0=zr, in1=den, op=ALU.mult), reads=["sm_zr", "sm_den"], writes=["sm_zr"])
    P.V(lambda e: e.tensor_tensor(out=zi, in0=abi, in1=a_re, op=ALU.mult), reads=["sm_abi", "s5p"], writes=["sm_zi"])
    P.V(lambda e: e.tensor_tensor(out=tA, in0=m1, in1=a_im, op=ALU.mult), reads=["sm_m1", "s5p"], writes=["sin_tmp"])
    P.V(lambda e: e.tensor_tensor(out=zi, in0=zi, in1=tA, op=ALU.subtract), reads=["sm_zi", "sin_tmp"], writes=["sm_zi"])
    P.V(lambda e: e.tensor_tensor(out=zi, in0=zi, in1=den, op=ALU.mult), reads=["sm_zi", "sm_den"], writes=["sm_zi"])
    Bex = [kb.sb("Bex%d" % i, [128, 4 * 128], BF16) for i in range(2)]
    Cex = [kb.sb("Cex%d" % i, [128, 4 * 128], BF16) for i in range(2)]
    BT = [kb.sb("BT%d" % i, [128, 4 * 128], BF16) for i in range(2)]
    bbt = kb.sb("bbt", [128, 64])
    for i in range(2):
        P.V(lambda e, i=i: e.memset(Bex[i][:], 0.0), writes=[("Bex", i)])
        P.V(lambda e, i=i: e.memset(Cex[i][:], 0.0), writes=[("Cex", i)])
    for j in range(4):
        bre, bim = s5v[:, j, 3:19], s5v[:, j, 19:35]
        cre, cim = s5v[:, j, 35:51], s5v[:, j, 51:67]
        zrj, zij = zr[:, j:j + 1], zi[:, j:j + 1]
        P.V(lambda e, bre=bre, zrj=zrj: e.tensor_scalar(out=bbt[:, 0:16], in0=bre, scalar1=zrj, scalar2=None, op0=ALU.mult), reads=["s5p", "sm_zr"], writes=["bbt"])
        P.V(lambda e, bim=bim, zij=zij: e.tensor_scalar(out=bbt[:, 16:32], in0=bim, scalar1=zij, scalar2=None, op0=ALU.mult), reads=["s5p", "sm_zi"], writes=["bbt"])
        P.V(lambda e, bim=bim, zrj=zrj: e.tensor_scalar(out=bbt[:, 32:48], in0=bim, scalar1=zrj, scalar2=None, op0=ALU.mult), reads=["s5p", "sm_zr"], writes=["bbt"])
        P.V(lambda e, bre=bre, zij=zij: e.tensor_scalar(out=bbt[:, 48:64], in0=bre, scalar1=zij, scalar2=None, op0=ALU.mult), reads=["s5p", "sm_zi"], writes=["bbt"])
        for hh_ in range(2):
            ps_ = slice(64 * hh_, 64 * hh_ + 64)
            cs = slice(j * 128 + (2 * j + hh_) * 16, j * 128 + (2 * j + hh_) * 16 + 16)
            P.V(lambda e, ps_=ps_, cs=cs: e.tensor_tensor(out=Bex[0][ps_, cs], in0=bbt[ps_, 0:16], in1=bbt[ps_, 16:32], op=ALU.subtract), reads=["bbt"], writes=[("Bex", 0)])
            P.V(lambda e, ps_=ps_, cs=cs: e.tensor_tensor(out=Bex[1][ps_, cs], in0=bbt[ps_, 32:48], in1=bbt[ps_, 48:64], op=ALU.add), reads=["bbt"], writes=[("Bex", 1)])
            P.V(lambda e, ps_=ps_, cs=cs, cre=cre: e.tensor_copy(out=Cex[0][ps_, cs], in_=cre[ps_, :]), reads=["s5p"], writes=[("Cex", 0)])
            P.V(lambda e, ps_=ps_, cs=cs, cim=cim: e.tensor_scalar(out=Cex[1][ps_, cs], in0=cim[ps_, :], scalar1=-1.0, scalar2=None, op0=ALU.mult), reads=["s5p"], writes=[("Cex", 1)])
    P.tiny_mode = False
    for i in range(2):
        bt_ps = banks[i][:].bitcast(BF16)
        for j in range(4):
            P.T(lambda e, i=i, j=j, bt_ps=bt_ps: e.transpose(out=bt_ps[:, j * 128:(j + 1) * 128], in_=Bex[i][:, j * 128:(j + 1) * 128], identity=ident[:]),
                reads=[("Bex", i), "ident"], writes=[BK[i]])
        P.V(lambda e, i=i, bt_ps=bt_ps: e.tensor_copy(out=BT[i][:], in_=bt_ps[:, 0:512]), reads=[BK[i]], writes=[("BT", i)])
    cosT = [kb.sb("cosT%d" % j, [128, TT]) for j in range(4)]
    sinT = [kb.sb("sinT%d" % j, [128, TT]) for j in range(4)]
    for j in range(4):
        ang = S[0]
        P.V(lambda e, j=j, ang=ang: e.tensor_scalar(out=ang[:], in0=tau[:], scalar1=th[:, j:j + 1], scalar2=None, op0=ALU.mult), reads=["tau", "sm_th"], writes=[SK[0]])
        sin_table(sinT[j][:], ang[:], [128, TT], 0.0, S[1][:], tmpi[:], [SK[0]], [("sinT", j)], tkey=SK[1])
        sin_table(cosT[j][:], ang[:], [128, TT], math.pi / 2, S[1][:], tmpi[:], [SK[0]], [("cosT", j)], tkey=SK[1])
    init = kb.sb("s5init", [128, 8])
    P.V(lambda e: e.memset(init[:], 0.0), writes=["s5init"])
    s5in = [kb.sb("s5in%d" % i, [128, TT], BF16) for i in range(2)]
    tmpc = kb.sb("s5tmpc", [128, 2])
    for t in (range(NT_L) if "s" in parts else ()):
        a = t % 2
        tsl = slice(t * TT, (t + 1) * TT)
        u_t = s5in[a]
        P.dma("sync", lambda e, u_t=u_t, tsl=tsl: e.dma_start(out=u_t[:], in_=s5u_d[:, tsl]), writes=[("s5in", a)], key=("s5in", a))
        for j in range(4):
            jsl = slice(j * 128, (j + 1) * 128)
            b_re, b_im = banks[(2 * j) % 4], banks[(2 * j + 1) % 4]
            kre, kim = BK[(2 * j) % 4], BK[(2 * j + 1) % 4]
            P.T(lambda e, jsl=jsl, b_re=b_re, u_t=u_t: e.matmul(b_re[:], lhsT=BT[0][:, jsl], rhs=u_t[:], start=True, stop=True), reads=[("BT", 0), ("s5in", a)], writes=[kre])
            P.T(lambda e, jsl=jsl, b_im=b_im, u_t=u_t: e.matmul(b_im[:], lhsT=BT[1][:, jsl], rhs=u_t[:], start=True, stop=True), reads=[("BT", 1), ("s5in", a)], writes=[kim])
            w1, w2, wnr, wni, wr, wi = S[2], S[3], S[4], S[5], S[6], S[7]
            cT, sT = cosT[j], sinT[j]
            rd = [("cosT", j), ("sinT", j)]
            P.V(lambda e, cT=cT, b_re=b_re: e.tensor_tensor(out=w1[:], in0=b_re[:], in1=cT[:], op=ALU.mult), reads=[kre] + rd, writes=[SK[2]])
            P.V(lambda e, sT=sT, b_im=b_im: e.tensor_tensor(out=w2[:], in0=b_im[:], in1=sT[:], op=ALU.mult), reads=[kim] + rd, writes=[SK[3]])
            P.V(lambda e: e.tensor_tensor(out=wnr[:], in0=w1[:], in1=w2[:], op=ALU.add), reads=[SK[2], SK[3]], writes=[SK[4]])
            P.V(lambda e, cT=cT, b_im=b_im: e.tensor_tensor(out=w1[:], in0=b_im[:], in1=cT[:], op=ALU.mult), reads=[kim] + rd, writes=[SK[2]])
            P.V(lambda e, sT=sT, b_re=b_re: e.tensor_tensor(out=w2[:], in0=b_re[:], in1=sT[:], op=ALU.mult), reads=[kre] + rd, writes=[SK[3]])
            P.V(lambda e: e.tensor_tensor(out=wni[:], in0=w1[:], in1=w2[:], op=ALU.subtract), reads=[SK[2], SK[3]], writes=[SK[5]])
            P.V(lambda e, j=j: e.tensor_tensor_scan(out=wr[:], data0=mag[:, j:j + 1].to_broadcast([128, TT]), data1=wnr[:], initial=init[:, j:j + 1],
                                                    op0=ALU.mult, op1=ALU.add), reads=[SK[4], "sm_mag", "s5init"], writes=[SK[6]])
            P.V(lambda e, j=j: e.tensor_tensor_scan(out=wi[:], data0=mag[:, j:j + 1].to_broadcast([128, TT]), data1=wni[:], initial=init[:, 4 + j:5 + j],
                                                    op0=ALU.mult, op1=ALU.add), reads=[SK[5], "sm_mag", "s5init"], writes=[SK[7]])
            P.tiny_mode = True
            P.V(lambda e, j=j: e.tensor_tensor(out=tmpc[:, 0:1], in0=wr[:, TT - 1:TT], in1=c512[:, j:j + 1], op=ALU.mult), reads=[SK[6], "sm_c512"], writes=["tmpc"])
            P.V(lambda e, j=j: e.tensor_tensor(out=tmpc[:, 1:2], in0=wr[:, TT - 1:TT], in1=s512[:, j:j + 1], op=ALU.mult), reads=[SK[6], "sm_s512"], writes=["tmpc"])
            P.V(lambda e, j=j: e.scalar_tensor_tensor(out=init[:, j:j + 1], in0=wi[:, TT - 1:TT], scalar=ns512[:, j:j + 1], in1=tmpc[:, 0:1], op0=ALU.mult, op1=ALU.add),
                reads=[SK[7], "ns512", "tmpc"], writes=["s5init"])
            P.V(lambda e, j=j: e.scalar_tensor_tensor(out=init[:, 4 + j:5 + j], in0=wi[:, TT - 1:TT], scalar=c512[:, j:j + 1], in1=tmpc[:, 1:2], op0=ALU.mult, op1=ALU.add),
                reads=[SK[7], "sm_c512", "tmpc"], writes=["s5init"])
            P.tiny_mode = False
            xr, xi = SB[0 + 2 * (j % 2)], SB[1 + 2 * (j % 2)]
            kxr, kxi = SBK[0 + 2 * (j % 2)], SBK[1 + 2 * (j % 2)]
            P.V(lambda e, cT=cT: e.tensor_tensor(out=w1[:], in0=wr[:], in1=cT[:], op=ALU.mult), reads=[SK[6]] + rd, writes=[SK[2]])
            P.V(lambda e, sT=sT: e.tensor_tensor(out=w2[:], in0=wi[:], in1=sT[:], op=ALU.mult), reads=[SK[7]] + rd, writes=[SK[3]])
            P.V(lambda e, xr=xr: e.tensor_tensor(out=xr[:], in0=w1[:], in1=w2[:], op=ALU.subtract), reads=[SK[2], SK[3]], writes=[kxr])
            P.V(lambda e, sT=sT: e.tensor_tensor(out=w1[:], in0=wr[:], in1=sT[:], op=ALU.mult), reads=[SK[6]] + rd, writes=[SK[2]])
            P.V(lambda e, cT=cT: e.tensor_tensor(out=w2[:], in0=wi[:], in1=cT[:], op=ALU.mult), reads=[SK[7]] + rd, writes=[SK[3]])
            P.V(lambda e, xi=xi: e.tensor_tensor(out=xi[:], in0=w1[:], in1=w2[:], op=ALU.add), reads=[SK[2], SK[3]], writes=[kxi])
            P.T(lambda e, jsl=jsl, xr=xr, j=j: e.matmul(banks[4][:], lhsT=Cex[0][:, jsl], rhs=xr[:], start=(j == 0), stop=False), reads=[("Cex", 0), kxr], writes=[BK[4]])
            P.T(lambda e, jsl=jsl, xi=xi, j=j: e.matmul(banks[4][:], lhsT=Cex[1][:, jsl], rhs=xi[:], start=False, stop=(j == 3)), reads=[("Cex", 1), kxi], writes=[BK[4]])
        yv, y2, sgm = S[8], S[9], S[10]
        P.V(lambda e, u_t=u_t: e.scalar_tensor_tensor(out=yv[:], in0=u_t[:], scalar=s5d[:, 0:1], in1=banks[4][:], op0=ALU.mult, op1=ALU.add),
            reads=[("s5in", a), "s5d", BK[4]], writes=[SK[8]])
        P.V(lambda e: e.tensor_tensor(out=y2[:], in0=yv[:], in1=yv[:], op=ALU.mult), reads=[SK[8]], writes=[SK[9]])
        P.V(lambda e: e.tensor_scalar(out=y2[:], in0=y2[:], scalar1=0.044715, scalar2=1.0, op0=ALU.mult, op1=ALU.add), reads=[SK[9]], writes=[SK[9]])
        P.V(lambda e: e.tensor_tensor(out=y2[:], in0=y2[:], in1=yv[:], op=ALU.mult), reads=[SK[9], SK[8]], writes=[SK[9]])
        P.A(lambda e: e.activation(out=sgm[:], in_=y2[:], func=AF.Sigmoid, scale=2.0 * math.sqrt(2.0 / math.pi)), reads=[SK[9]], writes=[SK[10]])
        zo = SB[4 + a]
        P.V(lambda e, zo=zo: e.tensor_tensor(out=zo[:], in0=yv[:], in1=sgm[:], op=ALU.mult), reads=[SK[8], SK[10]], writes=[SBK[4 + a]])
        P.dma("sync", lambda e, zo=zo, tsl=tsl: e.dma_start(out=zb_o[:, tsl], in_=zo[:]), reads=[SBK[4 + a]], key=("yo", a))

    qr = kb.sb("qr", [128, L], BF16); kr = kb.sb("kr", [128, L], BF16); vv = kb.sb("vv", [128, L], BF16)
    Oacc = kb.sb("Oacc", [128, L]); Dacc = kb.sb("Dacc", [128, L])
    posi = kb.sb("posi", [32, TT], I32)
    vtok = [kb.sb("vtok%d" % i, [128, 4 * 128], BF16) for i in range(2)]
    for i in range(2):
        P.V(lambda e, i=i: e.memset(vtok[i][:], 0.0), writes=[("vtok", i)])
    SCL = float(128 ** -0.5)
    for gi, dil in (enumerate((1, 4, 16)) if "a" in parts else ()):
        for t in range(NT_L):
            tsl = slice(t * TT, (t + 1) * TT)
            for i, dst in enumerate((qr, kr, vv)):
                P.dma("sync", lambda e, i=i, dst=dst, tsl=tsl, gi=gi: e.dma_start(out=dst[:, tsl], in_=att_d[3 * gi + i, :, tsl]),
                      writes=[("att", i, t)], key=("attld", i, t))
            P.dma("sync", lambda e, tsl=tsl: e.dma_start(out=posi[:], in_=pos_d[:, tsl].partition_broadcast(32)), writes=["posi"], key="posi")
            ang, tmpf, sinS, cosS, r1, r2 = S[0], S[1], S[2], S[3], S[4], S[5]
            P.V(lambda e: e.tensor_copy(out=ang[0:32, :], in_=posi[:]), reads=["posi"], writes=[SK[0]])
            P.V(lambda e: e.tensor_scalar(out=ang[0:32, :], in0=ang[0:32, :], scalar1=ropec[:, 0:1], scalar2=None, op0=ALU.mult), reads=[SK[0], "ropec"], writes=[SK[0]])
            sin_table(sinS[0:32, :], ang[0:32, :], [32, TT], 0.0, tmpf[0:32, :], tmpi[0:32, :], [SK[0]], [SK[2]], tkey=SK[1])
            P.V(lambda e: e.tensor_scalar(out=sinS[0:32, :], in0=sinS[0:32, :], scalar1=ropec[:, 1:2], scalar2=None, op0=ALU.mult), reads=[SK[2], "ropec"], writes=[SK[2]])
            sin_table(cosS[0:32, :], ang[0:32, :], [32, TT], math.pi / 2, tmpf[0:32, :], tmpi[0:32, :], [SK[0]], [SK[3]], tkey=SK[1])
            for i, dst in enumerate((qr, kr)):
                bsw = banks[i]
                P.T(lambda e, dst=dst, tsl=tsl, bsw=bsw: e.matmul(bsw[0:32, :], lhsT=psw[:], rhs=dst[0:32, tsl], start=True, stop=True),
                    reads=["psw", ("att", i, t)], writes=[BK[i]])
                P.V(lambda e, bsw=bsw: e.tensor_tensor(out=r1[0:32, :], in0=bsw[0:32, :], in1=sinS[0:32, :], op=ALU.mult), reads=[BK[i], SK[2]], writes=[SK[4]])
                P.V(lambda e, dst=dst, tsl=tsl: e.tensor_tensor(out=r2[0:32, :], in0=dst[0:32, tsl], in1=cosS[0:32, :], op=ALU.mult), reads=[("att", i, t), SK[3]], writes=[SK[5]])
                P.V(lambda e, dst=dst, tsl=tsl: e.tensor_tensor(out=dst[0:32, tsl], in0=r1[0:32, :], in1=r2[0:32, :], op=ALU.add), reads=[SK[4], SK[5]], writes=[("att", i, t)])
        nper = L // dil // 128
        allkeys = [("att", i, t) for i in range(3) for t in range(NT_L)]
        for r in range(dil):
            for qd in range(nper // 4):
                n0 = qd * 4
                def toks(n):
                    st = r + dil * 128 * n
                    return slice(st, st + dil * 127 + 1, dil)
                vt_ps = banks[2][:].bitcast(BF16)
                vcur = vtok[(r * (nper // 4) + qd) % 2]
                vprev = vtok[(r * (nper // 4) + qd + 1) % 2]
                kvc, kvp = ("vtok", (r * (nper // 4) + qd) % 2), ("vtok", (r * (nper // 4) + qd + 1) % 2)
                for k_ in range(4):
                    P.T(lambda e, k_=k_, tk=toks(n0 + k_): e.transpose(out=vt_ps[:, k_ * 128:(k_ + 1) * 128], in_=vv[:, tk], identity=ident[:]),
                        reads=allkeys[2 * NT_L:] + ["ident"], writes=[BK[2]])
                P.A(lambda e, vcur=vcur: e.activation(out=vcur[:], in_=vt_ps[:, 0:512], func=AF.Copy), reads=[BK[2]], writes=[kvc])
                pm = [SB[0], SB[1]]
                for pr in range(2):
                    sb_ = banks[pr]
                    for k2 in range(2):
                        n = n0 + pr * 2 + k2
                        P.T(lambda e, tk=toks(n), k2=k2, sb_=sb_: e.matmul(sb_[:, k2 * 256:k2 * 256 + 128], lhsT=kr[:, tk], rhs=qr[:, tk], start=True, stop=True),
                            reads=allkeys[:2 * NT_L], writes=[BK[pr]])
                        np_ = n - 1 if n > 0 else n
                        P.T(lambda e, tk=toks(n), tkp=toks(np_), k2=k2, sb_=sb_: e.matmul(sb_[:, k2 * 256 + 128:k2 * 256 + 256], lhsT=kr[:, tkp], rhs=qr[:, tk], start=True, stop=True),
                            reads=allkeys[:2 * NT_L], writes=[BK[pr]])
                    ex = S[6 + pr]
                    P.A(lambda e, ex=ex, sb_=sb_: e.activation(out=ex[:], in_=sb_[:], func=AF.Exp, scale=SCL), reads=[BK[pr]], writes=[SK[6 + pr]])
                    mk = pmask[:, 0:TT] if (n0 == 0 and pr == 0) else pmask[:, TT:2 * TT]
                    P.V(lambda e, ex=ex, mk=mk, pr=pr: e.tensor_tensor(out=pm[pr][:], in0=ex[:], in1=mk, op=ALU.mult), reads=[SK[6 + pr], "pmask"], writes=[SBK[pr]])
                for k_ in range(4):
                    pr, k2 = k_ // 2, k_ % 2
                    pc = pm[pr][:, k2 * 256:k2 * 256 + 128]
                    pp = pm[pr][:, k2 * 256 + 128:k2 * 256 + 256]
                    vc = vcur[:, k_ * 128:(k_ + 1) * 128]
                    vp = vcur[:, (k_ - 1) * 128:k_ * 128] if k_ > 0 else vprev[:, 3 * 128:4 * 128]
                    osl = slice(k_ * 128, (k_ + 1) * 128)
                    P.T(lambda e, vc=vc, pc=pc, osl=osl: e.matmul(banks[3][:, osl], lhsT=vc, rhs=pc, start=True, stop=False), reads=[kvc, SBK[pr]], writes=[BK[3]])
                    P.T(lambda e, vp=vp, pp=pp, osl=osl: e.matmul(banks[3][:, osl], lhsT=vp, rhs=pp, start=False, stop=True), reads=[kvc, kvp, SBK[pr]], writes=[BK[3]])
                    P.T(lambda e, pc=pc, osl=osl: e.matmul(banks[4][:, osl], lhsT=ones[:], rhs=pc, start=True, stop=False), reads=["ones", SBK[pr]], writes=[BK[4]])
                    P.T(lambda e, pp=pp, osl=osl: e.matmul(banks[4][:, osl], lhsT=ones[:], rhs=pp, start=False, stop=True), reads=["ones", SBK[pr]], writes=[BK[4]])
                st = r + dil * 128 * n0
                dsl = slice(st, st + dil * 511 + 1, dil)
                if gi == 0:
                    P.A(lambda e, dsl=dsl: e.activation(out=Oacc[:, dsl], in_=banks[3][:], func=AF.Copy), reads=[BK[3]], writes=["Oacc"])
                    P.V(lambda e, dsl=dsl: e.tensor_copy(out=Dacc[:, dsl], in_=banks[4][:]), reads=[BK[4]], writes=["Dacc"])
                else:
                    P.V(lambda e, dsl=dsl: e.tensor_tensor(out=Oacc[:, dsl], in0=banks[3][:], in1=Oacc[:, dsl], op=ALU.add), reads=[BK[3], "Oacc"], writes=["Oacc"])
                    P.V(lambda e, dsl=dsl: e.tensor_tensor(out=Dacc[:, dsl], in0=banks[4][:], in1=Dacc[:, dsl], op=ALU.add), reads=[BK[4], "Dacc"], writes=["Dacc"])
    for t in (range(NT_L) if "a" in parts else ()):
        a = t % 2
        tsl = slice(t * TT, (t + 1) * TT)
        rc = S[8 + a]
        P.V(lambda e, rc=rc, tsl=tsl: e.reciprocal(out=rc[:], in_=Dacc[:, tsl]), reads=["Dacc"], writes=[SK[8 + a]])
        yo = SB[4 + a]
        P.V(lambda e, rc=rc, tsl=tsl, yo=yo: e.tensor_tensor(out=yo[:], in0=Oacc[:, tsl], in1=rc[:], op=ALU.mult), reads=["Oacc", SK[8 + a]], writes=[SBK[4 + a]])
        P.dma("sync", lambda e, yo=yo, tsl=tsl: e.dma_start(out=yd_o[:, tsl], in_=yo[:]), reads=[SBK[4 + a]], key=("yo", a))
    return kb.finish()


CONV_W = 31
HALO = CONV_W - 1


def build_C(final):
    kb = KB()
    P = kb.P
    G = 512
    x1_d = kb.din("x1T", [D, TOK])
    ya_d = kb.din("yaT", [512, TOK], BF16)
    zb_d = kb.din("zbT", [512, TOK], BF16)
    yd_d = kb.din("ydT", [512, TOK], BF16)
    cv_d = kb.din("convT", [1024, TOK + HALO], BF16)
    gn_d = kb.din("gains3", [128, 24])
    cvp_d = kb.din("convp", [128, 4 * 34])
    wgt_d = kb.din("w_gates", [D, 4096])
    wbr_d = kb.din("w_branch", [4, 512, D])
    wo_d = kb.din("w_out", [D, D])
    wglu_d = kb.din("w_glu", [512, 1024])
    wg_d = kb.din("w_gate", [D, DFF])
    wu_d = kb.din("w_up", [D, DFF])
    wd_d = kb.din("w_down", [DFF, D])
    out_o = kb.dout("outT", [D, TOK])
    tp = TokenPhase(kb, G)
    gn = kb.sb("gn", [128, 24]); cvp = kb.sb("cvp", [128, 4 * 34])
    P.dma("sync", lambda e: e.dma_start(out=gn[:], in_=gn_d), writes=["gains"], key="gains")
    P.dma("sync", lambda e: e.dma_start(out=cvp[:], in_=cvp_d), writes=["cvp"], key="cvp")
    cvv = cvp[:].rearrange("p (c k) -> p c k", k=34)
    ybuf = [kb.sb("ybuf%d" % k, [128, 4 * G], BF16) for k in range(4)]
    zbt = kb.sb("zbt", [128, 4 * G], BF16)
    mrg = kb.sb("mrg", [128, 8 * G], BF16)
    ca = kb.sb("ca", [128, G + HALO], BF16); cb = kb.sb("cb", [128, G + HALO], BF16)
    zc = kb.sb("zc", [128, G + HALO]); sgb = kb.sb("sgb", [128, G + HALO])
    vch = [kb.sb("vch%d" % c, [128, G]) for c in range(4)]
    vb = kb.sb("vb", [128, 4 * G], BF16); vsq = kb.sb("vsq", [128, 4 * G], BF16)
    mean = kb.sb("mean", [128, G]); var = kb.sb("var", [128, G]); tmpc = kb.sb("tmpcv", [128, G])
    macc = [kb.sb("macc%d" % i, [128, G]) for i in range(2)]
    sgt = [kb.sb("sgt%d" % i, [128, G]) for i in range(2)]
    ostage = [kb.sb("ostage%d" % i, [128, G]) for i in range(2)]
    x1_v = x1_d.rearrange("(c p) t -> p c t", p=128)
    out_v = out_o.rearrange("(c p) t -> p c t", p=128)
    wgt_v = wgt_d.rearrange("(c p) n -> p c n", p=128)
    wo_v = wo_d.rearrange("(c p) n -> p c n", p=128)
    wglu_v = wglu_d.rearrange("(c p) n -> p c n", p=128)
    xkeys = [("x", c, 0) for c in range(8)]
    for grp in range(TOK // G):
        t0 = grp * G
        xsb = tp.xT[:].rearrange("p (c t) -> p c t", t=G)
        P.dma("sync", lambda e, t0=t0: e.dma_start(out=xsb, in_=x1_v[:, :, t0:t0 + G]), writes=xkeys, key="xload")
        for k, src in ((0, ya_d), (3, yd_d)):
            P.dma("sync", lambda e, k=k, src=src, t0=t0: e.dma_start(out=ybuf[k][:].rearrange("p (c t) -> p c t", t=G),
                                                                  in_=src.rearrange("(c p) t -> p c t", p=128)[:, :, t0:t0 + G]),
                  writes=[("ybuf", k)], key=("ybuf", k))
        P.dma("sync", lambda e, t0=t0: e.dma_start(out=zbt[:].rearrange("p (c t) -> p c t", t=G),
                                                 in_=zb_d.rearrange("(c p) t -> p c t", p=128)[:, :, t0:t0 + G]), writes=["zbt"], key="zbt")
        tp.rmsnorm(gn[:, 0:8], 0)
        for cbk in range(2):
            wa, wak = tp.wload(wglu_v[:, :, cbk * 256:(cbk + 1) * 256], 4, 256)
            wgl, wgk = tp.wload(wglu_v[:, :, 512 + cbk * 256:512 + (cbk + 1) * 256], 4, 256)
            for sub in range(2):
                c = cbk * 2 + sub
                ba, bak = tp.bank("gate")
                kb.mm_group(ba[:], bak, [(wa[:, k4, sub * 128:(sub + 1) * 128], zbt[:, k4 * G:(k4 + 1) * G]) for k4 in range(4)], reads=[wak, "zbt"])
                bg, bgk = tp.bank("up")
                kb.mm_group(bg[:], bgk, [(wgl[:, k4, sub * 128:(sub + 1) * 128], zbt[:, k4 * G:(k4 + 1) * G]) for k4 in range(4)], reads=[wgk, "zbt"])
                si = kb.rr("sgt", 2)
                P.A(lambda e, si=si, bg=bg: e.activation(out=sgt[si][:], in_=bg[:], func=AF.Sigmoid), reads=[bgk], writes=[("sgt", si)])
                P.V(lambda e, si=si, ba=ba, c=c: e.tensor_tensor(out=ybuf[1][:, c * G:(c + 1) * G], in0=ba[:], in1=sgt[si][:], op=ALU.mult),
                    reads=[bak, ("sgt", si)], writes=[("ybuf", 1)])
        for c in range(4):
            P.dma("sync", lambda e, c=c, t0=t0: e.dma_start(out=ca[:], in_=cv_d[c * 128:(c + 1) * 128, t0:t0 + G + HALO]), writes=["ca"], key="ca")
            P.dma("sync", lambda e, c=c, t0=t0: e.dma_start(out=cb[:], in_=cv_d[512 + c * 128:512 + (c + 1) * 128, t0:t0 + G + HALO]), writes=["cb"], key="cb")
            P.A(lambda e: e.activation(out=sgb[:], in_=cb[:], func=AF.Sigmoid), reads=["cb"], writes=["sgb"])
            P.V(lambda e: e.tensor_tensor(out=zc[:], in0=ca[:], in1=sgb[:], op=ALU.mult), reads=["ca", "sgb"], writes=["zc"])
            v = vch[c]
            P.V(lambda e, v=v, c=c: e.tensor_scalar(out=v[:], in0=zc[:, 0:G], scalar1=cvv[:, c, 0:1], scalar2=None, op0=ALU.mult),
                reads=["zc", "cvp"], writes=[("vch", c)])
            for j in range(1, CONV_W):
                P.V(lambda e, v=v, c=c, j=j: e.scalar_tensor_tensor(out=v[:], in0=zc[:, j:j + G], scalar=cvv[:, c, j:j + 1], in1=v[:], op0=ALU.mult, op1=ALU.add),
                    reads=["zc", "cvp", ("vch", c)], writes=[("vch", c)])
            P.V(lambda e, v=v, c=c: e.tensor_scalar(out=v[:], in0=v[:], scalar1=cvv[:, c, 31:32], scalar2=None, op0=ALU.add),
                reads=[("vch", c), "cvp"], writes=[("vch", c)])
            P.A(lambda e, v=v, c=c: e.activation(out=vb[:, c * G:(c + 1) * G], in_=v[:], func=AF.Copy), reads=[("vch", c)], writes=[("vb", c)])
            P.A(lambda e, v=v, c=c: e.activation(out=vsq[:, c * G:(c + 1) * G], in_=v[:], func=AF.Square), reads=[("vch", c)], writes=[("vsq", c)])
        b1, b1k = tp.bank("stat")
        kb.mm_group(b1[:], b1k, [(tp.ones[:], vb[:, c * G:(c + 1) * G]) for c in range(4)], reads=["ones"] + [("vb", c) for c in range(4)])
        P.V(lambda e, b1=b1: e.tensor_scalar(out=mean[:], in0=b1[:], scalar1=1.0 / 512, scalar2=None, op0=ALU.mult), reads=[b1k], writes=["mean"])
        b2, b2k = tp.bank("stat")
        kb.mm_group(b2[:], b2k, [(tp.ones[:], vsq[:, c * G:(c + 1) * G]) for c in range(4)], reads=["ones"] + [("vsq", c) for c in range(4)])
        P.V(lambda e: e.tensor_tensor(out=tmpc[:], in0=mean[:], in1=mean[:], op=ALU.mult), reads=["mean"], writes=["tmpcv"])
        P.V(lambda e, b2=b2: e.scalar_tensor_tensor(out=var[:], in0=b2[:], scalar=1.0 / 512, in1=tmpc[:], op0=ALU.mult, op1=ALU.subtract),
            reads=[b2k, "tmpcv"], writes=["var"])
        P.A(lambda e: e.activation(out=var[:], in_=var[:], func=AF.Sqrt, bias=EPS, scale=1.0), reads=["var"], writes=["var"])
        P.V(lambda e: e.reciprocal(out=var[:], in_=var[:]), reads=["var"], writes=["var"])
        for c in range(4):
            v = vch[c]
            P.V(lambda e, v=v: e.tensor_tensor(out=v[:], in0=v[:], in1=mean[:], op=ALU.subtract), reads=[("vch", c), "mean"], writes=[("vch", c)])
            P.V(lambda e, v=v: e.tensor_tensor(out=v[:], in0=v[:], in1=var[:], op=ALU.mult), reads=[("vch", c), "var"], writes=[("vch", c)])
            P.V(lambda e, v=v, c=c: e.tensor_scalar(out=v[:], in0=v[:], scalar1=cvv[:, c, 32:33], scalar2=cvv[:, c, 33:34], op0=ALU.mult, op1=ALU.add),
                reads=[("vch", c), "cvp"], writes=[("vch", c)])
            P.A(lambda e, v=v, c=c: e.activation(out=ybuf[2][:, c * G:(c + 1) * G], in_=v[:], func=AF.Silu), reads=[("vch", c)], writes=[("ybuf", 2)])
        for mblk in range(4):
            for k in range(4):
                gv, gk = tp.wload(wgt_v[:, :, k * 1024 + mblk * 256:k * 1024 + (mblk + 1) * 256], 8, 256)
                bv, bk_ = tp.wload(wbr_d[k].rearrange("(c p) n -> p c n", p=128)[:, :, mblk * 256:(mblk + 1) * 256], 4, 256)
                for sub in range(2):
                    m = mblk * 2 + sub
                    bg, bgk = tp.bank("gate")
                    kb.mm_group(bg[:], bgk, [(gv[:, c, sub * 128:(sub + 1) * 128], tp.h(c, 0)) for c in range(8)], reads=[gk] + [("h", c, 0) for c in range(8)])
                    by, byk = tp.bank("up")
                    kb.mm_group(by[:], byk, [(bv[:, c4, sub * 128:(sub + 1) * 128], ybuf[k][:, c4 * G:(c4 + 1) * G]) for c4 in range(4)], reads=[bk_, ("ybuf", k)])
                    si = kb.rr("sgt", 2)
                    P.A(lambda e, si=si, bg=bg: e.activation(out=sgt[si][:], in_=bg[:], func=AF.Sigmoid), reads=[bgk], writes=[("sgt", si)])
                    if k == 0:
                        P.V(lambda e, si=si, by=by, sub=sub: e.tensor_tensor(out=macc[sub][:], in0=by[:], in1=sgt[si][:], op=ALU.mult),
                            reads=[byk, ("sgt", si)], writes=[("macc", sub)])
                    else:
                        P.V(lambda e, si=si, by=by: e.tensor_tensor(out=sgt[si][:], in0=by[:], in1=sgt[si][:], op=ALU.mult),
                            reads=[byk, ("sgt", si)], writes=[("sgt", si)])
                        if k < 3:
                            P.V(lambda e, si=si, sub=sub: e.tensor_tensor(out=macc[sub][:], in0=macc[sub][:], in1=sgt[si][:], op=ALU.add),
                                reads=[("macc", sub), ("sgt", si)], writes=[("macc", sub)])
                        else:
                            P.V(lambda e, si=si, sub=sub, m=m: e.tensor_tensor(out=mrg[:, m * G:(m + 1) * G], in0=macc[sub][:], in1=sgt[si][:], op=ALU.add),
                                reads=[("macc", sub), ("sgt", si)], writes=[("mrg", m)])
        for mblk in range(4):
            ov, ok_ = tp.wload(wo_v[:, :, mblk * 256:(mblk + 1) * 256], 8, 256)
            for sub in range(2):
                m = mblk * 2 + sub
                ba, bak = tp.bank("acc")
                kb.mm_group(ba[:], bak, [(ov[:, c, sub * 128:(sub + 1) * 128], mrg[:, c * G:(c + 1) * G]) for c in range(8)], reads=[ok_] + [("mrg", c) for c in range(8)])
                P.V(lambda e, ba=ba, m=m: e.tensor_tensor(out=tp.x(m, 0), in0=ba[:], in1=tp.x(m, 0), op=ALU.add), reads=[bak, ("x", m, 0)], writes=[("x", m, 0)])
        tp.rmsnorm(gn[:, 8:16], 0)
        tp.ffn(wg_d, wu_d, wd_d)
        if final:
            tp.rmsnorm(gn[:, 16:24], 0, out_f32=True)
            for c in range(8):
                si = kb.rr("ostage", 2)
                P.V(lambda e, c=c, si=si: e.scalar_tensor_tensor(out=ostage[si][:], in0=tp.x(c, 0), scalar=gn[:, 16 + c:17 + c], in1=tp.rs[:], op0=ALU.mult, op1=ALU.mult),
                    reads=[("x", c, 0), "rs", "gains"], writes=[("ostage", si)])
                P.dma("sync", lambda e, c=c, si=si, t0=t0: e.dma_start(out=out_o[c * 128:(c + 1) * 128, t0:t0 + G], in_=ostage[si][:]), reads=[("ostage", si)], key=("ostage", si))
        else:
            P.dma("sync", lambda e, t0=t0: e.dma_start(out=out_v[:, :, t0:t0 + G], in_=xsb), reads=xkeys, key="xstore")
    return kb.finish()


def _run(nc, in_maps):
    res = run_bass_kernel_spmd(nc, in_maps, core_ids=list(range(NCORES)))
    return res.results


def _gain_tile(g):
    return np.ascontiguousarray(g.reshape(8, 128).T)


import ml_dtypes

NPBF = ml_dtypes.bfloat16
ROPE_THETA = 500000.0


def _consts_B():
    cmask = np.ones((128, TT), np.float32)
    cmask[:, ::HGC] = 0.0
    s_ = np.arange(HGC)[:, None]
    t_ = np.arange(HGC)[None, :]
    amask = np.tile((s_ <= t_).astype(np.float32), (1, TT // HGC))
    j = np.arange(128)[:, None]
    i = np.arange(128)[None, :]
    cur = (j <= i).astype(np.float32)
    prev = (j >= i).astype(np.float32)
    z = np.zeros_like(cur)
    pmask = np.stack([np.concatenate([cur, z, cur, prev], 1), np.concatenate([cur, prev, cur, prev], 1)]).astype(np.float32)
    ident = np.eye(128, dtype=np.float32).astype(NPBF)
    psw = np.zeros((32, 32), np.float32)
    psw[(np.arange(32) + 16) % 32, np.arange(32)] = 1.0
    invf = (np.float32(ROPE_THETA) ** (-(np.arange(16, dtype=np.float32) / np.float32(16)))).astype(np.float32)
    ropec = np.stack([np.concatenate([invf, invf]), np.concatenate([-np.ones(16), np.ones(16)])], 1).astype(np.float32)
    tau = np.tile(np.arange(TT, dtype=np.float32)[None, :], (128, 1))
    return {"cmask": cmask, "amask": amask, "pmask": pmask, "ident": ident, "psw": psw.astype(NPBF), "ropec": ropec, "tau": tau}


def _b_params(inp, l, hh):
    lg = inp["hg_lb_logits"]
    sl = slice(hh * 128, (hh + 1) * 128)
    hgp = np.stack([lg[0, sl], lg[1, sl], inp["hg_gnorm"][l, sl], np.full(128, 1.0 if l > 0 else 0.0, np.float32)], 1).astype(np.float32)
    s5p = np.zeros((128, 4, 67), np.float32)
    for j in range(4):
        for h in range(2):
            g = hh * 8 + 2 * j + h
            ps_ = slice(64 * h, 64 * h + 64)
            s5p[ps_, j, 0] = inp["s5_a_re"][l, g]
            s5p[ps_, j, 1] = inp["s5_a_im"][l, g]
            s5p[ps_, j, 2] = inp["s5_log_dt"][l, g]
            s5p[ps_, j, 3:19] = inp["s5_b_re"][l, g]
            s5p[ps_, j, 19:35] = inp["s5_b_im"][l, g]
            s5p[ps_, j, 35:51] = inp["s5_c_re"][l, g].T
            s5p[ps_, j, 51:67] = inp["s5_c_im"][l, g].T
    s5d = np.ascontiguousarray(inp["s5_d"][l, sl][:, None]).astype(np.float32)
    return {"hgp": hgp, "s5p": s5p.reshape(128, 4 * 67), "s5d": s5d}


def _b_acts(projT_b, hh):
    hg4 = np.stack([projT_b[k * 512 + hh * 128: k * 512 + hh * 128 + 128] for k in range(4)])
    s5u = np.ascontiguousarray(projT_b[2048 + hh * 128: 2048 + hh * 128 + 128])
    att9 = np.stack([projT_b[3584 + i * 1536 + (gi * 4 + hh) * 128: 3584 + i * 1536 + (gi * 4 + hh) * 128 + 128]
                     for gi in range(3) for i in range(3)])
    return {"hg4": np.ascontiguousarray(hg4), "s5u": s5u, "att9": np.ascontiguousarray(att9)}


def _c_params(inp, l):
    gains3 = np.concatenate([_gain_tile(inp["mix_norm"][l]), _gain_tile(inp["ffn2_norm"][l]), _gain_tile(inp["final_norm"])], 1).astype(np.float32)
    convp = np.zeros((128, 4, 34), np.float32)
    for c in range(4):
        sl = slice(c * 128, (c + 1) * 128)
        convp[:, c, 0:31] = inp["conv_w"][l][:, sl].T
        convp[:, c, 31] = inp["conv_b"][l, sl]
        convp[:, c, 32] = inp["conv_ln_g"][l, sl]
        convp[:, c, 33] = inp["conv_ln_b"][l, sl]
    return {"gains3": gains3, "convp": convp.reshape(128, 4 * 34),
            "w_gates": np.ascontiguousarray(inp["w_in"][l][:, NMIX:]), "w_branch": inp["w_branch"][l], "w_out": inp["w_out"][l],
            "w_glu": inp["s5_w_glu"][l], "w_gate": inp["ffn2_w_gate"][l], "w_up": inp["ffn2_w_up"][l], "w_down": inp["ffn2_w_down"][l]}


def _c_conv_halo(conv_b, q):
    t0 = q * TOK
    out = np.zeros((1024, TOK + HALO), conv_b.dtype)
    lo = max(t0 - HALO, 0)
    out[:, HALO - (t0 - lo):] = conv_b[:, lo:t0 + TOK]
    return out


def kernel(**inputs):
    inp = {k: np.asarray(v) for k, v in inputs.items()}
    x = inp["x"]
    progA = build_A()
    progB = {p: build_B((p,)) for p in "hsa"}
    progC = [build_C(False), build_C(True)]
    consts = _consts_B()
    pos = [np.ascontiguousarray(inp["positions"][b][None, :]).astype(np.int32) for b in range(B)]
    xT = [np.ascontiguousarray(x[c // 4, (c % 4) * TOK:(c % 4 + 1) * TOK, :].T) for c in range(NCORES)]
    for l in range(DEPTH):
        wa = {"g_ffn1": _gain_tile(inp["ffn1_norm"][l]), "g_mix": _gain_tile(inp["mix_norm"][l]),
              "w_gate": inp["ffn1_w_gate"][l], "w_up": inp["ffn1_w_up"][l], "w_down": inp["ffn1_w_down"][l],
              "w_in": np.ascontiguousarray(inp["w_in"][l][:, :NMIX])}
        resA = _run(progA, [dict(wa, xT=xT[c]) for c in range(NCORES)])
        x1T = [resA[c]["x1T"] for c in range(NCORES)]
        projT_b = [np.concatenate([resA[4 * b + q]["projT"] for q in range(4)], axis=1) for b in range(B)]
        del resA
        ycat = {}
        for part, big, outk in (("h", "hg4", "ya"), ("s", "s5u", "zb"), ("a", "att9", "yd")):
            in_maps = []
            for c in range(NCORES):
                b, hh = c // 4, c % 4
                m = dict(consts)
                m.update(_b_params(inp, l, hh))
                m[big] = _b_acts(projT_b[b], hh)[big]
                m["pos"] = pos[b]
                in_maps.append(m)
            resB = _run(progB[part], in_maps)
            ycat[outk] = [np.concatenate([resB[4 * b + hh][outk] for hh in range(4)], axis=0) for b in range(B)]
            del resB
        wc = _c_params(inp, l)
        in_maps = []
        for c in range(NCORES):
            b, q = c // 4, c % 4
            tsl = slice(q * TOK, (q + 1) * TOK)
            m = dict(wc)
            m["x1T"] = x1T[c]
            m["yaT"] = np.ascontiguousarray(ycat["ya"][b][:, tsl])
            m["zbT"] = np.ascontiguousarray(ycat["zb"][b][:, tsl])
            m["ydT"] = np.ascontiguousarray(ycat["yd"][b][:, tsl])
            m["convT"] = _c_conv_halo(projT_b[b][2560:3584], q)
            in_maps.append(m)
        resC = _run(progC[1 if l == DEPTH - 1 else 0], in_maps)
        xT = [resC[c]["outT"] for c in range(NCORES)]
        del resC
    out = np.empty((B, L, D), np.float32)
    for c in range(NCORES):
        out[c // 4, (c % 4) * TOK:(c % 4 + 1) * TOK, :] = xT[c].T
    return out
```

```python
import contextlib
import math

import numpy as np
import concourse.bass as bass
import concourse.mybir as mybir
from concourse.bass_utils import run_bass_kernel_spmd

F32 = mybir.dt.float32
BF16 = mybir.dt.bfloat16
I32 = mybir.dt.int32
AF = mybir.ActivationFunctionType
ALU = mybir.AluOpType

NCORES = 8
D = 1024
DFF = 2816
B = 2
L = 8192
DEPTH = 2
TOK = 2048
TT = 512
EPS = 1e-6
NMIX = 8192

ENGINES = ("sync", "scalar", "gpsimd", "vector", "tensor")


class _Op:
    __slots__ = ("eng", "fn", "deps", "is_dma", "sem_key", "ticket", "signals", "idx", "tiny")


class Prog:
    def __init__(self, nc):
        self.nc = nc
        self.ops = []
        self.last_writer = {}
        self.readers = {}
        self.tiny_mode = False

    def _add(self, eng, fn, reads, writes, is_dma=False, sem_key=None):
        op = _Op()
        op.eng, op.fn, op.is_dma, op.sem_key = eng, fn, is_dma, sem_key
        op.signals, op.ticket, op.idx = False, None, len(self.ops)
        op.tiny = self.tiny_mode and not is_dma and eng != "tensor"
        deps = set()
        for r in reads:
            w = self.last_writer.get(r)
            if w is not None:
                deps.add(w)
        for w_ in writes:
            w = self.last_writer.get(w_)
            if w is not None:
                deps.add(w)
            deps.update(self.readers.get(w_, ()))
        deps.discard(op.idx)
        op.deps = deps
        for r in reads:
            self.readers.setdefault(r, []).append(op.idx)
        for w_ in writes:
            self.last_writer[w_] = op.idx
            self.readers[w_] = []
        self.ops.append(op)
        return op

    def op(self, eng, fn, reads=(), writes=()):
        return self._add(eng, fn, tuple(reads), tuple(writes))

    def dma(self, eng, fn, reads=(), writes=(), key=None):
        return self._add(eng, fn, tuple(reads), tuple(writes), is_dma=True, sem_key=key)

    def V(self, fn, reads=(), writes=()):
        return self.op("vector", fn, reads, writes)

    def A(self, fn, reads=(), writes=()):
        return self.op("scalar", fn, reads, writes)

    def G(self, fn, reads=(), writes=()):
        return self.op("gpsimd", fn, reads, writes)

    def T(self, fn, reads=(), writes=()):
        return self.op("tensor", fn, reads, writes)

    def emit(self):
        nc, ops = self.nc, self.ops
        for op in ops:
            for d in op.deps:
                dop = ops[d]
                if dop.is_dma or dop.eng != op.eng or dop.tiny:
                    dop.signals = True
        for op in ops:
            if op.is_dma:
                op.signals = True
        eng_cnt = {e: 0 for e in ENGINES}
        key_cnt = {}
        for op in ops:
            if not op.signals:
                continue
            if op.is_dma:
                key_cnt[op.sem_key] = key_cnt.get(op.sem_key, 0) + 16
                op.ticket = key_cnt[op.sem_key]
            else:
                eng_cnt[op.eng] += 1
                op.ticket = eng_cnt[op.eng]
        with contextlib.ExitStack() as es:
            eng_sem = {e: es.enter_context(nc.semaphore("es_" + e)) for e in ENGINES}
            key_sem = {k: es.enter_context(nc.semaphore("ds_%d" % i)) for i, k in enumerate(key_cnt)}
            block = es.enter_context(nc.Block())

            def semval(dop):
                if dop.is_dma:
                    return key_sem[dop.sem_key], ("k", dop.sem_key), dop.ticket
                return eng_sem[dop.eng], ("e", dop.eng), dop.ticket

            def make_body(ename):
                def body(eng):
                    waited = {}
                    for op in ops:
                        if op.eng != ename:
                            continue
                        need = {}
                        for d in op.deps:
                            dop = ops[d]
                            if (not dop.is_dma) and dop.eng == ename and not dop.tiny:
                                continue
                            sem, skey, val = semval(dop)
                            if waited.get(skey, 0) >= val:
                                continue
                            if need.get(skey, (None, 0))[1] < val:
                                need[skey] = (sem, val)
                        for skey, (sem, val) in need.items():
                            eng.wait_ge(sem, val)
                            waited[skey] = val
                        inst = op.fn(eng)
                        if op.signals:
                            sem, skey, val = semval(op)
                            inst.then_inc(sem, 16 if op.is_dma else 1)
                    if ename == "sync":
                        for k, c in key_cnt.items():
                            if waited.get(("k", k), 0) < c:
                                eng.wait_ge(key_sem[k], c)
                        for e2, c in eng_cnt.items():
                            if c > 0 and e2 != "sync":
                                eng.wait_ge(eng_sem[e2], c)
                return body

            block.sync(make_body("sync"))
            block.scalar(make_body("scalar"))
            block.gpsimd(make_body("gpsimd"))
            block.vector(make_body("vector"))
            block.tensor(make_body("tensor"))


class KB:
    def __init__(self):
        self.nc = bass.Bass("TRN2", target_bir_lowering=False)
        self.es = contextlib.ExitStack()
        self.P = Prog(self.nc)
        self._rr = {}
        self._uid = 0

    def din(self, name, shape, dt=F32):
        return self.nc.dram_tensor(name, list(shape), dt, kind="ExternalInput").ap()

    def dout(self, name, shape, dt=F32):
        return self.nc.dram_tensor(name, list(shape), dt, kind="ExternalOutput").ap()

    def sb(self, name, shape, dt=F32):
        return self.es.enter_context(self.nc.sbuf_tensor("sb_" + name, list(shape), dt))

    def ps(self, name, shape, dt=F32):
        return self.es.enter_context(self.nc.psum_tensor("ps_" + name, list(shape), dt))

    def rr(self, name, n):
        i = self._rr.get(name, 0)
        self._rr[name] = i + 1
        return i % n

    def uid(self):
        self._uid += 1
        return self._uid

    def finish(self):
        self.P.emit()
        self.es.close()
        return self.nc

    def mm_group(self, out_ap, out_key, pairs, reads):
        n = len(pairs)
        for i, (l, r) in enumerate(pairs):
            self.P.T(lambda e, l=l, r=r, i=i: e.matmul(out_ap, lhsT=l, rhs=r, start=(i == 0), stop=(i == n - 1)),
                     reads=reads, writes=[out_key])


class TokenPhase:
    def __init__(self, kb, G):
        self.kb = kb
        self.G = G
        self.NT = G // TT
        kb_ = kb
        self.xT = kb_.sb("xT", [128, 8 * G], F32)
        self.hT = kb_.sb("hT", [128, 8 * G], BF16)
        self.hid = kb_.sb("hid", [128, 22 * G], BF16)
        self.sq = kb_.sb("sq", [128, 8 * TT], BF16)
        self.rs = kb_.sb("rs", [128, TT], F32)
        self.ones = kb_.sb("ones", [128, 128], BF16)
        self.sg = [kb_.sb("sg%d" % i, [128, TT], F32) for i in range(2)]
        self.wslot = [kb_.sb("wslot%d" % i, [128, 22 * 256], BF16) for i in range(3)]
        self.banks = [kb_.ps("bank%d" % i, [128, TT], F32) for i in range(8)]
        kb.P.V(lambda e: e.memset(self.ones[:], 1.0), writes=["ones"])

    def x(self, c, tt):
        return self.xT[:, c * self.G + tt * TT: c * self.G + (tt + 1) * TT]

    def h(self, c, tt):
        return self.hT[:, c * self.G + tt * TT: c * self.G + (tt + 1) * TT]

    def hd(self, j, tt):
        return self.hid[:, j * self.G + tt * TT: j * self.G + (tt + 1) * TT]

    def bank(self, purpose):
        groups = {"gate": (0, 1), "up": (2, 3), "acc": (4, 5, 6), "stat": (7,)}[purpose]
        i = groups[self.kb.rr("bank_" + purpose, len(groups))]
        return self.banks[i], ("bank", i)

    def wload(self, src_ap, nk, ncols):
        kb = self.kb
        si = kb.rr("wslot", 3)
        slot = self.wslot[si]
        view = slot[:, 0:nk * ncols].rearrange("p (c n) -> p c n", n=ncols)
        kb.P.dma("gpsimd", lambda e: e.dma_start(out=view, in_=src_ap), writes=[("wslot", si)], key=("wslot", si))
        return view, ("wslot", si)

    def rmsnorm(self, gain_ap, tt, out_f32=False):
        kb, P = self.kb, self.kb.P
        for c in range(8):
            P.A(lambda e, c=c: e.activation(out=self.sq[:, c * TT:(c + 1) * TT], in_=self.x(c, tt), func=AF.Square),
                reads=[("x", c, tt)], writes=[("sq", c)])
        bk, bkey = self.bank("stat")
        kb.mm_group(bk[:], bkey, [(self.ones[:], self.sq[:, c * TT:(c + 1) * TT]) for c in range(8)],
                    reads=["ones"] + [("sq", c) for c in range(8)])
        P.A(lambda e: e.activation(out=self.rs[:], in_=bk[:], func=AF.Sqrt, bias=EPS, scale=1.0 / D),
            reads=[bkey], writes=["rs"])
        P.V(lambda e: e.reciprocal(out=self.rs[:], in_=self.rs[:]), reads=["rs"], writes=["rs"])
        if out_f32:
            return
        for c in range(8):
            P.V(lambda e, c=c: e.scalar_tensor_tensor(out=self.h(c, tt), in0=self.x(c, tt), scalar=gain_ap[:, c:c + 1],
                                                      in1=self.rs[:], op0=ALU.mult, op1=ALU.mult),
                reads=[("x", c, tt), "rs", "gains"], writes=[("h", c, tt)])

    def ffn(self, wg, wu, wd):
        kb, P, NT = self.kb, self.kb.P, self.NT
        wg_v = wg.rearrange("(c p) n -> p c n", p=128)
        wu_v = wu.rearrange("(c p) n -> p c n", p=128)
        wd_v = wd.rearrange("(c p) n -> p c n", p=128)
        for blk in range(DFF // 256):
            gv, gk = self.wload(wg_v[:, :, blk * 256:(blk + 1) * 256], 8, 256)
            uv, uk = self.wload(wu_v[:, :, blk * 256:(blk + 1) * 256], 8, 256)
            for sub in range(2):
                j = blk * 2 + sub
                for tt in range(NT):
                    hreads = [("h", c, tt) for c in range(8)]
                    bg, bgk = self.bank("gate")
                    kb.mm_group(bg[:], bgk, [(gv[:, c, sub * 128:(sub + 1) * 128], self.h(c, tt)) for c in range(8)],
                                reads=[gk] + hreads)
                    bu, buk = self.bank("up")
                    kb.mm_group(bu[:], buk, [(uv[:, c, sub * 128:(sub + 1) * 128], self.h(c, tt)) for c in range(8)],
                                reads=[uk] + hreads)
                    si = kb.rr("sg", 2)
                    sg = self.sg[si]
                    P.A(lambda e, sg=sg, bg=bg: e.activation(out=sg[:], in_=bg[:], func=AF.Silu),
                        reads=[bgk], writes=[("sg", si)])
                    P.V(lambda e, sg=sg, bu=bu, j=j, tt=tt: e.tensor_tensor(out=self.hd(j, tt), in0=bu[:], in1=sg[:], op=ALU.mult),
                        reads=[buk, ("sg", si)], writes=[("hid", j, tt)])
        for mblk in range(D // 256):
            dv, dk = self.wload(wd_v[:, :, mblk * 256:(mblk + 1) * 256], 22, 256)
            for sub in range(2):
                m = mblk * 2 + sub
                for tt in range(NT):
                    ba, bak = self.bank("acc")
                    kb.mm_group(ba[:], bak, [(dv[:, j, sub * 128:(sub + 1) * 128], self.hd(j, tt)) for j in range(22)],
                                reads=[dk] + [("hid", j, tt) for j in range(22)])
                    P.V(lambda e, ba=ba, m=m, tt=tt: e.scalar_tensor_tensor(out=self.x(m, tt), in0=ba[:], scalar=0.5, in1=self.x(m, tt),
                                                                             op0=ALU.mult, op1=ALU.add),
                        reads=[bak, ("x", m, tt)], writes=[("x", m, tt)])


def build_A():
    kb = KB()
    P = kb.P
    xT_d = kb.din("xT", [D, TOK])
    g1_d = kb.din("g_ffn1", [128, 8])
    g2_d = kb.din("g_mix", [128, 8])
    wg_d = kb.din("w_gate", [D, DFF])
    wu_d = kb.din("w_up", [D, DFF])
    wd_d = kb.din("w_down", [DFF, D])
    win_d = kb.din("w_in", [D, NMIX])
    x1_o = kb.dout("x1T", [D, TOK])
    pj_o = kb.dout("projT", [NMIX, TOK], BF16)
    G = 1024
    tp = TokenPhase(kb, G)
    NT = tp.NT
    g1 = kb.sb("g1", [128, 8])
    g2 = kb.sb("g2", [128, 8])
    stage = [kb.sb("stage%d" % i, [128, TT], BF16) for i in range(4)]
    P.dma("sync", lambda e: e.dma_start(out=g1[:], in_=g1_d), writes=["gains"], key="gains")
    P.dma("sync", lambda e: e.dma_start(out=g2[:], in_=g2_d), writes=["gains"], key="gains")
    xT_v = xT_d.rearrange("(c p) t -> p c t", p=128)
    x1_v = x1_o.rearrange("(c p) t -> p c t", p=128)
    win_v = win_d.rearrange("(c p) n -> p c n", p=128)
    xkeys = [("x", c, tt) for c in range(8) for tt in range(NT)]
    for grp in range(TOK // G):
        t0 = grp * G
        xsb = tp.xT[:].rearrange("p (c t) -> p c t", t=G)
        P.dma("sync", lambda e, t0=t0: e.dma_start(out=xsb, in_=xT_v[:, :, t0:t0 + G]), writes=xkeys, key="xload")
        for tt in range(NT):
            tp.rmsnorm(g1, tt)
        tp.ffn(wg_d, wu_d, wd_d)
        P.dma("sync", lambda e, t0=t0: e.dma_start(out=x1_v[:, :, t0:t0 + G], in_=xsb), reads=xkeys, key="x1store")
        for tt in range(NT):
            tp.rmsnorm(g2, tt)
        for blk in range(NMIX // 256):
            wv, wk = tp.wload(win_v[:, :, blk * 256:(blk + 1) * 256], 8, 256)
            for sub in range(2):
                col = blk * 256 + sub * 128
                for tt in range(NT):
                    ba, bak = tp.bank("acc")
                    kb.mm_group(ba[:], bak, [(wv[:, c, sub * 128:(sub + 1) * 128], tp.h(c, tt)) for c in range(8)],
                                reads=[wk] + [("h", c, tt) for c in range(8)])
                    si = kb.rr("stage", 4)
                    st = stage[si]
                    if si % 2 == 0:
                        P.A(lambda e, st=st, ba=ba: e.activation(out=st[:], in_=ba[:], func=AF.Copy), reads=[bak], writes=[("stage", si)])
                    else:
                        P.V(lambda e, st=st, ba=ba: e.tensor_copy(out=st[:], in_=ba[:]), reads=[bak], writes=[("stage", si)])
                    P.dma("sync", lambda e, st=st, col=col, tt=tt, t0=t0: e.dma_start(
                        out=pj_o[col:col + 128, t0 + tt * TT:t0 + (tt + 1) * TT], in_=st[:]),
                        reads=[("stage", si)], key=("stage", si))
    return kb.finish()


_DBG = 0
NT_L = L // TT
HGC = 64
TWO_PI = 2.0 * math.pi


def build_B(parts=("h", "s", "a")):
    kb = KB()
    P = kb.P
    hg_d = kb.din("hg4", [4, 128, L], BF16) if "h" in parts else None
    s5u_d = kb.din("s5u", [128, L], BF16) if "s" in parts else None
    att_d = kb.din("att9", [9, 128, L], BF16) if "a" in parts else None
    pos_d = kb.din("pos", [1, L], I32)
    cmask_d = kb.din("cmask", [128, TT])
    amask_d = kb.din("amask", [64, TT])
    pmask_d = kb.din("pmask", [2, 128, TT])
    ident_d = kb.din("ident", [128, 128], BF16)
    psw_d = kb.din("psw", [32, 32], BF16)
    ropec_d = kb.din("ropec", [32, 2])
    tau_d = kb.din("tau", [128, TT])
    hgp_d = kb.din("hgp", [128, 4])
    s5p_d = kb.din("s5p", [128, 4 * 67])
    s5d_d = kb.din("s5d", [128, 1])
    ya_o = kb.dout("ya", [128, L], BF16) if "h" in parts else None
    zb_o = kb.dout("zb", [128, L], BF16) if "s" in parts else None
    yd_o = kb.dout("yd", [128, L], BF16) if "a" in parts else None

    NS = 11
    S = [kb.sb("scr%d" % i, [128, TT], F32) for i in range(NS)]
    SK = [("scr", i) for i in range(NS)]
    NSB = 6
    SB = [kb.sb("scb%d" % i, [128, TT], BF16) for i in range(NSB)]
    SBK = [("scb", i) for i in range(NSB)]
    banks = [kb.ps("bank%d" % i, [128, TT], F32) for i in range(8)]
    BK = [("bank", i) for i in range(8)]
    cmask = kb.sb("cmask", [128, TT]); amask = kb.sb("amask", [64, TT])
    pmask = kb.sb("pmask", [128, 2 * TT], BF16)
    ident = kb.sb("ident", [128, 128], BF16); psw = kb.sb("psw", [32, 32], BF16)
    ropec = kb.sb("ropec", [32, 2]); tau = kb.sb("tau", [128, TT])
    hgp = kb.sb("hgp", [128, 4]); s5p = kb.sb("s5p", [128, 4 * 67]); s5d = kb.sb("s5d", [128, 1])
    ones = kb.sb("ones", [128, 128], BF16)
    P.V(lambda e: e.memset(ones[:], 1.0), writes=["ones"])
    for nm, dst, src in (("cmask", cmask, cmask_d), ("amask", amask, amask_d), ("ident", ident, ident_d), ("psw", psw, psw_d),
                         ("ropec", ropec, ropec_d), ("tau", tau, tau_d), ("hgp", hgp, hgp_d), ("s5p", s5p, s5p_d), ("s5d", s5d, s5d_d)):
        P.dma("sync", lambda e, dst=dst, src=src: e.dma_start(out=dst[:], in_=src), writes=[nm], key="c_" + nm)
    P.dma("gpsimd", lambda e: e.dma_start(out=pmask[:].rearrange("p (a t) -> p a t", a=2), in_=pmask_d.rearrange("a p t -> p a t")),
          writes=["pmask"], key="c_pmask")

    def sin_table(out_ap, ang_ap, shape, shift, tmpf, tmpi, reads, writes, scale_ap=None, tkey="sin_tmp2"):
        P.V(lambda e: e.tensor_scalar(out=tmpf, in0=ang_ap, scalar1=1.0 / TWO_PI, scalar2=shift / TWO_PI, op0=ALU.mult, op1=ALU.add),
            reads=reads, writes=["sin_tmp", tkey])
        P.V(lambda e: e.tensor_copy(out=tmpi, in_=tmpf), reads=["sin_tmp"], writes=["sin_tmpi"])
        P.V(lambda e: e.tensor_copy(out=tmpf, in_=tmpi), reads=["sin_tmpi"], writes=["sin_tmp", tkey])
        P.V(lambda e: e.scalar_tensor_tensor(out=tmpf, in0=tmpf, scalar=-TWO_PI, in1=ang_ap, op0=ALU.mult, op1=ALU.add),
            reads=["sin_tmp"] + list(reads), writes=["sin_tmp", tkey])
        P.V(lambda e: e.tensor_scalar(out=tmpf, in0=tmpf, scalar1=-math.pi - shift, scalar2=math.pi - shift, op0=ALU.max, op1=ALU.min),
            reads=["sin_tmp"], writes=["sin_tmp", tkey])
        if scale_ap is None:
            P.A(lambda e: e.activation(out=out_ap, in_=tmpf, func=AF.Sin, bias=shiftc[shift][:shape[0], 0:1]), reads=["sin_tmp", "shiftc", tkey], writes=writes)
        else:
            assert shift == 0.0
            P.A(lambda e: e.activation(out=out_ap, in_=tmpf, func=AF.Sin, scale=scale_ap), reads=["sin_tmp", tkey], writes=writes)

    shiftc = {0.0: kb.sb("shift0", [128, 1]), math.pi / 2: kb.sb("shift1", [128, 1])}
    P.V(lambda e: e.memset(shiftc[0.0][:], 0.0), writes=["shiftc"])
    P.V(lambda e: e.memset(shiftc[math.pi / 2][:], math.pi / 2), writes=["shiftc"])
    tmpi = kb.sb("tmpi", [128, TT], I32)

    lb = kb.sb("lb", [128, 1]); oml = kb.sb("oml", [128, 1])
    P.tiny_mode = True
    P.V(lambda e: e.tensor_tensor(out=lb[:], in0=hgp[:, 1:2], in1=hgp[:, 0:1], op=ALU.subtract), reads=["hgp"], writes=["lb"])
    P.A(lambda e: e.activation(out=lb[:], in_=lb[:], func=AF.Sigmoid), reads=["lb"], writes=["lb"])
    P.V(lambda e: e.tensor_tensor(out=lb[:], in0=lb[:], in1=hgp[:, 3:4], op=ALU.mult), reads=["lb", "hgp"], writes=["lb"])
    P.V(lambda e: e.tensor_scalar(out=oml[:], in0=lb[:], scalar1=-1.0, scalar2=1.0, op0=ALU.mult, op1=ALU.add), reads=["lb"], writes=["oml"])
    P.tiny_mode = False
    Sst = kb.sb("Sst", [128, 128]); Sb = kb.sb("Sb", [128, 128], BF16)
    P.V(lambda e: e.memset(Sst[:], 0.0), writes=["Sst"])
    esc = kb.sb("esc", [128, 32])
    hin = [[kb.sb("hin%d_%d" % (a, i), [128, TT], BF16) for i in range(4)] for a in range(2)]
    KT = kb.sb("KTtok", [64, 8 * 128], BF16); VT = kb.sb("VTtok", [64, 8 * 128], BF16)
    pT = [kb.ps("pT%d" % i, [128, 1024], BF16) for i in range(0)]
    QSC = float(128 ** -0.5)
    for t in (range(NT_L) if "h" in parts else ()):
        a = t % 2
        tsl = slice(t * TT, (t + 1) * TT)
        for i in range(4):
            P.dma("sync", lambda e, i=i, a=a, tsl=tsl: e.dma_start(out=hin[a][i][:], in_=hg_d[i, :, tsl]),
                  writes=[("hin", a, i)], key=("hin", a, i))
        q_t, f_t, i_t, g_t = hin[a]
        sg, t1, lf, bb, bq, eq, ek, qs, osb, rst, sgl = (S[k] for k in range(11))
        Qt, Kt, attm, osq = SB[0], SB[1], SB[2], SB[3]
        P.A(lambda e, f_t=f_t: e.activation(out=sg[:], in_=f_t[:], func=AF.Sigmoid), reads=[("hin", a, 1)], writes=[SK[0]])
        P.V(lambda e: e.tensor_scalar(out=t1[:], in0=sg[:], scalar1=oml[:, 0:1], scalar2=lb[:, 0:1], op0=ALU.mult, op1=ALU.add),
            reads=[SK[0], "lb", "oml"], writes=[SK[1]])
        P.A(lambda e: e.activation(out=lf[:], in_=t1[:], func=AF.Ln), reads=[SK[1]], writes=[SK[2]])
        P.V(lambda e: e.tensor_tensor_scan(out=bb[:], data0=cmask[:], data1=lf[:], initial=0.0, op0=ALU.mult, op1=ALU.add),
            reads=[SK[2], "cmask"], writes=[SK[3]])
        b3 = bb[:].rearrange("p (c t) -> p c t", t=HGC)
        P.V(lambda e, b3=b3: e.tensor_tensor(out=bq[:].rearrange("p (c t) -> p c t", t=HGC), in0=b3,
                                             in1=b3[:, :, 32:33].to_broadcast([128, 8, HGC]), op=ALU.subtract),
            reads=[SK[3]], writes=[SK[4]])
        P.A(lambda e: e.activation(out=eq[:], in_=bq[:], func=AF.Exp), reads=[SK[4]], writes=[SK[5]])
        P.A(lambda e: e.activation(out=ek[:], in_=bq[:], func=AF.Exp, scale=-1.0), reads=[SK[4]], writes=[SK[6]])
        P.A(lambda e, q_t=q_t: e.activation(out=qs[:], in_=q_t[:], func=AF.Silu), reads=[("hin", a, 0)], writes=[SK[7]])
        P.V(lambda e: e.scalar_tensor_tensor(out=Qt[:], in0=qs[:], scalar=QSC, in1=eq[:], op0=ALU.mult, op1=ALU.mult),
            reads=[SK[7], SK[5]], writes=[SBK[0]])
        P.V(lambda e: e.tensor_scalar(out=t1[:], in0=t1[:], scalar1=-1.0, scalar2=1.0, op0=ALU.mult, op1=ALU.add), reads=[SK[1]], writes=[SK[1]])
        P.V(lambda e: e.tensor_tensor(out=Kt[:], in0=t1[:], in1=ek[:], op=ALU.mult), reads=[SK[1], SK[6]], writes=[SBK[1]])
        if _DBG == 1:
            continue
        P.tiny_mode = True
        P.V(lambda e, b3=b3: e.tensor_copy(out=esc[:, 0:8], in_=b3[:, :, 32]), reads=[SK[3]], writes=["esc"])
        P.A(lambda e: e.activation(out=esc[:, 8:16], in_=esc[:, 0:8], func=AF.Exp), reads=["esc"], writes=["esc"])
        P.A(lambda e, b3=b3: e.activation(out=esc[:, 16:24], in_=b3[:, :, 63], func=AF.Exp), reads=[SK[3], "esc"], writes=["esc"])
        P.V(lambda e, b3=b3: e.tensor_tensor(out=esc[:, 24:32], in0=b3[:, :, 63], in1=esc[:, 0:8], op=ALU.subtract), reads=[SK[3], "esc"], writes=["esc"])
        P.A(lambda e: e.activation(out=esc[:, 24:32], in_=esc[:, 24:32], func=AF.Exp), reads=["esc"], writes=["esc"])
        if _DBG == 2:
            continue
        P.tiny_mode = False
        kt_ps = banks[0][:].bitcast(BF16)
        vt_ps = banks[1][:].bitcast(BF16)
        for n in range(8):
            P.T(lambda e, n=n: e.transpose(out=kt_ps[0:64, n * 128:(n + 1) * 128], in_=Kt[:, n * 64:(n + 1) * 64], identity=ident[:]),
                reads=[SBK[1], "ident"], writes=[BK[0]])
        for n in range(8):
            P.T(lambda e, n=n, i_t=i_t: e.transpose(out=vt_ps[0:64, n * 128:(n + 1) * 128], in_=i_t[:, n * 64:(n + 1) * 64], identity=ident[:]),
                reads=[("hin", a, 2), "ident"], writes=[BK[1]])
        P.A(lambda e: e.activation(out=KT[:], in_=kt_ps[0:64, :], func=AF.Copy), reads=[BK[0]], writes=["KT"])
        P.V(lambda e: e.tensor_copy(out=VT[:], in_=vt_ps[0:64, :]), reads=[BK[1]], writes=["VT"])
        if _DBG == 3:
            continue
        for n in range(8):
            P.T(lambda e, n=n: e.matmul(banks[2][0:64, n * 64:(n + 1) * 64], lhsT=Kt[:, n * 64:(n + 1) * 64], rhs=Qt[:, n * 64:(n + 1) * 64],
                                        start=True, stop=True), reads=[SBK[0], SBK[1]], writes=[BK[2]])
        P.V(lambda e: e.tensor_tensor(out=attm[0:64, :], in0=banks[2][0:64, :], in1=amask[:], op=ALU.mult), reads=[BK[2], "amask"], writes=[SBK[2]])
        if _DBG == 4:
            continue
        for n in range(8):
            bi = 3 + n // 4
            P.T(lambda e, n=n, bi=bi: e.matmul(banks[bi][:, (n % 4) * 128:(n % 4 + 1) * 128], lhsT=KT[:, n * 128:(n + 1) * 128],
                                               rhs=VT[:, n * 128:(n + 1) * 128], start=True, stop=True), reads=["KT", "VT"], writes=[BK[bi]])
        P.tiny_mode = True
        for n in range(8):
            bi = 3 + n // 4
            P.V(lambda e, n=n: e.tensor_scalar(out=Sb[:], in0=Sst[:], scalar1=esc[:, 8 + n:9 + n], scalar2=None, op0=ALU.mult),
                reads=["Sst", "esc"], writes=["Sb"])
            P.T(lambda e, n=n: e.matmul(banks[5][:, n * 64:(n + 1) * 64], lhsT=Sb[:], rhs=Qt[:, n * 64:(n + 1) * 64], start=True, stop=False),
                reads=["Sb", SBK[0]], writes=[BK[5]])
            P.T(lambda e, n=n: e.matmul(banks[5][:, n * 64:(n + 1) * 64], lhsT=VT[:, n * 128:(n + 1) * 128], rhs=attm[0:64, n * 64:(n + 1) * 64],
                                        start=False, stop=True), reads=["VT", SBK[2]], writes=[BK[5]])
            P.V(lambda e, n=n: e.tensor_scalar(out=Sst[:], in0=Sst[:], scalar1=esc[:, 16 + n:17 + n], scalar2=None, op0=ALU.mult),
                reads=["Sst", "esc"], writes=["Sst"])
            P.V(lambda e, n=n, bi=bi: e.scalar_tensor_tensor(out=Sst[:], in0=banks[bi][:, (n % 4) * 128:(n % 4 + 1) * 128], scalar=esc[:, 24 + n:25 + n],
                                                             in1=Sst[:], op0=ALU.mult, op1=ALU.add), reads=[BK[bi], "Sst", "esc"], writes=["Sst"])
        P.tiny_mode = False
        if _DBG == 5:
            continue
        if _DBG != 11:
            pass
        if _DBG != 10:
            P.V(lambda e: e.tensor_copy(out=osb[:], in_=banks[5][:]), reads=[BK[5]], writes=[SK[8]])
        if _DBG == 20 and t == 0:
            P.dma("sync", lambda e: e.dma_start(out=zb_o[:, 0:512], in_=Qt[:]), reads=[SBK[0]], key="dbg0")
            P.dma("sync", lambda e: e.dma_start(out=zb_o[:, 512:1024], in_=Kt[:]), reads=[SBK[1]], key="dbg1")
            P.dma("sync", lambda e: e.dma_start(out=zb_o[0:64, 1024:1536], in_=attm[0:64, :]), reads=[SBK[2]], key="dbg2")
            P.dma("gpsimd", lambda e: e.dma_start(out=zb_o[:, 1536:2048], in_=bb[:]), reads=[SK[3]], key="dbg3")
            P.dma("sync", lambda e: e.dma_start(out=yd_o[0:64, 0:1024], in_=KT[:]), reads=["KT"], key="dbg4")
            P.dma("sync", lambda e: e.dma_start(out=yd_o[0:64, 1024:2048], in_=VT[:]), reads=["VT"], key="dbg5")
            P.dma("gpsimd", lambda e: e.dma_start(out=yd_o[:, 2048:2560], in_=osb[:]), reads=[SK[8]], key="dbg6")
            P.dma("gpsimd", lambda e: e.dma_start(out=yd_o[:, 2560:2688], in_=Sst[:]), reads=["Sst"], key="dbg7")
            P.dma("gpsimd", lambda e: e.dma_start(out=yd_o[:, 2688:2720], in_=esc[:]), reads=["esc"], key="dbg8")
        P.A(lambda e: e.activation(out=osq[:], in_=osb[:], func=AF.Square), reads=[SK[8]], writes=[SBK[3]])
        if _DBG in (10, 11):
            continue
        if _DBG == 6:
            continue
        P.T(lambda e: e.matmul(banks[6][:], lhsT=ones[:], rhs=osq[:], start=True, stop=True), reads=["ones", SBK[3]], writes=[BK[6]])
        P.A(lambda e: e.activation(out=rst[:], in_=banks[6][:], func=AF.Sqrt, bias=EPS, scale=1.0 / 128), reads=[BK[6]], writes=[SK[9]])
        P.V(lambda e: e.reciprocal(out=rst[:], in_=rst[:]), reads=[SK[9]], writes=[SK[9]])
        if _DBG == 7:
            continue
        P.A(lambda e, g_t=g_t: e.activation(out=sgl[:], in_=g_t[:], func=AF.Silu), reads=[("hin", a, 3)], writes=[SK[10]])
        P.V(lambda e: e.scalar_tensor_tensor(out=osb[:], in0=osb[:], scalar=hgp[:, 2:3], in1=rst[:], op0=ALU.mult, op1=ALU.mult),
            reads=[SK[8], SK[9], "hgp"], writes=[SK[8]])
        if _DBG == 8:
            continue
        yo = SB[4 + a]
        P.V(lambda e, yo=yo: e.tensor_tensor(out=yo[:], in0=osb[:], in1=sgl[:], op=ALU.mult), reads=[SK[8], SK[10]], writes=[SBK[4 + a]])
        if _DBG == 9:
            continue
        P.dma("sync", lambda e, yo=yo, tsl=tsl: e.dma_start(out=ya_o[:, tsl], in_=yo[:]), reads=[SBK[4 + a]], key=("yo", a))

    s5v = s5p[:].rearrange("p (j k) -> p j k", k=67)
    a_re, a_im, ldt = s5v[:, :, 0], s5v[:, :, 1], s5v[:, :, 2]
    sm = kb.sb("s5small", [128, 64])
    def col(i):
        return sm[:, 4 * i:4 * i + 4]
    dt_, adt, mag, th, cth, sth, abr, abi, den, m1, zr, zi, tA, tB, c512, s512 = (col(i) for i in range(16))
    smi = kb.sb("s5smalli", [128, 4], I32)
    P.tiny_mode = True
    P.A(lambda e: e.activation(out=dt_, in_=ldt, func=AF.Exp), reads=["s5p"], writes=["sm_dt"])
    P.V(lambda e: e.tensor_tensor(out=adt, in0=a_re, in1=dt_, op=ALU.mult), reads=["s5p", "sm_dt"], writes=["sm_adt"])
    P.A(lambda e: e.activation(out=mag, in_=adt, func=AF.Exp), reads=["sm_adt"], writes=["sm_mag"])
    P.V(lambda e: e.tensor_tensor(out=th, in0=a_im, in1=dt_, op=ALU.mult), reads=["s5p", "sm_dt"], writes=["sm_th"])
    sin_table(sth, th, [128, 4], 0.0, tA, smi[:], ["sm_th"], ["sm_sth"])
    sin_table(cth, th, [128, 4], math.pi / 2, tA, smi[:], ["sm_th"], ["sm_cth"])
    P.V(lambda e: e.tensor_scalar(out=tB, in0=th, scalar1=float(TT), scalar2=None, op0=ALU.mult), reads=["sm_th"], writes=["sm_tB"])
    sin_table(s512, tB, [128, 4], 0.0, tA, smi[:], ["sm_tB"], ["sm_s512"])
    sin_table(c512, tB, [128, 4], math.pi / 2, tA, smi[:], ["sm_tB"], ["sm_c512"])
    ns512 = kb.sb("ns512", [128, 4])
    P.V(lambda e: e.tensor_scalar(out=ns512[:], in0=s512, scalar1=-1.0, scalar2=None, op0=ALU.mult), reads=["sm_s512"], writes=["ns512"])
    P.V(lambda e: e.tensor_tensor(out=abr, in0=mag, in1=cth, op=ALU.mult), reads=["sm_mag", "sm_cth"], writes=["sm_abr"])
    P.V(lambda e: e.tensor_tensor(out=abi, in0=mag, in1=sth, op=ALU.mult), reads=["sm_mag", "sm_sth"], writes=["sm_abi"])
    P.V(lambda e: e.tensor_tensor(out=den, in0=a_re, in1=a_re, op=ALU.mult), reads=["s5p"], writes=["sm_den"])
    P.V(lambda e: e.tensor_tensor(out=tA, in0=a_im, in1=a_im, op=ALU.mult), reads=["s5p", "sm_c512"], writes=["sin_tmp"])
    P.V(lambda e: e.tensor_tensor(out=den, in0=den, in1=tA, op=ALU.add), reads=["sm_den", "sin_tmp"], writes=["sm_den"])
    P.V(lambda e: e.reciprocal(out=den, in_=den), reads=["sm_den"], writes=["sm_den"])
    P.V(lambda e: e.tensor_scalar(out=m1, in0=abr, scalar1=-1.0, scalar2=None, op0=ALU.add), reads=["sm_abr"], writes=["sm_m1"])
    P.V(lambda e: e.tensor_tensor(out=zr, in0=m1, in1=a_re, op=ALU.mult), reads=["sm_m1", "s5p"], writes=["sm_zr"])
    P.V(lambda e: e.tensor_tensor(out=tA, in0=abi, in1=a_im, op=ALU.mult), reads=["sm_abi", "s5p"], writes=["sin_tmp"])
    P.V(lambda e: e.tensor_tensor(out=zr, in0=zr, in1=tA, op=ALU.add), reads=["sm_zr", "sin_tmp"], writes=["sm_zr"])
    P.V(lambda e: e.tensor_tensor(out=zr, in0=zr, in1=den, op=ALU.mult), reads=["sm_zr", "sm_den"], writes=["sm_zr"])
    P.V(lambda e: e.tensor_tensor(out=zi, in0=abi, in1=a_re, op=ALU.mult), reads=["sm_abi", "s5p"], writes=["sm_zi"])
    P.V(lambda e: e.tensor_tensor(out=tA, in0=m1, in1=a_im, op=ALU.mult), reads=["sm_m1", "s5p"], writes=["sin_tmp"])
    P.V(lambda e: e.tensor_tensor(out=zi, in0=zi, in1=tA, op=ALU.subtract), reads=["sm_zi", "sin_tmp"], writes=["sm_zi"])
    P.V(lambda e: e.tensor_tensor(out=zi, in0=zi, in1=den, op=ALU.mult), reads=["sm_zi", "sm_den"], writes=["sm_zi"])
    Bex = [kb.sb("Bex%d" % i, [128, 4 * 128], BF16) for i in range(2)]
    Cex = [kb.sb("Cex%d" % i, [128, 4 * 128], BF16) for i in range(2)]
    BT = [kb.sb("BT%d" % i, [128, 4 * 128], BF16) for i in range(2)]
    bbt = kb.sb("bbt", [128, 64])
    for i in range(2):
        P.V(lambda e, i=i: e.memset(Bex[i][:], 0.0), writes=[("Bex", i)])
        P.V(lambda e, i=i: e.memset(Cex[i][:], 0.0), writes=[("Cex", i)])
    for j in range(4):
        bre, bim = s5v[:, j, 3:19], s5v[:, j, 19:35]
        cre, cim = s5v[:, j, 35:51], s5v[:, j, 51:67]
        zrj, zij = zr[:, j:j + 1], zi[:, j:j + 1]
        P.V(lambda e, bre=bre, zrj=zrj: e.tensor_scalar(out=bbt[:, 0:16], in0=bre, scalar1=zrj, scalar2=None, op0=ALU.mult), reads=["s5p", "sm_zr"], writes=["bbt"])
        P.V(lambda e, bim=bim, zij=zij: e.tensor_scalar(out=bbt[:, 16:32], in0=bim, scalar1=zij, scalar2=None, op0=ALU.mult), reads=["s5p", "sm_zi"], writes=["bbt"])
        P.V(lambda e, bim=bim, zrj=zrj: e.tensor_scalar(out=bbt[:, 32:48], in0=bim, scalar1=zrj, scalar2=None, op0=ALU.mult), reads=["s5p", "sm_zr"], writes=["bbt"])
        P.V(lambda e, bre=bre, zij=zij: e.tensor_scalar(out=bbt[:, 48:64], in0=bre, scalar1=zij, scalar2=None, op0=ALU.mult), reads=["s5p", "sm_zi"], writes=["bbt"])
        for hh_ in range(2):
            ps_ = slice(64 * hh_, 64 * hh_ + 64)
            cs = slice(j * 128 + (2 * j + hh_) * 16, j * 128 + (2 * j + hh_) * 16 + 16)
            P.V(lambda e, ps_=ps_, cs=cs: e.tensor_tensor(out=Bex[0][ps_, cs], in0=bbt[ps_, 0:16], in1=bbt[ps_, 16:32], op=ALU.subtract), reads=["bbt"], writes=[("Bex", 0)])
            P.V(lambda e, ps_=ps_, cs=cs: e.tensor_tensor(out=Bex[1][ps_, cs], in0=bbt[ps_, 32:48], in1=bbt[ps_, 48:64], op=ALU.add), reads=["bbt"], writes=[("Bex", 1)])
            P.V(lambda e, ps_=ps_, cs=cs, cre=cre: e.tensor_copy(out=Cex[0][ps_, cs], in_=cre[ps_, :]), reads=["s5p"], writes=[("Cex", 0)])
            P.V(lambda e, ps_=ps_, cs=cs, cim=cim: e.tensor_scalar(out=Cex[1][ps_, cs], in0=cim[ps_, :], scalar1=-1.0, scalar2=None, op0=ALU.mult), reads=["s5p"], writes=[("Cex", 1)])
    P.tiny_mode = False
    for i in range(2):
        bt_ps = banks[i][:].bitcast(BF16)
        for j in range(4):
            P.T(lambda e, i=i, j=j, bt_ps=bt_ps: e.transpose(out=bt_ps[:, j * 128:(j + 1) * 128], in_=Bex[i][:, j * 128:(j + 1) * 128], identity=ident[:]),
                reads=[("Bex", i), "ident"], writes=[BK[i]])
        P.V(lambda e, i=i, bt_ps=bt_ps: e.tensor_copy(out=BT[i][:], in_=bt_ps[:, 0:512]), reads=[BK[i]], writes=[("BT", i)])
    cosT = [kb.sb("cosT%d" % j, [128, TT]) for j in range(4)]
    sinT = [kb.sb("sinT%d" % j, [128, TT]) for j in range(4)]
    for j in range(4):
        ang = S[0]
        P.V(lambda e, j=j, ang=ang: e.tensor_scalar(out=ang[:], in0=tau[:], scalar1=th[:, j:j + 1], scalar2=None, op0=ALU.mult), reads=["tau", "sm_th"], writes=[SK[0]])
        sin_table(sinT[j][:], ang[:], [128, TT], 0.0, S[1][:], tmpi[:], [SK[0]], [("sinT", j)], tkey=SK[1])
        sin_table(cosT[j][:], ang[:], [128, TT], math.pi / 2, S[1][:], tmpi[:], [SK[0]], [("cosT", j)], tkey=SK[1])
    init = kb.sb("s5init", [128, 8])
    P.V(lambda e: e.memset(init[:], 0.0), writes=["s5init"])
    s5in = [kb.sb("s5in%d" % i, [128, TT], BF16) for i in range(2)]
    tmpc = kb.sb("s5tmpc", [128, 2])
    for t in (range(NT_L) if "s" in parts else ()):
        a = t % 2
        tsl = slice(t * TT, (t + 1) * TT)
        u_t = s5in[a]
        P.dma("sync", lambda e, u_t=u_t, tsl=tsl: e.dma_start(out=u_t[:], in_=s5u_d[:, tsl]), writes=[("s5in", a)], key=("s5in", a))
        for j in range(4):
            jsl = slice(j * 128, (j + 1) * 128)
            b_re, b_im = banks[(2 * j) % 4], banks[(2 * j + 1) % 4]
            kre, kim = BK[(2 * j) % 4], BK[(2 * j + 1) % 4]
            P.T(lambda e, jsl=jsl, b_re=b_re, u_t=u_t: e.matmul(b_re[:], lhsT=BT[0][:, jsl], rhs=u_t[:], start=True, stop=True), reads=[("BT", 0), ("s5in", a)], writes=[kre])
            P.T(lambda e, jsl=jsl, b_im=b_im, u_t=u_t: e.matmul(b_im[:], lhsT=BT[1][:, jsl], rhs=u_t[:], start=True, stop=True), reads=[("BT", 1), ("s5in", a)], writes=[kim])
            w1, w2, wnr, wni, wr, wi = S[2], S[3], S[4], S[5], S[6], S[7]
            cT, sT = cosT[j], sinT[j]
            rd = [("cosT", j), ("sinT", j)]
            P.V(lambda e, cT=cT, b_re=b_re: e.tensor_tensor(out=w1[:], in0=b_re[:], in1=cT[:], op=ALU.mult), reads=[kre] + rd, writes=[SK[2]])
            P.V(lambda e, sT=sT, b_im=b_im: e.tensor_tensor(out=w2[:], in0=b_im[:], in1=sT[:], op=ALU.mult), reads=[kim] + rd, writes=[SK[3]])
            P.V(lambda e: e.tensor_tensor(out=wnr[:], in0=w1[:], in1=w2[:], op=ALU.add), reads=[SK[2], SK[3]], writes=[SK[4]])
            P.V(lambda e, cT=cT, b_im=b_im: e.tensor_tensor(out=w1[:], in0=b_im[:], in1=cT[:], op=ALU.mult), reads=[kim] + rd, writes=[SK[2]])
            P.V(lambda e, sT=sT, b_re=b_re: e.tensor_tensor(out=w2[:], in0=b_re[:], in1=sT[:], op=ALU.mult), reads=[kre] + rd, writes=[SK[3]])
            P.V(lambda e: e.tensor_tensor(out=wni[:], in0=w1[:], in1=w2[:], op=ALU.subtract), reads=[SK[2], SK[3]], writes=[SK[5]])
            P.V(lambda e, j=j: e.tensor_tensor_scan(out=wr[:], data0=mag[:, j:j + 1].to_broadcast([128, TT]), data1=wnr[:], initial=init[:, j:j + 1],
                                                    op0=ALU.mult, op1=ALU.add), reads=[SK[4], "sm_mag", "s5init"], writes=[SK[6]])
            P.V(lambda e, j=j: e.tensor_tensor_scan(out=wi[:], data0=mag[:, j:j + 1].to_broadcast([128, TT]), data1=wni[:], initial=init[:, 4 + j:5 + j],
                                                    op0=ALU.mult, op1=ALU.add), reads=[SK[5], "sm_mag", "s5init"], writes=[SK[7]])
            P.tiny_mode = True
            P.V(lambda e, j=j: e.tensor_tensor(out=tmpc[:, 0:1], in0=wr[:, TT - 1:TT], in1=c512[:, j:j + 1], op=ALU.mult), reads=[SK[6], "sm_c512"], writes=["tmpc"])
            P.V(lambda e, j=j: e.tensor_tensor(out=tmpc[:, 1:2], in0=wr[:, TT - 1:TT], in1=s512[:, j:j + 1], op=ALU.mult), reads=[SK[6], "sm_s512"], writes=["tmpc"])
            P.V(lambda e, j=j: e.scalar_tensor_tensor(out=init[:, j:j + 1], in0=wi[:, TT - 1:TT], scalar=ns512[:, j:j + 1], in1=tmpc[:, 0:1], op0=ALU.mult, op1=ALU.add),
                reads=[SK[7], "ns512", "tmpc"], writes=["s5init"])
            P.V(lambda e, j=j: e.scalar_tensor_tensor(out=init[:, 4 + j:5 + j], in0=wi[:, TT - 1:TT], scalar=c512[:, j:j + 1], in1=tmpc[:, 1:2], op0=ALU.mult, op1=ALU.add),
                reads=[SK[7], "sm_c512", "tmpc"], writes=["s5init"])
            P.tiny_mode = False
            xr, xi = SB[0 + 2 * (j % 2)], SB[1 + 2 * (j % 2)]
            kxr, kxi = SBK[0 + 2 * (j % 2)], SBK[1 + 2 * (j % 2)]
            P.V(lambda e, cT=cT: e.tensor_tensor(out=w1[:], in0=wr[:], in1=cT[:], op=ALU.mult), reads=[SK[6]] + rd, writes=[SK[2]])
            P.V(lambda e, sT=sT: e.tensor_tensor(out=w2[:], in0=wi[:], in1=sT[:], op=ALU.mult), reads=[SK[7]] + rd, writes=[SK[3]])
            P.V(lambda e, xr=xr: e.tensor_tensor(out=xr[:], in0=w1[:], in1=w2[:], op=ALU.subtract), reads=[SK[2], SK[3]], writes=[kxr])
            P.V(lambda e, sT=sT: e.tensor_tensor(out=w1[:], in0=wr[:], in1=sT[:], op=ALU.mult), reads=[SK[6]] + rd, writes=[SK[2]])
            P.V(lambda e, cT=cT: e.tensor_tensor(out=w2[:], in0=wi[:], in1=cT[:], op=ALU.mult), reads=[SK[7]] + rd, writes=[SK[3]])
            P.V(lambda e, xi=xi: e.tensor_tensor(out=xi[:], in0=w1[:], in1=w2[:], op=ALU.add), reads=[SK[2], SK[3]], writes=[kxi])
            P.T(lambda e, jsl=jsl, xr=xr, j=j: e.matmul(banks[4][:], lhsT=Cex[0][:, jsl], rhs=xr[:], start=(j == 0), stop=False), reads=[("Cex", 0), kxr], writes=[BK[4]])
            P.T(lambda e, jsl=jsl, xi=xi, j=j: e.matmul(banks[4][:], lhsT=Cex[1][:, jsl], rhs=xi[:], start=False, stop=(j == 3)), reads=[("Cex", 1), kxi], writes=[BK[4]])
        yv, y2, sgm = S[8], S[9], S[10]
        P.V(lambda e, u_t=u_t: e.scalar_tensor_tensor(out=yv[:], in0=u_t[:], scalar=s5d[:, 0:1], in1=banks[4][:], op0=ALU.mult, op1=ALU.add),
            reads=[("s5in", a), "s5d", BK[4]], writes=[SK[8]])
        P.V(lambda e: e.tensor_tensor(out=y2[:], in0=yv[:], in1=yv[:], op=ALU.mult), reads=[SK[8]], writes=[SK[9]])
        P.V(lambda e: e.tensor_scalar(out=y2[:], in0=y2[:], scalar1=0.044715, scalar2=1.0, op0=ALU.mult, op1=ALU.add), reads=[SK[9]], writes=[SK[9]])
        P.V(lambda e: e.tensor_tensor(out=y2[:], in0=y2[:], in1=yv[:], op=ALU.mult), reads=[SK[9], SK[8]], writes=[SK[9]])
        P.A(lambda e: e.activation(out=sgm[:], in_=y2[:], func=AF.Sigmoid, scale=2.0 * math.sqrt(2.0 / math.pi)), reads=[SK[9]], writes=[SK[10]])
        zo = SB[4 + a]
        P.V(lambda e, zo=zo: e.tensor_tensor(out=zo[:], in0=yv[:], in1=sgm[:], op=ALU.mult), reads=[SK[8], SK[10]], writes=[SBK[4 + a]])
        P.dma("sync", lambda e, zo=zo, tsl=tsl: e.dma_start(out=zb_o[:, tsl], in_=zo[:]), reads=[SBK[4 + a]], key=("yo", a))

    qr = kb.sb("qr", [128, L], BF16); kr = kb.sb("kr", [128, L], BF16); vv = kb.sb("vv", [128, L], BF16)
    Oacc = kb.sb("Oacc", [128, L]); Dacc = kb.sb("Dacc", [128, L])
    posi = kb.sb("posi", [32, TT], I32)
    vtok = [kb.sb("vtok%d" % i, [128, 4 * 128], BF16) for i in range(2)]
    for i in range(2):
        P.V(lambda e, i=i: e.memset(vtok[i][:], 0.0), writes=[("vtok", i)])
    SCL = float(128 ** -0.5)
    for gi, dil in (enumerate((1, 4, 16)) if "a" in parts else ()):
        for t in range(NT_L):
            tsl = slice(t * TT, (t + 1) * TT)
            for i, dst in enumerate((qr, kr, vv)):
                P.dma("sync", lambda e, i=i, dst=dst, tsl=tsl, gi=gi: e.dma_start(out=dst[:, tsl], in_=att_d[3 * gi + i, :, tsl]),
                      writes=[("att", i, t)], key=("attld", i, t))
            P.dma("sync", lambda e, tsl=tsl: e.dma_start(out=posi[:], in_=pos_d[:, tsl].partition_broadcast(32)), writes=["posi"], key="posi")
            ang, tmpf, sinS, cosS, r1, r2 = S[0], S[1], S[2], S[3], S[4], S[5]
            P.V(lambda e: e.tensor_copy(out=ang[0:32, :], in_=posi[:]), reads=["posi"], writes=[SK[0]])
            P.V(lambda e: e.tensor_scalar(out=ang[0:32, :], in0=ang[0:32, :], scalar1=ropec[:, 0:1], scalar2=None, op0=ALU.mult), reads=[SK[0], "ropec"], writes=[SK[0]])
            sin_table(sinS[0:32, :], ang[0:32, :], [32, TT], 0.0, tmpf[0:32, :], tmpi[0:32, :], [SK[0]], [SK[2]], tkey=SK[1])
            P.V(lambda e: e.tensor_scalar(out=sinS[0:32, :], in0=sinS[0:32, :], scalar1=ropec[:, 1:2], scalar2=None, op0=ALU.mult), reads=[SK[2], "ropec"], writes=[SK[2]])
            sin_table(cosS[0:32, :], ang[0:32, :], [32, TT], math.pi / 2, tmpf[0:32, :], tmpi[0:32, :], [SK[0]], [SK[3]], tkey=SK[1])
            for i, dst in enumerate((qr, kr)):
                bsw = banks[i]
                P.T(lambda e, dst=dst, tsl=tsl, bsw=bsw: e.matmul(bsw[0:32, :], lhsT=psw[:], rhs=dst[0:32, tsl], start=True, stop=True),
                    reads=["psw", ("att", i, t)], writes=[BK[i]])
                P.V(lambda e, bsw=bsw: e.tensor_tensor(out=r1[0:32, :], in0=bsw[0:32, :], in1=sinS[0:32, :], op=ALU.mult), reads=[BK[i], SK[2]], writes=[SK[4]])
                P.V(lambda e, dst=dst, tsl=tsl: e.tensor_tensor(out=r2[0:32, :], in0=dst[0:32, tsl], in1=cosS[0:32, :], op=ALU.mult), reads=[("att", i, t), SK[3]], writes=[SK[5]])
                P.V(lambda e, dst=dst, tsl=tsl: e.tensor_tensor(out=dst[0:32, tsl], in0=r1[0:32, :], in1=r2[0:32, :], op=ALU.add), reads=[SK[4], SK[5]], writes=[("att", i, t)])
        nper = L // dil // 128
        allkeys = [("att", i, t) for i in range(3) for t in range(NT_L)]
        for r in range(dil):
            for qd in range(nper // 4):
                n0 = qd * 4
                def toks(n):
                    st = r + dil * 128 * n
                    return slice(st, st + dil * 127 + 1, dil)
                vt_ps = banks[2][:].bitcast(BF16)
                vcur = vtok[(r * (nper // 4) + qd) % 2]
                vprev = vtok[(r * (nper // 4) + qd + 1) % 2]
                kvc, kvp = ("vtok", (r * (nper // 4) + qd) % 2), ("vtok", (r * (nper // 4) + qd + 1) % 2)
                for k_ in range(4):
                    P.T(lambda e, k_=k_, tk=toks(n0 + k_): e.transpose(out=vt_ps[:, k_ * 128:(k_ + 1) * 128], in_=vv[:, tk], identity=ident[:]),
                        reads=allkeys[2 * NT_L:] + ["ident"], writes=[BK[2]])
                P.A(lambda e, vcur=vcur: e.activation(out=vcur[:], in_=vt_ps[:, 0:512], func=AF.Copy), reads=[BK[2]], writes=[kvc])
                pm = [SB[0], SB[1]]
                for pr in range(2):
                    sb_ = banks[pr]
                    for k2 in range(2):
                        n = n0 + pr * 2 + k2
                        P.T(lambda e, tk=toks(n), k2=k2, sb_=sb_: e.matmul(sb_[:, k2 * 256:k2 * 256 + 128], lhsT=kr[:, tk], rhs=qr[:, tk], start=True, stop=True),
                            reads=allkeys[:2 * NT_L], writes=[BK[pr]])
                        np_ = n - 1 if n > 0 else n
                        P.T(lambda e, tk=toks(n), tkp=toks(np_), k2=k2, sb_=sb_: e.matmul(sb_[:, k2 * 256 + 128:k2 * 256 + 256], lhsT=kr[:, tkp], rhs=qr[:, tk], start=True, stop=True),
                            reads=allkeys[:2 * NT_L], writes=[BK[pr]])
                    ex = S[6 + pr]
                    P.A(lambda e, ex=ex, sb_=sb_: e.activation(out=ex[:], in_=sb_[:], func=AF.Exp, scale=SCL), reads=[BK[pr]], writes=[SK[6 + pr]])
                    mk = pmask[:, 0:TT] if (n0 == 0 and pr == 0) else pmask[:, TT:2 * TT]
                    P.V(lambda e, ex=ex, mk=mk, pr=pr: e.tensor_tensor(out=pm[pr][:], in0=ex[:], in1=mk, op=ALU.mult), reads=[SK[6 + pr], "pmask"], writes=[SBK[pr]])
                for k_ in range(4):
                    pr, k2 = k_ // 2, k_ % 2
                    pc = pm[pr][:, k2 * 256:k2 * 256 + 128]
                    pp = pm[pr][:, k2 * 256 + 128:k2 * 256 + 256]
                    vc = vcur[:, k_ * 128:(k_ + 1) * 128]
                    vp = vcur[:, (k_ - 1) * 128:k_ * 128] if k_ > 0 else vprev[:, 3 * 128:4 * 128]
                    osl = slice(k_ * 128, (k_ + 1) * 128)
                    P.T(lambda e, vc=vc, pc=pc, osl=osl: e.matmul(banks[3][:, osl], lhsT=vc, rhs=pc, start=True, stop=False), reads=[kvc, SBK[pr]], writes=[BK[3]])
                    P.T(lambda e, vp=vp, pp=pp, osl=osl: e.matmul(banks[3][:, osl], lhsT=vp, rhs=pp, start=False, stop=True), reads=[kvc, kvp, SBK[pr]], writes=[BK[3]])
                    P.T(lambda e, pc=pc, osl=osl: e.matmul(banks[4][:, osl], lhsT=ones[:], rhs=pc, start=True, stop=False), reads=["ones", SBK[pr]], writes=[BK[4]])
                    P.T(lambda e, pp=pp, osl=osl: e.matmul(banks[4][:, osl], lhsT=ones[:], rhs=pp, start=False, stop=True), reads=["ones", SBK[pr]], writes=[BK[4]])
                st = r + dil * 128 * n0
                dsl = slice(st, st + dil * 511 + 1, dil)
                if gi == 0:
                    P.A(lambda e, dsl=dsl: e.activation(out=Oacc[:, dsl], in_=banks[3][:], func=AF.Copy), reads=[BK[3]], writes=["Oacc"])
                    P.V(lambda e, dsl=dsl: e.tensor_copy(out=Dacc[:, dsl], in_=banks[4][:]), reads=[BK[4]], writes=["Dacc"])
                else:
                    P.V(lambda e, dsl=dsl: e.tensor_tensor(out=Oacc[:, dsl], in0=banks[3][:], in1=Oacc[:, dsl], op=ALU.add), reads=[BK[3], "Oacc"], writes=["Oacc"])
                    P.V(lambda e, dsl=dsl: e.tensor_tensor(out=Dacc[:, dsl], in0=banks[4][:], in1=Dacc[:, dsl], op=ALU.add), reads=[BK[4], "Dacc"], writes=["Dacc"])
    for t in (range(NT_L) if "a" in parts else ()):
        a = t % 2
        tsl = slice(t * TT, (t + 1) * TT)
        rc = S[8 + a]
        P.V(lambda e, rc=rc, tsl=tsl: e.reciprocal(out=rc[:], in_=Dacc[:, tsl]), reads=["Dacc"], writes=[SK[8 + a]])
        yo = SB[4 + a]
        P.V(lambda e, rc=rc, tsl=tsl, yo=yo: e.tensor_tensor(out=yo[:], in0=Oacc[:, tsl], in1=rc[:], op=ALU.mult), reads=["Oacc", SK[8 + a]], writes=[SBK[4 + a]])
        P.dma("sync", lambda e, yo=yo, tsl=tsl: e.dma_start(out=yd_o[:, tsl], in_=yo[:]), reads=[SBK[4 + a]], key=("yo", a))
    return kb.finish()


CONV_W = 31
HALO = CONV_W - 1


def build_C(final):
    kb = KB()
    P = kb.P
    G = 1024
    x1_d = kb.din("x1T", [D, TOK])
    ya_d = kb.din("yaT", [512, TOK], BF16)
    zb_d = kb.din("zbT", [512, TOK], BF16)
    yd_d = kb.din("ydT", [512, TOK], BF16)
    cv_d = kb.din("convT", [1024, TOK + HALO], BF16)
    gn_d = kb.din("gains3", [128, 24])
    cvp_d = kb.din("convp", [128, 4 * 34])
    wgt_d = kb.din("w_gates", [D, 4096])
    wbr_d = kb.din("w_branch", [4, 512, D])
    wo_d = kb.din("w_out", [D, D])
    wglu_d = kb.din("w_glu", [512, 1024])
    wg_d = kb.din("w_gate", [D, DFF])
    wu_d = kb.din("w_up", [D, DFF])
    wd_d = kb.din("w_down", [DFF, D])
    out_o = kb.dout("outT", [D, TOK])
    tp = TokenPhase(kb, G)
    NT = tp.NT
    gn = kb.sb("gn", [128, 24]); cvp = kb.sb("cvp", [128, 4 * 34])
    P.dma("sync", lambda e: e.dma_start(out=gn[:], in_=gn_d), writes=["gains"], key="gains")
    P.dma("sync", lambda e: e.dma_start(out=cvp[:], in_=cvp_d), writes=["cvp"], key="cvp")
    cvv = cvp[:].rearrange("p (c k) -> p c k", k=34)
    yb3 = kb.sb("yb3", [128, 4 * G], BF16)
    zbt = kb.sb("zbt", [128, 4 * G], BF16)

    def ybr(k, c4, tt):
        if k < 3:
            j = 8 + 4 * k + c4
            return tp.hd(j, tt), ("hid", j, tt)
        return yb3[:, c4 * G + tt * TT:c4 * G + (tt + 1) * TT], ("yb3", c4, tt)

    def zbr(c4, tt):
        return zbt[:, c4 * G + tt * TT:c4 * G + (tt + 1) * TT], ("zbt", c4, tt)

    ca = kb.sb("ca", [128, TT + HALO], BF16); cb = kb.sb("cb", [128, TT + HALO], BF16)
    zc = kb.sb("zc", [128, TT + HALO]); sgb = kb.sb("sgb", [128, TT + HALO])
    vch = [kb.sb("vch%d" % c, [128, TT]) for c in range(4)]
    vb = kb.sb("vb", [128, 4 * TT], BF16); vsq = kb.sb("vsq", [128, 4 * TT], BF16)
    mean = kb.sb("mean", [128, TT]); var = kb.sb("var", [128, TT]); tmpc = kb.sb("tmpcv", [128, TT])
    macc = [kb.sb("macc%d" % i, [128, TT]) for i in range(4)]
    sgt = tp.sg
    x1_v = x1_d.rearrange("(c p) t -> p c t", p=128)
    out_v = out_o.rearrange("(c p) t -> p c t", p=128)
    wgt_v = wgt_d.rearrange("(c p) n -> p c n", p=128)
    wo_v = wo_d.rearrange("(c p) n -> p c n", p=128)
    wglu_v = wglu_d.rearrange("(c p) n -> p c n", p=128)
    xkeys = [("x", c, tt) for c in range(8) for tt in range(NT)]
    for grp in range(TOK // G):
        t0 = grp * G
        xsb = tp.xT[:].rearrange("p (c t) -> p c t", t=G)
        P.dma("sync", lambda e, t0=t0: e.dma_start(out=xsb, in_=x1_v[:, :, t0:t0 + G]), writes=xkeys, key="xload")
        P.dma("sync", lambda e, t0=t0: e.dma_start(out=tp.hid[:, 8 * G:12 * G].rearrange("p (c t) -> p c t", t=G),
                                                 in_=ya_d.rearrange("(c p) t -> p c t", p=128)[:, :, t0:t0 + G]),
              writes=[("hid", j, tt) for j in range(8, 12) for tt in range(NT)], key="yaload")
        P.dma("sync", lambda e, t0=t0: e.dma_start(out=yb3[:].rearrange("p (c t) -> p c t", t=G),
                                                 in_=yd_d.rearrange("(c p) t -> p c t", p=128)[:, :, t0:t0 + G]),
              writes=[("yb3", c4, tt) for c4 in range(4) for tt in range(NT)], key="ydload")
        P.dma("sync", lambda e, t0=t0: e.dma_start(out=zbt[:].rearrange("p (c t) -> p c t", t=G),
                                                 in_=zb_d.rearrange("(c p) t -> p c t", p=128)[:, :, t0:t0 + G]),
              writes=[("zbt", c4, tt) for c4 in range(4) for tt in range(NT)], key="zbload")
        for tt in range(NT):
            tp.rmsnorm(gn[:, 0:8], tt)
        for cbk in range(2):
            wa, wak = tp.wload(wglu_v[:, :, cbk * 256:(cbk + 1) * 256], 4, 256)
            wgl, wgk = tp.wload(wglu_v[:, :, 512 + cbk * 256:512 + (cbk + 1) * 256], 4, 256)
            for sub in range(2):
                c = cbk * 2 + sub
                for tt in range(NT):
                    zr_ = [zbr(k4, tt) for k4 in range(4)]
                    ba, bak = tp.bank("gate")
                    kb.mm_group(ba[:], bak, [(wa[:, k4, sub * 128:(sub + 1) * 128], zr_[k4][0]) for k4 in range(4)], reads=[wak] + [z[1] for z in zr_])
                    bg, bgk = tp.bank("up")
                    kb.mm_group(bg[:], bgk, [(wgl[:, k4, sub * 128:(sub + 1) * 128], zr_[k4][0]) for k4 in range(4)], reads=[wgk] + [z[1] for z in zr_])
                    si = kb.rr("sg", 2)
                    P.A(lambda e, si=si, bg=bg: e.activation(out=sgt[si][:], in_=bg[:], func=AF.Sigmoid), reads=[bgk], writes=[("sg", si)])
                    yo, yok = ybr(1, c, tt)
                    P.V(lambda e, si=si, ba=ba, yo=yo: e.tensor_tensor(out=yo, in0=ba[:], in1=sgt[si][:], op=ALU.mult),
                        reads=[bak, ("sg", si)], writes=[yok])
        for tt in range(NT):
            tb = t0 + tt * TT
            for c in range(4):
                P.dma("sync", lambda e, c=c, tb=tb: e.dma_start(out=ca[:], in_=cv_d[c * 128:(c + 1) * 128, tb:tb + TT + HALO]), writes=["ca"], key="ca")
                P.dma("sync", lambda e, c=c, tb=tb: e.dma_start(out=cb[:], in_=cv_d[512 + c * 128:512 + (c + 1) * 128, tb:tb + TT + HALO]), writes=["cb"], key="cb")
                P.A(lambda e: e.activation(out=sgb[:], in_=cb[:], func=AF.Sigmoid), reads=["cb"], writes=["sgb"])
                P.V(lambda e: e.tensor_tensor(out=zc[:], in0=ca[:], in1=sgb[:], op=ALU.mult), reads=["ca", "sgb"], writes=["zc"])
                v = vch[c]
                P.V(lambda e, v=v, c=c: e.tensor_scalar(out=v[:], in0=zc[:, 0:TT], scalar1=cvv[:, c, 0:1], scalar2=None, op0=ALU.mult),
                    reads=["zc", "cvp"], writes=[("vch", c)])
                for j in range(1, CONV_W):
                    P.V(lambda e, v=v, c=c, j=j: e.scalar_tensor_tensor(out=v[:], in0=zc[:, j:j + TT], scalar=cvv[:, c, j:j + 1], in1=v[:], op0=ALU.mult, op1=ALU.add),
                        reads=["zc", "cvp", ("vch", c)], writes=[("vch", c)])
                P.V(lambda e, v=v, c=c: e.tensor_scalar(out=v[:], in0=v[:], scalar1=cvv[:, c, 31:32], scalar2=None, op0=ALU.add),
                    reads=[("vch", c), "cvp"], writes=[("vch", c)])
                P.A(lambda e, v=v, c=c: e.activation(out=vb[:, c * TT:(c + 1) * TT], in_=v[:], func=AF.Copy), reads=[("vch", c)], writes=[("vb", c)])
                P.A(lambda e, v=v, c=c: e.activation(out=vsq[:, c * TT:(c + 1) * TT], in_=v[:], func=AF.Square), reads=[("vch", c)], writes=[("vsq", c)])
            b1, b1k = tp.bank("stat")
            kb.mm_group(b1[:], b1k, [(tp.ones[:], vb[:, c * TT:(c + 1) * TT]) for c in range(4)], reads=["ones"] + [("vb", c) for c in range(4)])
            P.V(lambda e, b1=b1: e.tensor_scalar(out=mean[:], in0=b1[:], scalar1=1.0 / 512, scalar2=None, op0=ALU.mult), reads=[b1k], writes=["mean"])
            b2, b2k = tp.bank("stat")
            kb.mm_group(b2[:], b2k, [(tp.ones[:], vsq[:, c * TT:(c + 1) * TT]) for c in range(4)], reads=["ones"] + [("vsq", c) for c in range(4)])
            P.V(lambda e: e.tensor_tensor(out=tmpc[:], in0=mean[:], in1=mean[:], op=ALU.mult), reads=["mean"], writes=["tmpcv"])
            P.V(lambda e, b2=b2: e.scalar_tensor_tensor(out=var[:], in0=b2[:], scalar=1.0 / 512, in1=tmpc[:], op0=ALU.mult, op1=ALU.subtract),
                reads=[b2k, "tmpcv"], writes=["var"])
            P.A(lambda e: e.activation(out=var[:], in_=var[:], func=AF.Sqrt, bias=EPS, scale=1.0), reads=["var"], writes=["var"])
            P.V(lambda e: e.reciprocal(out=var[:], in_=var[:]), reads=["var"], writes=["var"])
            for c in range(4):
                v = vch[c]
                P.V(lambda e, v=v: e.tensor_tensor(out=v[:], in0=v[:], in1=mean[:], op=ALU.subtract), reads=[("vch", c), "mean"], writes=[("vch", c)])
                P.V(lambda e, v=v: e.tensor_tensor(out=v[:], in0=v[:], in1=var[:], op=ALU.mult), reads=[("vch", c), "var"], writes=[("vch", c)])
                P.V(lambda e, v=v, c=c: e.tensor_scalar(out=v[:], in0=v[:], scalar1=cvv[:, c, 32:33], scalar2=cvv[:, c, 33:34], op0=ALU.mult, op1=ALU.add),
                    reads=[("vch", c), "cvp"], writes=[("vch", c)])
                yo, yok = ybr(2, c, tt)
                P.A(lambda e, v=v, yo=yo: e.activation(out=yo, in_=v[:], func=AF.Silu), reads=[("vch", c)], writes=[yok])
        for mblk in range(4):
            for k in range(4):
                gv, gk = tp.wload(wgt_v[:, :, k * 1024 + mblk * 256:k * 1024 + (mblk + 1) * 256], 8, 256)
                bv, bk_ = tp.wload(wbr_d[k].rearrange("(c p) n -> p c n", p=128)[:, :, mblk * 256:(mblk + 1) * 256], 4, 256)
                for sub in range(2):
                    m = mblk * 2 + sub
                    for tt in range(NT):
                        ai = sub * NT + tt
                        yr_ = [ybr(k, c4, tt) for c4 in range(4)]
                        bg, bgk = tp.bank("gate")
                        kb.mm_group(bg[:], bgk, [(gv[:, c, sub * 128:(sub + 1) * 128], tp.h(c, tt)) for c in range(8)], reads=[gk] + [("h", c, tt) for c in range(8)])
                        by, byk = tp.bank("up")
                        kb.mm_group(by[:], byk, [(bv[:, c4, sub * 128:(sub + 1) * 128], yr_[c4][0]) for c4 in range(4)], reads=[bk_] + [y[1] for y in yr_])
                        si = kb.rr("sg", 2)
                        P.A(lambda e, si=si, bg=bg: e.activation(out=sgt[si][:], in_=bg[:], func=AF.Sigmoid), reads=[bgk], writes=[("sg", si)])
                        if k == 0:
                            P.V(lambda e, si=si, by=by, ai=ai: e.tensor_tensor(out=macc[ai][:], in0=by[:], in1=sgt[si][:], op=ALU.mult),
                                reads=[byk, ("sg", si)], writes=[("macc", ai)])
                        else:
                            P.V(lambda e, si=si, by=by: e.tensor_tensor(out=sgt[si][:], in0=by[:], in1=sgt[si][:], op=ALU.mult),
                                reads=[byk, ("sg", si)], writes=[("sg", si)])
                            if k < 3:
                                P.V(lambda e, si=si, ai=ai: e.tensor_tensor(out=macc[ai][:], in0=macc[ai][:], in1=sgt[si][:], op=ALU.add),
                                    reads=[("macc", ai), ("sg", si)], writes=[("macc", ai)])
                            else:
                                P.V(lambda e, si=si, ai=ai, m=m, tt=tt: e.tensor_tensor(out=tp.hd(m, tt), in0=macc[ai][:], in1=sgt[si][:], op=ALU.add),
                                    reads=[("macc", ai), ("sg", si)], writes=[("hid", m, tt)])
        for mblk in range(4):
            ov, ok_ = tp.wload(wo_v[:, :, mblk * 256:(mblk + 1) * 256], 8, 256)
            for sub in range(2):
                m = mblk * 2 + sub
                for tt in range(NT):
                    ba, bak = tp.bank("acc")
                    kb.mm_group(ba[:], bak, [(ov[:, c, sub * 128:(sub + 1) * 128], tp.hd(c, tt)) for c in range(8)], reads=[ok_] + [("hid", c, tt) for c in range(8)])
                    P.V(lambda e, ba=ba, m=m, tt=tt: e.tensor_tensor(out=tp.x(m, tt), in0=ba[:], in1=tp.x(m, tt), op=ALU.add), reads=[bak, ("x", m, tt)], writes=[("x", m, tt)])
        for tt in range(NT):
            tp.rmsnorm(gn[:, 8:16], tt)
        tp.ffn(wg_d, wu_d, wd_d)
        if final:
            for tt in range(NT):
                tp.rmsnorm(gn[:, 16:24], tt, out_f32=True)
                for c in range(8):
                    si = kb.rr("ostage", 4)
                    P.V(lambda e, c=c, si=si, tt=tt: e.scalar_tensor_tensor(out=macc[si][:], in0=tp.x(c, tt), scalar=gn[:, 16 + c:17 + c], in1=tp.rs[:], op0=ALU.mult, op1=ALU.mult),
                        reads=[("x", c, tt), "rs", "gains"], writes=[("macc", si)])
                    P.dma("sync", lambda e, c=c, si=si, tt=tt, t0=t0: e.dma_start(out=out_o[c * 128:(c + 1) * 128, t0 + tt * TT:t0 + (tt + 1) * TT], in_=macc[si][:]),
                          reads=[("macc", si)], key=("ostage", si))
        else:
            P.dma("sync", lambda e, t0=t0: e.dma_start(out=out_v[:, :, t0:t0 + G], in_=xsb), reads=xkeys, key="xstore")
    return kb.finish()


def _run(nc, in_maps):
    res = run_bass_kernel_spmd(nc, in_maps, core_ids=list(range(NCORES)))
    return res.results


def _gain_tile(g):
    return np.ascontiguousarray(g.reshape(8, 128).T)


import ml_dtypes

NPBF = ml_dtypes.bfloat16
ROPE_THETA = 500000.0


def _consts_B():
    cmask = np.ones((128, TT), np.float32)
    cmask[:, ::HGC] = 0.0
    s_ = np.arange(HGC)[:, None]
    t_ = np.arange(HGC)[None, :]
    amask = np.tile((s_ <= t_).astype(np.float32), (1, TT // HGC))
    j = np.arange(128)[:, None]
    i = np.arange(128)[None, :]
    cur = (j <= i).astype(np.float32)
    prev = (j >= i).astype(np.float32)
    z = np.zeros_like(cur)
    pmask = np.stack([np.concatenate([cur, z, cur, prev], 1), np.concatenate([cur, prev, cur, prev], 1)]).astype(np.float32)
    ident = np.eye(128, dtype=np.float32).astype(NPBF)
    psw = np.zeros((32, 32), np.float32)
    psw[(np.arange(32) + 16) % 32, np.arange(32)] = 1.0
    invf = (np.float32(ROPE_THETA) ** (-(np.arange(16, dtype=np.float32) / np.float32(16)))).astype(np.float32)
    ropec = np.stack([np.concatenate([invf, invf]), np.concatenate([-np.ones(16), np.ones(16)])], 1).astype(np.float32)
    tau = np.tile(np.arange(TT, dtype=np.float32)[None, :], (128, 1))
    return {"cmask": cmask, "amask": amask, "pmask": pmask, "ident": ident, "psw": psw.astype(NPBF), "ropec": ropec, "tau": tau}


def _b_params(inp, l, hh):
    lg = inp["hg_lb_logits"]
    sl = slice(hh * 128, (hh + 1) * 128)
    hgp = np.stack([lg[0, sl], lg[1, sl], inp["hg_gnorm"][l, sl], np.full(128, 1.0 if l > 0 else 0.0, np.float32)], 1).astype(np.float32)
    s5p = np.zeros((128, 4, 67), np.float32)
    for j in range(4):
        for h in range(2):
            g = hh * 8 + 2 * j + h
            ps_ = slice(64 * h, 64 * h + 64)
            s5p[ps_, j, 0] = inp["s5_a_re"][l, g]
            s5p[ps_, j, 1] = inp["s5_a_im"][l, g]
            s5p[ps_, j, 2] = inp["s5_log_dt"][l, g]
            s5p[ps_, j, 3:19] = inp["s5_b_re"][l, g]
            s5p[ps_, j, 19:35] = inp["s5_b_im"][l, g]
            s5p[ps_, j, 35:51] = inp["s5_c_re"][l, g].T
            s5p[ps_, j, 51:67] = inp["s5_c_im"][l, g].T
    s5d = np.ascontiguousarray(inp["s5_d"][l, sl][:, None]).astype(np.float32)
    return {"hgp": hgp, "s5p": s5p.reshape(128, 4 * 67), "s5d": s5d}


def _b_acts(projT_b, hh):
    hg4 = np.stack([projT_b[k * 512 + hh * 128: k * 512 + hh * 128 + 128] for k in range(4)])
    s5u = np.ascontiguousarray(projT_b[2048 + hh * 128: 2048 + hh * 128 + 128])
    att9 = np.stack([projT_b[3584 + i * 1536 + (gi * 4 + hh) * 128: 3584 + i * 1536 + (gi * 4 + hh) * 128 + 128]
                     for gi in range(3) for i in range(3)])
    return {"hg4": np.ascontiguousarray(hg4), "s5u": s5u, "att9": np.ascontiguousarray(att9)}


def _c_params(inp, l):
    gains3 = np.concatenate([_gain_tile(inp["mix_norm"][l]), _gain_tile(inp["ffn2_norm"][l]), _gain_tile(inp["final_norm"])], 1).astype(np.float32)
    convp = np.zeros((128, 4, 34), np.float32)
    for c in range(4):
        sl = slice(c * 128, (c + 1) * 128)
        convp[:, c, 0:31] = inp["conv_w"][l][:, sl].T
        convp[:, c, 31] = inp["conv_b"][l, sl]
        convp[:, c, 32] = inp["conv_ln_g"][l, sl]
        convp[:, c, 33] = inp["conv_ln_b"][l, sl]
    return {"gains3": gains3, "convp": convp.reshape(128, 4 * 34),
            "w_gates": np.ascontiguousarray(inp["w_in"][l][:, NMIX:]), "w_branch": inp["w_branch"][l], "w_out": inp["w_out"][l],
            "w_glu": inp["s5_w_glu"][l], "w_gate": inp["ffn2_w_gate"][l], "w_up": inp["ffn2_w_up"][l], "w_down": inp["ffn2_w_down"][l]}


def _c_conv_halo(conv_b, q):
    t0 = q * TOK
    out = np.zeros((1024, TOK + HALO), conv_b.dtype)
    lo = max(t0 - HALO, 0)
    out[:, HALO - (t0 - lo):] = conv_b[:, lo:t0 + TOK]
    return out


def kernel(**inputs):
    inp = {k: np.asarray(v) for k, v in inputs.items()}
    x = inp["x"]
    progA = build_A()
    progB = {p: build_B((p,)) for p in "hsa"}
    progC = [build_C(False), build_C(True)]
    consts = _consts_B()
    pos = [np.ascontiguousarray(inp["positions"][b][None, :]).astype(np.int32) for b in range(B)]
    xT = [np.ascontiguousarray(x[c // 4, (c % 4) * TOK:(c % 4 + 1) * TOK, :].T) for c in range(NCORES)]
    for l in range(DEPTH):
        wa = {"g_ffn1": _gain_tile(inp["ffn1_norm"][l]), "g_mix": _gain_tile(inp["mix_norm"][l]),
              "w_gate": inp["ffn1_w_gate"][l], "w_up": inp["ffn1_w_up"][l], "w_down": inp["ffn1_w_down"][l],
              "w_in": np.ascontiguousarray(inp["w_in"][l][:, :NMIX])}
        resA = _run(progA, [dict(wa, xT=xT[c]) for c in range(NCORES)])
        x1T = [resA[c]["x1T"] for c in range(NCORES)]
        projT_b = [np.concatenate([resA[4 * b + q]["projT"] for q in range(4)], axis=1) for b in range(B)]
        del resA
        ycat = {}
        for part, big, outk in (("h", "hg4", "ya"), ("s", "s5u", "zb"), ("a", "att9", "yd")):
            in_maps = []
            for c in range(NCORES):
                b, hh = c // 4, c % 4
                m = dict(consts)
                m.update(_b_params(inp, l, hh))
                m[big] = _b_acts(projT_b[b], hh)[big]
                m["pos"] = pos[b]
                in_maps.append(m)
            resB = _run(progB[part], in_maps)
            ycat[outk] = [np.concatenate([resB[4 * b + hh][outk] for hh in range(4)], axis=0) for b in range(B)]
            del resB
        wc = _c_params(inp, l)
        in_maps = []
        for c in range(NCORES):
            b, q = c // 4, c % 4
            tsl = slice(q * TOK, (q + 1) * TOK)
            m = dict(wc)
            m["x1T"] = x1T[c]
            m["yaT"] = np.ascontiguousarray(ycat["ya"][b][:, tsl])
            m["zbT"] = np.ascontiguousarray(ycat["zb"][b][:, tsl])
            m["ydT"] = np.ascontiguousarray(ycat["yd"][b][:, tsl])
            m["convT"] = _c_conv_halo(projT_b[b][2560:3584], q)
            in_maps.append(m)
        resC = _run(progC[1 if l == DEPTH - 1 else 0], in_maps)
        xT = [resC[c]["outT"] for c in range(NCORES)]
        del resC
    out = np.empty((B, L, D), np.float32)
    for c in range(NCORES):
        out[c // 4, (c % 4) * TOK:(c % 4 + 1) * TOK, :] = xT[c].T
    return out
```

```python
import contextlib
import math

import numpy as np
import concourse.bass as bass
import concourse.mybir as mybir
from concourse.bass_utils import run_bass_kernel_spmd

F32 = mybir.dt.float32
BF16 = mybir.dt.bfloat16
I32 = mybir.dt.int32
AF = mybir.ActivationFunctionType
ALU = mybir.AluOpType

NCORES = 8
D = 1024
DFF = 2816
B = 2
L = 8192
DEPTH = 2
TOK = 2048
TT = 512
EPS = 1e-6
NMIX = 8192

ENGINES = ("sync", "scalar", "gpsimd", "vector", "tensor")


class _Op:
    __slots__ = ("eng", "fn", "deps", "is_dma", "sem_key", "ticket", "signals", "idx", "tiny")


class Prog:
    def __init__(self, nc):
        self.nc = nc
        self.ops = []
        self.last_writer = {}
        self.readers = {}
        self.tiny_mode = False

    def _add(self, eng, fn, reads, writes, is_dma=False, sem_key=None):
        op = _Op()
        op.eng, op.fn, op.is_dma, op.sem_key = eng, fn, is_dma, sem_key
        op.signals, op.ticket, op.idx = False, None, len(self.ops)
        op.tiny = self.tiny_mode and not is_dma and eng != "tensor"
        deps = set()
        for r in reads:
            w = self.last_writer.get(r)
            if w is not None:
                deps.add(w)
        for w_ in writes:
            w = self.last_writer.get(w_)
            if w is not None:
                deps.add(w)
            deps.update(self.readers.get(w_, ()))
        deps.discard(op.idx)
        op.deps = deps
        for r in reads:
            self.readers.setdefault(r, []).append(op.idx)
        for w_ in writes:
            self.last_writer[w_] = op.idx
            self.readers[w_] = []
        self.ops.append(op)
        return op

    def op(self, eng, fn, reads=(), writes=()):
        return self._add(eng, fn, tuple(reads), tuple(writes))

    def dma(self, eng, fn, reads=(), writes=(), key=None):
        return self._add(eng, fn, tuple(reads), tuple(writes), is_dma=True, sem_key=key)

    def V(self, fn, reads=(), writes=()):
        return self.op("vector", fn, reads, writes)

    def A(self, fn, reads=(), writes=()):
        return self.op("scalar", fn, reads, writes)

    def G(self, fn, reads=(), writes=()):
        return self.op("gpsimd", fn, reads, writes)

    def T(self, fn, reads=(), writes=()):
        return self.op("tensor", fn, reads, writes)

    def emit(self):
        nc, ops = self.nc, self.ops
        for op in ops:
            for d in op.deps:
                dop = ops[d]
                if dop.is_dma or dop.eng != op.eng or dop.tiny:
                    dop.signals = True
        for op in ops:
            if op.is_dma:
                op.signals = True
        eng_cnt = {e: 0 for e in ENGINES}
        key_cnt = {}
        for op in ops:
            if not op.signals:
                continue
            if op.is_dma:
                key_cnt[op.sem_key] = key_cnt.get(op.sem_key, 0) + 16
                op.ticket = key_cnt[op.sem_key]
            else:
                eng_cnt[op.eng] += 1
                op.ticket = eng_cnt[op.eng]
        with contextlib.ExitStack() as es:
            eng_sem = {e: es.enter_context(nc.semaphore("es_" + e)) for e in ENGINES}
            key_sem = {k: es.enter_context(nc.semaphore("ds_%d" % i)) for i, k in enumerate(key_cnt)}
            block = es.enter_context(nc.Block())

            def semval(dop):
                if dop.is_dma:
                    return key_sem[dop.sem_key], ("k", dop.sem_key), dop.ticket
                return eng_sem[dop.eng], ("e", dop.eng), dop.ticket

            def make_body(ename):
                def body(eng):
                    waited = {}
                    for op in ops:
                        if op.eng != ename:
                            continue
                        need = {}
                        for d in op.deps:
                            dop = ops[d]
                            if (not dop.is_dma) and dop.eng == ename and not dop.tiny:
                                continue
                            sem, skey, val = semval(dop)
                            if waited.get(skey, 0) >= val:
                                continue
                            if need.get(skey, (None, 0))[1] < val:
                                need[skey] = (sem, val)
                        for skey, (sem, val) in need.items():
                            eng.wait_ge(sem, val)
                            waited[skey] = val
                        inst = op.fn(eng)
                        if op.signals:
                            sem, skey, val = semval(op)
                            inst.then_inc(sem, 16 if op.is_dma else 1)
                    if ename == "sync":
                        for k, c in key_cnt.items():
                            if waited.get(("k", k), 0) < c:
                                eng.wait_ge(key_sem[k], c)
                        for e2, c in eng_cnt.items():
                            if c > 0 and e2 != "sync":
                                eng.wait_ge(eng_sem[e2], c)
                return body

            block.sync(make_body("sync"))
            block.scalar(make_body("scalar"))
            block.gpsimd(make_body("gpsimd"))
            block.vector(make_body("vector"))
            block.tensor(make_body("tensor"))


class KB:
    def __init__(self):
        self.nc = bass.Bass("TRN2", target_bir_lowering=False)
        self.es = contextlib.ExitStack()
        self.P = Prog(self.nc)
        self._rr = {}
        self._uid = 0

    def din(self, name, shape, dt=F32):
        return self.nc.dram_tensor(name, list(shape), dt, kind="ExternalInput").ap()

    def dout(self, name, shape, dt=F32):
        return self.nc.dram_tensor(name, list(shape), dt, kind="ExternalOutput").ap()

    def sb(self, name, shape, dt=F32):
        return self.es.enter_context(self.nc.sbuf_tensor("sb_" + name, list(shape), dt))

    def ps(self, name, shape, dt=F32):
        return self.es.enter_context(self.nc.psum_tensor("ps_" + name, list(shape), dt))

    def rr(self, name, n):
        i = self._rr.get(name, 0)
        self._rr[name] = i + 1
        return i % n

    def uid(self):
        self._uid += 1
        return self._uid

    def finish(self):
        self.P.emit()
        self.es.close()
        return self.nc

    def mm_group(self, out_ap, out_key, pairs, reads):
        n = len(pairs)
        for i, (l, r) in enumerate(pairs):
            self.P.T(lambda e, l=l, r=r, i=i: e.matmul(out_ap, lhsT=l, rhs=r, start=(i == 0), stop=(i == n - 1)),
                     reads=reads, writes=[out_key])


class TokenPhase:
    def __init__(self, kb, G):
        self.kb = kb
        self.G = G
        self.NT = G // TT
        kb_ = kb
        self.xT = kb_.sb("xT", [128, 8 * G], F32)
        self.hT = kb_.sb("hT", [128, 8 * G], BF16)
        self.hid = kb_.sb("hid", [128, 22 * G], BF16)
        self.sq = kb_.sb("sq", [128, 8 * TT], BF16)
        self.rs = kb_.sb("rs", [128, TT], F32)
        self.ones = kb_.sb("ones", [128, 128], BF16)
        self.sg = [kb_.sb("sg%d" % i, [128, TT], F32) for i in range(2)]
        self.wslot = [kb_.sb("wslot%d" % i, [128, 22 * 256], BF16) for i in range(3)]
        self.banks = [kb_.ps("bank%d" % i, [128, TT], F32) for i in range(8)]
        kb.P.V(lambda e: e.memset(self.ones[:], 1.0), writes=["ones"])

    def x(self, c, tt):
        return self.xT[:, c * self.G + tt * TT: c * self.G + (tt + 1) * TT]

    def h(self, c, tt):
        return self.hT[:, c * self.G + tt * TT: c * self.G + (tt + 1) * TT]

    def hd(self, j, tt):
        return self.hid[:, j * self.G + tt * TT: j * self.G + (tt + 1) * TT]

    def bank(self, purpose):
        groups = {"gate": (0, 1), "up": (2, 3), "acc": (4, 5, 6), "stat": (7,)}[purpose]
        i = groups[self.kb.rr("bank_" + purpose, len(groups))]
        return self.banks[i], ("bank", i)

    def wload(self, src_ap, nk, ncols):
        kb = self.kb
        si = kb.rr("wslot", 3)
        slot = self.wslot[si]
        view = slot[:, 0:nk * ncols].rearrange("p (c n) -> p c n", n=ncols)
        kb.P.dma("gpsimd", lambda e: e.dma_start(out=view, in_=src_ap), writes=[("wslot", si)], key=("wslot", si))
        return view, ("wslot", si)

    def rmsnorm(self, gain_ap, tt, out_f32=False):
        kb, P = self.kb, self.kb.P
        for c in range(8):
            P.A(lambda e, c=c: e.activation(out=self.sq[:, c * TT:(c + 1) * TT], in_=self.x(c, tt), func=AF.Square),
                reads=[("x", c, tt)], writes=[("sq", c)])
        bk, bkey = self.bank("stat")
        kb.mm_group(bk[:], bkey, [(self.ones[:], self.sq[:, c * TT:(c + 1) * TT]) for c in range(8)],
                    reads=["ones"] + [("sq", c) for c in range(8)])
        P.A(lambda e: e.activation(out=self.rs[:], in_=bk[:], func=AF.Sqrt, bias=EPS, scale=1.0 / D),
            reads=[bkey], writes=["rs"])
        P.V(lambda e: e.reciprocal(out=self.rs[:], in_=self.rs[:]), reads=["rs"], writes=["rs"])
        if out_f32:
            return
        for c in range(8):
            P.V(lambda e, c=c: e.scalar_tensor_tensor(out=self.h(c, tt), in0=self.x(c, tt), scalar=gain_ap[:, c:c + 1],
                                                      in1=self.rs[:], op0=ALU.mult, op1=ALU.mult),
                reads=[("x", c, tt), "rs", "gains"], writes=[("h", c, tt)])

    def ffn(self, wg, wu, wd):
        kb, P, NT = self.kb, self.kb.P, self.NT
        wg_v = wg.rearrange("(c p) n -> p c n", p=128)
        wu_v = wu.rearrange("(c p) n -> p c n", p=128)
        wd_v = wd.rearrange("(c p) n -> p c n", p=128)
        for blk in range(DFF // 256):
            gv, gk = self.wload(wg_v[:, :, blk * 256:(blk + 1) * 256], 8, 256)
            uv, uk = self.wload(wu_v[:, :, blk * 256:(blk + 1) * 256], 8, 256)
            for sub in range(2):
                j = blk * 2 + sub
                for tt in range(NT):
                    hreads = [("h", c, tt) for c in range(8)]
                    bg, bgk = self.bank("gate")
                    kb.mm_group(bg[:], bgk, [(gv[:, c, sub * 128:(sub + 1) * 128], self.h(c, tt)) for c in range(8)],
                                reads=[gk] + hreads)
                    bu, buk = self.bank("up")
                    kb.mm_group(bu[:], buk, [(uv[:, c, sub * 128:(sub + 1) * 128], self.h(c, tt)) for c in range(8)],
                                reads=[uk] + hreads)
                    si = kb.rr("sg", 2)
                    sg = self.sg[si]
                    P.A(lambda e, sg=sg, bg=bg: e.activation(out=sg[:], in_=bg[:], func=AF.Silu),
                        reads=[bgk], writes=[("sg", si)])
                    P.V(lambda e, sg=sg, bu=bu, j=j, tt=tt: e.tensor_tensor(out=self.hd(j, tt), in0=bu[:], in1=sg[:], op=ALU.mult),
                        reads=[buk, ("sg", si)], writes=[("hid", j, tt)])
        for mblk in range(D // 256):
            dv, dk = self.wload(wd_v[:, :, mblk * 256:(mblk + 1) * 256], 22, 256)
            for sub in range(2):
                m = mblk * 2 + sub
                for tt in range(NT):
                    ba, bak = self.bank("acc")
                    kb.mm_group(ba[:], bak, [(dv[:, j, sub * 128:(sub + 1) * 128], self.hd(j, tt)) for j in range(22)],
                                reads=[dk] + [("hid", j, tt) for j in range(22)])
                    P.V(lambda e, ba=ba, m=m, tt=tt: e.scalar_tensor_tensor(out=self.x(m, tt), in0=ba[:], scalar=0.5, in1=self.x(m, tt),
                                                                             op0=ALU.mult, op1=ALU.add),
                        reads=[bak, ("x", m, tt)], writes=[("x", m, tt)])


def build_A():
    kb = KB()
    P = kb.P
    xT_d = kb.din("xT", [D, TOK])
    g1_d = kb.din("g_ffn1", [128, 8])
    g2_d = kb.din("g_mix", [128, 8])
    wg_d = kb.din("w_gate", [D, DFF])
    wu_d = kb.din("w_up", [D, DFF])
    wd_d = kb.din("w_down", [DFF, D])
    win_d = kb.din("w_in", [D, NMIX])
    x1_o = kb.dout("x1T", [D, TOK])
    pj_o = kb.dout("projT", [NMIX, TOK], BF16)
    G = 1024
    tp = TokenPhase(kb, G)
    NT = tp.NT
    g1 = kb.sb("g1", [128, 8])
    g2 = kb.sb("g2", [128, 8])
    stage = [kb.sb("stage%d" % i, [128, TT], BF16) for i in range(4)]
    P.dma("sync", lambda e: e.dma_start(out=g1[:], in_=g1_d), writes=["gains"], key="gains")
    P.dma("sync", lambda e: e.dma_start(out=g2[:], in_=g2_d), writes=["gains"], key="gains")
    xT_v = xT_d.rearrange("(c p) t -> p c t", p=128)
    x1_v = x1_o.rearrange("(c p) t -> p c t", p=128)
    win_v = win_d.rearrange("(c p) n -> p c n", p=128)
    xkeys = [("x", c, tt) for c in range(8) for tt in range(NT)]
    for grp in range(TOK // G):
        t0 = grp * G
        xsb = tp.xT[:].rearrange("p (c t) -> p c t", t=G)
        P.dma("sync", lambda e, t0=t0: e.dma_start(out=xsb, in_=xT_v[:, :, t0:t0 + G]), writes=xkeys, key="xload")
        for tt in range(NT):
            tp.rmsnorm(g1, tt)
        tp.ffn(wg_d, wu_d, wd_d)
        P.dma("sync", lambda e, t0=t0: e.dma_start(out=x1_v[:, :, t0:t0 + G], in_=xsb), reads=xkeys, key="x1store")
        for tt in range(NT):
            tp.rmsnorm(g2, tt)
        for blk in range(NMIX // 256):
            wv, wk = tp.wload(win_v[:, :, blk * 256:(blk + 1) * 256], 8, 256)
            for sub in range(2):
                col = blk * 256 + sub * 128
                for tt in range(NT):
                    ba, bak = tp.bank("acc")
                    kb.mm_group(ba[:], bak, [(wv[:, c, sub * 128:(sub + 1) * 128], tp.h(c, tt)) for c in range(8)],
                                reads=[wk] + [("h", c, tt) for c in range(8)])
                    si = kb.rr("stage", 4)
                    st = stage[si]
                    if si % 2 == 0:
                        P.A(lambda e, st=st, ba=ba: e.activation(out=st[:], in_=ba[:], func=AF.Copy), reads=[bak], writes=[("stage", si)])
                    else:
                        P.V(lambda e, st=st, ba=ba: e.tensor_copy(out=st[:], in_=ba[:]), reads=[bak], writes=[("stage", si)])
                    P.dma("sync", lambda e, st=st, col=col, tt=tt, t0=t0: e.dma_start(
                        out=pj_o[col:col + 128, t0 + tt * TT:t0 + (tt + 1) * TT], in_=st[:]),
                        reads=[("stage", si)], key=("stage", si))
    return kb.finish()


_DBG = 0
NT_L = L // TT
HGC = 64
TWO_PI = 2.0 * math.pi


def build_B(parts=("h", "s", "a")):
    kb = KB()
    P = kb.P
    hg_d = kb.din("hg4", [4, 128, L], BF16) if "h" in parts else None
    s5u_d = kb.din("s5u", [128, L], BF16) if "s" in parts else None
    att_d = kb.din("att9", [9, 128, L], BF16) if "a" in parts else None
    pos_d = kb.din("pos", [1, L], I32)
    cmask_d = kb.din("cmask", [128, TT])
    amask_d = kb.din("amask", [64, TT])
    pmask_d = kb.din("pmask", [2, 128, TT])
    ident_d = kb.din("ident", [128, 128], BF16)
    psw_d = kb.din("psw", [32, 32], BF16)
    ropec_d = kb.din("ropec", [32, 2])
    tau_d = kb.din("tau", [128, TT])
    hgp_d = kb.din("hgp", [128, 4])
    s5p_d = kb.din("s5p", [128, 4 * 67])
    s5d_d = kb.din("s5d", [128, 1])
    ya_o = kb.dout("ya", [128, L], BF16) if "h" in parts else None
    zb_o = kb.dout("zb", [128, L], BF16) if "s" in parts else None
    yd_o = kb.dout("yd", [128, L], BF16) if "a" in parts else None

    NS = 11
    S = [kb.sb("scr%d" % i, [128, TT], F32) for i in range(NS)]
    SK = [("scr", i) for i in range(NS)]
    NSB = 6
    SB = [kb.sb("scb%d" % i, [128, TT], BF16) for i in range(NSB)]
    SBK = [("scb", i) for i in range(NSB)]
    banks = [kb.ps("bank%d" % i, [128, TT], F32) for i in range(8)]
    BK = [("bank", i) for i in range(8)]
    cmask = kb.sb("cmask", [128, TT]); amask = kb.sb("amask", [64, TT])
    pmask = kb.sb("pmask", [128, 2 * TT], BF16)
    ident = kb.sb("ident", [128, 128], BF16); psw = kb.sb("psw", [32, 32], BF16)
    ropec = kb.sb("ropec", [32, 2]); tau = kb.sb("tau", [128, TT])
    hgp = kb.sb("hgp", [128, 4]); s5p = kb.sb("s5p", [128, 4 * 67]); s5d = kb.sb("s5d", [128, 1])
    ones = kb.sb("ones", [128, 128], BF16)
    P.V(lambda e: e.memset(ones[:], 1.0), writes=["ones"])
    for nm, dst, src in (("cmask", cmask, cmask_d), ("amask", amask, amask_d), ("ident", ident, ident_d), ("psw", psw, psw_d),
                         ("ropec", ropec, ropec_d), ("tau", tau, tau_d), ("hgp", hgp, hgp_d), ("s5p", s5p, s5p_d), ("s5d", s5d, s5d_d)):
        P.dma("sync", lambda e, dst=dst, src=src: e.dma_start(out=dst[:], in_=src), writes=[nm], key="c_" + nm)
    P.dma("gpsimd", lambda e: e.dma_start(out=pmask[:].rearrange("p (a t) -> p a t", a=2), in_=pmask_d.rearrange("a p t -> p a t")),
          writes=["pmask"], key="c_pmask")

    def sin_table(out_ap, ang_ap, shape, shift, tmpf, tmpi, reads, writes, scale_ap=None, tkey="sin_tmp2"):
        P.V(lambda e: e.tensor_scalar(out=tmpf, in0=ang_ap, scalar1=1.0 / TWO_PI, scalar2=shift / TWO_PI, op0=ALU.mult, op1=ALU.add),
            reads=reads, writes=["sin_tmp", tkey])
        P.V(lambda e: e.tensor_copy(out=tmpi, in_=tmpf), reads=["sin_tmp"], writes=["sin_tmpi"])
        P.V(lambda e: e.tensor_copy(out=tmpf, in_=tmpi), reads=["sin_tmpi"], writes=["sin_tmp", tkey])
        P.V(lambda e: e.scalar_tensor_tensor(out=tmpf, in0=tmpf, scalar=-TWO_PI, in1=ang_ap, op0=ALU.mult, op1=ALU.add),
            reads=["sin_tmp"] + list(reads), writes=["sin_tmp", tkey])
        P.V(lambda e: e.tensor_scalar(out=tmpf, in0=tmpf, scalar1=-math.pi - shift, scalar2=math.pi - shift, op0=ALU.max, op1=ALU.min),
            reads=["sin_tmp"], writes=["sin_tmp", tkey])
        if scale_ap is None:
            P.A(lambda e: e.activation(out=out_ap, in_=tmpf, func=AF.Sin, bias=shiftc[shift][:shape[0], 0:1]), reads=["sin_tmp", "shiftc", tkey], writes=writes)
        else:
            assert shift == 0.0
            P.A(lambda e: e.activation(out=out_ap, in_=tmpf, func=AF.Sin, scale=scale_ap), reads=["sin_tmp", tkey], writes=writes)

    shiftc = {0.0: kb.sb("shift0", [128, 1]), math.pi / 2: kb.sb("shift1", [128, 1])}
    P.V(lambda e: e.memset(shiftc[0.0][:], 0.0), writes=["shiftc"])
    P.V(lambda e: e.memset(shiftc[math.pi / 2][:], math.pi / 2), writes=["shiftc"])
    tmpi = kb.sb("tmpi", [128, TT], I32)

    lb = kb.sb("lb", [128, 1]); oml = kb.sb("oml", [128, 1])
    P.tiny_mode = True
    P.V(lambda e: e.tensor_tensor(out=lb[:], in0=hgp[:, 1:2], in1=hgp[:, 0:1], op=ALU.subtract), reads=["hgp"], writes=["lb"])
    P.A(lambda e: e.activation(out=lb[:], in_=lb[:], func=AF.Sigmoid), reads=["lb"], writes=["lb"])
    P.V(lambda e: e.tensor_tensor(out=lb[:], in0=lb[:], in1=hgp[:, 3:4], op=ALU.mult), reads=["lb", "hgp"], writes=["lb"])
    P.V(lambda e: e.tensor_scalar(out=oml[:], in0=lb[:], scalar1=-1.0, scalar2=1.0, op0=ALU.mult, op1=ALU.add), reads=["lb"], writes=["oml"])
    hc1 = kb.sb("hc1", [128, 1]); hc2 = kb.sb("hc2", [128, 1])
    P.V(lambda e: e.tensor_scalar(out=hc1[:], in0=oml[:], scalar1=0.5, scalar2=None, op0=ALU.mult), reads=["oml"], writes=["hc"])
    P.V(lambda e: e.tensor_tensor(out=hc2[:], in0=hc1[:], in1=lb[:], op=ALU.add), reads=["hc", "lb"], writes=["hc"])
    P.tiny_mode = False
    Sst = kb.sb("Sst", [128, 128]); Sb = kb.sb("Sb", [128, 128], BF16)
    P.V(lambda e: e.memset(Sst[:], 0.0), writes=["Sst"])
    esc = kb.sb("esc", [128, 32])
    hin = [[kb.sb("hin%d_%d" % (a, i), [128, TT], BF16) for i in range(4)] for a in range(2)]
    KT = kb.sb("KTtok", [64, 8 * 128], BF16); VT = kb.sb("VTtok", [64, 8 * 128], BF16)
    pT = [kb.ps("pT%d" % i, [128, 1024], BF16) for i in range(0)]
    QSC = float(128 ** -0.5)
    for t in (range(NT_L) if "h" in parts else ()):
        a = t % 2
        tsl = slice(t * TT, (t + 1) * TT)
        for i in range(4):
            P.dma("sync", lambda e, i=i, a=a, tsl=tsl: e.dma_start(out=hin[a][i][:], in_=hg_d[i, :, tsl]),
                  writes=[("hin", a, i)], key=("hin", a, i))
        q_t, f_t, i_t, g_t = hin[a]
        sg, t1, lf, bb, bq, eq, ek, qs, osb, rst, sgl = (S[k] for k in range(11))
        Qt, Kt, attm, osq = SB[0], SB[1], SB[2], SB[3]
        P.A(lambda e, f_t=f_t: e.activation(out=sg[:], in_=f_t[:], func=AF.Tanh, scale=0.5), reads=[("hin", a, 1)], writes=[SK[0]])
        P.A(lambda e, q_t=q_t: e.activation(out=qs[:], in_=q_t[:], func=AF.Silu), reads=[("hin", a, 0)], writes=[SK[7]])
        P.A(lambda e, g_t=g_t: e.activation(out=sgl[:], in_=g_t[:], func=AF.Silu), reads=[("hin", a, 3)], writes=[SK[10]])
        P.V(lambda e: e.tensor_scalar(out=t1[:], in0=sg[:], scalar1=hc1[:, 0:1], scalar2=hc2[:, 0:1], op0=ALU.mult, op1=ALU.add),
            reads=[SK[0], "hc"], writes=[SK[1]])
        P.A(lambda e: e.activation(out=lf[:], in_=t1[:], func=AF.Ln), reads=[SK[1]], writes=[SK[2]])
        P.V(lambda e: e.tensor_tensor_scan(out=bb[:], data0=cmask[:], data1=lf[:], initial=0.0, op0=ALU.mult, op1=ALU.add),
            reads=[SK[2], "cmask"], writes=[SK[3]])
        b3 = bb[:].rearrange("p (c t) -> p c t", t=HGC)
        P.V(lambda e, b3=b3: e.tensor_tensor(out=bq[:].rearrange("p (c t) -> p c t", t=HGC), in0=b3,
                                             in1=b3[:, :, 32:33].to_broadcast([128, 8, HGC]), op=ALU.subtract),
            reads=[SK[3]], writes=[SK[4]])
        P.A(lambda e: e.activation(out=eq[:], in_=bq[:], func=AF.Exp), reads=[SK[4]], writes=[SK[5]])
        P.A(lambda e: e.activation(out=ek[:], in_=bq[:], func=AF.Exp, scale=-1.0), reads=[SK[4]], writes=[SK[6]])
        P.V(lambda e: e.scalar_tensor_tensor(out=Qt[:], in0=qs[:], scalar=QSC, in1=eq[:], op0=ALU.mult, op1=ALU.mult),
            reads=[SK[7], SK[5]], writes=[SBK[0]])
        P.V(lambda e: e.tensor_scalar(out=t1[:], in0=t1[:], scalar1=-1.0, scalar2=1.0, op0=ALU.mult, op1=ALU.add), reads=[SK[1]], writes=[SK[1]])
        P.V(lambda e: e.tensor_tensor(out=Kt[:], in0=t1[:], in1=ek[:], op=ALU.mult), reads=[SK[1], SK[6]], writes=[SBK[1]])
        if _DBG == 1:
            continue
        P.tiny_mode = True
        P.V(lambda e, b3=b3: e.tensor_copy(out=esc[:, 0:8], in_=b3[:, :, 32]), reads=[SK[3]], writes=["esc"])
        P.A(lambda e: e.activation(out=esc[:, 8:16], in_=esc[:, 0:8], func=AF.Exp), reads=["esc"], writes=["esc"])
        P.A(lambda e, b3=b3: e.activation(out=esc[:, 16:24], in_=b3[:, :, 63], func=AF.Exp), reads=[SK[3], "esc"], writes=["esc"])
        P.V(lambda e, b3=b3: e.tensor_tensor(out=esc[:, 24:32], in0=b3[:, :, 63], in1=esc[:, 0:8], op=ALU.subtract), reads=[SK[3], "esc"], writes=["esc"])
        P.A(lambda e: e.activation(out=esc[:, 24:32], in_=esc[:, 24:32], func=AF.Exp), reads=["esc"], writes=["esc"])
        if _DBG == 2:
            continue
        P.tiny_mode = False
        kt_ps = banks[0][:].bitcast(BF16)
        vt_ps = banks[1][:].bitcast(BF16)
        for n in range(8):
            P.T(lambda e, n=n: e.transpose(out=kt_ps[0:64, n * 128:(n + 1) * 128], in_=Kt[:, n * 64:(n + 1) * 64], identity=ident[:]),
                reads=[SBK[1], "ident"], writes=[BK[0]])
        for n in range(8):
            P.T(lambda e, n=n, i_t=i_t: e.transpose(out=vt_ps[0:64, n * 128:(n + 1) * 128], in_=i_t[:, n * 64:(n + 1) * 64], identity=ident[:]),
                reads=[("hin", a, 2), "ident"], writes=[BK[1]])
        P.A(lambda e: e.activation(out=KT[:], in_=kt_ps[0:64, :], func=AF.Copy), reads=[BK[0]], writes=["KT"])
        P.V(lambda e: e.tensor_copy(out=VT[:], in_=vt_ps[0:64, :]), reads=[BK[1]], writes=["VT"])
        if _DBG == 3:
            continue
        for n in range(8):
            P.T(lambda e, n=n: e.matmul(banks[2][0:64, n * 64:(n + 1) * 64], lhsT=Kt[:, n * 64:(n + 1) * 64], rhs=Qt[:, n * 64:(n + 1) * 64],
                                        start=True, stop=True), reads=[SBK[0], SBK[1]], writes=[BK[2]])
        P.V(lambda e: e.tensor_tensor(out=attm[0:64, :], in0=banks[2][0:64, :], in1=amask[:], op=ALU.mult), reads=[BK[2], "amask"], writes=[SBK[2]])
        if _DBG == 4:
            continue
        for n in range(8):
            bi = 3 + n // 4
            P.T(lambda e, n=n, bi=bi: e.matmul(banks[bi][:, (n % 4) * 128:(n % 4 + 1) * 128], lhsT=KT[:, n * 128:(n + 1) * 128],
                                               rhs=VT[:, n * 128:(n + 1) * 128], start=True, stop=True), reads=["KT", "VT"], writes=[BK[bi]])
        P.tiny_mode = True
        for n in range(8):
            bi = 3 + n // 4
            P.V(lambda e, n=n: e.tensor_scalar(out=Sb[:], in0=Sst[:], scalar1=esc[:, 8 + n:9 + n], scalar2=None, op0=ALU.mult),
                reads=["Sst", "esc"], writes=["Sb"])
            P.T(lambda e, n=n: e.matmul(banks[5][:, n * 64:(n + 1) * 64], lhsT=Sb[:], rhs=Qt[:, n * 64:(n + 1) * 64], start=True, stop=False),
                reads=["Sb", SBK[0]], writes=[BK[5]])
            P.T(lambda e, n=n: e.matmul(banks[5][:, n * 64:(n + 1) * 64], lhsT=VT[:, n * 128:(n + 1) * 128], rhs=attm[0:64, n * 64:(n + 1) * 64],
                                        start=False, stop=True), reads=["VT", SBK[2]], writes=[BK[5]])
            P.V(lambda e, n=n: e.tensor_scalar(out=Sst[:], in0=Sst[:], scalar1=esc[:, 16 + n:17 + n], scalar2=None, op0=ALU.mult),
                reads=["Sst", "esc"], writes=["Sst"])
            P.V(lambda e, n=n, bi=bi: e.scalar_tensor_tensor(out=Sst[:], in0=banks[bi][:, (n % 4) * 128:(n % 4 + 1) * 128], scalar=esc[:, 24 + n:25 + n],
                                                             in1=Sst[:], op0=ALU.mult, op1=ALU.add), reads=[BK[bi], "Sst", "esc"], writes=["Sst"])
        P.tiny_mode = False
        if _DBG == 5:
            continue
        if _DBG != 11:
            pass
        if _DBG != 10:
            P.V(lambda e: e.tensor_copy(out=osb[:], in_=banks[5][:]), reads=[BK[5]], writes=[SK[8]])
        if _DBG == 20 and t == 0:
            P.dma("sync", lambda e: e.dma_start(out=zb_o[:, 0:512], in_=Qt[:]), reads=[SBK[0]], key="dbg0")
            P.dma("sync", lambda e: e.dma_start(out=zb_o[:, 512:1024], in_=Kt[:]), reads=[SBK[1]], key="dbg1")
            P.dma("sync", lambda e: e.dma_start(out=zb_o[0:64, 1024:1536], in_=attm[0:64, :]), reads=[SBK[2]], key="dbg2")
            P.dma("gpsimd", lambda e: e.dma_start(out=zb_o[:, 1536:2048], in_=bb[:]), reads=[SK[3]], key="dbg3")
            P.dma("sync", lambda e: e.dma_start(out=yd_o[0:64, 0:1024], in_=KT[:]), reads=["KT"], key="dbg4")
            P.dma("sync", lambda e: e.dma_start(out=yd_o[0:64, 1024:2048], in_=VT[:]), reads=["VT"], key="dbg5")
            P.dma("gpsimd", lambda e: e.dma_start(out=yd_o[:, 2048:2560], in_=osb[:]), reads=[SK[8]], key="dbg6")
            P.dma("gpsimd", lambda e: e.dma_start(out=yd_o[:, 2560:2688], in_=Sst[:]), reads=["Sst"], key="dbg7")
            P.dma("gpsimd", lambda e: e.dma_start(out=yd_o[:, 2688:2720], in_=esc[:]), reads=["esc"], key="dbg8")
        P.A(lambda e: e.activation(out=osq[:], in_=osb[:], func=AF.Square), reads=[SK[8]], writes=[SBK[3]])
        if _DBG in (10, 11):
            continue
        if _DBG == 6:
            continue
        P.T(lambda e: e.matmul(banks[6][:], lhsT=ones[:], rhs=osq[:], start=True, stop=True), reads=["ones", SBK[3]], writes=[BK[6]])
        P.A(lambda e: e.activation(out=rst[:], in_=banks[6][:], func=AF.Ln, bias=EPS, scale=1.0 / 128), reads=[BK[6]], writes=[SK[9]])
        P.A(lambda e: e.activation(out=rst[:], in_=rst[:], func=AF.Exp, scale=-0.5), reads=[SK[9]], writes=[SK[9]])
        P.V(lambda e: e.scalar_tensor_tensor(out=osb[:], in0=osb[:], scalar=hgp[:, 2:3], in1=rst[:], op0=ALU.mult, op1=ALU.mult),
            reads=[SK[8], SK[9], "hgp"], writes=[SK[8]])
        if _DBG == 8:
            continue
        yo = SB[4 + a]
        P.V(lambda e, yo=yo: e.tensor_tensor(out=yo[:], in0=osb[:], in1=sgl[:], op=ALU.mult), reads=[SK[8], SK[10]], writes=[SBK[4 + a]])
        if _DBG == 9:
            continue
        P.dma("sync", lambda e, yo=yo, tsl=tsl: e.dma_start(out=ya_o[:, tsl], in_=yo[:]), reads=[SBK[4 + a]], key=("yo", a))

    s5v = s5p[:].rearrange("p (j k) -> p j k", k=67)
    a_re, a_im, ldt = s5v[:, :, 0], s5v[:, :, 1], s5v[:, :, 2]
    sm = kb.sb("s5small", [128, 64])
    def col(i):
        return sm[:, 4 * i:4 * i + 4]
    dt_, adt, mag, th, cth, sth, abr, abi, den, m1, zr, zi, tA, tB, c512, s512 = (col(i) for i in range(16))
    smi = kb.sb("s5smalli", [128, 4], I32)
    P.tiny_mode = True
    P.A(lambda e: e.activation(out=dt_, in_=ldt, func=AF.Exp), reads=["s5p"], writes=["sm_dt"])
    P.V(lambda e: e.tensor_tensor(out=adt, in0=a_re, in1=dt_, op=ALU.mult), reads=["s5p", "sm_dt"], writes=["sm_adt"])
    P.A(lambda e: e.activation(out=mag, in_=adt, func=AF.Exp), reads=["sm_adt"], writes=["sm_mag"])
    P.V(lambda e: e.tensor_tensor(out=th, in0=a_im, in1=dt_, op=ALU.mult), reads=["s5p", "sm_dt"], writes=["sm_th"])
    sin_table(sth, th, [128, 4], 0.0, tA, smi[:], ["sm_th"], ["sm_sth"])
    sin_table(cth, th, [128, 4], math.pi / 2, tA, smi[:], ["sm_th"], ["sm_cth"])
    P.V(lambda e: e.tensor_scalar(out=tB, in0=th, scalar1=float(TT), scalar2=None, op0=ALU.mult), reads=["sm_th"], writes=["sm_tB"])
    sin_table(s512, tB, [128, 4], 0.0, tA, smi[:], ["sm_tB"], ["sm_s512"])
    sin_table(c512, tB, [128, 4], math.pi / 2, tA, smi[:], ["sm_tB"], ["sm_c512"])
    ns512 = kb.sb("ns512", [128, 4])
    P.V(lambda e: e.tensor_scalar(out=ns512[:], in0=s512, scalar1=-1.0, scalar2=None, op0=ALU.mult), reads=["sm_s512"], writes=["ns512"])
    P.V(lambda e: e.tensor_tensor(out=abr, in0=mag, in1=cth, op=ALU.mult), reads=["sm_mag", "sm_cth"], writes=["sm_abr"])
    P.V(lambda e: e.tensor_tensor(out=abi, in0=mag, in1=sth, op=ALU.mult), reads=["sm_mag", "sm_sth"], writes=["sm_abi"])
    P.V(lambda e: e.tensor_tensor(out=den, in0=a_re, in1=a_re, op=ALU.mult), reads=["s5p"], writes=["sm_den"])
    P.V(lambda e: e.tensor_tensor(out=tA, in0=a_im, in1=a_im, op=ALU.mult), reads=["s5p", "sm_c512"], writes=["sin_tmp"])
    P.V(lambda e: e.tensor_tensor(out=den, in0=den, in1=tA, op=ALU.add), reads=["sm_den", "sin_tmp"], writes=["sm_den"])
    P.V(lambda e: e.reciprocal(out=den, in_=den), reads=["sm_den"], writes=["sm_den"])
    P.V(lambda e: e.tensor_scalar(out=m1, in0=abr, scalar1=-1.0, scalar2=None, op0=ALU.add), reads=["sm_abr"], writes=["sm_m1"])
    P.V(lambda e: e.tensor_tensor(out=zr, in0=m1, in1=a_re, op=ALU.mult), reads=["sm_m1", "s5p"], writes=["sm_zr"])
    P.V(lambda e: e.tensor_tensor(out=tA, in0=abi, in1=a_im, op=ALU.mult), reads=["sm_abi", "s5p"], writes=["sin_tmp"])
    P.V(lambda e: e.tensor_tensor(out=zr, in0=zr, in1=tA, op=ALU.add), reads=["sm_zr", "sin_tmp"], writes=["sm_zr"])
    P.V(lambda e: e.tensor_tensor(out=zr, in0=zr, in1=den, op=ALU.mult), reads=["sm_zr", "sm_den"], writes=["sm_zr"])
    P.V(lambda e: e.tensor_tensor(out=zi, in0=abi, in1=a_re, op=ALU.mult), reads=["sm_abi", "s5p"], writes=["sm_zi"])
    P.V(lambda e: e.tensor_tensor(out=tA, in0=m1, in1=a_im, op=ALU.mult), reads=["sm_m1", "s5p"], writes=["sin_tmp"])
    P.V(lambda e: e.tensor_tensor(out=zi, in0=zi, in1=tA, op=ALU.subtract), reads=["sm_zi", "sin_tmp"], writes=["sm_zi"])
    P.V(lambda e: e.tensor_tensor(out=zi, in0=zi, in1=den, op=ALU.mult), reads=["sm_zi", "sm_den"], writes=["sm_zi"])
    Bex = [kb.sb("Bex%d" % i, [128, 4 * 128], BF16) for i in range(2)]
    Cex = [kb.sb("Cex%d" % i, [128, 4 * 128], BF16) for i in range(2)]
    BT = [kb.sb("BT%d" % i, [128, 4 * 128], BF16) for i in range(2)]
    bbt = kb.sb("bbt", [128, 64])
    for i in range(2):
        P.V(lambda e, i=i: e.memset(Bex[i][:], 0.0), writes=[("Bex", i)])
        P.V(lambda e, i=i: e.memset(Cex[i][:], 0.0), writes=[("Cex", i)])
    for j in range(4):
        bre, bim = s5v[:, j, 3:19], s5v[:, j, 19:35]
        cre, cim = s5v[:, j, 35:51], s5v[:, j, 51:67]
        zrj, zij = zr[:, j:j + 1], zi[:, j:j + 1]
        P.V(lambda e, bre=bre, zrj=zrj: e.tensor_scalar(out=bbt[:, 0:16], in0=bre, scalar1=zrj, scalar2=None, op0=ALU.mult), reads=["s5p", "sm_zr"], writes=["bbt"])
        P.V(lambda e, bim=bim, zij=zij: e.tensor_scalar(out=bbt[:, 16:32], in0=bim, scalar1=zij, scalar2=None, op0=ALU.mult), reads=["s5p", "sm_zi"], writes=["bbt"])
        P.V(lambda e, bim=bim, zrj=zrj: e.tensor_scalar(out=bbt[:, 32:48], in0=bim, scalar1=zrj, scalar2=None, op0=ALU.mult), reads=["s5p", "sm_zr"], writes=["bbt"])
        P.V(lambda e, bre=bre, zij=zij: e.tensor_scalar(out=bbt[:, 48:64], in0=bre, scalar1=zij, scalar2=None, op0=ALU.mult), reads=["s5p", "sm_zi"], writes=["bbt"])
        for hh_ in range(2):
            ps_ = slice(64 * hh_, 64 * hh_ + 64)
            cs = slice(j * 128 + (2 * j + hh_) * 16, j * 128 + (2 * j + hh_) * 16 + 16)
            P.V(lambda e, ps_=ps_, cs=cs: e.tensor_tensor(out=Bex[0][ps_, cs], in0=bbt[ps_, 0:16], in1=bbt[ps_, 16:32], op=ALU.subtract), reads=["bbt"], writes=[("Bex", 0)])
            P.V(lambda e, ps_=ps_, cs=cs: e.tensor_tensor(out=Bex[1][ps_, cs], in0=bbt[ps_, 32:48], in1=bbt[ps_, 48:64], op=ALU.add), reads=["bbt"], writes=[("Bex", 1)])
            P.V(lambda e, ps_=ps_, cs=cs, cre=cre: e.tensor_copy(out=Cex[0][ps_, cs], in_=cre[ps_, :]), reads=["s5p"], writes=[("Cex", 0)])
            P.V(lambda e, ps_=ps_, cs=cs, cim=cim: e.tensor_scalar(out=Cex[1][ps_, cs], in0=cim[ps_, :], scalar1=-1.0, scalar2=None, op0=ALU.mult), reads=["s5p"], writes=[("Cex", 1)])
    P.tiny_mode = False
    for i in range(2):
        bt_ps = banks[i][:].bitcast(BF16)
        for j in range(4):
            P.T(lambda e, i=i, j=j, bt_ps=bt_ps: e.transpose(out=bt_ps[:, j * 128:(j + 1) * 128], in_=Bex[i][:, j * 128:(j + 1) * 128], identity=ident[:]),
                reads=[("Bex", i), "ident"], writes=[BK[i]])
        P.V(lambda e, i=i, bt_ps=bt_ps: e.tensor_copy(out=BT[i][:], in_=bt_ps[:, 0:512]), reads=[BK[i]], writes=[("BT", i)])
    cosT = [kb.sb("cosT%d" % j, [128, TT]) for j in range(4)]
    sinT = [kb.sb("sinT%d" % j, [128, TT]) for j in range(4)]
    for j in range(4):
        ang = S[0]
        P.V(lambda e, j=j, ang=ang: e.tensor_scalar(out=ang[:], in0=tau[:], scalar1=th[:, j:j + 1], scalar2=None, op0=ALU.mult), reads=["tau", "sm_th"], writes=[SK[0]])
        sin_table(sinT[j][:], ang[:], [128, TT], 0.0, S[1][:], tmpi[:], [SK[0]], [("sinT", j)], tkey=SK[1])
        sin_table(cosT[j][:], ang[:], [128, TT], math.pi / 2, S[1][:], tmpi[:], [SK[0]], [("cosT", j)], tkey=SK[1])
    init = kb.sb("s5init", [128, 8])
    P.V(lambda e: e.memset(init[:], 0.0), writes=["s5init"])
    s5in = [kb.sb("s5in%d" % i, [128, TT], BF16) for i in range(2)]
    tmpc = kb.sb("s5tmpc", [128, 2])
    for t in (range(NT_L) if "s" in parts else ()):
        a = t % 2
        tsl = slice(t * TT, (t + 1) * TT)
        u_t = s5in[a]
        P.dma("sync", lambda e, u_t=u_t, tsl=tsl: e.dma_start(out=u_t[:], in_=s5u_d[:, tsl]), writes=[("s5in", a)], key=("s5in", a))
        for j in range(4):
            jsl = slice(j * 128, (j + 1) * 128)
            b_re, b_im = banks[(2 * j) % 4], banks[(2 * j + 1) % 4]
            kre, kim = BK[(2 * j) % 4], BK[(2 * j + 1) % 4]
            P.T(lambda e, jsl=jsl, b_re=b_re, u_t=u_t: e.matmul(b_re[:], lhsT=BT[0][:, jsl], rhs=u_t[:], start=True, stop=True), reads=[("BT", 0), ("s5in", a)], writes=[kre])
            P.T(lambda e, jsl=jsl, b_im=b_im, u_t=u_t: e.matmul(b_im[:], lhsT=BT[1][:, jsl], rhs=u_t[:], start=True, stop=True), reads=[("BT", 1), ("s5in", a)], writes=[kim])
            w1, w2, wnr, wni, wr, wi = S[2], S[3], S[4], S[5], S[6], S[7]
            cT, sT = cosT[j], sinT[j]
            rd = [("cosT", j), ("sinT", j)]
            P.V(lambda e, cT=cT, b_re=b_re: e.tensor_tensor(out=w1[:], in0=b_re[:], in1=cT[:], op=ALU.mult), reads=[kre] + rd, writes=[SK[2]])
            P.V(lambda e, sT=sT, b_im=b_im: e.tensor_tensor(out=w2[:], in0=b_im[:], in1=sT[:], op=ALU.mult), reads=[kim] + rd, writes=[SK[3]])
            P.V(lambda e: e.tensor_tensor(out=wnr[:], in0=w1[:], in1=w2[:], op=ALU.add), reads=[SK[2], SK[3]], writes=[SK[4]])
            P.V(lambda e, cT=cT, b_im=b_im: e.tensor_tensor(out=w1[:], in0=b_im[:], in1=cT[:], op=ALU.mult), reads=[kim] + rd, writes=[SK[2]])
            P.V(lambda e, sT=sT, b_re=b_re: e.tensor_tensor(out=w2[:], in0=b_re[:], in1=sT[:], op=ALU.mult), reads=[kre] + rd, writes=[SK[3]])
            P.V(lambda e: e.tensor_tensor(out=wni[:], in0=w1[:], in1=w2[:], op=ALU.subtract), reads=[SK[2], SK[3]], writes=[SK[5]])
            P.V(lambda e, j=j: e.tensor_tensor_scan(out=wr[:], data0=mag[:, j:j + 1].to_broadcast([128, TT]), data1=wnr[:], initial=init[:, j:j + 1],
                                                    op0=ALU.mult, op1=ALU.add), reads=[SK[4], "sm_mag", "s5init"], writes=[SK[6]])
            P.V(lambda e, j=j: e.tensor_tensor_scan(out=wi[:], data0=mag[:, j:j + 1].to_broadcast([128, TT]), data1=wni[:], initial=init[:, 4 + j:5 + j],
                                                    op0=ALU.mult, op1=ALU.add), reads=[SK[5], "sm_mag", "s5init"], writes=[SK[7]])
            P.tiny_mode = True
            P.V(lambda e, j=j: e.tensor_tensor(out=tmpc[:, 0:1], in0=wr[:, TT - 1:TT], in1=c512[:, j:j + 1], op=ALU.mult), reads=[SK[6], "sm_c512"], writes=["tmpc"])
            P.V(lambda e, j=j: e.tensor_tensor(out=tmpc[:, 1:2], in0=wr[:, TT - 1:TT], in1=s512[:, j:j + 1], op=ALU.mult), reads=[SK[6], "sm_s512"], writes=["tmpc"])
            P.V(lambda e, j=j: e.scalar_tensor_tensor(out=init[:, j:j + 1], in0=wi[:, TT - 1:TT], scalar=ns512[:, j:j + 1], in1=tmpc[:, 0:1], op0=ALU.mult, op1=ALU.add),
                reads=[SK[7], "ns512", "tmpc"], writes=["s5init"])
            P.V(lambda e, j=j: e.scalar_tensor_tensor(out=init[:, 4 + j:5 + j], in0=wi[:, TT - 1:TT], scalar=c512[:, j:j + 1], in1=tmpc[:, 1:2], op0=ALU.mult, op1=ALU.add),
                reads=[SK[7], "sm_c512", "tmpc"], writes=["s5init"])
            P.tiny_mode = False
            xr, xi = SB[0 + 2 * (j % 2)], SB[1 + 2 * (j % 2)]
            kxr, kxi = SBK[0 + 2 * (j % 2)], SBK[1 + 2 * (j % 2)]
            P.V(lambda e, cT=cT: e.tensor_tensor(out=w1[:], in0=wr[:], in1=cT[:], op=ALU.mult), reads=[SK[6]] + rd, writes=[SK[2]])
            P.V(lambda e, sT=sT: e.tensor_tensor(out=w2[:], in0=wi[:], in1=sT[:], op=ALU.mult), reads=[SK[7]] + rd, writes=[SK[3]])
            P.V(lambda e, xr=xr: e.tensor_tensor(out=xr[:], in0=w1[:], in1=w2[:], op=ALU.subtract), reads=[SK[2], SK[3]], writes=[kxr])
            P.V(lambda e, sT=sT: e.tensor_tensor(out=w1[:], in0=wr[:], in1=sT[:], op=ALU.mult), reads=[SK[6]] + rd, writes=[SK[2]])
            P.V(lambda e, cT=cT: e.tensor_tensor(out=w2[:], in0=wi[:], in1=cT[:], op=ALU.mult), reads=[SK[7]] + rd, writes=[SK[3]])
            P.V(lambda e, xi=xi: e.tensor_tensor(out=xi[:], in0=w1[:], in1=w2[:], op=ALU.add), reads=[SK[2], SK[3]], writes=[kxi])
            P.T(lambda e, jsl=jsl, xr=xr, j=j: e.matmul(banks[4][:], lhsT=Cex[0][:, jsl], rhs=xr[:], start=(j == 0), stop=False), reads=[("Cex", 0), kxr], writes=[BK[4]])
            P.T(lambda e, jsl=jsl, xi=xi, j=j: e.matmul(banks[4][:], lhsT=Cex[1][:, jsl], rhs=xi[:], start=False, stop=(j == 3)), reads=[("Cex", 1), kxi], writes=[BK[4]])
        yv, y2, sgm = S[8], S[9], S[10]
        P.V(lambda e, u_t=u_t: e.scalar_tensor_tensor(out=yv[:], in0=u_t[:], scalar=s5d[:, 0:1], in1=banks[4][:], op0=ALU.mult, op1=ALU.add),
            reads=[("s5in", a), "s5d", BK[4]], writes=[SK[8]])
        P.V(lambda e: e.tensor_tensor(out=y2[:], in0=yv[:], in1=yv[:], op=ALU.mult), reads=[SK[8]], writes=[SK[9]])
        P.V(lambda e: e.tensor_scalar(out=y2[:], in0=y2[:], scalar1=0.044715, scalar2=1.0, op0=ALU.mult, op1=ALU.add), reads=[SK[9]], writes=[SK[9]])
        P.V(lambda e: e.tensor_tensor(out=y2[:], in0=y2[:], in1=yv[:], op=ALU.mult), reads=[SK[9], SK[8]], writes=[SK[9]])
        P.A(lambda e: e.activation(out=sgm[:], in_=y2[:], func=AF.Sigmoid, scale=2.0 * math.sqrt(2.0 / math.pi)), reads=[SK[9]], writes=[SK[10]])
        zo = SB[4 + a]
        P.V(lambda e, zo=zo: e.tensor_tensor(out=zo[:], in0=yv[:], in1=sgm[:], op=ALU.mult), reads=[SK[8], SK[10]], writes=[SBK[4 + a]])
        P.dma("sync", lambda e, zo=zo, tsl=tsl: e.dma_start(out=zb_o[:, tsl], in_=zo[:]), reads=[SBK[4 + a]], key=("yo", a))

    qr = kb.sb("qr", [128, L], BF16); kr = kb.sb("kr", [128, L], BF16); vv = kb.sb("vv", [128, L], BF16)
    Oacc = kb.sb("Oacc", [128, L]); Dacc = kb.sb("Dacc", [128, L])
    posi = kb.sb("posi", [32, TT], I32)
    vtok = [kb.sb("vtok%d" % i, [128, 4 * 128], BF16) for i in range(2)]
    for i in range(2):
        P.V(lambda e, i=i: e.memset(vtok[i][:], 0.0), writes=[("vtok", i)])
    SCL = float(128 ** -0.5)
    for gi, dil in (enumerate((1, 4, 16)) if "a" in parts else ()):
        for t in range(NT_L):
            tsl = slice(t * TT, (t + 1) * TT)
            for i, dst in enumerate((qr, kr, vv)):
                P.dma("sync", lambda e, i=i, dst=dst, tsl=tsl, gi=gi: e.dma_start(out=dst[:, tsl], in_=att_d[3 * gi + i, :, tsl]),
                      writes=[("att", i, t)], key=("attld", i, t))
            P.dma("sync", lambda e, tsl=tsl: e.dma_start(out=posi[:], in_=pos_d[:, tsl].partition_broadcast(32)), writes=["posi"], key="posi")
            ang, tmpf, sinS, cosS, r1, r2 = S[0], S[1], S[2], S[3], S[4], S[5]
            P.V(lambda e: e.tensor_copy(out=ang[0:32, :], in_=posi[:]), reads=["posi"], writes=[SK[0]])
            P.V(lambda e: e.tensor_scalar(out=ang[0:32, :], in0=ang[0:32, :], scalar1=ropec[:, 0:1], scalar2=None, op0=ALU.mult), reads=[SK[0], "ropec"], writes=[SK[0]])
            sin_table(sinS[0:32, :], ang[0:32, :], [32, TT], 0.0, tmpf[0:32, :], tmpi[0:32, :], [SK[0]], [SK[2]], tkey=SK[1])
            P.V(lambda e: e.tensor_scalar(out=sinS[0:32, :], in0=sinS[0:32, :], scalar1=ropec[:, 1:2], scalar2=None, op0=ALU.mult), reads=[SK[2], "ropec"], writes=[SK[2]])
            sin_table(cosS[0:32, :], ang[0:32, :], [32, TT], math.pi / 2, tmpf[0:32, :], tmpi[0:32, :], [SK[0]], [SK[3]], tkey=SK[1])
            for i, dst in enumerate((qr, kr)):
                bsw = banks[i]
                P.T(lambda e, dst=dst, tsl=tsl, bsw=bsw: e.matmul(bsw[0:32, :], lhsT=psw[:], rhs=dst[0:32, tsl], start=True, stop=True),
                    reads=["psw", ("att", i, t)], writes=[BK[i]])
                P.V(lambda e, bsw=bsw: e.tensor_tensor(out=r1[0:32, :], in0=bsw[0:32, :], in1=sinS[0:32, :], op=ALU.mult), reads=[BK[i], SK[2]], writes=[SK[4]])
                P.V(lambda e, dst=dst, tsl=tsl: e.tensor_tensor(out=r2[0:32, :], in0=dst[0:32, tsl], in1=cosS[0:32, :], op=ALU.mult), reads=[("att", i, t), SK[3]], writes=[SK[5]])
                P.V(lambda e, dst=dst, tsl=tsl: e.tensor_tensor(out=dst[0:32, tsl], in0=r1[0:32, :], in1=r2[0:32, :], op=ALU.add), reads=[SK[4], SK[5]], writes=[("att", i, t)])
        nper = L // dil // 128
        allkeys = [("att", i, t) for i in range(3) for t in range(NT_L)]
        for r in range(dil):
            for qd in range(nper // 4):
                n0 = qd * 4
                def toks(n):
                    st = r + dil * 128 * n
                    return slice(st, st + dil * 127 + 1, dil)
                vt_ps = banks[2][:].bitcast(BF16)
                vcur = vtok[(r * (nper // 4) + qd) % 2]
                vprev = vtok[(r * (nper // 4) + qd + 1) % 2]
                kvc, kvp = ("vtok", (r * (nper // 4) + qd) % 2), ("vtok", (r * (nper // 4) + qd + 1) % 2)
                for k_ in range(4):
                    P.T(lambda e, k_=k_, tk=toks(n0 + k_): e.transpose(out=vt_ps[:, k_ * 128:(k_ + 1) * 128], in_=vv[:, tk], identity=ident[:]),
                        reads=allkeys[2 * NT_L:] + ["ident"], writes=[BK[2]])
                P.A(lambda e, vcur=vcur: e.activation(out=vcur[:], in_=vt_ps[:, 0:512], func=AF.Copy), reads=[BK[2]], writes=[kvc])
                pm = [SB[0], SB[1]]
                for pr in range(2):
                    sb_ = banks[pr]
                    for k2 in range(2):
                        n = n0 + pr * 2 + k2
                        P.T(lambda e, tk=toks(n), k2=k2, sb_=sb_: e.matmul(sb_[:, k2 * 256:k2 * 256 + 128], lhsT=kr[:, tk], rhs=qr[:, tk], start=True, stop=True),
                            reads=allkeys[:2 * NT_L], writes=[BK[pr]])
                        np_ = n - 1 if n > 0 else n
                        P.T(lambda e, tk=toks(n), tkp=toks(np_), k2=k2, sb_=sb_: e.matmul(sb_[:, k2 * 256 + 128:k2 * 256 + 256], lhsT=kr[:, tkp], rhs=qr[:, tk], start=True, stop=True),
                            reads=allkeys[:2 * NT_L], writes=[BK[pr]])
                    ex = S[6 + pr]
                    P.A(lambda e, ex=ex, sb_=sb_: e.activation(out=ex[:], in_=sb_[:], func=AF.Exp, scale=SCL), reads=[BK[pr]], writes=[SK[6 + pr]])
                    mk = pmask[:, 0:TT] if (n0 == 0 and pr == 0) else pmask[:, TT:2 * TT]
                    P.V(lambda e, ex=ex, mk=mk, pr=pr: e.tensor_tensor(out=pm[pr][:], in0=ex[:], in1=mk, op=ALU.mult), reads=[SK[6 + pr], "pmask"], writes=[SBK[pr]])
                for k_ in range(4):
                    pr, k2 = k_ // 2, k_ % 2
                    pc = pm[pr][:, k2 * 256:k2 * 256 + 128]
                    pp = pm[pr][:, k2 * 256 + 128:k2 * 256 + 256]
                    vc = vcur[:, k_ * 128:(k_ + 1) * 128]
                    vp = vcur[:, (k_ - 1) * 128:k_ * 128] if k_ > 0 else vprev[:, 3 * 128:4 * 128]
                    osl = slice(k_ * 128, (k_ + 1) * 128)
                    P.T(lambda e, vc=vc, pc=pc, osl=osl: e.matmul(banks[3][:, osl], lhsT=vc, rhs=pc, start=True, stop=False), reads=[kvc, SBK[pr]], writes=[BK[3]])
                    P.T(lambda e, vp=vp, pp=pp, osl=osl: e.matmul(banks[3][:, osl], lhsT=vp, rhs=pp, start=False, stop=True), reads=[kvc, kvp, SBK[pr]], writes=[BK[3]])
                    P.T(lambda e, pc=pc, osl=osl: e.matmul(banks[4][:, osl], lhsT=ones[:], rhs=pc, start=True, stop=False), reads=["ones", SBK[pr]], writes=[BK[4]])
                    P.T(lambda e, pp=pp, osl=osl: e.matmul(banks[4][:, osl], lhsT=ones[:], rhs=pp, start=False, stop=True), reads=["ones", SBK[pr]], writes=[BK[4]])
                st = r + dil * 128 * n0
                dsl = slice(st, st + dil * 511 + 1, dil)
                if gi == 0:
                    P.A(lambda e, dsl=dsl: e.activation(out=Oacc[:, dsl], in_=banks[3][:], func=AF.Copy), reads=[BK[3]], writes=["Oacc"])
                    P.V(lambda e, dsl=dsl: e.tensor_copy(out=Dacc[:, dsl], in_=banks[4][:]), reads=[BK[4]], writes=["Dacc"])
                else:
                    P.V(lambda e, dsl=dsl: e.tensor_tensor(out=Oacc[:, dsl], in0=banks[3][:], in1=Oacc[:, dsl], op=ALU.add), reads=[BK[3], "Oacc"], writes=["Oacc"])
                    P.V(lambda e, dsl=dsl: e.tensor_tensor(out=Dacc[:, dsl], in0=banks[4][:], in1=Dacc[:, dsl], op=ALU.add), reads=[BK[4], "Dacc"], writes=["Dacc"])
    for t in (range(NT_L) if "a" in parts else ()):
        a = t % 2
        tsl = slice(t * TT, (t + 1) * TT)
        rc = S[8 + a]
        P.V(lambda e, rc=rc, tsl=tsl: e.reciprocal(out=rc[:], in_=Dacc[:, tsl]), reads=["Dacc"], writes=[SK[8 + a]])
        yo = SB[4 + a]
        P.V(lambda e, rc=rc, tsl=tsl, yo=yo: e.tensor_tensor(out=yo[:], in0=Oacc[:, tsl], in1=rc[:], op=ALU.mult), reads=["Oacc", SK[8 + a]], writes=[SBK[4 + a]])
        P.dma("sync", lambda e, yo=yo, tsl=tsl: e.dma_start(out=yd_o[:, tsl], in_=yo[:]), reads=[SBK[4 + a]], key=("yo", a))
    return kb.finish()


CONV_W = 31
HALO = CONV_W - 1


def build_C(final):
    kb = KB()
    P = kb.P
    G = 1024
    x1_d = kb.din("x1T", [D, TOK])
    ya_d = kb.din("yaT", [512, TOK], BF16)
    zb_d = kb.din("zbT", [512, TOK], BF16)
    yd_d = kb.din("ydT", [512, TOK], BF16)
    cv_d = kb.din("convT", [1024, TOK + HALO], BF16)
    gn_d = kb.din("gains3", [128, 24])
    cvp_d = kb.din("convp", [128, 4 * 34])
    wgt_d = kb.din("w_gates", [D, 4096])
    wbr_d = kb.din("w_branch", [4, 512, D])
    wo_d = kb.din("w_out", [D, D])
    wglu_d = kb.din("w_glu", [512, 1024])
    wg_d = kb.din("w_gate", [D, DFF])
    wu_d = kb.din("w_up", [D, DFF])
    wd_d = kb.din("w_down", [DFF, D])
    out_o = kb.dout("outT", [D, TOK])
    tp = TokenPhase(kb, G)
    NT = tp.NT
    gn = kb.sb("gn", [128, 24]); cvp = kb.sb("cvp", [128, 4 * 34])
    P.dma("sync", lambda e: e.dma_start(out=gn[:], in_=gn_d), writes=["gains"], key="gains")
    P.dma("sync", lambda e: e.dma_start(out=cvp[:], in_=cvp_d), writes=["cvp"], key="cvp")
    cvv = cvp[:].rearrange("p (c k) -> p c k", k=34)
    yb3 = kb.sb("yb3", [128, 4 * G], BF16)
    zbt = kb.sb("zbt", [128, 4 * G], BF16)

    def ybr(k, c4, tt):
        if k < 3:
            j = 8 + 4 * k + c4
            return tp.hd(j, tt), ("hid", j, tt)
        return yb3[:, c4 * G + tt * TT:c4 * G + (tt + 1) * TT], ("yb3", c4, tt)

    def zbr(c4, tt):
        return zbt[:, c4 * G + tt * TT:c4 * G + (tt + 1) * TT], ("zbt", c4, tt)

    ca = kb.sb("ca", [128, TT + HALO], BF16); cb = kb.sb("cb", [128, TT + HALO], BF16)
    zc = kb.sb("zc", [128, TT + HALO]); sgb = kb.sb("sgb", [128, TT + HALO])
    vch = [kb.sb("vch%d" % c, [128, TT]) for c in range(4)]
    vb = kb.sb("vb", [128, 4 * TT], BF16); vsq = kb.sb("vsq", [128, 4 * TT], BF16)
    mean = kb.sb("mean", [128, TT]); var = kb.sb("var", [128, TT]); tmpc = kb.sb("tmpcv", [128, TT])
    macc = [kb.sb("macc%d" % i, [128, TT]) for i in range(4)]
    sgt = tp.sg
    x1_v = x1_d.rearrange("(c p) t -> p c t", p=128)
    out_v = out_o.rearrange("(c p) t -> p c t", p=128)
    wgt_v = wgt_d.rearrange("(c p) n -> p c n", p=128)
    wo_v = wo_d.rearrange("(c p) n -> p c n", p=128)
    wglu_v = wglu_d.rearrange("(c p) n -> p c n", p=128)
    xkeys = [("x", c, tt) for c in range(8) for tt in range(NT)]
    for grp in range(TOK // G):
        t0 = grp * G
        xsb = tp.xT[:].rearrange("p (c t) -> p c t", t=G)
        P.dma("sync", lambda e, t0=t0: e.dma_start(out=xsb, in_=x1_v[:, :, t0:t0 + G]), writes=xkeys, key="xload")
        P.dma("sync", lambda e, t0=t0: e.dma_start(out=tp.hid[:, 8 * G:12 * G].rearrange("p (c t) -> p c t", t=G),
                                                 in_=ya_d.rearrange("(c p) t -> p c t", p=128)[:, :, t0:t0 + G]),
              writes=[("hid", j, tt) for j in range(8, 12) for tt in range(NT)], key="yaload")
        P.dma("sync", lambda e, t0=t0: e.dma_start(out=yb3[:].rearrange("p (c t) -> p c t", t=G),
                                                 in_=yd_d.rearrange("(c p) t -> p c t", p=128)[:, :, t0:t0 + G]),
              writes=[("yb3", c4, tt) for c4 in range(4) for tt in range(NT)], key="ydload")
        P.dma("sync", lambda e, t0=t0: e.dma_start(out=zbt[:].rearrange("p (c t) -> p c t", t=G),
                                                 in_=zb_d.rearrange("(c p) t -> p c t", p=128)[:, :, t0:t0 + G]),
              writes=[("zbt", c4, tt) for c4 in range(4) for tt in range(NT)], key="zbload")
        for tt in range(NT):
            tp.rmsnorm(gn[:, 0:8], tt)
        for cbk in range(2):
            wa, wak = tp.wload(wglu_v[:, :, cbk * 256:(cbk + 1) * 256], 4, 256)
            wgl, wgk = tp.wload(wglu_v[:, :, 512 + cbk * 256:512 + (cbk + 1) * 256], 4, 256)
            for sub in range(2):
                c = cbk * 2 + sub
                for tt in range(NT):
                    zr_ = [zbr(k4, tt) for k4 in range(4)]
                    ba, bak = tp.bank("gate")
                    kb.mm_group(ba[:], bak, [(wa[:, k4, sub * 128:(sub + 1) * 128], zr_[k4][0]) for k4 in range(4)], reads=[wak] + [z[1] for z in zr_])
                    bg, bgk = tp.bank("up")
                    kb.mm_group(bg[:], bgk, [(wgl[:, k4, sub * 128:(sub + 1) * 128], zr_[k4][0]) for k4 in range(4)], reads=[wgk] + [z[1] for z in zr_])
                    si = kb.rr("sg", 2)
                    P.A(lambda e, si=si, bg=bg: e.activation(out=sgt[si][:], in_=bg[:], func=AF.Sigmoid), reads=[bgk], writes=[("sg", si)])
                    yo, yok = ybr(1, c, tt)
                    P.V(lambda e, si=si, ba=ba, yo=yo: e.tensor_tensor(out=yo, in0=ba[:], in1=sgt[si][:], op=ALU.mult),
                        reads=[bak, ("sg", si)], writes=[yok])
        for tt in range(NT):
            tb = t0 + tt * TT
            for c in range(4):
                P.dma("sync", lambda e, c=c, tb=tb: e.dma_start(out=ca[:], in_=cv_d[c * 128:(c + 1) * 128, tb:tb + TT + HALO]), writes=["ca"], key="ca")
                P.dma("sync", lambda e, c=c, tb=tb: e.dma_start(out=cb[:], in_=cv_d[512 + c * 128:512 + (c + 1) * 128, tb:tb + TT + HALO]), writes=["cb"], key="cb")
                P.A(lambda e: e.activation(out=sgb[:], in_=cb[:], func=AF.Sigmoid), reads=["cb"], writes=["sgb"])
                P.V(lambda e: e.tensor_tensor(out=zc[:], in0=ca[:], in1=sgb[:], op=ALU.mult), reads=["ca", "sgb"], writes=["zc"])
                v = vch[c]
                P.V(lambda e, v=v, c=c: e.tensor_scalar(out=v[:], in0=zc[:, 0:TT], scalar1=cvv[:, c, 0:1], scalar2=None, op0=ALU.mult),
                    reads=["zc", "cvp"], writes=[("vch", c)])
                for j in range(1, CONV_W):
                    P.V(lambda e, v=v, c=c, j=j: e.scalar_tensor_tensor(out=v[:], in0=zc[:, j:j + TT], scalar=cvv[:, c, j:j + 1], in1=v[:], op0=ALU.mult, op1=ALU.add),
                        reads=["zc", "cvp", ("vch", c)], writes=[("vch", c)])
                P.V(lambda e, v=v, c=c: e.tensor_scalar(out=v[:], in0=v[:], scalar1=cvv[:, c, 31:32], scalar2=None, op0=ALU.add),
                    reads=[("vch", c), "cvp"], writes=[("vch", c)])
                P.A(lambda e, v=v, c=c: e.activation(out=vb[:, c * TT:(c + 1) * TT], in_=v[:], func=AF.Copy), reads=[("vch", c)], writes=[("vb", c)])
                P.A(lambda e, v=v, c=c: e.activation(out=vsq[:, c * TT:(c + 1) * TT], in_=v[:], func=AF.Square), reads=[("vch", c)], writes=[("vsq", c)])
            b1, b1k = tp.bank("stat")
            kb.mm_group(b1[:], b1k, [(tp.ones[:], vb[:, c * TT:(c + 1) * TT]) for c in range(4)], reads=["ones"] + [("vb", c) for c in range(4)])
            P.V(lambda e, b1=b1: e.tensor_scalar(out=mean[:], in0=b1[:], scalar1=1.0 / 512, scalar2=None, op0=ALU.mult), reads=[b1k], writes=["mean"])
            b2, b2k = tp.bank("stat")
            kb.mm_group(b2[:], b2k, [(tp.ones[:], vsq[:, c * TT:(c + 1) * TT]) for c in range(4)], reads=["ones"] + [("vsq", c) for c in range(4)])
            P.V(lambda e: e.tensor_tensor(out=tmpc[:], in0=mean[:], in1=mean[:], op=ALU.mult), reads=["mean"], writes=["tmpcv"])
            P.V(lambda e, b2=b2: e.scalar_tensor_tensor(out=var[:], in0=b2[:], scalar=1.0 / 512, in1=tmpc[:], op0=ALU.mult, op1=ALU.subtract),
                reads=[b2k, "tmpcv"], writes=["var"])
            P.A(lambda e: e.activation(out=var[:], in_=var[:], func=AF.Sqrt, bias=EPS, scale=1.0), reads=["var"], writes=["var"])
            P.V(lambda e: e.reciprocal(out=var[:], in_=var[:]), reads=["var"], writes=["var"])
            for c in range(4):
                v = vch[c]
                P.V(lambda e, v=v: e.tensor_tensor(out=v[:], in0=v[:], in1=mean[:], op=ALU.subtract), reads=[("vch", c), "mean"], writes=[("vch", c)])
                P.V(lambda e, v=v: e.tensor_tensor(out=v[:], in0=v[:], in1=var[:], op=ALU.mult), reads=[("vch", c), "var"], writes=[("vch", c)])
                P.V(lambda e, v=v, c=c: e.tensor_scalar(out=v[:], in0=v[:], scalar1=cvv[:, c, 32:33], scalar2=cvv[:, c, 33:34], op0=ALU.mult, op1=ALU.add),
                    reads=[("vch", c), "cvp"], writes=[("vch", c)])
                yo, yok = ybr(2, c, tt)
                P.A(lambda e, v=v, yo=yo: e.activation(out=yo, in_=v[:], func=AF.Silu), reads=[("vch", c)], writes=[yok])
        for mblk in range(4):
            for k in range(4):
                gv, gk = tp.wload(wgt_v[:, :, k * 1024 + mblk * 256:k * 1024 + (mblk + 1) * 256], 8, 256)
                bv, bk_ = tp.wload(wbr_d[k].rearrange("(c p) n -> p c n", p=128)[:, :, mblk * 256:(mblk + 1) * 256], 4, 256)
                for sub in range(2):
                    m = mblk * 2 + sub
                    for tt in range(NT):
                        ai = sub * NT + tt
                        yr_ = [ybr(k, c4, tt) for c4 in range(4)]
                        bg, bgk = tp.bank("gate")
                        kb.mm_group(bg[:], bgk, [(gv[:, c, sub * 128:(sub + 1) * 128], tp.h(c, tt)) for c in range(8)], reads=[gk] + [("h", c, tt) for c in range(8)])
                        by, byk = tp.bank("up")
                        kb.mm_group(by[:], byk, [(bv[:, c4, sub * 128:(sub + 1) * 128], yr_[c4][0]) for c4 in range(4)], reads=[bk_] + [y[1] for y in yr_])
                        si = kb.rr("sg", 2)
                        P.A(lambda e, si=si, bg=bg: e.activation(out=sgt[si][:], in_=bg[:], func=AF.Sigmoid), reads=[bgk], writes=[("sg", si)])
                        if k == 0:
                            P.V(lambda e, si=si, by=by, ai=ai: e.tensor_tensor(out=macc[ai][:], in0=by[:], in1=sgt[si][:], op=ALU.mult),
                                reads=[byk, ("sg", si)], writes=[("macc", ai)])
                        else:
                            P.V(lambda e, si=si, by=by: e.tensor_tensor(out=sgt[si][:], in0=by[:], in1=sgt[si][:], op=ALU.mult),
                                reads=[byk, ("sg", si)], writes=[("sg", si)])
                            if k < 3:
                                P.V(lambda e, si=si, ai=ai: e.tensor_tensor(out=macc[ai][:], in0=macc[ai][:], in1=sgt[si][:], op=ALU.add),
                                    reads=[("macc", ai), ("sg", si)], writes=[("macc", ai)])
                            else:
                                P.V(lambda e, si=si, ai=ai, m=m, tt=tt: e.tensor_tensor(out=tp.hd(m, tt), in0=macc[ai][:], in1=sgt[si][:], op=ALU.add),
                                    reads=[("macc", ai), ("sg", si)], writes=[("hid", m, tt)])
        for mblk in range(4):
            ov, ok_ = tp.wload(wo_v[:, :, mblk * 256:(mblk + 1) * 256], 8, 256)
            for sub in range(2):
                m = mblk * 2 + sub
                for tt in range(NT):
                    ba, bak = tp.bank("acc")
                    kb.mm_group(ba[:], bak, [(ov[:, c, sub * 128:(sub + 1) * 128], tp.hd(c, tt)) for c in range(8)], reads=[ok_] + [("hid", c, tt) for c in range(8)])
                    P.V(lambda e, ba=ba, m=m, tt=tt: e.tensor_tensor(out=tp.x(m, tt), in0=ba[:], in1=tp.x(m, tt), op=ALU.add), reads=[bak, ("x", m, tt)], writes=[("x", m, tt)])
        for tt in range(NT):
            tp.rmsnorm(gn[:, 8:16], tt)
        tp.ffn(wg_d, wu_d, wd_d)
        if final:
            for tt in range(NT):
                tp.rmsnorm(gn[:, 16:24], tt, out_f32=True)
                for c in range(8):
                    si = kb.rr("ostage", 4)
                    P.V(lambda e, c=c, si=si, tt=tt: e.scalar_tensor_tensor(out=macc[si][:], in0=tp.x(c, tt), scalar=gn[:, 16 + c:17 + c], in1=tp.rs[:], op0=ALU.mult, op1=ALU.mult),
                        reads=[("x", c, tt), "rs", "gains"], writes=[("macc", si)])
                    P.dma("sync", lambda e, c=c, si=si, tt=tt, t0=t0: e.dma_start(out=out_o[c * 128:(c + 1) * 128, t0 + tt * TT:t0 + (tt + 1) * TT], in_=macc[si][:]),
                          reads=[("macc", si)], key=("ostage", si))
        else:
            P.dma("sync", lambda e, t0=t0: e.dma_start(out=out_v[:, :, t0:t0 + G], in_=xsb), reads=xkeys, key="xstore")
    return kb.finish()


def _run(nc, in_maps):
    res = run_bass_kernel_spmd(nc, in_maps, core_ids=list(range(NCORES)))
    return res.results


def _gain_tile(g):
    return np.ascontiguousarray(g.reshape(8, 128).T)


import ml_dtypes

NPBF = ml_dtypes.bfloat16
ROPE_THETA = 500000.0


def _consts_B():
    cmask = np.ones((128, TT), np.float32)
    cmask[:, ::HGC] = 0.0
    s_ = np.arange(HGC)[:, None]
    t_ = np.arange(HGC)[None, :]
    amask = np.tile((s_ <= t_).astype(np.float32), (1, TT // HGC))
    j = np.arange(128)[:, None]
    i = np.arange(128)[None, :]
    cur = (j <= i).astype(np.float32)
    prev = (j >= i).astype(np.float32)
    z = np.zeros_like(cur)
    pmask = np.stack([np.concatenate([cur, z, cur, prev], 1), np.concatenate([cur, prev, cur, prev], 1)]).astype(np.float32)
    ident = np.eye(128, dtype=np.float32).astype(NPBF)
    psw = np.zeros((32, 32), np.float32)
    psw[(np.arange(32) + 16) % 32, np.arange(32)] = 1.0
    invf = (np.float32(ROPE_THETA) ** (-(np.arange(16, dtype=np.float32) / np.float32(16)))).astype(np.float32)
    ropec = np.stack([np.concatenate([invf, invf]), np.concatenate([-np.ones(16), np.ones(16)])], 1).astype(np.float32)
    tau = np.tile(np.arange(TT, dtype=np.float32)[None, :], (128, 1))
    return {"cmask": cmask, "amask": amask, "pmask": pmask, "ident": ident, "psw": psw.astype(NPBF), "ropec": ropec, "tau": tau}


def _b_params(inp, l, hh):
    lg = inp["hg_lb_logits"]
    sl = slice(hh * 128, (hh + 1) * 128)
    hgp = np.stack([lg[0, sl], lg[1, sl], inp["hg_gnorm"][l, sl], np.full(128, 1.0 if l > 0 else 0.0, np.float32)], 1).astype(np.float32)
    s5p = np.zeros((128, 4, 67), np.float32)
    for j in range(4):
        for h in range(2):
            g = hh * 8 + 2 * j + h
            ps_ = slice(64 * h, 64 * h + 64)
            s5p[ps_, j, 0] = inp["s5_a_re"][l, g]
            s5p[ps_, j, 1] = inp["s5_a_im"][l, g]
            s5p[ps_, j, 2] = inp["s5_log_dt"][l, g]
            s5p[ps_, j, 3:19] = inp["s5_b_re"][l, g]
            s5p[ps_, j, 19:35] = inp["s5_b_im"][l, g]
            s5p[ps_, j, 35:51] = inp["s5_c_re"][l, g].T
            s5p[ps_, j, 51:67] = inp["s5_c_im"][l, g].T
    s5d = np.ascontiguousarray(inp["s5_d"][l, sl][:, None]).astype(np.float32)
    return {"hgp": hgp, "s5p": s5p.reshape(128, 4 * 67), "s5d": s5d}


def _b_acts(projT_b, hh):
    hg4 = np.stack([projT_b[k * 512 + hh * 128: k * 512 + hh * 128 + 128] for k in range(4)])
    s5u = np.ascontiguousarray(projT_b[2048 + hh * 128: 2048 + hh * 128 + 128])
    att9 = np.stack([projT_b[3584 + i * 1536 + (gi * 4 + hh) * 128: 3584 + i * 1536 + (gi * 4 + hh) * 128 + 128]
                     for gi in range(3) for i in range(3)])
    return {"hg4": np.ascontiguousarray(hg4), "s5u": s5u, "att9": np.ascontiguousarray(att9)}


def _c_params(inp, l):
    gains3 = np.concatenate([_gain_tile(inp["mix_norm"][l]), _gain_tile(inp["ffn2_norm"][l]), _gain_tile(inp["final_norm"])], 1).astype(np.float32)
    convp = np.zeros((128, 4, 34), np.float32)
    for c in range(4):
        sl = slice(c * 128, (c + 1) * 128)
        convp[:, c, 0:31] = inp["conv_w"][l][:, sl].T
        convp[:, c, 31] = inp["conv_b"][l, sl]
        convp[:, c, 32] = inp["conv_ln_g"][l, sl]
        convp[:, c, 33] = inp["conv_ln_b"][l, sl]
    return {"gains3": gains3, "convp": convp.reshape(128, 4 * 34),
            "w_gates": np.ascontiguousarray(inp["w_in"][l][:, NMIX:]), "w_branch": inp["w_branch"][l], "w_out": inp["w_out"][l],
            "w_glu": inp["s5_w_glu"][l], "w_gate": inp["ffn2_w_gate"][l], "w_up": inp["ffn2_w_up"][l], "w_down": inp["ffn2_w_down"][l]}


def _c_conv_halo(conv_b, q):
    t0 = q * TOK
    out = np.zeros((1024, TOK + HALO), conv_b.dtype)
    lo = max(t0 - HALO, 0)
    out[:, HALO - (t0 - lo):] = conv_b[:, lo:t0 + TOK]
    return out


def kernel(**inputs):
    inp = {k: np.asarray(v) for k, v in inputs.items()}
    x = inp["x"]
    progA = build_A()
    progB = {p: build_B((p,)) for p in "hsa"}
    progC = [build_C(False), build_C(True)]
    consts = _consts_B()
    pos = [np.ascontiguousarray(inp["positions"][b][None, :]).astype(np.int32) for b in range(B)]
    xT = [np.ascontiguousarray(x[c // 4, (c % 4) * TOK:(c % 4 + 1) * TOK, :].T) for c in range(NCORES)]
    for l in range(DEPTH):
        wa = {"g_ffn1": _gain_tile(inp["ffn1_norm"][l]), "g_mix": _gain_tile(inp["mix_norm"][l]),
              "w_gate": inp["ffn1_w_gate"][l], "w_up": inp["ffn1_w_up"][l], "w_down": inp["ffn1_w_down"][l],
              "w_in": np.ascontiguousarray(inp["w_in"][l][:, :NMIX])}
        resA = _run(progA, [dict(wa, xT=xT[c]) for c in range(NCORES)])
        x1T = [resA[c]["x1T"] for c in range(NCORES)]
        projT_b = [np.concatenate([resA[4 * b + q]["projT"] for q in range(4)], axis=1) for b in range(B)]
        del resA
        ycat = {}
        for part, big, outk in (("h", "hg4", "ya"), ("s", "s5u", "zb"), ("a", "att9", "yd")):
            in_maps = []
            for c in range(NCORES):
                b, hh = c // 4, c % 4
                m = dict(consts)
                m.update(_b_params(inp, l, hh))
                m[big] = _b_acts(projT_b[b], hh)[big]
                m["pos"] = pos[b]
                in_maps.append(m)
            resB = _run(progB[part], in_maps)
            ycat[outk] = [np.concatenate([resB[4 * b + hh][outk] for hh in range(4)], axis=0) for b in range(B)]
            del resB
        wc = _c_params(inp, l)
        in_maps = []
        for c in range(NCORES):
            b, q = c // 4, c % 4
            tsl = slice(q * TOK, (q + 1) * TOK)
            m = dict(wc)
            m["x1T"] = x1T[c]
            m["yaT"] = np.ascontiguousarray(ycat["ya"][b][:, tsl])
            m["zbT"] = np.ascontiguousarray(ycat["zb"][b][:, tsl])
            m["ydT"] = np.ascontiguousarray(ycat["yd"][b][:, tsl])
            m["convT"] = _c_conv_halo(projT_b[b][2560:3584], q)
            in_maps.append(m)
        resC = _run(progC[1 if l == DEPTH - 1 else 0], in_maps)
        xT = [resC[c]["outT"] for c in range(NCORES)]
        del resC
    out = np.empty((B, L, D), np.float32)
    for c in range(NCORES):
        out[c // 4, (c % 4) * TOK:(c % 4 + 1) * TOK, :] = xT[c].T
    return out
```

```python
import contextlib
import math

import numpy as np
import concourse.bass as bass
import concourse.mybir as mybir
from concourse.bass_utils import run_bass_kernel_spmd

F32 = mybir.dt.float32
BF16 = mybir.dt.bfloat16
I32 = mybir.dt.int32
AF = mybir.ActivationFunctionType
ALU = mybir.AluOpType

NCORES = 8
D = 1024
DFF = 2816
B = 2
L = 8192
DEPTH = 2
TOK = 2048
TT = 512
EPS = 1e-6
NMIX = 8192

ENGINES = ("sync", "scalar", "gpsimd", "vector", "tensor")


class _Op:
    __slots__ = ("eng", "fn", "deps", "is_dma", "sem_key", "ticket", "signals", "idx", "tiny")


class Prog:
    def __init__(self, nc):
        self.nc = nc
        self.ops = []
        self.last_writer = {}
        self.readers = {}
        self.tiny_mode = False

    def _add(self, eng, fn, reads, writes, is_dma=False, sem_key=None):
        op = _Op()
        op.eng, op.fn, op.is_dma, op.sem_key = eng, fn, is_dma, sem_key
        op.signals, op.ticket, op.idx = False, None, len(self.ops)
        op.tiny = self.tiny_mode and not is_dma and eng != "tensor"
        deps = set()
        for r in reads:
            w = self.last_writer.get(r)
            if w is not None:
                deps.add(w)
        for w_ in writes:
            w = self.last_writer.get(w_)
            if w is not None:
                deps.add(w)
            deps.update(self.readers.get(w_, ()))
        deps.discard(op.idx)
        op.deps = deps
        for r in reads:
            self.readers.setdefault(r, []).append(op.idx)
        for w_ in writes:
            self.last_writer[w_] = op.idx
            self.readers[w_] = []
        self.ops.append(op)
        return op

    def op(self, eng, fn, reads=(), writes=()):
        return self._add(eng, fn, tuple(reads), tuple(writes))

    def dma(self, eng, fn, reads=(), writes=(), key=None):
        return self._add(eng, fn, tuple(reads), tuple(writes), is_dma=True, sem_key=key)

    def V(self, fn, reads=(), writes=()):
        return self.op("vector", fn, reads, writes)

    def A(self, fn, reads=(), writes=()):
        return self.op("scalar", fn, reads, writes)

    def G(self, fn, reads=(), writes=()):
        return self.op("gpsimd", fn, reads, writes)

    def T(self, fn, reads=(), writes=()):
        return self.op("tensor", fn, reads, writes)

    def emit(self):
        nc, ops = self.nc, self.ops
        for op in ops:
            for d in op.deps:
                dop = ops[d]
                if dop.is_dma or dop.eng != op.eng or dop.tiny:
                    dop.signals = True
        for op in ops:
            if op.is_dma:
                op.signals = True
        eng_cnt = {e: 0 for e in ENGINES}
        key_cnt = {}
        for op in ops:
            if not op.signals:
                continue
            if op.is_dma:
                key_cnt[op.sem_key] = key_cnt.get(op.sem_key, 0) + 16
                op.ticket = key_cnt[op.sem_key]
            else:
                eng_cnt[op.eng] += 1
                op.ticket = eng_cnt[op.eng]
        with contextlib.ExitStack() as es:
            eng_sem = {e: es.enter_context(nc.semaphore("es_" + e)) for e in ENGINES}
            key_sem = {k: es.enter_context(nc.semaphore("ds_%d" % i)) for i, k in enumerate(key_cnt)}
            block = es.enter_context(nc.Block())

            def semval(dop):
                if dop.is_dma:
                    return key_sem[dop.sem_key], ("k", dop.sem_key), dop.ticket
                return eng_sem[dop.eng], ("e", dop.eng), dop.ticket

            def make_body(ename):
                def body(eng):
                    waited = {}
                    for op in ops:
                        if op.eng != ename:
                            continue
                        need = {}
                        for d in op.deps:
                            dop = ops[d]
                            if (not dop.is_dma) and dop.eng == ename and not dop.tiny:
                                continue
                            sem, skey, val = semval(dop)
                            if waited.get(skey, 0) >= val:
                                continue
                            if need.get(skey, (None, 0))[1] < val:
                                need[skey] = (sem, val)
                        for skey, (sem, val) in need.items():
                            eng.wait_ge(sem, val)
                            waited[skey] = val
                        inst = op.fn(eng)
                        if op.signals:
                            sem, skey, val = semval(op)
                            inst.then_inc(sem, 16 if op.is_dma else 1)
                    if ename == "sync":
                        for k, c in key_cnt.items():
                            if waited.get(("k", k), 0) < c:
                                eng.wait_ge(key_sem[k], c)
                        for e2, c in eng_cnt.items():
                            if c > 0 and e2 != "sync":
                                eng.wait_ge(eng_sem[e2], c)
                return body

            block.sync(make_body("sync"))
            block.scalar(make_body("scalar"))
            block.gpsimd(make_body("gpsimd"))
            block.vector(make_body("vector"))
            block.tensor(make_body("tensor"))


class KB:
    def __init__(self):
        self.nc = bass.Bass("TRN2", target_bir_lowering=False)
        self.es = contextlib.ExitStack()
        self.P = Prog(self.nc)
        self._rr = {}
        self._uid = 0

    def din(self, name, shape, dt=F32):
        return self.nc.dram_tensor(name, list(shape), dt, kind="ExternalInput").ap()

    def dout(self, name, shape, dt=F32):
        return self.nc.dram_tensor(name, list(shape), dt, kind="ExternalOutput").ap()

    def sb(self, name, shape, dt=F32):
        return self.es.enter_context(self.nc.sbuf_tensor("sb_" + name, list(shape), dt))

    def ps(self, name, shape, dt=F32):
        return self.es.enter_context(self.nc.psum_tensor("ps_" + name, list(shape), dt))

    def rr(self, name, n):
        i = self._rr.get(name, 0)
        self._rr[name] = i + 1
        return i % n

    def uid(self):
        self._uid += 1
        return self._uid

    def finish(self):
        self.P.emit()
        self.es.close()
        return self.nc

    def mm_group(self, out_ap, out_key, pairs, reads):
        n = len(pairs)
        for i, (l, r) in enumerate(pairs):
            self.P.T(lambda e, l=l, r=r, i=i: e.matmul(out_ap, lhsT=l, rhs=r, start=(i == 0), stop=(i == n - 1)),
                     reads=reads, writes=[out_key])


class TokenPhase:
    def __init__(self, kb, G):
        self.kb = kb
        self.G = G
        self.NT = G // TT
        kb_ = kb
        self.xT = kb_.sb("xT", [128, 8 * G], F32)
        self.hT = kb_.sb("hT", [128, 8 * G], BF16)
        self.hid = kb_.sb("hid", [128, 22 * G], BF16)
        self.sq = kb_.sb("sq", [128, 8 * TT], BF16)
        self.rs = kb_.sb("rs", [128, TT], F32)
        self.ones = kb_.sb("ones", [128, 128], BF16)
        self.sg = [kb_.sb("sg%d" % i, [128, TT], F32) for i in range(2)]
        self.wslot = [kb_.sb("wslot%d" % i, [128, 22 * 256], BF16) for i in range(3)]
        self.banks = [kb_.ps("bank%d" % i, [128, TT], F32) for i in range(8)]
        kb.P.V(lambda e: e.memset(self.ones[:], 1.0), writes=["ones"])

    def x(self, c, tt):
        return self.xT[:, c * self.G + tt * TT: c * self.G + (tt + 1) * TT]

    def h(self, c, tt):
        return self.hT[:, c * self.G + tt * TT: c * self.G + (tt + 1) * TT]

    def hd(self, j, tt):
        return self.hid[:, j * self.G + tt * TT: j * self.G + (tt + 1) * TT]

    def bank(self, purpose):
        groups = {"gate": (0, 1), "up": (2, 3), "acc": (4, 5, 6), "stat": (7,)}[purpose]
        i = groups[self.kb.rr("bank_" + purpose, len(groups))]
        return self.banks[i], ("bank", i)

    def wload(self, src_ap, nk, ncols):
        kb = self.kb
        si = kb.rr("wslot", 3)
        slot = self.wslot[si]
        view = slot[:, 0:nk * ncols].rearrange("p (c n) -> p c n", n=ncols)
        kb.P.dma("gpsimd", lambda e: e.dma_start(out=view, in_=src_ap), writes=[("wslot", si)], key=("wslot", si))
        return view, ("wslot", si)

    def rmsnorm(self, gain_ap, tt, out_f32=False):
        kb, P = self.kb, self.kb.P
        for c in range(8):
            P.A(lambda e, c=c: e.activation(out=self.sq[:, c * TT:(c + 1) * TT], in_=self.x(c, tt), func=AF.Square),
                reads=[("x", c, tt)], writes=[("sq", c)])
        bk, bkey = self.bank("stat")
        kb.mm_group(bk[:], bkey, [(self.ones[:], self.sq[:, c * TT:(c + 1) * TT]) for c in range(8)],
                    reads=["ones"] + [("sq", c) for c in range(8)])
        P.A(lambda e: e.activation(out=self.rs[:], in_=bk[:], func=AF.Sqrt, bias=EPS, scale=1.0 / D),
            reads=[bkey], writes=["rs"])
        P.V(lambda e: e.reciprocal(out=self.rs[:], in_=self.rs[:]), reads=["rs"], writes=["rs"])
        if out_f32:
            return
        for c in range(8):
            P.V(lambda e, c=c: e.scalar_tensor_tensor(out=self.h(c, tt), in0=self.x(c, tt), scalar=gain_ap[:, c:c + 1],
                                                      in1=self.rs[:], op0=ALU.mult, op1=ALU.mult),
                reads=[("x", c, tt), "rs", "gains"], writes=[("h", c, tt)])

    def ffn(self, wg, wu, wd):
        kb, P, NT = self.kb, self.kb.P, self.NT
        wg_v = wg.rearrange("(c p) n -> p c n", p=128)
        wu_v = wu.rearrange("(c p) n -> p c n", p=128)
        wd_v = wd.rearrange("(c p) n -> p c n", p=128)
        for blk in range(DFF // 256):
            gv, gk = self.wload(wg_v[:, :, blk * 256:(blk + 1) * 256], 8, 256)
            uv, uk = self.wload(wu_v[:, :, blk * 256:(blk + 1) * 256], 8, 256)
            for sub in range(2):
                j = blk * 2 + sub
                for tt in range(NT):
                    hreads = [("h", c, tt) for c in range(8)]
                    bg, bgk = self.bank("gate")
                    kb.mm_group(bg[:], bgk, [(gv[:, c, sub * 128:(sub + 1) * 128], self.h(c, tt)) for c in range(8)],
                                reads=[gk] + hreads)
                    bu, buk = self.bank("up")
                    kb.mm_group(bu[:], buk, [(uv[:, c, sub * 128:(sub + 1) * 128], self.h(c, tt)) for c in range(8)],
                                reads=[uk] + hreads)
                    si = kb.rr("sg", 2)
                    sg = self.sg[si]
                    P.A(lambda e, sg=sg, bg=bg: e.activation(out=sg[:], in_=bg[:], func=AF.Silu),
                        reads=[bgk], writes=[("sg", si)])
                    P.V(lambda e, sg=sg, bu=bu, j=j, tt=tt: e.tensor_tensor(out=self.hd(j, tt), in0=bu[:], in1=sg[:], op=ALU.mult),
                        reads=[buk, ("sg", si)], writes=[("hid", j, tt)])
        for mblk in range(D // 256):
            dv, dk = self.wload(wd_v[:, :, mblk * 256:(mblk + 1) * 256], 22, 256)
            for sub in range(2):
                m = mblk * 2 + sub
                for tt in range(NT):
                    ba, bak = self.bank("acc")
                    kb.mm_group(ba[:], bak, [(dv[:, j, sub * 128:(sub + 1) * 128], self.hd(j, tt)) for j in range(22)],
                                reads=[dk] + [("hid", j, tt) for j in range(22)])
                    P.V(lambda e, ba=ba, m=m, tt=tt: e.scalar_tensor_tensor(out=self.x(m, tt), in0=ba[:], scalar=0.5, in1=self.x(m, tt),
                                                                             op0=ALU.mult, op1=ALU.add),
                        reads=[bak, ("x", m, tt)], writes=[("x", m, tt)])


def build_A():
    kb = KB()
    P = kb.P
    xT_d = kb.din("xT", [D, TOK])
    g1_d = kb.din("g_ffn1", [128, 8])
    g2_d = kb.din("g_mix", [128, 8])
    wg_d = kb.din("w_gate", [D, DFF])
    wu_d = kb.din("w_up", [D, DFF])
    wd_d = kb.din("w_down", [DFF, D])
    win_d = kb.din("w_in", [D, NMIX])
    x1_o = kb.dout("x1T", [D, TOK])
    pj_o = kb.dout("projT", [NMIX, TOK], BF16)
    G = 1024
    tp = TokenPhase(kb, G)
    NT = tp.NT
    g1 = kb.sb("g1", [128, 8])
    g2 = kb.sb("g2", [128, 8])
    stage = [kb.sb("stage%d" % i, [128, TT], BF16) for i in range(4)]
    P.dma("sync", lambda e: e.dma_start(out=g1[:], in_=g1_d), writes=["gains"], key="gains")
    P.dma("sync", lambda e: e.dma_start(out=g2[:], in_=g2_d), writes=["gains"], key="gains")
    xT_v = xT_d.rearrange("(c p) t -> p c t", p=128)
    x1_v = x1_o.rearrange("(c p) t -> p c t", p=128)
    win_v = win_d.rearrange("(c p) n -> p c n", p=128)
    xkeys = [("x", c, tt) for c in range(8) for tt in range(NT)]
    for grp in range(TOK // G):
        t0 = grp * G
        xsb = tp.xT[:].rearrange("p (c t) -> p c t", t=G)
        P.dma("sync", lambda e, t0=t0: e.dma_start(out=xsb, in_=xT_v[:, :, t0:t0 + G]), writes=xkeys, key="xload")
        for tt in range(NT):
            tp.rmsnorm(g1, tt)
        tp.ffn(wg_d, wu_d, wd_d)
        P.dma("sync", lambda e, t0=t0: e.dma_start(out=x1_v[:, :, t0:t0 + G], in_=xsb), reads=xkeys, key="x1store")
        for tt in range(NT):
            tp.rmsnorm(g2, tt)
        for blk in range(NMIX // 256):
            wv, wk = tp.wload(win_v[:, :, blk * 256:(blk + 1) * 256], 8, 256)
            for sub in range(2):
                col = blk * 256 + sub * 128
                for tt in range(NT):
                    ba, bak = tp.bank("acc")
                    kb.mm_group(ba[:], bak, [(wv[:, c, sub * 128:(sub + 1) * 128], tp.h(c, tt)) for c in range(8)],
                                reads=[wk] + [("h", c, tt) for c in range(8)])
                    si = kb.rr("stage", 4)
                    st = stage[si]
                    if si % 2 == 0:
                        P.A(lambda e, st=st, ba=ba: e.activation(out=st[:], in_=ba[:], func=AF.Copy), reads=[bak], writes=[("stage", si)])
                    else:
                        P.V(lambda e, st=st, ba=ba: e.tensor_copy(out=st[:], in_=ba[:]), reads=[bak], writes=[("stage", si)])
                    P.dma("sync", lambda e, st=st, col=col, tt=tt, t0=t0: e.dma_start(
                        out=pj_o[col:col + 128, t0 + tt * TT:t0 + (tt + 1) * TT], in_=st[:]),
                        reads=[("stage", si)], key=("stage", si))
    return kb.finish()


_DBG = 0
NT_L = L // TT
HGC = 64
TWO_PI = 2.0 * math.pi


def build_B(parts=("h", "s", "a")):
    kb = KB()
    P = kb.P
    hg_d = kb.din("hg4", [4, 128, L], BF16) if "h" in parts else None
    s5u_d = kb.din("s5u", [128, L], BF16) if "s" in parts else None
    att_d = kb.din("att9", [9, 128, L], BF16) if "a" in parts else None
    pos_d = kb.din("pos", [1, L], I32)
    cmask_d = kb.din("cmask", [128, TT])
    amask_d = kb.din("amask", [64, TT])
    pmask_d = kb.din("pmask", [2, 128, TT])
    ident_d = kb.din("ident", [128, 128], BF16)
    psw_d = kb.din("psw", [32, 32], BF16)
    ropec_d = kb.din("ropec", [32, 2])
    tau_d = kb.din("tau", [128, TT])
    hgp_d = kb.din("hgp", [128, 4])
    s5p_d = kb.din("s5p", [128, 4 * 67])
    s5d_d = kb.din("s5d", [128, 1])
    ya_o = kb.dout("ya", [128, L], BF16) if "h" in parts else None
    zb_o = kb.dout("zb", [128, L], BF16) if "s" in parts else None
    yd_o = kb.dout("yd", [128, L], BF16) if "a" in parts else None
    tab_o = kb.dout("ropetab", [2, 32, L], F32) if "a" in parts else None

    NS = 11
    S = [kb.sb("scr%d" % i, [128, TT], F32) for i in range(NS)]
    SK = [("scr", i) for i in range(NS)]
    NSB = 6
    SB = [kb.sb("scb%d" % i, [128, TT], BF16) for i in range(NSB)]
    SBK = [("scb", i) for i in range(NSB)]
    banks = [kb.ps("bank%d" % i, [128, TT], F32) for i in range(8)]
    BK = [("bank", i) for i in range(8)]
    cmask = kb.sb("cmask", [128, TT]); amask = kb.sb("amask", [64, TT])
    pmask = kb.sb("pmask", [128, 2 * TT], BF16)
    ident = kb.sb("ident", [128, 128], BF16); psw = kb.sb("psw", [32, 32], BF16)
    ropec = kb.sb("ropec", [32, 2]); tau = kb.sb("tau", [128, TT])
    hgp = kb.sb("hgp", [128, 4]); s5p = kb.sb("s5p", [128, 4 * 67]); s5d = kb.sb("s5d", [128, 1])
    ones = kb.sb("ones", [128, 128], BF16)
    P.V(lambda e: e.memset(ones[:], 1.0), writes=["ones"])
    for nm, dst, src in (("cmask", cmask, cmask_d), ("amask", amask, amask_d), ("ident", ident, ident_d), ("psw", psw, psw_d),
                         ("ropec", ropec, ropec_d), ("tau", tau, tau_d), ("hgp", hgp, hgp_d), ("s5p", s5p, s5p_d), ("s5d", s5d, s5d_d)):
        P.dma("sync", lambda e, dst=dst, src=src: e.dma_start(out=dst[:], in_=src), writes=[nm], key="c_" + nm)
    P.dma("gpsimd", lambda e: e.dma_start(out=pmask[:].rearrange("p (a t) -> p a t", a=2), in_=pmask_d.rearrange("a p t -> p a t")),
          writes=["pmask"], key="c_pmask")

    def sin_table(out_ap, ang_ap, shape, shift, tmpf, tmpi, reads, writes, scale_ap=None, tkey="sin_tmp2"):
        P.V(lambda e: e.tensor_scalar(out=tmpf, in0=ang_ap, scalar1=1.0 / TWO_PI, scalar2=shift / TWO_PI, op0=ALU.mult, op1=ALU.add),
            reads=reads, writes=["sin_tmp", tkey])
        P.V(lambda e: e.tensor_copy(out=tmpi, in_=tmpf), reads=["sin_tmp"], writes=["sin_tmpi"])
        P.V(lambda e: e.tensor_copy(out=tmpf, in_=tmpi), reads=["sin_tmpi"], writes=["sin_tmp", tkey])
        P.V(lambda e: e.scalar_tensor_tensor(out=tmpf, in0=tmpf, scalar=-TWO_PI, in1=ang_ap, op0=ALU.mult, op1=ALU.add),
            reads=["sin_tmp"] + list(reads), writes=["sin_tmp", tkey])
        P.V(lambda e: e.tensor_scalar(out=tmpf, in0=tmpf, scalar1=-math.pi - shift, scalar2=math.pi - shift, op0=ALU.max, op1=ALU.min),
            reads=["sin_tmp"], writes=["sin_tmp", tkey])
        if scale_ap is None:
            P.A(lambda e: e.activation(out=out_ap, in_=tmpf, func=AF.Sin, bias=shiftc[shift][:shape[0], 0:1]), reads=["sin_tmp", "shiftc", tkey], writes=writes)
        else:
            assert shift == 0.0
            P.A(lambda e: e.activation(out=out_ap, in_=tmpf, func=AF.Sin, scale=scale_ap), reads=["sin_tmp", tkey], writes=writes)

    shiftc = {0.0: kb.sb("shift0", [128, 1]), math.pi / 2: kb.sb("shift1", [128, 1])}
    P.V(lambda e: e.memset(shiftc[0.0][:], 0.0), writes=["shiftc"])
    P.V(lambda e: e.memset(shiftc[math.pi / 2][:], math.pi / 2), writes=["shiftc"])
    tmpi = kb.sb("tmpi", [128, TT], I32)

    lb = kb.sb("lb", [128, 1]); oml = kb.sb("oml", [128, 1])
    P.tiny_mode = True
    P.V(lambda e: e.tensor_tensor(out=lb[:], in0=hgp[:, 1:2], in1=hgp[:, 0:1], op=ALU.subtract), reads=["hgp"], writes=["lb"])
    P.A(lambda e: e.activation(out=lb[:], in_=lb[:], func=AF.Sigmoid), reads=["lb"], writes=["lb"])
    P.V(lambda e: e.tensor_tensor(out=lb[:], in0=lb[:], in1=hgp[:, 3:4], op=ALU.mult), reads=["lb", "hgp"], writes=["lb"])
    P.V(lambda e: e.tensor_scalar(out=oml[:], in0=lb[:], scalar1=-1.0, scalar2=1.0, op0=ALU.mult, op1=ALU.add), reads=["lb"], writes=["oml"])
    hc1 = kb.sb("hc1", [128, 1]); hc2 = kb.sb("hc2", [128, 1])
    P.V(lambda e: e.tensor_scalar(out=hc1[:], in0=oml[:], scalar1=0.5, scalar2=None, op0=ALU.mult), reads=["oml"], writes=["hc"])
    P.V(lambda e: e.tensor_tensor(out=hc2[:], in0=hc1[:], in1=lb[:], op=ALU.add), reads=["hc", "lb"], writes=["hc"])
    P.tiny_mode = False
    Sst = kb.sb("Sst", [128, 128]); Sb = kb.sb("Sb", [128, 128], BF16)
    P.V(lambda e: e.memset(Sst[:], 0.0), writes=["Sst"])
    esc = kb.sb("esc", [128, 32])
    hin = [[kb.sb("hin%d_%d" % (a, i), [128, TT], BF16) for i in range(4)] for a in range(2)]
    KT = kb.sb("KTtok", [64, 8 * 128], BF16); VT = kb.sb("VTtok", [64, 8 * 128], BF16)
    pT = [kb.ps("pT%d" % i, [128, 1024], BF16) for i in range(0)]
    QSC = float(128 ** -0.5)
    for t in (range(NT_L) if "h" in parts else ()):
        a = t % 2
        tsl = slice(t * TT, (t + 1) * TT)
        for i in range(4):
            P.dma("sync", lambda e, i=i, a=a, tsl=tsl: e.dma_start(out=hin[a][i][:], in_=hg_d[i, :, tsl]),
                  writes=[("hin", a, i)], key=("hin", a, i))
        q_t, f_t, i_t, g_t = hin[a]
        sg, t1, lf, bb, bq, eq, ek, qs, osb, rst, sgl = (S[k] for k in range(11))
        Qt, Kt, attm, osq = SB[0], SB[1], SB[2], SB[3]
        P.A(lambda e, f_t=f_t: e.activation(out=sg[:], in_=f_t[:], func=AF.Tanh, scale=0.5), reads=[("hin", a, 1)], writes=[SK[0]])
        P.A(lambda e, q_t=q_t: e.activation(out=qs[:], in_=q_t[:], func=AF.Silu), reads=[("hin", a, 0)], writes=[SK[7]])
        P.A(lambda e, g_t=g_t: e.activation(out=sgl[:], in_=g_t[:], func=AF.Silu), reads=[("hin", a, 3)], writes=[SK[10]])
        P.V(lambda e: e.tensor_scalar(out=t1[:], in0=sg[:], scalar1=hc1[:, 0:1], scalar2=hc2[:, 0:1], op0=ALU.mult, op1=ALU.add),
            reads=[SK[0], "hc"], writes=[SK[1]])
        P.A(lambda e: e.activation(out=lf[:], in_=t1[:], func=AF.Ln), reads=[SK[1]], writes=[SK[2]])
        P.V(lambda e: e.tensor_tensor_scan(out=bb[:], data0=cmask[:], data1=lf[:], initial=0.0, op0=ALU.mult, op1=ALU.add),
            reads=[SK[2], "cmask"], writes=[SK[3]])
        b3 = bb[:].rearrange("p (c t) -> p c t", t=HGC)
        P.V(lambda e, b3=b3: e.tensor_tensor(out=bq[:].rearrange("p (c t) -> p c t", t=HGC), in0=b3,
                                             in1=b3[:, :, 32:33].to_broadcast([128, 8, HGC]), op=ALU.subtract),
            reads=[SK[3]], writes=[SK[4]])
        P.A(lambda e: e.activation(out=eq[:], in_=bq[:], func=AF.Exp), reads=[SK[4]], writes=[SK[5]])
        P.A(lambda e: e.activation(out=ek[:], in_=bq[:], func=AF.Exp, scale=-1.0), reads=[SK[4]], writes=[SK[6]])
        P.V(lambda e: e.scalar_tensor_tensor(out=Qt[:], in0=qs[:], scalar=QSC, in1=eq[:], op0=ALU.mult, op1=ALU.mult),
            reads=[SK[7], SK[5]], writes=[SBK[0]])
        P.V(lambda e: e.tensor_scalar(out=t1[:], in0=t1[:], scalar1=-1.0, scalar2=1.0, op0=ALU.mult, op1=ALU.add), reads=[SK[1]], writes=[SK[1]])
        P.V(lambda e: e.tensor_tensor(out=Kt[:], in0=t1[:], in1=ek[:], op=ALU.mult), reads=[SK[1], SK[6]], writes=[SBK[1]])
        if _DBG == 1:
            continue
        P.tiny_mode = True
        P.V(lambda e, b3=b3: e.tensor_copy(out=esc[:, 0:8], in_=b3[:, :, 32]), reads=[SK[3]], writes=["esc"])
        P.A(lambda e: e.activation(out=esc[:, 8:16], in_=esc[:, 0:8], func=AF.Exp), reads=["esc"], writes=["esc"])
        P.A(lambda e, b3=b3: e.activation(out=esc[:, 16:24], in_=b3[:, :, 63], func=AF.Exp), reads=[SK[3], "esc"], writes=["esc"])
        P.V(lambda e, b3=b3: e.tensor_tensor(out=esc[:, 24:32], in0=b3[:, :, 63], in1=esc[:, 0:8], op=ALU.subtract), reads=[SK[3], "esc"], writes=["esc"])
        P.A(lambda e: e.activation(out=esc[:, 24:32], in_=esc[:, 24:32], func=AF.Exp), reads=["esc"], writes=["esc"])
        if _DBG == 2:
            continue
        P.tiny_mode = False
        kt_ps = banks[0][:].bitcast(BF16)
        vt_ps = banks[1][:].bitcast(BF16)
        for n in range(8):
            P.T(lambda e, n=n: e.transpose(out=kt_ps[0:64, n * 128:(n + 1) * 128], in_=Kt[:, n * 64:(n + 1) * 64], identity=ident[:]),
                reads=[SBK[1], "ident"], writes=[BK[0]])
        for n in range(8):
            P.T(lambda e, n=n, i_t=i_t: e.transpose(out=vt_ps[0:64, n * 128:(n + 1) * 128], in_=i_t[:, n * 64:(n + 1) * 64], identity=ident[:]),
                reads=[("hin", a, 2), "ident"], writes=[BK[1]])
        P.A(lambda e: e.activation(out=KT[:], in_=kt_ps[0:64, :], func=AF.Copy), reads=[BK[0]], writes=["KT"])
        P.V(lambda e: e.tensor_copy(out=VT[:], in_=vt_ps[0:64, :]), reads=[BK[1]], writes=["VT"])
        if _DBG == 3:
            continue
        for n in range(8):
            P.T(lambda e, n=n: e.matmul(banks[2][0:64, n * 64:(n + 1) * 64], lhsT=Kt[:, n * 64:(n + 1) * 64], rhs=Qt[:, n * 64:(n + 1) * 64],
                                        start=True, stop=True), reads=[SBK[0], SBK[1]], writes=[BK[2]])
        P.V(lambda e: e.tensor_tensor(out=attm[0:64, :], in0=banks[2][0:64, :], in1=amask[:], op=ALU.mult), reads=[BK[2], "amask"], writes=[SBK[2]])
        if _DBG == 4:
            continue
        for n in range(8):
            bi = 3 + n // 4
            P.T(lambda e, n=n, bi=bi: e.matmul(banks[bi][:, (n % 4) * 128:(n % 4 + 1) * 128], lhsT=KT[:, n * 128:(n + 1) * 128],
                                               rhs=VT[:, n * 128:(n + 1) * 128], start=True, stop=True), reads=["KT", "VT"], writes=[BK[bi]])
        P.tiny_mode = True
        for n in range(8):
            bi = 3 + n // 4
            P.V(lambda e, n=n: e.tensor_scalar(out=Sb[:], in0=Sst[:], scalar1=esc[:, 8 + n:9 + n], scalar2=None, op0=ALU.mult),
                reads=["Sst", "esc"], writes=["Sb"])
            P.T(lambda e, n=n: e.matmul(banks[5][:, n * 64:(n + 1) * 64], lhsT=Sb[:], rhs=Qt[:, n * 64:(n + 1) * 64], start=True, stop=False),
                reads=["Sb", SBK[0]], writes=[BK[5]])
            P.T(lambda e, n=n: e.matmul(banks[5][:, n * 64:(n + 1) * 64], lhsT=VT[:, n * 128:(n + 1) * 128], rhs=attm[0:64, n * 64:(n + 1) * 64],
                                        start=False, stop=True), reads=["VT", SBK[2]], writes=[BK[5]])
            P.V(lambda e, n=n: e.tensor_scalar(out=Sst[:], in0=Sst[:], scalar1=esc[:, 16 + n:17 + n], scalar2=None, op0=ALU.mult),
                reads=["Sst", "esc"], writes=["Sst"])
            P.V(lambda e, n=n, bi=bi: e.scalar_tensor_tensor(out=Sst[:], in0=banks[bi][:, (n % 4) * 128:(n % 4 + 1) * 128], scalar=esc[:, 24 + n:25 + n],
                                                             in1=Sst[:], op0=ALU.mult, op1=ALU.add), reads=[BK[bi], "Sst", "esc"], writes=["Sst"])
        P.tiny_mode = False
        if _DBG == 5:
            continue
        if _DBG != 11:
            pass
        if _DBG != 10:
            P.V(lambda e: e.tensor_copy(out=osb[:], in_=banks[5][:]), reads=[BK[5]], writes=[SK[8]])
        if _DBG == 20 and t == 0:
            P.dma("sync", lambda e: e.dma_start(out=zb_o[:, 0:512], in_=Qt[:]), reads=[SBK[0]], key="dbg0")
            P.dma("sync", lambda e: e.dma_start(out=zb_o[:, 512:1024], in_=Kt[:]), reads=[SBK[1]], key="dbg1")
            P.dma("sync", lambda e: e.dma_start(out=zb_o[0:64, 1024:1536], in_=attm[0:64, :]), reads=[SBK[2]], key="dbg2")
            P.dma("gpsimd", lambda e: e.dma_start(out=zb_o[:, 1536:2048], in_=bb[:]), reads=[SK[3]], key="dbg3")
            P.dma("sync", lambda e: e.dma_start(out=yd_o[0:64, 0:1024], in_=KT[:]), reads=["KT"], key="dbg4")
            P.dma("sync", lambda e: e.dma_start(out=yd_o[0:64, 1024:2048], in_=VT[:]), reads=["VT"], key="dbg5")
            P.dma("gpsimd", lambda e: e.dma_start(out=yd_o[:, 2048:2560], in_=osb[:]), reads=[SK[8]], key="dbg6")
            P.dma("gpsimd", lambda e: e.dma_start(out=yd_o[:, 2560:2688], in_=Sst[:]), reads=["Sst"], key="dbg7")
            P.dma("gpsimd", lambda e: e.dma_start(out=yd_o[:, 2688:2720], in_=esc[:]), reads=["esc"], key="dbg8")
        P.A(lambda e: e.activation(out=osq[:], in_=osb[:], func=AF.Square), reads=[SK[8]], writes=[SBK[3]])
        if _DBG in (10, 11):
            continue
        if _DBG == 6:
            continue
        P.T(lambda e: e.matmul(banks[6][:], lhsT=ones[:], rhs=osq[:], start=True, stop=True), reads=["ones", SBK[3]], writes=[BK[6]])
        P.A(lambda e: e.activation(out=rst[:], in_=banks[6][:], func=AF.Ln, bias=EPS, scale=1.0 / 128), reads=[BK[6]], writes=[SK[9]])
        P.A(lambda e: e.activation(out=rst[:], in_=rst[:], func=AF.Exp, scale=-0.5), reads=[SK[9]], writes=[SK[9]])
        P.V(lambda e: e.scalar_tensor_tensor(out=osb[:], in0=osb[:], scalar=hgp[:, 2:3], in1=rst[:], op0=ALU.mult, op1=ALU.mult),
            reads=[SK[8], SK[9], "hgp"], writes=[SK[8]])
        if _DBG == 8:
            continue
        yo = SB[4 + a]
        P.V(lambda e, yo=yo: e.tensor_tensor(out=yo[:], in0=osb[:], in1=sgl[:], op=ALU.mult), reads=[SK[8], SK[10]], writes=[SBK[4 + a]])
        if _DBG == 9:
            continue
        P.dma("sync", lambda e, yo=yo, tsl=tsl: e.dma_start(out=ya_o[:, tsl], in_=yo[:]), reads=[SBK[4 + a]], key=("yo", a))

    s5v = s5p[:].rearrange("p (j k) -> p j k", k=67)
    a_re, a_im, ldt = s5v[:, :, 0], s5v[:, :, 1], s5v[:, :, 2]
    sm = kb.sb("s5small", [128, 64])
    def col(i):
        return sm[:, 4 * i:4 * i + 4]
    dt_, adt, mag, th, cth, sth, abr, abi, den, m1, zr, zi, tA, tB, c512, s512 = (col(i) for i in range(16))
    smi = kb.sb("s5smalli", [128, 4], I32)
    P.tiny_mode = True
    P.A(lambda e: e.activation(out=dt_, in_=ldt, func=AF.Exp), reads=["s5p"], writes=["sm_dt"])
    P.V(lambda e: e.tensor_tensor(out=adt, in0=a_re, in1=dt_, op=ALU.mult), reads=["s5p", "sm_dt"], writes=["sm_adt"])
    P.A(lambda e: e.activation(out=mag, in_=adt, func=AF.Exp), reads=["sm_adt"], writes=["sm_mag"])
    P.V(lambda e: e.tensor_tensor(out=th, in0=a_im, in1=dt_, op=ALU.mult), reads=["s5p", "sm_dt"], writes=["sm_th"])
    sin_table(sth, th, [128, 4], 0.0, tA, smi[:], ["sm_th"], ["sm_sth"])
    sin_table(cth, th, [128, 4], math.pi / 2, tA, smi[:], ["sm_th"], ["sm_cth"])
    P.V(lambda e: e.tensor_scalar(out=tB, in0=th, scalar1=float(TT), scalar2=None, op0=ALU.mult), reads=["sm_th"], writes=["sm_tB"])
    sin_table(s512, tB, [128, 4], 0.0, tA, smi[:], ["sm_tB"], ["sm_s512"])
    sin_table(c512, tB, [128, 4], math.pi / 2, tA, smi[:], ["sm_tB"], ["sm_c512"])
    ns512 = kb.sb("ns512", [128, 4])
    P.V(lambda e: e.tensor_scalar(out=ns512[:], in0=s512, scalar1=-1.0, scalar2=None, op0=ALU.mult), reads=["sm_s512"], writes=["ns512"])
    P.V(lambda e: e.tensor_tensor(out=abr, in0=mag, in1=cth, op=ALU.mult), reads=["sm_mag", "sm_cth"], writes=["sm_abr"])
    P.V(lambda e: e.tensor_tensor(out=abi, in0=mag, in1=sth, op=ALU.mult), reads=["sm_mag", "sm_sth"], writes=["sm_abi"])
    P.V(lambda e: e.tensor_tensor(out=den, in0=a_re, in1=a_re, op=ALU.mult), reads=["s5p"], writes=["sm_den"])
    P.V(lambda e: e.tensor_tensor(out=tA, in0=a_im, in1=a_im, op=ALU.mult), reads=["s5p", "sm_c512"], writes=["sin_tmp"])
    P.V(lambda e: e.tensor_tensor(out=den, in0=den, in1=tA, op=ALU.add), reads=["sm_den", "sin_tmp"], writes=["sm_den"])
    P.V(lambda e: e.reciprocal(out=den, in_=den), reads=["sm_den"], writes=["sm_den"])
    P.V(lambda e: e.tensor_scalar(out=m1, in0=abr, scalar1=-1.0, scalar2=None, op0=ALU.add), reads=["sm_abr"], writes=["sm_m1"])
    P.V(lambda e: e.tensor_tensor(out=zr, in0=m1, in1=a_re, op=ALU.mult), reads=["sm_m1", "s5p"], writes=["sm_zr"])
    P.V(lambda e: e.tensor_tensor(out=tA, in0=abi, in1=a_im, op=ALU.mult), reads=["sm_abi", "s5p"], writes=["sin_tmp"])
    P.V(lambda e: e.tensor_tensor(out=zr, in0=zr, in1=tA, op=ALU.add), reads=["sm_zr", "sin_tmp"], writes=["sm_zr"])
    P.V(lambda e: e.tensor_tensor(out=zr, in0=zr, in1=den, op=ALU.mult), reads=["sm_zr", "sm_den"], writes=["sm_zr"])
    P.V(lambda e: e.tensor_tensor(out=zi, in0=abi, in1=a_re, op=ALU.mult), reads=["sm_abi", "s5p"], writes=["sm_zi"])
    P.V(lambda e: e.tensor_tensor(out=tA, in0=m1, in1=a_im, op=ALU.mult), reads=["sm_m1", "s5p"], writes=["sin_tmp"])
    P.V(lambda e: e.tensor_tensor(out=zi, in0=zi, in1=tA, op=ALU.subtract), reads=["sm_zi", "sin_tmp"], writes=["sm_zi"])
    P.V(lambda e: e.tensor_tensor(out=zi, in0=zi, in1=den, op=ALU.mult), reads=["sm_zi", "sm_den"], writes=["sm_zi"])
    Bex = [kb.sb("Bex%d" % i, [128, 4 * 128], BF16) for i in range(2)]
    Cex = [kb.sb("Cex%d" % i, [128, 4 * 128], BF16) for i in range(2)]
    BT = [kb.sb("BT%d" % i, [128, 4 * 128], BF16) for i in range(2)]
    bbt = kb.sb("bbt", [128, 64])
    for i in range(2):
        P.V(lambda e, i=i: e.memset(Bex[i][:], 0.0), writes=[("Bex", i)])
        P.V(lambda e, i=i: e.memset(Cex[i][:], 0.0), writes=[("Cex", i)])
    for j in range(4):
        bre, bim = s5v[:, j, 3:19], s5v[:, j, 19:35]
        cre, cim = s5v[:, j, 35:51], s5v[:, j, 51:67]
        zrj, zij = zr[:, j:j + 1], zi[:, j:j + 1]
        P.V(lambda e, bre=bre, zrj=zrj: e.tensor_scalar(out=bbt[:, 0:16], in0=bre, scalar1=zrj, scalar2=None, op0=ALU.mult), reads=["s5p", "sm_zr"], writes=["bbt"])
        P.V(lambda e, bim=bim, zij=zij: e.tensor_scalar(out=bbt[:, 16:32], in0=bim, scalar1=zij, scalar2=None, op0=ALU.mult), reads=["s5p", "sm_zi"], writes=["bbt"])
        P.V(lambda e, bim=bim, zrj=zrj: e.tensor_scalar(out=bbt[:, 32:48], in0=bim, scalar1=zrj, scalar2=None, op0=ALU.mult), reads=["s5p", "sm_zr"], writes=["bbt"])
        P.V(lambda e, bre=bre, zij=zij: e.tensor_scalar(out=bbt[:, 48:64], in0=bre, scalar1=zij, scalar2=None, op0=ALU.mult), reads=["s5p", "sm_zi"], writes=["bbt"])
        for hh_ in range(2):
            ps_ = slice(64 * hh_, 64 * hh_ + 64)
            cs = slice(j * 128 + (2 * j + hh_) * 16, j * 128 + (2 * j + hh_) * 16 + 16)
            P.V(lambda e, ps_=ps_, cs=cs: e.tensor_tensor(out=Bex[0][ps_, cs], in0=bbt[ps_, 0:16], in1=bbt[ps_, 16:32], op=ALU.subtract), reads=["bbt"], writes=[("Bex", 0)])
            P.V(lambda e, ps_=ps_, cs=cs: e.tensor_tensor(out=Bex[1][ps_, cs], in0=bbt[ps_, 32:48], in1=bbt[ps_, 48:64], op=ALU.add), reads=["bbt"], writes=[("Bex", 1)])
            P.V(lambda e, ps_=ps_, cs=cs, cre=cre: e.tensor_copy(out=Cex[0][ps_, cs], in_=cre[ps_, :]), reads=["s5p"], writes=[("Cex", 0)])
            P.V(lambda e, ps_=ps_, cs=cs, cim=cim: e.tensor_scalar(out=Cex[1][ps_, cs], in0=cim[ps_, :], scalar1=-1.0, scalar2=None, op0=ALU.mult), reads=["s5p"], writes=[("Cex", 1)])
    P.tiny_mode = False
    for i in range(2):
        bt_ps = banks[i][:].bitcast(BF16)
        for j in range(4):
            P.T(lambda e, i=i, j=j, bt_ps=bt_ps: e.transpose(out=bt_ps[:, j * 128:(j + 1) * 128], in_=Bex[i][:, j * 128:(j + 1) * 128], identity=ident[:]),
                reads=[("Bex", i), "ident"], writes=[BK[i]])
        P.V(lambda e, i=i, bt_ps=bt_ps: e.tensor_copy(out=BT[i][:], in_=bt_ps[:, 0:512]), reads=[BK[i]], writes=[("BT", i)])
    cosT = [kb.sb("cosT%d" % j, [128, TT]) for j in range(4)]
    sinT = [kb.sb("sinT%d" % j, [128, TT]) for j in range(4)]
    for j in range(4):
        ang = S[0]
        P.V(lambda e, j=j, ang=ang: e.tensor_scalar(out=ang[:], in0=tau[:], scalar1=th[:, j:j + 1], scalar2=None, op0=ALU.mult), reads=["tau", "sm_th"], writes=[SK[0]])
        sin_table(sinT[j][:], ang[:], [128, TT], 0.0, S[1][:], tmpi[:], [SK[0]], [("sinT", j)], tkey=SK[1])
        sin_table(cosT[j][:], ang[:], [128, TT], math.pi / 2, S[1][:], tmpi[:], [SK[0]], [("cosT", j)], tkey=SK[1])
    init = kb.sb("s5init", [128, 8])
    P.V(lambda e: e.memset(init[:], 0.0), writes=["s5init"])
    s5in = [kb.sb("s5in%d" % i, [128, TT], BF16) for i in range(2)]
    tmpc = kb.sb("s5tmpc", [128, 2])
    for t in (range(NT_L) if "s" in parts else ()):
        a = t % 2
        tsl = slice(t * TT, (t + 1) * TT)
        u_t = s5in[a]
        P.dma("sync", lambda e, u_t=u_t, tsl=tsl: e.dma_start(out=u_t[:], in_=s5u_d[:, tsl]), writes=[("s5in", a)], key=("s5in", a))
        for j in range(4):
            jsl = slice(j * 128, (j + 1) * 128)
            b_re, b_im = banks[(2 * j) % 4], banks[(2 * j + 1) % 4]
            kre, kim = BK[(2 * j) % 4], BK[(2 * j + 1) % 4]
            P.T(lambda e, jsl=jsl, b_re=b_re, u_t=u_t: e.matmul(b_re[:], lhsT=BT[0][:, jsl], rhs=u_t[:], start=True, stop=True), reads=[("BT", 0), ("s5in", a)], writes=[kre])
            P.T(lambda e, jsl=jsl, b_im=b_im, u_t=u_t: e.matmul(b_im[:], lhsT=BT[1][:, jsl], rhs=u_t[:], start=True, stop=True), reads=[("BT", 1), ("s5in", a)], writes=[kim])
            w1, w2, wnr, wni, wr, wi = S[2], S[3], S[4], S[5], S[6], S[7]
            cT, sT = cosT[j], sinT[j]
            rd = [("cosT", j), ("sinT", j)]
            P.V(lambda e, cT=cT, b_re=b_re: e.tensor_tensor(out=w1[:], in0=b_re[:], in1=cT[:], op=ALU.mult), reads=[kre] + rd, writes=[SK[2]])
            P.V(lambda e, sT=sT, b_im=b_im: e.tensor_tensor(out=w2[:], in0=b_im[:], in1=sT[:], op=ALU.mult), reads=[kim] + rd, writes=[SK[3]])
            P.V(lambda e: e.tensor_tensor(out=wnr[:], in0=w1[:], in1=w2[:], op=ALU.add), reads=[SK[2], SK[3]], writes=[SK[4]])
            P.V(lambda e, cT=cT, b_im=b_im: e.tensor_tensor(out=w1[:], in0=b_im[:], in1=cT[:], op=ALU.mult), reads=[kim] + rd, writes=[SK[2]])
            P.V(lambda e, sT=sT, b_re=b_re: e.tensor_tensor(out=w2[:], in0=b_re[:], in1=sT[:], op=ALU.mult), reads=[kre] + rd, writes=[SK[3]])
            P.V(lambda e: e.tensor_tensor(out=wni[:], in0=w1[:], in1=w2[:], op=ALU.subtract), reads=[SK[2], SK[3]], writes=[SK[5]])
            P.V(lambda e, j=j: e.tensor_tensor_scan(out=wr[:], data0=mag[:, j:j + 1].to_broadcast([128, TT]), data1=wnr[:], initial=init[:, j:j + 1],
                                                    op0=ALU.mult, op1=ALU.add), reads=[SK[4], "sm_mag", "s5init"], writes=[SK[6]])
            P.V(lambda e, j=j: e.tensor_tensor_scan(out=wi[:], data0=mag[:, j:j + 1].to_broadcast([128, TT]), data1=wni[:], initial=init[:, 4 + j:5 + j],
                                                    op0=ALU.mult, op1=ALU.add), reads=[SK[5], "sm_mag", "s5init"], writes=[SK[7]])
            P.tiny_mode = True
            P.V(lambda e, j=j: e.tensor_tensor(out=tmpc[:, 0:1], in0=wr[:, TT - 1:TT], in1=c512[:, j:j + 1], op=ALU.mult), reads=[SK[6], "sm_c512"], writes=["tmpc"])
            P.V(lambda e, j=j: e.tensor_tensor(out=tmpc[:, 1:2], in0=wr[:, TT - 1:TT], in1=s512[:, j:j + 1], op=ALU.mult), reads=[SK[6], "sm_s512"], writes=["tmpc"])
            P.V(lambda e, j=j: e.scalar_tensor_tensor(out=init[:, j:j + 1], in0=wi[:, TT - 1:TT], scalar=ns512[:, j:j + 1], in1=tmpc[:, 0:1], op0=ALU.mult, op1=ALU.add),
                reads=[SK[7], "ns512", "tmpc"], writes=["s5init"])
            P.V(lambda e, j=j: e.scalar_tensor_tensor(out=init[:, 4 + j:5 + j], in0=wi[:, TT - 1:TT], scalar=c512[:, j:j + 1], in1=tmpc[:, 1:2], op0=ALU.mult, op1=ALU.add),
                reads=[SK[7], "sm_c512", "tmpc"], writes=["s5init"])
            P.tiny_mode = False
            xr, xi = SB[0 + 2 * (j % 2)], SB[1 + 2 * (j % 2)]
            kxr, kxi = SBK[0 + 2 * (j % 2)], SBK[1 + 2 * (j % 2)]
            P.V(lambda e, cT=cT: e.tensor_tensor(out=w1[:], in0=wr[:], in1=cT[:], op=ALU.mult), reads=[SK[6]] + rd, writes=[SK[2]])
            P.V(lambda e, sT=sT: e.tensor_tensor(out=w2[:], in0=wi[:], in1=sT[:], op=ALU.mult), reads=[SK[7]] + rd, writes=[SK[3]])
            P.V(lambda e, xr=xr: e.tensor_tensor(out=xr[:], in0=w1[:], in1=w2[:], op=ALU.subtract), reads=[SK[2], SK[3]], writes=[kxr])
            P.V(lambda e, sT=sT: e.tensor_tensor(out=w1[:], in0=wr[:], in1=sT[:], op=ALU.mult), reads=[SK[6]] + rd, writes=[SK[2]])
            P.V(lambda e, cT=cT: e.tensor_tensor(out=w2[:], in0=wi[:], in1=cT[:], op=ALU.mult), reads=[SK[7]] + rd, writes=[SK[3]])
            P.V(lambda e, xi=xi: e.tensor_tensor(out=xi[:], in0=w1[:], in1=w2[:], op=ALU.add), reads=[SK[2], SK[3]], writes=[kxi])
            P.T(lambda e, jsl=jsl, xr=xr, j=j: e.matmul(banks[4][:], lhsT=Cex[0][:, jsl], rhs=xr[:], start=(j == 0), stop=False), reads=[("Cex", 0), kxr], writes=[BK[4]])
            P.T(lambda e, jsl=jsl, xi=xi, j=j: e.matmul(banks[4][:], lhsT=Cex[1][:, jsl], rhs=xi[:], start=False, stop=(j == 3)), reads=[("Cex", 1), kxi], writes=[BK[4]])
        yv, y2, sgm = S[8], S[9], S[10]
        P.V(lambda e, u_t=u_t: e.scalar_tensor_tensor(out=yv[:], in0=u_t[:], scalar=s5d[:, 0:1], in1=banks[4][:], op0=ALU.mult, op1=ALU.add),
            reads=[("s5in", a), "s5d", BK[4]], writes=[SK[8]])
        P.V(lambda e: e.tensor_tensor(out=y2[:], in0=yv[:], in1=yv[:], op=ALU.mult), reads=[SK[8]], writes=[SK[9]])
        P.V(lambda e: e.tensor_scalar(out=y2[:], in0=y2[:], scalar1=0.044715, scalar2=1.0, op0=ALU.mult, op1=ALU.add), reads=[SK[9]], writes=[SK[9]])
        P.V(lambda e: e.tensor_tensor(out=y2[:], in0=y2[:], in1=yv[:], op=ALU.mult), reads=[SK[9], SK[8]], writes=[SK[9]])
        P.A(lambda e: e.activation(out=sgm[:], in_=y2[:], func=AF.Sigmoid, scale=2.0 * math.sqrt(2.0 / math.pi)), reads=[SK[9]], writes=[SK[10]])
        zo = SB[4 + a]
        P.V(lambda e, zo=zo: e.tensor_tensor(out=zo[:], in0=yv[:], in1=sgm[:], op=ALU.mult), reads=[SK[8], SK[10]], writes=[SBK[4 + a]])
        P.dma("sync", lambda e, zo=zo, tsl=tsl: e.dma_start(out=zb_o[:, tsl], in_=zo[:]), reads=[SBK[4 + a]], key=("yo", a))

    qr = kb.sb("qr", [128, L], BF16); kr = kb.sb("kr", [128, L], BF16); vv = kb.sb("vv", [128, L], BF16)
    Oacc = kb.sb("Oacc", [128, L]); Dacc = kb.sb("Dacc", [128, L])
    posi = kb.sb("posi", [32, TT], I32)
    vtok = [kb.sb("vtok%d" % i, [128, 4 * 128], BF16) for i in range(2)]
    for i in range(2):
        P.V(lambda e, i=i: e.memset(vtok[i][:], 0.0), writes=[("vtok", i)])
    SCL = float(128 ** -0.5)
    for gi, dil in (enumerate((1, 4, 16)) if "a" in parts else ()):
        for t in range(NT_L):
            tsl = slice(t * TT, (t + 1) * TT)
            for i, dst in enumerate((qr, kr, vv)):
                P.dma("sync", lambda e, i=i, dst=dst, tsl=tsl, gi=gi: e.dma_start(out=dst[:, tsl], in_=att_d[3 * gi + i, :, tsl]),
                      writes=[("att", i, t)], key=("attld", i, t))
            pa = t % 2
            ang, tmpf, r1, r2 = S[0], S[1], S[4], S[5]
            sinS, cosS = S[2 + 6 * pa], S[3 + 6 * pa]
            ksin, kcos = SK[2 + 6 * pa], SK[3 + 6 * pa]
            if gi == 0:
                P.dma("sync", lambda e, tsl=tsl: e.dma_start(out=posi[:], in_=pos_d[:, tsl].partition_broadcast(32)), writes=["posi"], key="posi")
                P.V(lambda e: e.tensor_copy(out=ang[0:32, :], in_=posi[:]), reads=["posi"], writes=[SK[0]])
                P.V(lambda e: e.tensor_scalar(out=ang[0:32, :], in0=ang[0:32, :], scalar1=ropec[:, 0:1], scalar2=None, op0=ALU.mult), reads=[SK[0], "ropec"], writes=[SK[0]])
                sin_table(sinS[0:32, :], ang[0:32, :], [32, TT], 0.0, tmpf[0:32, :], tmpi[0:32, :], [SK[0]], [ksin], tkey=SK[1])
                P.V(lambda e, sinS=sinS: e.tensor_scalar(out=sinS[0:32, :], in0=sinS[0:32, :], scalar1=ropec[:, 1:2], scalar2=None, op0=ALU.mult), reads=[ksin, "ropec"], writes=[ksin])
                sin_table(cosS[0:32, :], ang[0:32, :], [32, TT], math.pi / 2, tmpf[0:32, :], tmpi[0:32, :], [SK[0]], [kcos], tkey=SK[1])
                P.dma("sync", lambda e, sinS=sinS, tsl=tsl: e.dma_start(out=tab_o[0, :, tsl], in_=sinS[0:32, :]), reads=[ksin], writes=[("tab", 0, t)], key=("tabst", 0, pa))
                P.dma("sync", lambda e, cosS=cosS, tsl=tsl: e.dma_start(out=tab_o[1, :, tsl], in_=cosS[0:32, :]), reads=[kcos], writes=[("tab", 1, t)], key=("tabst", 1, pa))
            else:
                P.dma("sync", lambda e, sinS=sinS, tsl=tsl: e.dma_start(out=sinS[0:32, :], in_=tab_o[0, :, tsl]), reads=[("tab", 0, t)], writes=[ksin], key=("tabld", 0, pa))
                P.dma("sync", lambda e, cosS=cosS, tsl=tsl: e.dma_start(out=cosS[0:32, :], in_=tab_o[1, :, tsl]), reads=[("tab", 1, t)], writes=[kcos], key=("tabld", 1, pa))
            for i, dst in enumerate((qr, kr)):
                bsw = banks[i]
                P.T(lambda e, dst=dst, tsl=tsl, bsw=bsw: e.matmul(bsw[0:32, :], lhsT=psw[:], rhs=dst[0:32, tsl], start=True, stop=True),
                    reads=["psw", ("att", i, t)], writes=[BK[i]])
                P.V(lambda e, bsw=bsw, sinS=sinS: e.tensor_tensor(out=r1[0:32, :], in0=bsw[0:32, :], in1=sinS[0:32, :], op=ALU.mult), reads=[BK[i], ksin], writes=[SK[4]])
                P.V(lambda e, dst=dst, tsl=tsl, cosS=cosS: e.tensor_tensor(out=r2[0:32, :], in0=dst[0:32, tsl], in1=cosS[0:32, :], op=ALU.mult), reads=[("att", i, t), kcos], writes=[SK[5]])
                P.V(lambda e, dst=dst, tsl=tsl: e.tensor_tensor(out=dst[0:32, tsl], in0=r1[0:32, :], in1=r2[0:32, :], op=ALU.add), reads=[SK[4], SK[5]], writes=[("att", i, t)])
        nper = L // dil // 128
        allkeys = [("att", i, t) for i in range(3) for t in range(NT_L)]
        for r in range(dil):
            for qd in range(nper // 4):
                n0 = qd * 4
                def toks(n):
                    st = r + dil * 128 * n
                    return slice(st, st + dil * 127 + 1, dil)
                vt_ps = banks[2][:].bitcast(BF16)
                vcur = vtok[(r * (nper // 4) + qd) % 2]
                vprev = vtok[(r * (nper // 4) + qd + 1) % 2]
                kvc, kvp = ("vtok", (r * (nper // 4) + qd) % 2), ("vtok", (r * (nper // 4) + qd + 1) % 2)
                for k_ in range(4):
                    P.T(lambda e, k_=k_, tk=toks(n0 + k_): e.transpose(out=vt_ps[:, k_ * 128:(k_ + 1) * 128], in_=vv[:, tk], identity=ident[:]),
                        reads=allkeys[2 * NT_L:] + ["ident"], writes=[BK[2]])
                P.A(lambda e, vcur=vcur: e.activation(out=vcur[:], in_=vt_ps[:, 0:512], func=AF.Copy), reads=[BK[2]], writes=[kvc])
                pm = [SB[0], SB[1]]
                for pr in range(2):
                    sb_ = banks[pr]
                    for k2 in range(2):
                        n = n0 + pr * 2 + k2
                        P.T(lambda e, tk=toks(n), k2=k2, sb_=sb_: e.matmul(sb_[:, k2 * 256:k2 * 256 + 128], lhsT=kr[:, tk], rhs=qr[:, tk], start=True, stop=True),
                            reads=allkeys[:2 * NT_L], writes=[BK[pr]])
                        np_ = n - 1 if n > 0 else n
                        P.T(lambda e, tk=toks(n), tkp=toks(np_), k2=k2, sb_=sb_: e.matmul(sb_[:, k2 * 256 + 128:k2 * 256 + 256], lhsT=kr[:, tkp], rhs=qr[:, tk], start=True, stop=True),
                            reads=allkeys[:2 * NT_L], writes=[BK[pr]])
                    ex = S[6 + pr]
                    P.A(lambda e, ex=ex, sb_=sb_: e.activation(out=ex[:], in_=sb_[:], func=AF.Exp, scale=SCL), reads=[BK[pr]], writes=[SK[6 + pr]])
                    mk = pmask[:, 0:TT] if (n0 == 0 and pr == 0) else pmask[:, TT:2 * TT]
                    P.V(lambda e, ex=ex, mk=mk, pr=pr: e.tensor_tensor(out=pm[pr][:], in0=ex[:], in1=mk, op=ALU.mult), reads=[SK[6 + pr], "pmask"], writes=[SBK[pr]])
                for k_ in range(4):
                    pr, k2 = k_ // 2, k_ % 2
                    pc = pm[pr][:, k2 * 256:k2 * 256 + 128]
                    pp = pm[pr][:, k2 * 256 + 128:k2 * 256 + 256]
                    vc = vcur[:, k_ * 128:(k_ + 1) * 128]
                    vp = vcur[:, (k_ - 1) * 128:k_ * 128] if k_ > 0 else vprev[:, 3 * 128:4 * 128]
                    osl = slice(k_ * 128, (k_ + 1) * 128)
                    P.T(lambda e, vc=vc, pc=pc, osl=osl: e.matmul(banks[3][:, osl], lhsT=vc, rhs=pc, start=True, stop=False), reads=[kvc, SBK[pr]], writes=[BK[3]])
                    P.T(lambda e, vp=vp, pp=pp, osl=osl: e.matmul(banks[3][:, osl], lhsT=vp, rhs=pp, start=False, stop=True), reads=[kvc, kvp, SBK[pr]], writes=[BK[3]])
                    P.T(lambda e, pc=pc, osl=osl: e.matmul(banks[4][:, osl], lhsT=ones[:], rhs=pc, start=True, stop=False), reads=["ones", SBK[pr]], writes=[BK[4]])
                    P.T(lambda e, pp=pp, osl=osl: e.matmul(banks[4][:, osl], lhsT=ones[:], rhs=pp, start=False, stop=True), reads=["ones", SBK[pr]], writes=[BK[4]])
                st = r + dil * 128 * n0
                dsl = slice(st, st + dil * 511 + 1, dil)
                if gi == 0:
                    P.A(lambda e, dsl=dsl: e.activation(out=Oacc[:, dsl], in_=banks[3][:], func=AF.Copy), reads=[BK[3]], writes=["Oacc"])
                    P.V(lambda e, dsl=dsl: e.tensor_copy(out=Dacc[:, dsl], in_=banks[4][:]), reads=[BK[4]], writes=["Dacc"])
                else:
                    P.V(lambda e, dsl=dsl: e.tensor_tensor(out=Oacc[:, dsl], in0=banks[3][:], in1=Oacc[:, dsl], op=ALU.add), reads=[BK[3], "Oacc"], writes=["Oacc"])
                    P.V(lambda e, dsl=dsl: e.tensor_tensor(out=Dacc[:, dsl], in0=banks[4][:], in1=Dacc[:, dsl], op=ALU.add), reads=[BK[4], "Dacc"], writes=["Dacc"])
    for t in (range(NT_L) if "a" in parts else ()):
        a = t % 2
        tsl = slice(t * TT, (t + 1) * TT)
        rc = S[8 + a]
        P.V(lambda e, rc=rc, tsl=tsl: e.reciprocal(out=rc[:], in_=Dacc[:, tsl]), reads=["Dacc"], writes=[SK[8 + a]])
        yo = SB[4 + a]
        P.V(lambda e, rc=rc, tsl=tsl, yo=yo: e.tensor_tensor(out=yo[:], in0=Oacc[:, tsl], in1=rc[:], op=ALU.mult), reads=["Oacc", SK[8 + a]], writes=[SBK[4 + a]])
        P.dma("sync", lambda e, yo=yo, tsl=tsl: e.dma_start(out=yd_o[:, tsl], in_=yo[:]), reads=[SBK[4 + a]], key=("yo", a))
    return kb.finish()


CONV_W = 31
HALO = CONV_W - 1


def build_C(final):
    kb = KB()
    P = kb.P
    G = 1024
    x1_d = kb.din("x1T", [D, TOK])
    ya_d = kb.din("yaT", [512, TOK], BF16)
    zb_d = kb.din("zbT", [512, TOK], BF16)
    yd_d = kb.din("ydT", [512, TOK], BF16)
    cv_d = kb.din("convT", [1024, TOK + HALO], BF16)
    gn_d = kb.din("gains3", [128, 24])
    cvp_d = kb.din("convp", [128, 4 * 34])
    wgt_d = kb.din("w_gates", [D, 4096])
    wbr_d = kb.din("w_branch", [4, 512, D])
    wo_d = kb.din("w_out", [D, D])
    wglu_d = kb.din("w_glu", [512, 1024])
    wg_d = kb.din("w_gate", [D, DFF])
    wu_d = kb.din("w_up", [D, DFF])
    wd_d = kb.din("w_down", [DFF, D])
    out_o = kb.dout("outT", [D, TOK])
    tp = TokenPhase(kb, G)
    NT = tp.NT
    gn = kb.sb("gn", [128, 24]); cvp = kb.sb("cvp", [128, 4 * 34])
    P.dma("sync", lambda e: e.dma_start(out=gn[:], in_=gn_d), writes=["gains"], key="gains")
    P.dma("sync", lambda e: e.dma_start(out=cvp[:], in_=cvp_d), writes=["cvp"], key="cvp")
    cvv = cvp[:].rearrange("p (c k) -> p c k", k=34)
    yb3 = kb.sb("yb3", [128, 4 * G], BF16)
    zbt = kb.sb("zbt", [128, 4 * G], BF16)

    def ybr(k, c4, tt):
        if k < 3:
            j = 8 + 4 * k + c4
            return tp.hd(j, tt), ("hid", j, tt)
        return yb3[:, c4 * G + tt * TT:c4 * G + (tt + 1) * TT], ("yb3", c4, tt)

    def zbr(c4, tt):
        return zbt[:, c4 * G + tt * TT:c4 * G + (tt + 1) * TT], ("zbt", c4, tt)

    ca = kb.sb("ca", [128, TT + HALO], BF16); cb = kb.sb("cb", [128, TT + HALO], BF16)
    zc = kb.sb("zc", [128, TT + HALO]); sgb = kb.sb("sgb", [128, TT + HALO])
    vch = [kb.sb("vch%d" % c, [128, TT]) for c in range(4)]
    vb = kb.sb("vb", [128, 4 * TT], BF16); vsq = kb.sb("vsq", [128, 4 * TT], BF16)
    mean = kb.sb("mean", [128, TT]); var = kb.sb("var", [128, TT]); tmpc = kb.sb("tmpcv", [128, TT])
    macc = [kb.sb("macc%d" % i, [128, TT]) for i in range(4)]
    sgt = tp.sg
    x1_v = x1_d.rearrange("(c p) t -> p c t", p=128)
    out_v = out_o.rearrange("(c p) t -> p c t", p=128)
    wgt_v = wgt_d.rearrange("(c p) n -> p c n", p=128)
    wo_v = wo_d.rearrange("(c p) n -> p c n", p=128)
    wglu_v = wglu_d.rearrange("(c p) n -> p c n", p=128)
    xkeys = [("x", c, tt) for c in range(8) for tt in range(NT)]
    for grp in range(TOK // G):
        t0 = grp * G
        xsb = tp.xT[:].rearrange("p (c t) -> p c t", t=G)
        P.dma("sync", lambda e, t0=t0: e.dma_start(out=xsb, in_=x1_v[:, :, t0:t0 + G]), writes=xkeys, key="xload")
        P.dma("sync", lambda e, t0=t0: e.dma_start(out=tp.hid[:, 8 * G:12 * G].rearrange("p (c t) -> p c t", t=G),
                                                 in_=ya_d.rearrange("(c p) t -> p c t", p=128)[:, :, t0:t0 + G]),
              writes=[("hid", j, tt) for j in range(8, 12) for tt in range(NT)], key="yaload")
        P.dma("sync", lambda e, t0=t0: e.dma_start(out=yb3[:].rearrange("p (c t) -> p c t", t=G),
                                                 in_=yd_d.rearrange("(c p) t -> p c t", p=128)[:, :, t0:t0 + G]),
              writes=[("yb3", c4, tt) for c4 in range(4) for tt in range(NT)], key="ydload")
        P.dma("sync", lambda e, t0=t0: e.dma_start(out=zbt[:].rearrange("p (c t) -> p c t", t=G),
                                                 in_=zb_d.rearrange("(c p) t -> p c t", p=128)[:, :, t0:t0 + G]),
              writes=[("zbt", c4, tt) for c4 in range(4) for tt in range(NT)], key="zbload")
        for tt in range(NT):
            tp.rmsnorm(gn[:, 0:8], tt)
        for cbk in range(2):
            wa, wak = tp.wload(wglu_v[:, :, cbk * 256:(cbk + 1) * 256], 4, 256)
            wgl, wgk = tp.wload(wglu_v[:, :, 512 + cbk * 256:512 + (cbk + 1) * 256], 4, 256)
            for sub in range(2):
                c = cbk * 2 + sub
                for tt in range(NT):
                    zr_ = [zbr(k4, tt) for k4 in range(4)]
                    ba, bak = tp.bank("gate")
                    kb.mm_group(ba[:], bak, [(wa[:, k4, sub * 128:(sub + 1) * 128], zr_[k4][0]) for k4 in range(4)], reads=[wak] + [z[1] for z in zr_])
                    bg, bgk = tp.bank("up")
                    kb.mm_group(bg[:], bgk, [(wgl[:, k4, sub * 128:(sub + 1) * 128], zr_[k4][0]) for k4 in range(4)], reads=[wgk] + [z[1] for z in zr_])
                    si = kb.rr("sg", 2)
                    P.A(lambda e, si=si, bg=bg: e.activation(out=sgt[si][:], in_=bg[:], func=AF.Sigmoid), reads=[bgk], writes=[("sg", si)])
                    yo, yok = ybr(1, c, tt)
                    P.V(lambda e, si=si, ba=ba, yo=yo: e.tensor_tensor(out=yo, in0=ba[:], in1=sgt[si][:], op=ALU.mult),
                        reads=[bak, ("sg", si)], writes=[yok])
        for tt in range(NT):
            tb = t0 + tt * TT
            for c in range(4):
                P.dma("sync", lambda e, c=c, tb=tb: e.dma_start(out=ca[:], in_=cv_d[c * 128:(c + 1) * 128, tb:tb + TT + HALO]), writes=["ca"], key="ca")
                P.dma("sync", lambda e, c=c, tb=tb: e.dma_start(out=cb[:], in_=cv_d[512 + c * 128:512 + (c + 1) * 128, tb:tb + TT + HALO]), writes=["cb"], key="cb")
                P.A(lambda e: e.activation(out=sgb[:], in_=cb[:], func=AF.Sigmoid), reads=["cb"], writes=["sgb"])
                P.V(lambda e: e.tensor_tensor(out=zc[:], in0=ca[:], in1=sgb[:], op=ALU.mult), reads=["ca", "sgb"], writes=["zc"])
                v = vch[c]
                P.V(lambda e, v=v, c=c: e.tensor_scalar(out=v[:], in0=zc[:, 0:TT], scalar1=cvv[:, c, 0:1], scalar2=None, op0=ALU.mult),
                    reads=["zc", "cvp"], writes=[("vch", c)])
                for j in range(1, CONV_W):
                    P.V(lambda e, v=v, c=c, j=j: e.scalar_tensor_tensor(out=v[:], in0=zc[:, j:j + TT], scalar=cvv[:, c, j:j + 1], in1=v[:], op0=ALU.mult, op1=ALU.add),
                        reads=["zc", "cvp", ("vch", c)], writes=[("vch", c)])
                P.V(lambda e, v=v, c=c: e.tensor_scalar(out=v[:], in0=v[:], scalar1=cvv[:, c, 31:32], scalar2=None, op0=ALU.add),
                    reads=[("vch", c), "cvp"], writes=[("vch", c)])
                P.A(lambda e, v=v, c=c: e.activation(out=vb[:, c * TT:(c + 1) * TT], in_=v[:], func=AF.Copy), reads=[("vch", c)], writes=[("vb", c)])
                P.A(lambda e, v=v, c=c: e.activation(out=vsq[:, c * TT:(c + 1) * TT], in_=v[:], func=AF.Square), reads=[("vch", c)], writes=[("vsq", c)])
            b1, b1k = tp.bank("stat")
            kb.mm_group(b1[:], b1k, [(tp.ones[:], vb[:, c * TT:(c + 1) * TT]) for c in range(4)], reads=["ones"] + [("vb", c) for c in range(4)])
            P.V(lambda e, b1=b1: e.tensor_scalar(out=mean[:], in0=b1[:], scalar1=1.0 / 512, scalar2=None, op0=ALU.mult), reads=[b1k], writes=["mean"])
            b2, b2k = tp.bank("stat")
            kb.mm_group(b2[:], b2k, [(tp.ones[:], vsq[:, c * TT:(c + 1) * TT]) for c in range(4)], reads=["ones"] + [("vsq", c) for c in range(4)])
            P.V(lambda e: e.tensor_tensor(out=tmpc[:], in0=mean[:], in1=mean[:], op=ALU.mult), reads=["mean"], writes=["tmpcv"])
            P.V(lambda e, b2=b2: e.scalar_tensor_tensor(out=var[:], in0=b2[:], scalar=1.0 / 512, in1=tmpc[:], op0=ALU.mult, op1=ALU.subtract),
                reads=[b2k, "tmpcv"], writes=["var"])
            P.A(lambda e: e.activation(out=var[:], in_=var[:], func=AF.Sqrt, bias=EPS, scale=1.0), reads=["var"], writes=["var"])
            P.V(lambda e: e.reciprocal(out=var[:], in_=var[:]), reads=["var"], writes=["var"])
            for c in range(4):
                v = vch[c]
                P.V(lambda e, v=v: e.tensor_tensor(out=v[:], in0=v[:], in1=mean[:], op=ALU.subtract), reads=[("vch", c), "mean"], writes=[("vch", c)])
                P.V(lambda e, v=v: e.tensor_tensor(out=v[:], in0=v[:], in1=var[:], op=ALU.mult), reads=[("vch", c), "var"], writes=[("vch", c)])
                P.V(lambda e, v=v, c=c: e.tensor_scalar(out=v[:], in0=v[:], scalar1=cvv[:, c, 32:33], scalar2=cvv[:, c, 33:34], op0=ALU.mult, op1=ALU.add),
                    reads=[("vch", c), "cvp"], writes=[("vch", c)])
                yo, yok = ybr(2, c, tt)
                P.A(lambda e, v=v, yo=yo: e.activation(out=yo, in_=v[:], func=AF.Silu), reads=[("vch", c)], writes=[yok])
        for mblk in range(4):
            for k in range(4):
                gv, gk = tp.wload(wgt_v[:, :, k * 1024 + mblk * 256:k * 1024 + (mblk + 1) * 256], 8, 256)
                bv, bk_ = tp.wload(wbr_d[k].rearrange("(c p) n -> p c n", p=128)[:, :, mblk * 256:(mblk + 1) * 256], 4, 256)
                for sub in range(2):
                    m = mblk * 2 + sub
                    for tt in range(NT):
                        ai = sub * NT + tt
                        yr_ = [ybr(k, c4, tt) for c4 in range(4)]
                        bg, bgk = tp.bank("gate")
                        kb.mm_group(bg[:], bgk, [(gv[:, c, sub * 128:(sub + 1) * 128], tp.h(c, tt)) for c in range(8)], reads=[gk] + [("h", c, tt) for c in range(8)])
                        by, byk = tp.bank("up")
                        kb.mm_group(by[:], byk, [(bv[:, c4, sub * 128:(sub + 1) * 128], yr_[c4][0]) for c4 in range(4)], reads=[bk_] + [y[1] for y in yr_])
                        si = kb.rr("sg", 2)
                        P.A(lambda e, si=si, bg=bg: e.activation(out=sgt[si][:], in_=bg[:], func=AF.Sigmoid), reads=[bgk], writes=[("sg", si)])
                        if k == 0:
                            P.V(lambda e, si=si, by=by, ai=ai: e.tensor_tensor(out=macc[ai][:], in0=by[:], in1=sgt[si][:], op=ALU.mult),
                                reads=[byk, ("sg", si)], writes=[("macc", ai)])
                        else:
                            P.V(lambda e, si=si, by=by: e.tensor_tensor(out=sgt[si][:], in0=by[:], in1=sgt[si][:], op=ALU.mult),
                                reads=[byk, ("sg", si)], writes=[("sg", si)])
                            if k < 3:
                                P.V(lambda e, si=si, ai=ai: e.tensor_tensor(out=macc[ai][:], in0=macc[ai][:], in1=sgt[si][:], op=ALU.add),
                                    reads=[("macc", ai), ("sg", si)], writes=[("macc", ai)])
                            else:
                                P.V(lambda e, si=si, ai=ai, m=m, tt=tt: e.tensor_tensor(out=tp.hd(m, tt), in0=macc[ai][:], in1=sgt[si][:], op=ALU.add),
                                    reads=[("macc", ai), ("sg", si)], writes=[("hid", m, tt)])
        for mblk in range(4):
            ov, ok_ = tp.wload(wo_v[:, :, mblk * 256:(mblk + 1) * 256], 8, 256)
            for sub in range(2):
                m = mblk * 2 + sub
                for tt in range(NT):
                    ba, bak = tp.bank("acc")
                    kb.mm_group(ba[:], bak, [(ov[:, c, sub * 128:(sub + 1) * 128], tp.hd(c, tt)) for c in range(8)], reads=[ok_] + [("hid", c, tt) for c in range(8)])
                    P.V(lambda e, ba=ba, m=m, tt=tt: e.tensor_tensor(out=tp.x(m, tt), in0=ba[:], in1=tp.x(m, tt), op=ALU.add), reads=[bak, ("x", m, tt)], writes=[("x", m, tt)])
        for tt in range(NT):
            tp.rmsnorm(gn[:, 8:16], tt)
        tp.ffn(wg_d, wu_d, wd_d)
        if final:
            for tt in range(NT):
                tp.rmsnorm(gn[:, 16:24], tt, out_f32=True)
                for c in range(8):
                    si = kb.rr("ostage", 4)
                    P.V(lambda e, c=c, si=si, tt=tt: e.scalar_tensor_tensor(out=macc[si][:], in0=tp.x(c, tt), scalar=gn[:, 16 + c:17 + c], in1=tp.rs[:], op0=ALU.mult, op1=ALU.mult),
                        reads=[("x", c, tt), "rs", "gains"], writes=[("macc", si)])
                    P.dma("sync", lambda e, c=c, si=si, tt=tt, t0=t0: e.dma_start(out=out_o[c * 128:(c + 1) * 128, t0 + tt * TT:t0 + (tt + 1) * TT], in_=macc[si][:]),
                          reads=[("macc", si)], key=("ostage", si))
        else:
            P.dma("sync", lambda e, t0=t0: e.dma_start(out=out_v[:, :, t0:t0 + G], in_=xsb), reads=xkeys, key="xstore")
    return kb.finish()


def _run(nc, in_maps):
    res = run_bass_kernel_spmd(nc, in_maps, core_ids=list(range(NCORES)))
    return res.results


def _gain_tile(g):
    return np.ascontiguousarray(g.reshape(8, 128).T)


import ml_dtypes

NPBF = ml_dtypes.bfloat16
ROPE_THETA = 500000.0


def _consts_B():
    cmask = np.ones((128, TT), np.float32)
    cmask[:, ::HGC] = 0.0
    s_ = np.arange(HGC)[:, None]
    t_ = np.arange(HGC)[None, :]
    amask = np.tile((s_ <= t_).astype(np.float32), (1, TT // HGC))
    j = np.arange(128)[:, None]
    i = np.arange(128)[None, :]
    cur = (j <= i).astype(np.float32)
    prev = (j >= i).astype(np.float32)
    z = np.zeros_like(cur)
    pmask = np.stack([np.concatenate([cur, z, cur, prev], 1), np.concatenate([cur, prev, cur, prev], 1)]).astype(np.float32)
    ident = np.eye(128, dtype=np.float32).astype(NPBF)
    psw = np.zeros((32, 32), np.float32)
    psw[(np.arange(32) + 16) % 32, np.arange(32)] = 1.0
    invf = (np.float32(ROPE_THETA) ** (-(np.arange(16, dtype=np.float32) / np.float32(16)))).astype(np.float32)
    ropec = np.stack([np.concatenate([invf, invf]), np.concatenate([-np.ones(16), np.ones(16)])], 1).astype(np.float32)
    tau = np.tile(np.arange(TT, dtype=np.float32)[None, :], (128, 1))
    return {"cmask": cmask, "amask": amask, "pmask": pmask, "ident": ident, "psw": psw.astype(NPBF), "ropec": ropec, "tau": tau}


def _b_params(inp, l, hh):
    lg = inp["hg_lb_logits"]
    sl = slice(hh * 128, (hh + 1) * 128)
    hgp = np.stack([lg[0, sl], lg[1, sl], inp["hg_gnorm"][l, sl], np.full(128, 1.0 if l > 0 else 0.0, np.float32)], 1).astype(np.float32)
    s5p = np.zeros((128, 4, 67), np.float32)
    for j in range(4):
        for h in range(2):
            g = hh * 8 + 2 * j + h
            ps_ = slice(64 * h, 64 * h + 64)
            s5p[ps_, j, 0] = inp["s5_a_re"][l, g]
            s5p[ps_, j, 1] = inp["s5_a_im"][l, g]
            s5p[ps_, j, 2] = inp["s5_log_dt"][l, g]
            s5p[ps_, j, 3:19] = inp["s5_b_re"][l, g]
            s5p[ps_, j, 19:35] = inp["s5_b_im"][l, g]
            s5p[ps_, j, 35:51] = inp["s5_c_re"][l, g].T
            s5p[ps_, j, 51:67] = inp["s5_c_im"][l, g].T
    s5d = np.ascontiguousarray(inp["s5_d"][l, sl][:, None]).astype(np.float32)
    return {"hgp": hgp, "s5p": s5p.reshape(128, 4 * 67), "s5d": s5d}


def _b_acts(projT_b, hh):
    hg4 = np.stack([projT_b[k * 512 + hh * 128: k * 512 + hh * 128 + 128] for k in range(4)])
    s5u = np.ascontiguousarray(projT_b[2048 + hh * 128: 2048 + hh * 128 + 128])
    att9 = np.stack([projT_b[3584 + i * 1536 + (gi * 4 + hh) * 128: 3584 + i * 1536 + (gi * 4 + hh) * 128 + 128]
                     for gi in range(3) for i in range(3)])
    return {"hg4": np.ascontiguousarray(hg4), "s5u": s5u, "att9": np.ascontiguousarray(att9)}


def _c_params(inp, l):
    gains3 = np.concatenate([_gain_tile(inp["mix_norm"][l]), _gain_tile(inp["ffn2_norm"][l]), _gain_tile(inp["final_norm"])], 1).astype(np.float32)
    convp = np.zeros((128, 4, 34), np.float32)
    for c in range(4):
        sl = slice(c * 128, (c + 1) * 128)
        convp[:, c, 0:31] = inp["conv_w"][l][:, sl].T
        convp[:, c, 31] = inp["conv_b"][l, sl]
        convp[:, c, 32] = inp["conv_ln_g"][l, sl]
        convp[:, c, 33] = inp["conv_ln_b"][l, sl]
    return {"gains3": gains3, "convp": convp.reshape(128, 4 * 34),
            "w_gates": np.ascontiguousarray(inp["w_in"][l][:, NMIX:]), "w_branch": inp["w_branch"][l], "w_out": inp["w_out"][l],
            "w_glu": inp["s5_w_glu"][l], "w_gate": inp["ffn2_w_gate"][l], "w_up": inp["ffn2_w_up"][l], "w_down": inp["ffn2_w_down"][l]}


def _c_conv_halo(conv_b, q):
    t0 = q * TOK
    out = np.zeros((1024, TOK + HALO), conv_b.dtype)
    lo = max(t0 - HALO, 0)
    out[:, HALO - (t0 - lo):] = conv_b[:, lo:t0 + TOK]
    return out


def kernel(**inputs):
    inp = {k: np.asarray(v) for k, v in inputs.items()}
    x = inp["x"]
    progA = build_A()
    progB = {p: build_B((p,)) for p in "hsa"}
    progC = [build_C(False), build_C(True)]
    consts = _consts_B()
    pos = [np.ascontiguousarray(inp["positions"][b][None, :]).astype(np.int32) for b in range(B)]
    xT = [np.ascontiguousarray(x[c // 4, (c % 4) * TOK:(c % 4 + 1) * TOK, :].T) for c in range(NCORES)]
    for l in range(DEPTH):
        wa = {"g_ffn1": _gain_tile(inp["ffn1_norm"][l]), "g_mix": _gain_tile(inp["mix_norm"][l]),
              "w_gate": inp["ffn1_w_gate"][l], "w_up": inp["ffn1_w_up"][l], "w_down": inp["ffn1_w_down"][l],
              "w_in": np.ascontiguousarray(inp["w_in"][l][:, :NMIX])}
        resA = _run(progA, [dict(wa, xT=xT[c]) for c in range(NCORES)])
        x1T = [resA[c]["x1T"] for c in range(NCORES)]
        projT_b = [np.concatenate([resA[4 * b + q]["projT"] for q in range(4)], axis=1) for b in range(B)]
        del resA
        ycat = {}
        for part, big, outk in (("h", "hg4", "ya"), ("s", "s5u", "zb"), ("a", "att9", "yd")):
            in_maps = []
            for c in range(NCORES):
                b, hh = c // 4, c % 4
                m = dict(consts)
                m.update(_b_params(inp, l, hh))
                m[big] = _b_acts(projT_b[b], hh)[big]
                m["pos"] = pos[b]
                in_maps.append(m)
            resB = _run(progB[part], in_maps)
            ycat[outk] = [np.concatenate([resB[4 * b + hh][outk] for hh in range(4)], axis=0) for b in range(B)]
            del resB
        wc = _c_params(inp, l)
        in_maps = []
        for c in range(NCORES):
            b, q = c // 4, c % 4
            tsl = slice(q * TOK, (q + 1) * TOK)
            m = dict(wc)
            m["x1T"] = x1T[c]
            m["yaT"] = np.ascontiguousarray(ycat["ya"][b][:, tsl])
            m["zbT"] = np.ascontiguousarray(ycat["zb"][b][:, tsl])
            m["ydT"] = np.ascontiguousarray(ycat["yd"][b][:, tsl])
            m["convT"] = _c_conv_halo(projT_b[b][2560:3584], q)
            in_maps.append(m)
        resC = _run(progC[1 if l == DEPTH - 1 else 0], in_maps)
        xT = [resC[c]["outT"] for c in range(NCORES)]
        del resC
    out = np.empty((B, L, D), np.float32)
    for c in range(NCORES):
        out[c // 4, (c % 4) * TOK:(c % 4 + 1) * TOK, :] = xT[c].T
    return out
```

```python
import contextlib
import math

import numpy as np
import concourse.bass as bass
import concourse.mybir as mybir
from concourse.bass_utils import run_bass_kernel_spmd

F32 = mybir.dt.float32
BF16 = mybir.dt.bfloat16
I32 = mybir.dt.int32
AF = mybir.ActivationFunctionType
ALU = mybir.AluOpType

NCORES = 8
D = 1024
DFF = 2816
B = 2
L = 8192
DEPTH = 2
TOK = 2048
TT = 512
EPS = 1e-6
NMIX = 8192

ENGINES = ("sync", "scalar", "gpsimd", "vector", "tensor")


class _Op:
    __slots__ = ("eng", "fn", "deps", "is_dma", "sem_key", "ticket", "signals", "idx", "tiny")


class Prog:
    def __init__(self, nc):
        self.nc = nc
        self.ops = []
        self.last_writer = {}
        self.readers = {}
        self.tiny_mode = False

    def _add(self, eng, fn, reads, writes, is_dma=False, sem_key=None):
        op = _Op()
        op.eng, op.fn, op.is_dma, op.sem_key = eng, fn, is_dma, sem_key
        op.signals, op.ticket, op.idx = False, None, len(self.ops)
        op.tiny = self.tiny_mode and not is_dma and eng != "tensor"
        deps = set()
        for r in reads:
            w = self.last_writer.get(r)
            if w is not None:
                deps.add(w)
        for w_ in writes:
            w = self.last_writer.get(w_)
            if w is not None:
                deps.add(w)
            deps.update(self.readers.get(w_, ()))
        deps.discard(op.idx)
        op.deps = deps
        for r in reads:
            self.readers.setdefault(r, []).append(op.idx)
        for w_ in writes:
            self.last_writer[w_] = op.idx
            self.readers[w_] = []
        self.ops.append(op)
        return op

    def op(self, eng, fn, reads=(), writes=()):
        return self._add(eng, fn, tuple(reads), tuple(writes))

    def dma(self, eng, fn, reads=(), writes=(), key=None):
        return self._add(eng, fn, tuple(reads), tuple(writes), is_dma=True, sem_key=key)

    def V(self, fn, reads=(), writes=()):
        return self.op("vector", fn, reads, writes)

    def A(self, fn, reads=(), writes=()):
        return self.op("scalar", fn, reads, writes)

    def G(self, fn, reads=(), writes=()):
        return self.op("gpsimd", fn, reads, writes)

    def T(self, fn, reads=(), writes=()):
        return self.op("tensor", fn, reads, writes)

    def emit(self):
        nc, ops = self.nc, self.ops
        for op in ops:
            for d in op.deps:
                dop = ops[d]
                if dop.is_dma or dop.eng != op.eng or dop.tiny:
                    dop.signals = True
        for op in ops:
            if op.is_dma:
                op.signals = True
        eng_cnt = {e: 0 for e in ENGINES}
        key_cnt = {}
        for op in ops:
            if not op.signals:
                continue
            if op.is_dma:
                key_cnt[op.sem_key] = key_cnt.get(op.sem_key, 0) + 16
                op.ticket = key_cnt[op.sem_key]
            else:
                eng_cnt[op.eng] += 1
                op.ticket = eng_cnt[op.eng]
        with contextlib.ExitStack() as es:
            eng_sem = {e: es.enter_context(nc.semaphore("es_" + e)) for e in ENGINES}
            key_sem = {k: es.enter_context(nc.semaphore("ds_%d" % i)) for i, k in enumerate(key_cnt)}
            block = es.enter_context(nc.Block())

            def semval(dop):
                if dop.is_dma:
                    return key_sem[dop.sem_key], ("k", dop.sem_key), dop.ticket
                return eng_sem[dop.eng], ("e", dop.eng), dop.ticket

            def make_body(ename):
                def body(eng):
                    waited = {}
                    for op in ops:
                        if op.eng != ename:
                            continue
                        need = {}
                        for d in op.deps:
                            dop = ops[d]
                            if (not dop.is_dma) and dop.eng == ename and not dop.tiny:
                                continue
                            sem, skey, val = semval(dop)
                            if waited.get(skey, 0) >= val:
                                continue
                            if need.get(skey, (None, 0))[1] < val:
                                need[skey] = (sem, val)
                        for skey, (sem, val) in need.items():
                            eng.wait_ge(sem, val)
                            waited[skey] = val
                        inst = op.fn(eng)
                        if op.signals:
                            sem, skey, val = semval(op)
                            inst.then_inc(sem, 16 if op.is_dma else 1)
                    if ename == "sync":
                        for k, c in key_cnt.items():
                            if waited.get(("k", k), 0) < c:
                                eng.wait_ge(key_sem[k], c)
                        for e2, c in eng_cnt.items():
                            if c > 0 and e2 != "sync":
                                eng.wait_ge(eng_sem[e2], c)
                return body

            block.sync(make_body("sync"))
            block.scalar(make_body("scalar"))
            block.gpsimd(make_body("gpsimd"))
            block.vector(make_body("vector"))
            block.tensor(make_body("tensor"))


class KB:
    def __init__(self):
        self.nc = bass.Bass("TRN2", target_bir_lowering=False)
        self.es = contextlib.ExitStack()
        self.P = Prog(self.nc)
        self._rr = {}
        self._uid = 0

    def din(self, name, shape, dt=F32):
        return self.nc.dram_tensor(name, list(shape), dt, kind="ExternalInput").ap()

    def dout(self, name, shape, dt=F32):
        return self.nc.dram_tensor(name, list(shape), dt, kind="ExternalOutput").ap()

    def sb(self, name, shape, dt=F32):
        return self.es.enter_context(self.nc.sbuf_tensor("sb_" + name, list(shape), dt))

    def ps(self, name, shape, dt=F32):
        return self.es.enter_context(self.nc.psum_tensor("ps_" + name, list(shape), dt))

    def rr(self, name, n):
        i = self._rr.get(name, 0)
        self._rr[name] = i + 1
        return i % n

    def uid(self):
        self._uid += 1
        return self._uid

    def finish(self):
        self.P.emit()
        self.es.close()
        return self.nc

    def mm_group(self, out_ap, out_key, pairs, reads):
        n = len(pairs)
        for i, (l, r) in enumerate(pairs):
            self.P.T(lambda e, l=l, r=r, i=i: e.matmul(out_ap, lhsT=l, rhs=r, start=(i == 0), stop=(i == n - 1)),
                     reads=reads, writes=[out_key])


class TokenPhase:
    def __init__(self, kb, G):
        self.kb = kb
        self.G = G
        self.NT = G // TT
        kb_ = kb
        self.xT = kb_.sb("xT", [128, 8 * G], F32)
        self.hT = kb_.sb("hT", [128, 8 * G], BF16)
        self.hid = kb_.sb("hid", [128, 22 * G], BF16)
        self.sq = kb_.sb("sq", [128, 8 * TT], BF16)
        self.rs = kb_.sb("rs", [128, TT], F32)
        self.ones = kb_.sb("ones", [128, 128], BF16)
        self.sg = [kb_.sb("sg%d" % i, [128, TT], F32) for i in range(2)]
        self.wslot = [kb_.sb("wslot%d" % i, [128, 22 * 256], BF16) for i in range(3)]
        self.banks = [kb_.ps("bank%d" % i, [128, TT], F32) for i in range(8)]
        kb.P.V(lambda e: e.memset(self.ones[:], 1.0), writes=["ones"])

    def x(self, c, tt):
        return self.xT[:, c * self.G + tt * TT: c * self.G + (tt + 1) * TT]

    def h(self, c, tt):
        return self.hT[:, c * self.G + tt * TT: c * self.G + (tt + 1) * TT]

    def hd(self, j, tt):
        return self.hid[:, j * self.G + tt * TT: j * self.G + (tt + 1) * TT]

    def bank(self, purpose):
        groups = {"gate": (0, 1), "up": (2, 3), "acc": (4, 5, 6), "stat": (7,)}[purpose]
        i = groups[self.kb.rr("bank_" + purpose, len(groups))]
        return self.banks[i], ("bank", i)

    def wload(self, src_ap, nk, ncols):
        kb = self.kb
        si = kb.rr("wslot", 3)
        slot = self.wslot[si]
        view = slot[:, 0:nk * ncols].rearrange("p (c n) -> p c n", n=ncols)
        kb.P.dma("gpsimd", lambda e: e.dma_start(out=view, in_=src_ap), writes=[("wslot", si)], key=("wslot", si))
        return view, ("wslot", si)

    def rmsnorm(self, gain_ap, tt, out_f32=False):
        kb, P = self.kb, self.kb.P
        for c in range(8):
            P.A(lambda e, c=c: e.activation(out=self.sq[:, c * TT:(c + 1) * TT], in_=self.x(c, tt), func=AF.Square),
                reads=[("x", c, tt)], writes=[("sq", c)])
        bk, bkey = self.bank("stat")
        kb.mm_group(bk[:], bkey, [(self.ones[:], self.sq[:, c * TT:(c + 1) * TT]) for c in range(8)],
                    reads=["ones"] + [("sq", c) for c in range(8)])
        P.A(lambda e: e.activation(out=self.rs[:], in_=bk[:], func=AF.Sqrt, bias=EPS, scale=1.0 / D),
            reads=[bkey], writes=["rs"])
        P.V(lambda e: e.reciprocal(out=self.rs[:], in_=self.rs[:]), reads=["rs"], writes=["rs"])
        if out_f32:
            return
        for c in range(8):
            P.V(lambda e, c=c: e.scalar_tensor_tensor(out=self.h(c, tt), in0=self.x(c, tt), scalar=gain_ap[:, c:c + 1],
                                                      in1=self.rs[:], op0=ALU.mult, op1=ALU.mult),
                reads=[("x", c, tt), "rs", "gains"], writes=[("h", c, tt)])

    def ffn(self, wg, wu, wd):
        kb, P, NT = self.kb, self.kb.P, self.NT
        wg_v = wg.rearrange("(c p) n -> p c n", p=128)
        wu_v = wu.rearrange("(c p) n -> p c n", p=128)
        wd_v = wd.rearrange("(c p) n -> p c n", p=128)
        for blk in range(DFF // 256):
            gv, gk = self.wload(wg_v[:, :, blk * 256:(blk + 1) * 256], 8, 256)
            uv, uk = self.wload(wu_v[:, :, blk * 256:(blk + 1) * 256], 8, 256)
            for sub in range(2):
                j = blk * 2 + sub
                for tt in range(NT):
                    hreads = [("h", c, tt) for c in range(8)]
                    bg, bgk = self.bank("gate")
                    kb.mm_group(bg[:], bgk, [(gv[:, c, sub * 128:(sub + 1) * 128], self.h(c, tt)) for c in range(8)],
                                reads=[gk] + hreads)
                    bu, buk = self.bank("up")
                    kb.mm_group(bu[:], buk, [(uv[:, c, sub * 128:(sub + 1) * 128], self.h(c, tt)) for c in range(8)],
                                reads=[uk] + hreads)
                    si = kb.rr("sg", 2)
                    sg = self.sg[si]
                    P.A(lambda e, sg=sg, bg=bg: e.activation(out=sg[:], in_=bg[:], func=AF.Silu),
                        reads=[bgk], writes=[("sg", si)])
                    P.V(lambda e, sg=sg, bu=bu, j=j, tt=tt: e.tensor_tensor(out=self.hd(j, tt), in0=bu[:], in1=sg[:], op=ALU.mult),
                        reads=[buk, ("sg", si)], writes=[("hid", j, tt)])
        for mblk in range(D // 256):
            dv, dk = self.wload(wd_v[:, :, mblk * 256:(mblk + 1) * 256], 22, 256)
            for sub in range(2):
                m = mblk * 2 + sub
                for tt in range(NT):
                    ba, bak = self.bank("acc")
                    kb.mm_group(ba[:], bak, [(dv[:, j, sub * 128:(sub + 1) * 128], self.hd(j, tt)) for j in range(22)],
                                reads=[dk] + [("hid", j, tt) for j in range(22)])
                    P.V(lambda e, ba=ba, m=m, tt=tt: e.scalar_tensor_tensor(out=self.x(m, tt), in0=ba[:], scalar=0.5, in1=self.x(m, tt),
                                                                             op0=ALU.mult, op1=ALU.add),
                        reads=[bak, ("x", m, tt)], writes=[("x", m, tt)])


def build_A():
    kb = KB()
    P = kb.P
    xT_d = kb.din("xT", [D, TOK])
    g1_d = kb.din("g_ffn1", [128, 8])
    g2_d = kb.din("g_mix", [128, 8])
    wg_d = kb.din("w_gate", [D, DFF])
    wu_d = kb.din("w_up", [D, DFF])
    wd_d = kb.din("w_down", [DFF, D])
    win_d = kb.din("w_in", [D, NMIX])
    x1_o = kb.dout("x1T", [D, TOK])
    pj_o = kb.dout("projT", [NMIX, TOK], BF16)
    G = 1024
    tp = TokenPhase(kb, G)
    NT = tp.NT
    g1 = kb.sb("g1", [128, 8])
    g2 = kb.sb("g2", [128, 8])
    stage = [kb.sb("stage%d" % i, [128, TT], BF16) for i in range(4)]
    P.dma("sync", lambda e: e.dma_start(out=g1[:], in_=g1_d), writes=["gains"], key="gains")
    P.dma("sync", lambda e: e.dma_start(out=g2[:], in_=g2_d), writes=["gains"], key="gains")
    xT_v = xT_d.rearrange("(c p) t -> p c t", p=128)
    x1_v = x1_o.rearrange("(c p) t -> p c t", p=128)
    win_v = win_d.rearrange("(c p) n -> p c n", p=128)
    xkeys = [("x", c, tt) for c in range(8) for tt in range(NT)]
    for grp in range(TOK // G):
        t0 = grp * G
        xsb = tp.xT[:].rearrange("p (c t) -> p c t", t=G)
        P.dma("sync", lambda e, t0=t0: e.dma_start(out=xsb, in_=xT_v[:, :, t0:t0 + G]), writes=xkeys, key="xload")
        for tt in range(NT):
            tp.rmsnorm(g1, tt)
        tp.ffn(wg_d, wu_d, wd_d)
        P.dma("sync", lambda e, t0=t0: e.dma_start(out=x1_v[:, :, t0:t0 + G], in_=xsb), reads=xkeys, key="x1store")
        for tt in range(NT):
            tp.rmsnorm(g2, tt)
        for blk in range(NMIX // 256):
            wv, wk = tp.wload(win_v[:, :, blk * 256:(blk + 1) * 256], 8, 256)
            for sub in range(2):
                col = blk * 256 + sub * 128
                for tt in range(NT):
                    ba, bak = tp.bank("acc")
                    kb.mm_group(ba[:], bak, [(wv[:, c, sub * 128:(sub + 1) * 128], tp.h(c, tt)) for c in range(8)],
                                reads=[wk] + [("h", c, tt) for c in range(8)])
                    si = kb.rr("stage", 4)
                    st = stage[si]
                    if si % 2 == 0:
                        P.A(lambda e, st=st, ba=ba: e.activation(out=st[:], in_=ba[:], func=AF.Copy), reads=[bak], writes=[("stage", si)])
                    else:
                        P.V(lambda e, st=st, ba=ba: e.tensor_copy(out=st[:], in_=ba[:]), reads=[bak], writes=[("stage", si)])
                    P.dma("sync", lambda e, st=st, col=col, tt=tt, t0=t0: e.dma_start(
                        out=pj_o[col:col + 128, t0 + tt * TT:t0 + (tt + 1) * TT], in_=st[:]),
                        reads=[("stage", si)], key=("stage", si))
    return kb.finish()


_DBG = 0
NT_L = L // TT
HGC = 64
TWO_PI = 2.0 * math.pi


def build_B(parts=("h", "s", "a")):
    kb = KB()
    P = kb.P
    hg_d = kb.din("hg4", [4, 128, L], BF16) if "h" in parts else None
    s5u_d = kb.din("s5u", [128, L], BF16) if "s" in parts else None
    att_d = kb.din("att9", [9, 128, L], BF16) if "a" in parts else None
    pos_d = kb.din("pos", [1, L], I32)
    cmask_d = kb.din("cmask", [128, TT])
    amask_d = kb.din("amask", [64, TT])
    pmask_d = kb.din("pmask", [2, 128, TT])
    ident_d = kb.din("ident", [128, 128], BF16)
    psw_d = kb.din("psw", [32, 32], BF16)
    ropec_d = kb.din("ropec", [32, 2])
    tau_d = kb.din("tau", [128, TT])
    hgp_d = kb.din("hgp", [128, 4])
    s5p_d = kb.din("s5p", [128, 4 * 67])
    s5d_d = kb.din("s5d", [128, 1])
    ya_o = kb.dout("ya", [128, L], BF16) if "h" in parts else None
    zb_o = kb.dout("zb", [128, L], BF16) if "s" in parts else None
    yd_o = kb.dout("yd", [128, L], BF16) if "a" in parts else None
    tab_o = kb.dout("ropetab", [2, 32, L], F32) if "a" in parts else None

    NS = 11
    S = [kb.sb("scr%d" % i, [128, TT], F32) for i in range(NS)]
    SK = [("scr", i) for i in range(NS)]
    NSB = 6
    SB = [kb.sb("scb%d" % i, [128, TT], BF16) for i in range(NSB)]
    SBK = [("scb", i) for i in range(NSB)]
    banks = [kb.ps("bank%d" % i, [128, TT], F32) for i in range(8)]
    BK = [("bank", i) for i in range(8)]
    cmask = kb.sb("cmask", [128, TT]); amask = kb.sb("amask", [64, TT])
    pmask = kb.sb("pmask", [128, 2 * TT], BF16)
    ident = kb.sb("ident", [128, 128], BF16); psw = kb.sb("psw", [32, 32], BF16)
    ropec = kb.sb("ropec", [32, 2]); tau = kb.sb("tau", [128, TT])
    hgp = kb.sb("hgp", [128, 4]); s5p = kb.sb("s5p", [128, 4 * 67]); s5d = kb.sb("s5d", [128, 1])
    ones = kb.sb("ones", [128, 128], BF16)
    P.V(lambda e: e.memset(ones[:], 1.0), writes=["ones"])
    for nm, dst, src in (("cmask", cmask, cmask_d), ("amask", amask, amask_d), ("ident", ident, ident_d), ("psw", psw, psw_d),
                         ("ropec", ropec, ropec_d), ("tau", tau, tau_d), ("hgp", hgp, hgp_d), ("s5p", s5p, s5p_d), ("s5d", s5d, s5d_d)):
        P.dma("sync", lambda e, dst=dst, src=src: e.dma_start(out=dst[:], in_=src), writes=[nm], key="c_" + nm)
    P.dma("gpsimd", lambda e: e.dma_start(out=pmask[:].rearrange("p (a t) -> p a t", a=2), in_=pmask_d.rearrange("a p t -> p a t")),
          writes=["pmask"], key="c_pmask")

    def sin_table(out_ap, ang_ap, shape, shift, tmpf, tmpi, reads, writes, scale_ap=None, tkey="sin_tmp2"):
        P.V(lambda e: e.tensor_scalar(out=tmpf, in0=ang_ap, scalar1=1.0 / TWO_PI, scalar2=shift / TWO_PI, op0=ALU.mult, op1=ALU.add),
            reads=reads, writes=["sin_tmp", tkey])
        P.V(lambda e: e.tensor_copy(out=tmpi, in_=tmpf), reads=["sin_tmp"], writes=["sin_tmpi"])
        P.V(lambda e: e.tensor_copy(out=tmpf, in_=tmpi), reads=["sin_tmpi"], writes=["sin_tmp", tkey])
        P.V(lambda e: e.scalar_tensor_tensor(out=tmpf, in0=tmpf, scalar=-TWO_PI, in1=ang_ap, op0=ALU.mult, op1=ALU.add),
            reads=["sin_tmp"] + list(reads), writes=["sin_tmp", tkey])
        P.V(lambda e: e.tensor_scalar(out=tmpf, in0=tmpf, scalar1=-math.pi - shift, scalar2=math.pi - shift, op0=ALU.max, op1=ALU.min),
            reads=["sin_tmp"], writes=["sin_tmp", tkey])
        if scale_ap is None:
            P.A(lambda e: e.activation(out=out_ap, in_=tmpf, func=AF.Sin, bias=shiftc[shift][:shape[0], 0:1]), reads=["sin_tmp", "shiftc", tkey], writes=writes)
        else:
            assert shift == 0.0
            P.A(lambda e: e.activation(out=out_ap, in_=tmpf, func=AF.Sin, scale=scale_ap), reads=["sin_tmp", tkey], writes=writes)

    shiftc = {0.0: kb.sb("shift0", [128, 1]), math.pi / 2: kb.sb("shift1", [128, 1])}
    P.V(lambda e: e.memset(shiftc[0.0][:], 0.0), writes=["shiftc"])
    P.V(lambda e: e.memset(shiftc[math.pi / 2][:], math.pi / 2), writes=["shiftc"])
    tmpi = kb.sb("tmpi", [128, TT], I32)

    lb = kb.sb("lb", [128, 1]); oml = kb.sb("oml", [128, 1])
    P.tiny_mode = True
    P.V(lambda e: e.tensor_tensor(out=lb[:], in0=hgp[:, 1:2], in1=hgp[:, 0:1], op=ALU.subtract), reads=["hgp"], writes=["lb"])
    P.A(lambda e: e.activation(out=lb[:], in_=lb[:], func=AF.Sigmoid), reads=["lb"], writes=["lb"])
    P.V(lambda e: e.tensor_tensor(out=lb[:], in0=lb[:], in1=hgp[:, 3:4], op=ALU.mult), reads=["lb", "hgp"], writes=["lb"])
    P.V(lambda e: e.tensor_scalar(out=oml[:], in0=lb[:], scalar1=-1.0, scalar2=1.0, op0=ALU.mult, op1=ALU.add), reads=["lb"], writes=["oml"])
    hc1 = kb.sb("hc1", [128, 1]); hc2 = kb.sb("hc2", [128, 1])
    P.V(lambda e: e.tensor_scalar(out=hc1[:], in0=oml[:], scalar1=0.5, scalar2=None, op0=ALU.mult), reads=["oml"], writes=["hc"])
    P.V(lambda e: e.tensor_tensor(out=hc2[:], in0=hc1[:], in1=lb[:], op=ALU.add), reads=["hc", "lb"], writes=["hc"])
    P.tiny_mode = False
    Sst = kb.sb("Sst", [128, 128]); Sb = kb.sb("Sb", [128, 128], BF16)
    P.V(lambda e: e.memset(Sst[:], 0.0), writes=["Sst"])
    esc = kb.sb("esc", [128, 32])
    hin = [[kb.sb("hin%d_%d" % (a, i), [128, TT], BF16) for i in range(4)] for a in range(2)]
    KT = kb.sb("KTtok", [64, 8 * 128], BF16); VT = kb.sb("VTtok", [64, 8 * 128], BF16)
    pT = [kb.ps("pT%d" % i, [128, 1024], BF16) for i in range(0)]
    QSC = float(128 ** -0.5)
    for t in (range(NT_L) if "h" in parts else ()):
        a = t % 2
        tsl = slice(t * TT, (t + 1) * TT)
        for i in range(4):
            P.dma("sync", lambda e, i=i, a=a, tsl=tsl: e.dma_start(out=hin[a][i][:], in_=hg_d[i, :, tsl]),
                  writes=[("hin", a, i)], key=("hin", a, i))
        q_t, f_t, i_t, g_t = hin[a]
        sg, t1, lf, bb, bq, eq, ek, qs, osb, rst, sgl = (S[k] for k in range(11))
        Qt, Kt, attm, osq = SB[0], SB[1], SB[2], SB[3]
        P.A(lambda e, f_t=f_t: e.activation(out=sg[:], in_=f_t[:], func=AF.Tanh, scale=0.5), reads=[("hin", a, 1)], writes=[SK[0]])
        P.A(lambda e, q_t=q_t: e.activation(out=qs[:], in_=q_t[:], func=AF.Silu), reads=[("hin", a, 0)], writes=[SK[7]])
        P.A(lambda e, g_t=g_t: e.activation(out=sgl[:], in_=g_t[:], func=AF.Silu), reads=[("hin", a, 3)], writes=[SK[10]])
        P.V(lambda e: e.tensor_scalar(out=t1[:], in0=sg[:], scalar1=hc1[:, 0:1], scalar2=hc2[:, 0:1], op0=ALU.mult, op1=ALU.add),
            reads=[SK[0], "hc"], writes=[SK[1]])
        P.A(lambda e: e.activation(out=lf[:], in_=t1[:], func=AF.Ln), reads=[SK[1]], writes=[SK[2]])
        P.V(lambda e: e.tensor_tensor_scan(out=bb[:], data0=cmask[:], data1=lf[:], initial=0.0, op0=ALU.mult, op1=ALU.add),
            reads=[SK[2], "cmask"], writes=[SK[3]])
        b3 = bb[:].rearrange("p (c t) -> p c t", t=HGC)
        P.V(lambda e, b3=b3: e.tensor_tensor(out=bq[:].rearrange("p (c t) -> p c t", t=HGC), in0=b3,
                                             in1=b3[:, :, 32:33].to_broadcast([128, 8, HGC]), op=ALU.subtract),
            reads=[SK[3]], writes=[SK[4]])
        P.A(lambda e: e.activation(out=eq[:], in_=bq[:], func=AF.Exp), reads=[SK[4]], writes=[SK[5]])
        P.A(lambda e: e.activation(out=ek[:], in_=bq[:], func=AF.Exp, scale=-1.0), reads=[SK[4]], writes=[SK[6]])
        P.V(lambda e: e.scalar_tensor_tensor(out=Qt[:], in0=qs[:], scalar=QSC, in1=eq[:], op0=ALU.mult, op1=ALU.mult),
            reads=[SK[7], SK[5]], writes=[SBK[0]])
        P.V(lambda e: e.tensor_scalar(out=t1[:], in0=t1[:], scalar1=-1.0, scalar2=1.0, op0=ALU.mult, op1=ALU.add), reads=[SK[1]], writes=[SK[1]])
        P.V(lambda e: e.tensor_tensor(out=Kt[:], in0=t1[:], in1=ek[:], op=ALU.mult), reads=[SK[1], SK[6]], writes=[SBK[1]])
        if _DBG == 1:
            continue
        P.tiny_mode = True
        P.V(lambda e, b3=b3: e.tensor_copy(out=esc[:, 0:8], in_=b3[:, :, 32]), reads=[SK[3]], writes=["esc"])
        P.A(lambda e: e.activation(out=esc[:, 8:16], in_=esc[:, 0:8], func=AF.Exp), reads=["esc"], writes=["esc"])
        P.A(lambda e, b3=b3: e.activation(out=esc[:, 16:24], in_=b3[:, :, 63], func=AF.Exp), reads=[SK[3], "esc"], writes=["esc"])
        P.V(lambda e, b3=b3: e.tensor_tensor(out=esc[:, 24:32], in0=b3[:, :, 63], in1=esc[:, 0:8], op=ALU.subtract), reads=[SK[3], "esc"], writes=["esc"])
        P.A(lambda e: e.activation(out=esc[:, 24:32], in_=esc[:, 24:32], func=AF.Exp), reads=["esc"], writes=["esc"])
        if _DBG == 2:
            continue
        P.tiny_mode = False
        kt_ps = banks[0][:].bitcast(BF16)
        vt_ps = banks[1][:].bitcast(BF16)
        for n in range(8):
            P.T(lambda e, n=n: e.transpose(out=kt_ps[0:64, n * 128:(n + 1) * 128], in_=Kt[:, n * 64:(n + 1) * 64], identity=ident[:]),
                reads=[SBK[1], "ident"], writes=[BK[0]])
        for n in range(8):
            P.T(lambda e, n=n, i_t=i_t: e.transpose(out=vt_ps[0:64, n * 128:(n + 1) * 128], in_=i_t[:, n * 64:(n + 1) * 64], identity=ident[:]),
                reads=[("hin", a, 2), "ident"], writes=[BK[1]])
        P.A(lambda e: e.activation(out=KT[:], in_=kt_ps[0:64, :], func=AF.Copy), reads=[BK[0]], writes=["KT"])
        P.V(lambda e: e.tensor_copy(out=VT[:], in_=vt_ps[0:64, :]), reads=[BK[1]], writes=["VT"])
        if _DBG == 3:
            continue
        for n in range(8):
            P.T(lambda e, n=n: e.matmul(banks[2][0:64, n * 64:(n + 1) * 64], lhsT=Kt[:, n * 64:(n + 1) * 64], rhs=Qt[:, n * 64:(n + 1) * 64],
                                        start=True, stop=True), reads=[SBK[0], SBK[1]], writes=[BK[2]])
        P.V(lambda e: e.tensor_tensor(out=attm[0:64, :], in0=banks[2][0:64, :], in1=amask[:], op=ALU.mult), reads=[BK[2], "amask"], writes=[SBK[2]])
        if _DBG == 4:
            continue
        for n in range(8):
            bi = 3 + n // 4
            P.T(lambda e, n=n, bi=bi: e.matmul(banks[bi][:, (n % 4) * 128:(n % 4 + 1) * 128], lhsT=KT[:, n * 128:(n + 1) * 128],
                                               rhs=VT[:, n * 128:(n + 1) * 128], start=True, stop=True), reads=["KT", "VT"], writes=[BK[bi]])
        P.tiny_mode = True
        for n in range(8):
            bi = 3 + n // 4
            P.V(lambda e, n=n: e.tensor_scalar(out=Sb[:], in0=Sst[:], scalar1=esc[:, 8 + n:9 + n], scalar2=None, op0=ALU.mult),
                reads=["Sst", "esc"], writes=["Sb"])
            P.T(lambda e, n=n: e.matmul(banks[5][:, n * 64:(n + 1) * 64], lhsT=Sb[:], rhs=Qt[:, n * 64:(n + 1) * 64], start=True, stop=False),
                reads=["Sb", SBK[0]], writes=[BK[5]])
            P.T(lambda e, n=n: e.matmul(banks[5][:, n * 64:(n + 1) * 64], lhsT=VT[:, n * 128:(n + 1) * 128], rhs=attm[0:64, n * 64:(n + 1) * 64],
                                        start=False, stop=True), reads=["VT", SBK[2]], writes=[BK[5]])
            P.V(lambda e, n=n: e.tensor_scalar(out=Sst[:], in0=Sst[:], scalar1=esc[:, 16 + n:17 + n], scalar2=None, op0=ALU.mult),
                reads=["Sst", "esc"], writes=["Sst"])
            P.V(lambda e, n=n, bi=bi: e.scalar_tensor_tensor(out=Sst[:], in0=banks[bi][:, (n % 4) * 128:(n % 4 + 1) * 128], scalar=esc[:, 24 + n:25 + n],
                                                             in1=Sst[:], op0=ALU.mult, op1=ALU.add), reads=[BK[bi], "Sst", "esc"], writes=["Sst"])
        P.tiny_mode = False
        if _DBG == 5:
            continue
        if _DBG != 11:
            pass
        if _DBG != 10:
            P.V(lambda e: e.tensor_copy(out=osb[:], in_=banks[5][:]), reads=[BK[5]], writes=[SK[8]])
        if _DBG == 20 and t == 0:
            P.dma("sync", lambda e: e.dma_start(out=zb_o[:, 0:512], in_=Qt[:]), reads=[SBK[0]], key="dbg0")
            P.dma("sync", lambda e: e.dma_start(out=zb_o[:, 512:1024], in_=Kt[:]), reads=[SBK[1]], key="dbg1")
            P.dma("sync", lambda e: e.dma_start(out=zb_o[0:64, 1024:1536], in_=attm[0:64, :]), reads=[SBK[2]], key="dbg2")
            P.dma("gpsimd", lambda e: e.dma_start(out=zb_o[:, 1536:2048], in_=bb[:]), reads=[SK[3]], key="dbg3")
            P.dma("sync", lambda e: e.dma_start(out=yd_o[0:64, 0:1024], in_=KT[:]), reads=["KT"], key="dbg4")
            P.dma("sync", lambda e: e.dma_start(out=yd_o[0:64, 1024:2048], in_=VT[:]), reads=["VT"], key="dbg5")
            P.dma("gpsimd", lambda e: e.dma_start(out=yd_o[:, 2048:2560], in_=osb[:]), reads=[SK[8]], key="dbg6")
            P.dma("gpsimd", lambda e: e.dma_start(out=yd_o[:, 2560:2688], in_=Sst[:]), reads=["Sst"], key="dbg7")
            P.dma("gpsimd", lambda e: e.dma_start(out=yd_o[:, 2688:2720], in_=esc[:]), reads=["esc"], key="dbg8")
        P.A(lambda e: e.activation(out=osq[:], in_=osb[:], func=AF.Square), reads=[SK[8]], writes=[SBK[3]])
        if _DBG in (10, 11):
            continue
        if _DBG == 6:
            continue
        P.T(lambda e: e.matmul(banks[6][:], lhsT=ones[:], rhs=osq[:], start=True, stop=True), reads=["ones", SBK[3]], writes=[BK[6]])
        P.A(lambda e: e.activation(out=rst[:], in_=banks[6][:], func=AF.Ln, bias=EPS, scale=1.0 / 128), reads=[BK[6]], writes=[SK[9]])
        P.A(lambda e: e.activation(out=rst[:], in_=rst[:], func=AF.Exp, scale=-0.5), reads=[SK[9]], writes=[SK[9]])
        P.V(lambda e: e.scalar_tensor_tensor(out=osb[:], in0=osb[:], scalar=hgp[:, 2:3], in1=rst[:], op0=ALU.mult, op1=ALU.mult),
            reads=[SK[8], SK[9], "hgp"], writes=[SK[8]])
        if _DBG == 8:
            continue
        yo = SB[4 + a]
        P.V(lambda e, yo=yo: e.tensor_tensor(out=yo[:], in0=osb[:], in1=sgl[:], op=ALU.mult), reads=[SK[8], SK[10]], writes=[SBK[4 + a]])
        if _DBG == 9:
            continue
        P.dma("sync", lambda e, yo=yo, tsl=tsl: e.dma_start(out=ya_o[:, tsl], in_=yo[:]), reads=[SBK[4 + a]], key=("yo", a))

    s5v = s5p[:].rearrange("p (j k) -> p j k", k=67)
    a_re, a_im, ldt = s5v[:, :, 0], s5v[:, :, 1], s5v[:, :, 2]
    sm = kb.sb("s5small", [128, 64])
    def col(i):
        return sm[:, 4 * i:4 * i + 4]
    dt_, adt, mag, th, cth, sth, abr, abi, den, m1, zr, zi, tA, tB, c512, s512 = (col(i) for i in range(16))
    smi = kb.sb("s5smalli", [128, 4], I32)
    P.tiny_mode = True
    P.A(lambda e: e.activation(out=dt_, in_=ldt, func=AF.Exp), reads=["s5p"], writes=["sm_dt"])
    P.V(lambda e: e.tensor_tensor(out=adt, in0=a_re, in1=dt_, op=ALU.mult), reads=["s5p", "sm_dt"], writes=["sm_adt"])
    P.A(lambda e: e.activation(out=mag, in_=adt, func=AF.Exp), reads=["sm_adt"], writes=["sm_mag"])
    P.V(lambda e: e.tensor_tensor(out=th, in0=a_im, in1=dt_, op=ALU.mult), reads=["s5p", "sm_dt"], writes=["sm_th"])
    sin_table(sth, th, [128, 4], 0.0, tA, smi[:], ["sm_th"], ["sm_sth"])
    sin_table(cth, th, [128, 4], math.pi / 2, tA, smi[:], ["sm_th"], ["sm_cth"])
    P.V(lambda e: e.tensor_scalar(out=tB, in0=th, scalar1=float(TT), scalar2=None, op0=ALU.mult), reads=["sm_th"], writes=["sm_tB"])
    sin_table(s512, tB, [128, 4], 0.0, tA, smi[:], ["sm_tB"], ["sm_s512"])
    sin_table(c512, tB, [128, 4], math.pi / 2, tA, smi[:], ["sm_tB"], ["sm_c512"])
    ns512 = kb.sb("ns512", [128, 4])
    P.V(lambda e: e.tensor_scalar(out=ns512[:], in0=s512, scalar1=-1.0, scalar2=None, op0=ALU.mult), reads=["sm_s512"], writes=["ns512"])
    P.V(lambda e: e.tensor_tensor(out=abr, in0=mag, in1=cth, op=ALU.mult), reads=["sm_mag", "sm_cth"], writes=["sm_abr"])
    P.V(lambda e: e.tensor_tensor(out=abi, in0=mag, in1=sth, op=ALU.mult), reads=["sm_mag", "sm_sth"], writes=["sm_abi"])
    P.V(lambda e: e.tensor_tensor(out=den, in0=a_re, in1=a_re, op=ALU.mult), reads=["s5p"], writes=["sm_den"])
    P.V(lambda e: e.tensor_tensor(out=tA, in0=a_im, in1=a_im, op=ALU.mult), reads=["s5p", "sm_c512"], writes=["sin_tmp"])
    P.V(lambda e: e.tensor_tensor(out=den, in0=den, in1=tA, op=ALU.add), reads=["sm_den", "sin_tmp"], writes=["sm_den"])
    P.V(lambda e: e.reciprocal(out=den, in_=den), reads=["sm_den"], writes=["sm_den"])
    P.V(lambda e: e.tensor_scalar(out=m1, in0=abr, scalar1=-1.0, scalar2=None, op0=ALU.add), reads=["sm_abr"], writes=["sm_m1"])
    P.V(lambda e: e.tensor_tensor(out=zr, in0=m1, in1=a_re, op=ALU.mult), reads=["sm_m1", "s5p"], writes=["sm_zr"])
    P.V(lambda e: e.tensor_tensor(out=tA, in0=abi, in1=a_im, op=ALU.mult), reads=["sm_abi", "s5p"], writes=["sin_tmp"])
    P.V(lambda e: e.tensor_tensor(out=zr, in0=zr, in1=tA, op=ALU.add), reads=["sm_zr", "sin_tmp"], writes=["sm_zr"])
    P.V(lambda e: e.tensor_tensor(out=zr, in0=zr, in1=den, op=ALU.mult), reads=["sm_zr", "sm_den"], writes=["sm_zr"])
    P.V(lambda e: e.tensor_tensor(out=zi, in0=abi, in1=a_re, op=ALU.mult), reads=["sm_abi", "s5p"], writes=["sm_zi"])
    P.V(lambda e: e.tensor_tensor(out=tA, in0=m1, in1=a_im, op=ALU.mult), reads=["sm_m1", "s5p"], writes=["sin_tmp"])
    P.V(lambda e: e.tensor_tensor(out=zi, in0=zi, in1=tA, op=ALU.subtract), reads=["sm_zi", "sin_tmp"], writes=["sm_zi"])
    P.V(lambda e: e.tensor_tensor(out=zi, in0=zi, in1=den, op=ALU.mult), reads=["sm_zi", "sm_den"], writes=["sm_zi"])
    Bex = [kb.sb("Bex%d" % i, [128, 4 * 128], BF16) for i in range(2)]
    Cex = [kb.sb("Cex%d" % i, [128, 4 * 128], BF16) for i in range(2)]
    BT = [kb.sb("BT%d" % i, [128, 4 * 128], BF16) for i in range(2)]
    bbt = kb.sb("bbt", [128, 64])
    for i in range(2):
        P.V(lambda e, i=i: e.memset(Bex[i][:], 0.0), writes=[("Bex", i)])
        P.V(lambda e, i=i: e.memset(Cex[i][:], 0.0), writes=[("Cex", i)])
    for j in range(4):
        bre, bim = s5v[:, j, 3:19], s5v[:, j, 19:35]
        cre, cim = s5v[:, j, 35:51], s5v[:, j, 51:67]
        zrj, zij = zr[:, j:j + 1], zi[:, j:j + 1]
        P.V(lambda e, bre=bre, zrj=zrj: e.tensor_scalar(out=bbt[:, 0:16], in0=bre, scalar1=zrj, scalar2=None, op0=ALU.mult), reads=["s5p", "sm_zr"], writes=["bbt"])
        P.V(lambda e, bim=bim, zij=zij: e.tensor_scalar(out=bbt[:, 16:32], in0=bim, scalar1=zij, scalar2=None, op0=ALU.mult), reads=["s5p", "sm_zi"], writes=["bbt"])
        P.V(lambda e, bim=bim, zrj=zrj: e.tensor_scalar(out=bbt[:, 32:48], in0=bim, scalar1=zrj, scalar2=None, op0=ALU.mult), reads=["s5p", "sm_zr"], writes=["bbt"])
        P.V(lambda e, bre=bre, zij=zij: e.tensor_scalar(out=bbt[:, 48:64], in0=bre, scalar1=zij, scalar2=None, op0=ALU.mult), reads=["s5p", "sm_zi"], writes=["bbt"])
        for hh_ in range(2):
            ps_ = slice(64 * hh_, 64 * hh_ + 64)
            cs = slice(j * 128 + (2 * j + hh_) * 16, j * 128 + (2 * j + hh_) * 16 + 16)
            P.V(lambda e, ps_=ps_, cs=cs: e.tensor_tensor(out=Bex[0][ps_, cs], in0=bbt[ps_, 0:16], in1=bbt[ps_, 16:32], op=ALU.subtract), reads=["bbt"], writes=[("Bex", 0)])
            P.V(lambda e, ps_=ps_, cs=cs: e.tensor_tensor(out=Bex[1][ps_, cs], in0=bbt[ps_, 32:48], in1=bbt[ps_, 48:64], op=ALU.add), reads=["bbt"], writes=[("Bex", 1)])
            P.V(lambda e, ps_=ps_, cs=cs, cre=cre: e.tensor_copy(out=Cex[0][ps_, cs], in_=cre[ps_, :]), reads=["s5p"], writes=[("Cex", 0)])
            P.V(lambda e, ps_=ps_, cs=cs, cim=cim: e.tensor_scalar(out=Cex[1][ps_, cs], in0=cim[ps_, :], scalar1=-1.0, scalar2=None, op0=ALU.mult), reads=["s5p"], writes=[("Cex", 1)])
    P.tiny_mode = False
    for i in range(2):
        bt_ps = banks[i][:].bitcast(BF16)
        for j in range(4):
            P.T(lambda e, i=i, j=j, bt_ps=bt_ps: e.transpose(out=bt_ps[:, j * 128:(j + 1) * 128], in_=Bex[i][:, j * 128:(j + 1) * 128], identity=ident[:]),
                reads=[("Bex", i), "ident"], writes=[BK[i]])
        P.V(lambda e, i=i, bt_ps=bt_ps: e.tensor_copy(out=BT[i][:], in_=bt_ps[:, 0:512]), reads=[BK[i]], writes=[("BT", i)])
    cosT = [kb.sb("cosT%d" % j, [128, TT]) for j in range(4)]
    sinT = [kb.sb("sinT%d" % j, [128, TT]) for j in range(4)]
    for j in range(4):
        ang = S[0]
        P.V(lambda e, j=j, ang=ang: e.tensor_scalar(out=ang[:], in0=tau[:], scalar1=th[:, j:j + 1], scalar2=None, op0=ALU.mult), reads=["tau", "sm_th"], writes=[SK[0]])
        sin_table(sinT[j][:], ang[:], [128, TT], 0.0, S[1][:], tmpi[:], [SK[0]], [("sinT", j)], tkey=SK[1])
        sin_table(cosT[j][:], ang[:], [128, TT], math.pi / 2, S[1][:], tmpi[:], [SK[0]], [("cosT", j)], tkey=SK[1])
    init = kb.sb("s5init", [128, 8])
    P.V(lambda e: e.memset(init[:], 0.0), writes=["s5init"])
    s5in = [kb.sb("s5in%d" % i, [128, TT], BF16) for i in range(2)]
    tmpc = kb.sb("s5tmpc", [128, 2])
    for t in (range(NT_L) if "s" in parts else ()):
        a = t % 2
        tsl = slice(t * TT, (t + 1) * TT)
        u_t = s5in[a]
        if t == 0:
            P.dma("sync", lambda e, u_t=u_t, tsl=tsl: e.dma_start(out=u_t[:], in_=s5u_d[:, tsl]), writes=[("s5in", a)], key=("s5in", a))
        def emit_bu(tq, jq):
            aq = tq % 2
            uq = s5in[aq]
            jslq = slice(jq * 128, (jq + 1) * 128)
            bre_, bim_ = banks[(2 * jq) % 4], banks[(2 * jq + 1) % 4]
            P.T(lambda e: e.matmul(bre_[:], lhsT=BT[0][:, jslq], rhs=uq[:], start=True, stop=True), reads=[("BT", 0), ("s5in", aq)], writes=[BK[(2 * jq) % 4]])
            P.T(lambda e: e.matmul(bim_[:], lhsT=BT[1][:, jslq], rhs=uq[:], start=True, stop=True), reads=[("BT", 1), ("s5in", aq)], writes=[BK[(2 * jq + 1) % 4]])

        if t == 0:
            emit_bu(0, 0)
        for j in range(4):
            jsl = slice(j * 128, (j + 1) * 128)
            b_re, b_im = banks[(2 * j) % 4], banks[(2 * j + 1) % 4]
            kre, kim = BK[(2 * j) % 4], BK[(2 * j + 1) % 4]
            if j < 3:
                emit_bu(t, j + 1)
            elif t + 1 < NT_L:
                un = s5in[(t + 1) % 2]
                P.dma("sync", lambda e, un=un, t=t: e.dma_start(out=un[:], in_=s5u_d[:, (t + 1) * TT:(t + 2) * TT]), writes=[("s5in", (t + 1) % 2)], key=("s5in", (t + 1) % 2))
                emit_bu(t + 1, 0)
            w1, w2, wnr, wni, wr, wi = S[2], S[3], S[4], S[5], S[6], S[7]
            cT, sT = cosT[j], sinT[j]
            rd = [("cosT", j), ("sinT", j)]
            P.V(lambda e, cT=cT, b_re=b_re: e.tensor_tensor(out=w1[:], in0=b_re[:], in1=cT[:], op=ALU.mult), reads=[kre] + rd, writes=[SK[2]])
            P.V(lambda e, sT=sT, b_im=b_im: e.tensor_tensor(out=w2[:], in0=b_im[:], in1=sT[:], op=ALU.mult), reads=[kim] + rd, writes=[SK[3]])
            P.V(lambda e: e.tensor_tensor(out=wnr[:], in0=w1[:], in1=w2[:], op=ALU.add), reads=[SK[2], SK[3]], writes=[SK[4]])
            P.V(lambda e, cT=cT, b_im=b_im: e.tensor_tensor(out=w1[:], in0=b_im[:], in1=cT[:], op=ALU.mult), reads=[kim] + rd, writes=[SK[2]])
            P.V(lambda e, sT=sT, b_re=b_re: e.tensor_tensor(out=w2[:], in0=b_re[:], in1=sT[:], op=ALU.mult), reads=[kre] + rd, writes=[SK[3]])
            P.V(lambda e: e.tensor_tensor(out=wni[:], in0=w1[:], in1=w2[:], op=ALU.subtract), reads=[SK[2], SK[3]], writes=[SK[5]])
            P.V(lambda e, j=j: e.tensor_tensor_scan(out=wr[:], data0=mag[:, j:j + 1].to_broadcast([128, TT]), data1=wnr[:], initial=init[:, j:j + 1],
                                                    op0=ALU.mult, op1=ALU.add), reads=[SK[4], "sm_mag", "s5init"], writes=[SK[6]])
            P.V(lambda e, j=j: e.tensor_tensor_scan(out=wi[:], data0=mag[:, j:j + 1].to_broadcast([128, TT]), data1=wni[:], initial=init[:, 4 + j:5 + j],
                                                    op0=ALU.mult, op1=ALU.add), reads=[SK[5], "sm_mag", "s5init"], writes=[SK[7]])
            P.tiny_mode = True
            P.V(lambda e, j=j: e.tensor_tensor(out=tmpc[:, 0:1], in0=wr[:, TT - 1:TT], in1=c512[:, j:j + 1], op=ALU.mult), reads=[SK[6], "sm_c512"], writes=["tmpc"])
            P.V(lambda e, j=j: e.tensor_tensor(out=tmpc[:, 1:2], in0=wr[:, TT - 1:TT], in1=s512[:, j:j + 1], op=ALU.mult), reads=[SK[6], "sm_s512"], writes=["tmpc"])
            P.V(lambda e, j=j: e.scalar_tensor_tensor(out=init[:, j:j + 1], in0=wi[:, TT - 1:TT], scalar=ns512[:, j:j + 1], in1=tmpc[:, 0:1], op0=ALU.mult, op1=ALU.add),
                reads=[SK[7], "ns512", "tmpc"], writes=["s5init"])
            P.V(lambda e, j=j: e.scalar_tensor_tensor(out=init[:, 4 + j:5 + j], in0=wi[:, TT - 1:TT], scalar=c512[:, j:j + 1], in1=tmpc[:, 1:2], op0=ALU.mult, op1=ALU.add),
                reads=[SK[7], "sm_c512", "tmpc"], writes=["s5init"])
            P.tiny_mode = False
            xr, xi = SB[0 + 2 * (j % 2)], SB[1 + 2 * (j % 2)]
            kxr, kxi = SBK[0 + 2 * (j % 2)], SBK[1 + 2 * (j % 2)]
            P.V(lambda e, cT=cT: e.tensor_tensor(out=w1[:], in0=wr[:], in1=cT[:], op=ALU.mult), reads=[SK[6]] + rd, writes=[SK[2]])
            P.V(lambda e, sT=sT: e.tensor_tensor(out=w2[:], in0=wi[:], in1=sT[:], op=ALU.mult), reads=[SK[7]] + rd, writes=[SK[3]])
            P.V(lambda e, xr=xr: e.tensor_tensor(out=xr[:], in0=w1[:], in1=w2[:], op=ALU.subtract), reads=[SK[2], SK[3]], writes=[kxr])
            P.V(lambda e, sT=sT: e.tensor_tensor(out=w1[:], in0=wr[:], in1=sT[:], op=ALU.mult), reads=[SK[6]] + rd, writes=[SK[2]])
            P.V(lambda e, cT=cT: e.tensor_tensor(out=w2[:], in0=wi[:], in1=cT[:], op=ALU.mult), reads=[SK[7]] + rd, writes=[SK[3]])
            P.V(lambda e, xi=xi: e.tensor_tensor(out=xi[:], in0=w1[:], in1=w2[:], op=ALU.add), reads=[SK[2], SK[3]], writes=[kxi])
            P.T(lambda e, jsl=jsl, xr=xr, j=j: e.matmul(banks[4][:], lhsT=Cex[0][:, jsl], rhs=xr[:], start=(j == 0), stop=False), reads=[("Cex", 0), kxr], writes=[BK[4]])
            P.T(lambda e, jsl=jsl, xi=xi, j=j: e.matmul(banks[4][:], lhsT=Cex[1][:, jsl], rhs=xi[:], start=False, stop=(j == 3)), reads=[("Cex", 1), kxi], writes=[BK[4]])
        yv, y2, sgm = S[8], S[9], S[10]
        P.V(lambda e, u_t=u_t: e.scalar_tensor_tensor(out=yv[:], in0=u_t[:], scalar=s5d[:, 0:1], in1=banks[4][:], op0=ALU.mult, op1=ALU.add),
            reads=[("s5in", a), "s5d", BK[4]], writes=[SK[8]])
        P.V(lambda e: e.tensor_tensor(out=y2[:], in0=yv[:], in1=yv[:], op=ALU.mult), reads=[SK[8]], writes=[SK[9]])
        P.V(lambda e: e.tensor_scalar(out=y2[:], in0=y2[:], scalar1=0.044715, scalar2=1.0, op0=ALU.mult, op1=ALU.add), reads=[SK[9]], writes=[SK[9]])
        P.V(lambda e: e.tensor_tensor(out=y2[:], in0=y2[:], in1=yv[:], op=ALU.mult), reads=[SK[9], SK[8]], writes=[SK[9]])
        P.A(lambda e: e.activation(out=sgm[:], in_=y2[:], func=AF.Sigmoid, scale=2.0 * math.sqrt(2.0 / math.pi)), reads=[SK[9]], writes=[SK[10]])
        zo = SB[4 + a]
        P.V(lambda e, zo=zo: e.tensor_tensor(out=zo[:], in0=yv[:], in1=sgm[:], op=ALU.mult), reads=[SK[8], SK[10]], writes=[SBK[4 + a]])
        P.dma("sync", lambda e, zo=zo, tsl=tsl: e.dma_start(out=zb_o[:, tsl], in_=zo[:]), reads=[SBK[4 + a]], key=("yo", a))

    qr = kb.sb("qr", [128, L], BF16); kr = kb.sb("kr", [128, L], BF16); vv = kb.sb("vv", [128, L], BF16)
    Oacc = kb.sb("Oacc", [128, L]); Dacc = kb.sb("Dacc", [128, L])
    posi = kb.sb("posi", [32, TT], I32)
    vtok = [kb.sb("vtok%d" % i, [128, 4 * 128], BF16) for i in range(2)]
    for i in range(2):
        P.V(lambda e, i=i: e.memset(vtok[i][:], 0.0), writes=[("vtok", i)])
    SCL = float(128 ** -0.5)
    for gi, dil in (enumerate((1, 4, 16)) if "a" in parts else ()):
        for t in range(NT_L):
            tsl = slice(t * TT, (t + 1) * TT)
            for i, dst in enumerate((qr, kr, vv)):
                P.dma("sync", lambda e, i=i, dst=dst, tsl=tsl, gi=gi: e.dma_start(out=dst[:, tsl], in_=att_d[3 * gi + i, :, tsl]),
                      writes=[("att", i, t)], key=("attld", i, t))
            pa = t % 2
            ang, tmpf, r1, r2 = S[0], S[1], S[4], S[5]
            sinS, cosS = S[2 + 6 * pa], S[3 + 6 * pa]
            ksin, kcos = SK[2 + 6 * pa], SK[3 + 6 * pa]
            if gi == 0:
                P.dma("sync", lambda e, tsl=tsl: e.dma_start(out=posi[:], in_=pos_d[:, tsl].partition_broadcast(32)), writes=["posi"], key="posi")
                P.V(lambda e: e.tensor_copy(out=ang[0:32, :], in_=posi[:]), reads=["posi"], writes=[SK[0]])
                P.V(lambda e: e.tensor_scalar(out=ang[0:32, :], in0=ang[0:32, :], scalar1=ropec[:, 0:1], scalar2=None, op0=ALU.mult), reads=[SK[0], "ropec"], writes=[SK[0]])
                sin_table(sinS[0:32, :], ang[0:32, :], [32, TT], 0.0, tmpf[0:32, :], tmpi[0:32, :], [SK[0]], [ksin], tkey=SK[1])
                P.V(lambda e, sinS=sinS: e.tensor_scalar(out=sinS[0:32, :], in0=sinS[0:32, :], scalar1=ropec[:, 1:2], scalar2=None, op0=ALU.mult), reads=[ksin, "ropec"], writes=[ksin])
                sin_table(cosS[0:32, :], ang[0:32, :], [32, TT], math.pi / 2, tmpf[0:32, :], tmpi[0:32, :], [SK[0]], [kcos], tkey=SK[1])
                P.dma("sync", lambda e, sinS=sinS, tsl=tsl: e.dma_start(out=tab_o[0, :, tsl], in_=sinS[0:32, :]), reads=[ksin], writes=[("tab", 0, t)], key=("tabst", 0, pa))
                P.dma("sync", lambda e, cosS=cosS, tsl=tsl: e.dma_start(out=tab_o[1, :, tsl], in_=cosS[0:32, :]), reads=[kcos], writes=[("tab", 1, t)], key=("tabst", 1, pa))
            else:
                P.dma("sync", lambda e, sinS=sinS, tsl=tsl: e.dma_start(out=sinS[0:32, :], in_=tab_o[0, :, tsl]), reads=[("tab", 0, t)], writes=[ksin], key=("tabld", 0, pa))
                P.dma("sync", lambda e, cosS=cosS, tsl=tsl: e.dma_start(out=cosS[0:32, :], in_=tab_o[1, :, tsl]), reads=[("tab", 1, t)], writes=[kcos], key=("tabld", 1, pa))
            for i, dst in enumerate((qr, kr)):
                bsw = banks[i]
                P.T(lambda e, dst=dst, tsl=tsl, bsw=bsw: e.matmul(bsw[0:32, :], lhsT=psw[:], rhs=dst[0:32, tsl], start=True, stop=True),
                    reads=["psw", ("att", i, t)], writes=[BK[i]])
                P.V(lambda e, bsw=bsw, sinS=sinS: e.tensor_tensor(out=r1[0:32, :], in0=bsw[0:32, :], in1=sinS[0:32, :], op=ALU.mult), reads=[BK[i], ksin], writes=[SK[4]])
                P.V(lambda e, dst=dst, tsl=tsl, cosS=cosS: e.tensor_tensor(out=r2[0:32, :], in0=dst[0:32, tsl], in1=cosS[0:32, :], op=ALU.mult), reads=[("att", i, t), kcos], writes=[SK[5]])
                P.V(lambda e, dst=dst, tsl=tsl: e.tensor_tensor(out=dst[0:32, tsl], in0=r1[0:32, :], in1=r2[0:32, :], op=ALU.add), reads=[SK[4], SK[5]], writes=[("att", i, t)])
        nper = L // dil // 128
        allkeys = [("att", i, t) for i in range(3) for t in range(NT_L)]
        for r in range(dil):
            for qd in range(nper // 4):
                n0 = qd * 4
                def toks(n):
                    st = r + dil * 128 * n
                    return slice(st, st + dil * 127 + 1, dil)
                vt_ps = banks[2][:].bitcast(BF16)
                vcur = vtok[(r * (nper // 4) + qd) % 2]
                vprev = vtok[(r * (nper // 4) + qd + 1) % 2]
                kvc, kvp = ("vtok", (r * (nper // 4) + qd) % 2), ("vtok", (r * (nper // 4) + qd + 1) % 2)
                for k_ in range(4):
                    P.T(lambda e, k_=k_, tk=toks(n0 + k_): e.transpose(out=vt_ps[:, k_ * 128:(k_ + 1) * 128], in_=vv[:, tk], identity=ident[:]),
                        reads=allkeys[2 * NT_L:] + ["ident"], writes=[BK[2]])
                P.A(lambda e, vcur=vcur: e.activation(out=vcur[:], in_=vt_ps[:, 0:512], func=AF.Copy), reads=[BK[2]], writes=[kvc])
                pm = [SB[0], SB[1]]
                for pr in range(2):
                    sb_ = banks[pr]
                    for k2 in range(2):
                        n = n0 + pr * 2 + k2
                        P.T(lambda e, tk=toks(n), k2=k2, sb_=sb_: e.matmul(sb_[:, k2 * 256:k2 * 256 + 128], lhsT=kr[:, tk], rhs=qr[:, tk], start=True, stop=True),
                            reads=allkeys[:2 * NT_L], writes=[BK[pr]])
                        np_ = n - 1 if n > 0 else n
                        P.T(lambda e, tk=toks(n), tkp=toks(np_), k2=k2, sb_=sb_: e.matmul(sb_[:, k2 * 256 + 128:k2 * 256 + 256], lhsT=kr[:, tkp], rhs=qr[:, tk], start=True, stop=True),
                            reads=allkeys[:2 * NT_L], writes=[BK[pr]])
                    ex = S[6 + pr]
                    P.A(lambda e, ex=ex, sb_=sb_: e.activation(out=ex[:], in_=sb_[:], func=AF.Exp, scale=SCL), reads=[BK[pr]], writes=[SK[6 + pr]])
                    mk = pmask[:, 0:TT] if (n0 == 0 and pr == 0) else pmask[:, TT:2 * TT]
                    P.V(lambda e, ex=ex, mk=mk, pr=pr: e.tensor_tensor(out=pm[pr][:], in0=ex[:], in1=mk, op=ALU.mult), reads=[SK[6 + pr], "pmask"], writes=[SBK[pr]])
                for k_ in range(4):
                    pr, k2 = k_ // 2, k_ % 2
                    pc = pm[pr][:, k2 * 256:k2 * 256 + 128]
                    pp = pm[pr][:, k2 * 256 + 128:k2 * 256 + 256]
                    vc = vcur[:, k_ * 128:(k_ + 1) * 128]
                    vp = vcur[:, (k_ - 1) * 128:k_ * 128] if k_ > 0 else vprev[:, 3 * 128:4 * 128]
                    osl = slice(k_ * 128, (k_ + 1) * 128)
                    P.T(lambda e, vc=vc, pc=pc, osl=osl: e.matmul(banks[3][:, osl], lhsT=vc, rhs=pc, start=True, stop=False), reads=[kvc, SBK[pr]], writes=[BK[3]])
                    P.T(lambda e, vp=vp, pp=pp, osl=osl: e.matmul(banks[3][:, osl], lhsT=vp, rhs=pp, start=False, stop=True), reads=[kvc, kvp, SBK[pr]], writes=[BK[3]])
                    P.T(lambda e, pc=pc, osl=osl: e.matmul(banks[4][:, osl], lhsT=ones[:], rhs=pc, start=True, stop=False), reads=["ones", SBK[pr]], writes=[BK[4]])
                    P.T(lambda e, pp=pp, osl=osl: e.matmul(banks[4][:, osl], lhsT=ones[:], rhs=pp, start=False, stop=True), reads=["ones", SBK[pr]], writes=[BK[4]])
                st = r + dil * 128 * n0
                dsl = slice(st, st + dil * 511 + 1, dil)
                if gi == 0:
                    P.A(lambda e, dsl=dsl: e.activation(out=Oacc[:, dsl], in_=banks[3][:], func=AF.Copy), reads=[BK[3]], writes=["Oacc"])
                    P.V(lambda e, dsl=dsl: e.tensor_copy(out=Dacc[:, dsl], in_=banks[4][:]), reads=[BK[4]], writes=["Dacc"])
                else:
                    P.V(lambda e, dsl=dsl: e.tensor_tensor(out=Oacc[:, dsl], in0=banks[3][:], in1=Oacc[:, dsl], op=ALU.add), reads=[BK[3], "Oacc"], writes=["Oacc"])
                    P.V(lambda e, dsl=dsl: e.tensor_tensor(out=Dacc[:, dsl], in0=banks[4][:], in1=Dacc[:, dsl], op=ALU.add), reads=[BK[4], "Dacc"], writes=["Dacc"])
    for t in (range(NT_L) if "a" in parts else ()):
        a = t % 2
        tsl = slice(t * TT, (t + 1) * TT)
        rc = S[8 + a]
        P.V(lambda e, rc=rc, tsl=tsl: e.reciprocal(out=rc[:], in_=Dacc[:, tsl]), reads=["Dacc"], writes=[SK[8 + a]])
        yo = SB[4 + a]
        P.V(lambda e, rc=rc, tsl=tsl, yo=yo: e.tensor_tensor(out=yo[:], in0=Oacc[:, tsl], in1=rc[:], op=ALU.mult), reads=["Oacc", SK[8 + a]], writes=[SBK[4 + a]])
        P.dma("sync", lambda e, yo=yo, tsl=tsl: e.dma_start(out=yd_o[:, tsl], in_=yo[:]), reads=[SBK[4 + a]], key=("yo", a))
    return kb.finish()


CONV_W = 31
HALO = CONV_W - 1


def build_C(final):
    kb = KB()
    P = kb.P
    G = 1024
    x1_d = kb.din("x1T", [D, TOK])
    ya_d = kb.din("yaT", [512, TOK], BF16)
    zb_d = kb.din("zbT", [512, TOK], BF16)
    yd_d = kb.din("ydT", [512, TOK], BF16)
    cv_d = kb.din("convT", [1024, TOK + HALO], BF16)
    gn_d = kb.din("gains3", [128, 24])
    cvp_d = kb.din("convp", [128, 4 * 34])
    wgt_d = kb.din("w_gates", [D, 4096])
    wbr_d = kb.din("w_branch", [4, 512, D])
    wo_d = kb.din("w_out", [D, D])
    wglu_d = kb.din("w_glu", [512, 1024])
    wg_d = kb.din("w_gate", [D, DFF])
    wu_d = kb.din("w_up", [D, DFF])
    wd_d = kb.din("w_down", [DFF, D])
    out_o = kb.dout("outT", [D, TOK])
    tp = TokenPhase(kb, G)
    NT = tp.NT
    gn = kb.sb("gn", [128, 24]); cvp = kb.sb("cvp", [128, 4 * 34])
    P.dma("sync", lambda e: e.dma_start(out=gn[:], in_=gn_d), writes=["gains"], key="gains")
    P.dma("sync", lambda e: e.dma_start(out=cvp[:], in_=cvp_d), writes=["cvp"], key="cvp")
    cvv = cvp[:].rearrange("p (c k) -> p c k", k=34)
    yb3 = kb.sb("yb3", [128, 4 * G], BF16)
    zbt = kb.sb("zbt", [128, 4 * G], BF16)

    def ybr(k, c4, tt):
        if k < 3:
            j = 8 + 4 * k + c4
            return tp.hd(j, tt), ("hid", j, tt)
        return yb3[:, c4 * G + tt * TT:c4 * G + (tt + 1) * TT], ("yb3", c4, tt)

    def zbr(c4, tt):
        return zbt[:, c4 * G + tt * TT:c4 * G + (tt + 1) * TT], ("zbt", c4, tt)

    ca = kb.sb("ca", [128, TT + HALO], BF16); cb = kb.sb("cb", [128, TT + HALO], BF16)
    zc = kb.sb("zc", [128, TT + HALO]); sgb = kb.sb("sgb", [128, TT + HALO])
    vch = [kb.sb("vch%d" % c, [128, TT]) for c in range(4)]
    vb = kb.sb("vb", [128, 4 * TT], BF16); vsq = kb.sb("vsq", [128, 4 * TT], BF16)
    mean = kb.sb("mean", [128, TT]); var = kb.sb("var", [128, TT]); tmpc = kb.sb("tmpcv", [128, TT])
    macc = [kb.sb("macc%d" % i, [128, TT]) for i in range(4)]
    sgt = tp.sg
    x1_v = x1_d.rearrange("(c p) t -> p c t", p=128)
    out_v = out_o.rearrange("(c p) t -> p c t", p=128)
    wgt_v = wgt_d.rearrange("(c p) n -> p c n", p=128)
    wo_v = wo_d.rearrange("(c p) n -> p c n", p=128)
    wglu_v = wglu_d.rearrange("(c p) n -> p c n", p=128)
    xkeys = [("x", c, tt) for c in range(8) for tt in range(NT)]
    for grp in range(TOK // G):
        t0 = grp * G
        xsb = tp.xT[:].rearrange("p (c t) -> p c t", t=G)
        P.dma("sync", lambda e, t0=t0: e.dma_start(out=xsb, in_=x1_v[:, :, t0:t0 + G]), writes=xkeys, key="xload")
        P.dma("sync", lambda e, t0=t0: e.dma_start(out=tp.hid[:, 8 * G:12 * G].rearrange("p (c t) -> p c t", t=G),
                                                 in_=ya_d.rearrange("(c p) t -> p c t", p=128)[:, :, t0:t0 + G]),
              writes=[("hid", j, tt) for j in range(8, 12) for tt in range(NT)], key="yaload")
        P.dma("sync", lambda e, t0=t0: e.dma_start(out=yb3[:].rearrange("p (c t) -> p c t", t=G),
                                                 in_=yd_d.rearrange("(c p) t -> p c t", p=128)[:, :, t0:t0 + G]),
              writes=[("yb3", c4, tt) for c4 in range(4) for tt in range(NT)], key="ydload")
        P.dma("sync", lambda e, t0=t0: e.dma_start(out=zbt[:].rearrange("p (c t) -> p c t", t=G),
                                                 in_=zb_d.rearrange("(c p) t -> p c t", p=128)[:, :, t0:t0 + G]),
              writes=[("zbt", c4, tt) for c4 in range(4) for tt in range(NT)], key="zbload")
        for tt in range(NT):
            tp.rmsnorm(gn[:, 0:8], tt)
        for cbk in range(2):
            wa, wak = tp.wload(wglu_v[:, :, cbk * 256:(cbk + 1) * 256], 4, 256)
            wgl, wgk = tp.wload(wglu_v[:, :, 512 + cbk * 256:512 + (cbk + 1) * 256], 4, 256)
            for sub in range(2):
                c = cbk * 2 + sub
                for tt in range(NT):
                    zr_ = [zbr(k4, tt) for k4 in range(4)]
                    ba, bak = tp.bank("gate")
                    kb.mm_group(ba[:], bak, [(wa[:, k4, sub * 128:(sub + 1) * 128], zr_[k4][0]) for k4 in range(4)], reads=[wak] + [z[1] for z in zr_])
                    bg, bgk = tp.bank("up")
                    kb.mm_group(bg[:], bgk, [(wgl[:, k4, sub * 128:(sub + 1) * 128], zr_[k4][0]) for k4 in range(4)], reads=[wgk] + [z[1] for z in zr_])
                    si = kb.rr("sg", 2)
                    P.A(lambda e, si=si, bg=bg: e.activation(out=sgt[si][:], in_=bg[:], func=AF.Sigmoid), reads=[bgk], writes=[("sg", si)])
                    yo, yok = ybr(1, c, tt)
                    P.V(lambda e, si=si, ba=ba, yo=yo: e.tensor_tensor(out=yo, in0=ba[:], in1=sgt[si][:], op=ALU.mult),
                        reads=[bak, ("sg", si)], writes=[yok])
        for tt in range(NT):
            tb = t0 + tt * TT
            for c in range(4):
                P.dma("sync", lambda e, c=c, tb=tb: e.dma_start(out=ca[:], in_=cv_d[c * 128:(c + 1) * 128, tb:tb + TT + HALO]), writes=["ca"], key="ca")
                P.dma("sync", lambda e, c=c, tb=tb: e.dma_start(out=cb[:], in_=cv_d[512 + c * 128:512 + (c + 1) * 128, tb:tb + TT + HALO]), writes=["cb"], key="cb")
                P.A(lambda e: e.activation(out=sgb[:], in_=cb[:], func=AF.Sigmoid), reads=["cb"], writes=["sgb"])
                P.V(lambda e: e.tensor_tensor(out=zc[:], in0=ca[:], in1=sgb[:], op=ALU.mult), reads=["ca", "sgb"], writes=["zc"])
                v = vch[c]
                P.V(lambda e, v=v, c=c: e.tensor_scalar(out=v[:], in0=zc[:, 0:TT], scalar1=cvv[:, c, 0:1], scalar2=None, op0=ALU.mult),
                    reads=["zc", "cvp"], writes=[("vch", c)])
                for j in range(1, CONV_W):
                    P.V(lambda e, v=v, c=c, j=j: e.scalar_tensor_tensor(out=v[:], in0=zc[:, j:j + TT], scalar=cvv[:, c, j:j + 1], in1=v[:], op0=ALU.mult, op1=ALU.add),
                        reads=["zc", "cvp", ("vch", c)], writes=[("vch", c)])
                P.V(lambda e, v=v, c=c: e.tensor_scalar(out=v[:], in0=v[:], scalar1=cvv[:, c, 31:32], scalar2=None, op0=ALU.add),
                    reads=[("vch", c), "cvp"], writes=[("vch", c)])
                P.A(lambda e, v=v, c=c: e.activation(out=vb[:, c * TT:(c + 1) * TT], in_=v[:], func=AF.Copy), reads=[("vch", c)], writes=[("vb", c)])
                P.A(lambda e, v=v, c=c: e.activation(out=vsq[:, c * TT:(c + 1) * TT], in_=v[:], func=AF.Square), reads=[("vch", c)], writes=[("vsq", c)])
            b1, b1k = tp.bank("stat")
            kb.mm_group(b1[:], b1k, [(tp.ones[:], vb[:, c * TT:(c + 1) * TT]) for c in range(4)], reads=["ones"] + [("vb", c) for c in range(4)])
            P.V(lambda e, b1=b1: e.tensor_scalar(out=mean[:], in0=b1[:], scalar1=1.0 / 512, scalar2=None, op0=ALU.mult), reads=[b1k], writes=["mean"])
            b2, b2k = tp.bank("stat")
            kb.mm_group(b2[:], b2k, [(tp.ones[:], vsq[:, c * TT:(c + 1) * TT]) for c in range(4)], reads=["ones"] + [("vsq", c) for c in range(4)])
            P.V(lambda e: e.tensor_tensor(out=tmpc[:], in0=mean[:], in1=mean[:], op=ALU.mult), reads=["mean"], writes=["tmpcv"])
            P.V(lambda e, b2=b2: e.scalar_tensor_tensor(out=var[:], in0=b2[:], scalar=1.0 / 512, in1=tmpc[:], op0=ALU.mult, op1=ALU.subtract),
                reads=[b2k, "tmpcv"], writes=["var"])
            P.A(lambda e: e.activation(out=var[:], in_=var[:], func=AF.Sqrt, bias=EPS, scale=1.0), reads=["var"], writes=["var"])
            P.V(lambda e: e.reciprocal(out=var[:], in_=var[:]), reads=["var"], writes=["var"])
            for c in range(4):
                v = vch[c]
                P.V(lambda e, v=v: e.tensor_tensor(out=v[:], in0=v[:], in1=mean[:], op=ALU.subtract), reads=[("vch", c), "mean"], writes=[("vch", c)])
                P.V(lambda e, v=v: e.tensor_tensor(out=v[:], in0=v[:], in1=var[:], op=ALU.mult), reads=[("vch", c), "var"], writes=[("vch", c)])
                P.V(lambda e, v=v, c=c: e.tensor_scalar(out=v[:], in0=v[:], scalar1=cvv[:, c, 32:33], scalar2=cvv[:, c, 33:34], op0=ALU.mult, op1=ALU.add),
                    reads=[("vch", c), "cvp"], writes=[("vch", c)])
                yo, yok = ybr(2, c, tt)
                P.A(lambda e, v=v, yo=yo: e.activation(out=yo, in_=v[:], func=AF.Silu), reads=[("vch", c)], writes=[yok])
        for mblk in range(4):
            for k in range(4):
                gv, gk = tp.wload(wgt_v[:, :, k * 1024 + mblk * 256:k * 1024 + (mblk + 1) * 256], 8, 256)
                bv, bk_ = tp.wload(wbr_d[k].rearrange("(c p) n -> p c n", p=128)[:, :, mblk * 256:(mblk + 1) * 256], 4, 256)
                for sub in range(2):
                    m = mblk * 2 + sub
                    for tt in range(NT):
                        ai = sub * NT + tt
                        yr_ = [ybr(k, c4, tt) for c4 in range(4)]
                        bg, bgk = tp.bank("gate")
                        kb.mm_group(bg[:], bgk, [(gv[:, c, sub * 128:(sub + 1) * 128], tp.h(c, tt)) for c in range(8)], reads=[gk] + [("h", c, tt) for c in range(8)])
                        by, byk = tp.bank("up")
                        kb.mm_group(by[:], byk, [(bv[:, c4, sub * 128:(sub + 1) * 128], yr_[c4][0]) for c4 in range(4)], reads=[bk_] + [y[1] for y in yr_])
                        si = kb.rr("sg", 2)
                        P.A(lambda e, si=si, bg=bg: e.activation(out=sgt[si][:], in_=bg[:], func=AF.Sigmoid), reads=[bgk], writes=[("sg", si)])
                        if k == 0:
                            P.V(lambda e, si=si, by=by, ai=ai: e.tensor_tensor(out=macc[ai][:], in0=by[:], in1=sgt[si][:], op=ALU.mult),
                                reads=[byk, ("sg", si)], writes=[("macc", ai)])
                        else:
                            P.V(lambda e, si=si, by=by: e.tensor_tensor(out=sgt[si][:], in0=by[:], in1=sgt[si][:], op=ALU.mult),
                                reads=[byk, ("sg", si)], writes=[("sg", si)])
                            if k < 3:
                                P.V(lambda e, si=si, ai=ai: e.tensor_tensor(out=macc[ai][:], in0=macc[ai][:], in1=sgt[si][:], op=ALU.add),
                                    reads=[("macc", ai), ("sg", si)], writes=[("macc", ai)])
                            else:
                                P.V(lambda e, si=si, ai=ai, m=m, tt=tt: e.tensor_tensor(out=tp.hd(m, tt), in0=macc[ai][:], in1=sgt[si][:], op=ALU.add),
                                    reads=[("macc", ai), ("sg", si)], writes=[("hid", m, tt)])
        for mblk in range(4):
            ov, ok_ = tp.wload(wo_v[:, :, mblk * 256:(mblk + 1) * 256], 8, 256)
            for sub in range(2):
                m = mblk * 2 + sub
                for tt in range(NT):
                    ba, bak = tp.bank("acc")
                    kb.mm_group(ba[:], bak, [(ov[:, c, sub * 128:(sub + 1) * 128], tp.hd(c, tt)) for c in range(8)], reads=[ok_] + [("hid", c, tt) for c in range(8)])
                    P.V(lambda e, ba=ba, m=m, tt=tt: e.tensor_tensor(out=tp.x(m, tt), in0=ba[:], in1=tp.x(m, tt), op=ALU.add), reads=[bak, ("x", m, tt)], writes=[("x", m, tt)])
        for tt in range(NT):
            tp.rmsnorm(gn[:, 8:16], tt)
        tp.ffn(wg_d, wu_d, wd_d)
        if final:
            for tt in range(NT):
                tp.rmsnorm(gn[:, 16:24], tt, out_f32=True)
                for c in range(8):
                    si = kb.rr("ostage", 4)
                    P.V(lambda e, c=c, si=si, tt=tt: e.scalar_tensor_tensor(out=macc[si][:], in0=tp.x(c, tt), scalar=gn[:, 16 + c:17 + c], in1=tp.rs[:], op0=ALU.mult, op1=ALU.mult),
                        reads=[("x", c, tt), "rs", "gains"], writes=[("macc", si)])
                    P.dma("sync", lambda e, c=c, si=si, tt=tt, t0=t0: e.dma_start(out=out_o[c * 128:(c + 1) * 128, t0 + tt * TT:t0 + (tt + 1) * TT], in_=macc[si][:]),
                          reads=[("macc", si)], key=("ostage", si))
        else:
            P.dma("sync", lambda e, t0=t0: e.dma_start(out=out_v[:, :, t0:t0 + G], in_=xsb), reads=xkeys, key="xstore")
    return kb.finish()


def _run(nc, in_maps):
    res = run_bass_kernel_spmd(nc, in_maps, core_ids=list(range(NCORES)))
    return res.results


def _gain_tile(g):
    return np.ascontiguousarray(g.reshape(8, 128).T)


import ml_dtypes

NPBF = ml_dtypes.bfloat16
ROPE_THETA = 500000.0


def _consts_B():
    cmask = np.ones((128, TT), np.float32)
    cmask[:, ::HGC] = 0.0
    s_ = np.arange(HGC)[:, None]
    t_ = np.arange(HGC)[None, :]
    amask = np.tile((s_ <= t_).astype(np.float32), (1, TT // HGC))
    j = np.arange(128)[:, None]
    i = np.arange(128)[None, :]
    cur = (j <= i).astype(np.float32)
    prev = (j >= i).astype(np.float32)
    z = np.zeros_like(cur)
    pmask = np.stack([np.concatenate([cur, z, cur, prev], 1), np.concatenate([cur, prev, cur, prev], 1)]).astype(np.float32)
    ident = np.eye(128, dtype=np.float32).astype(NPBF)
    psw = np.zeros((32, 32), np.float32)
    psw[(np.arange(32) + 16) % 32, np.arange(32)] = 1.0
    invf = (np.float32(ROPE_THETA) ** (-(np.arange(16, dtype=np.float32) / np.float32(16)))).astype(np.float32)
    ropec = np.stack([np.concatenate([invf, invf]), np.concatenate([-np.ones(16), np.ones(16)])], 1).astype(np.float32)
    tau = np.tile(np.arange(TT, dtype=np.float32)[None, :], (128, 1))
    return {"cmask": cmask, "amask": amask, "pmask": pmask, "ident": ident, "psw": psw.astype(NPBF), "ropec": ropec, "tau": tau}


def _b_params(inp, l, hh):
    lg = inp["hg_lb_logits"]
    sl = slice(hh * 128, (hh + 1) * 128)
    hgp = np.stack([lg[0, sl], lg[1, sl], inp["hg_gnorm"][l, sl], np.full(128, 1.0 if l > 0 else 0.0, np.float32)], 1).astype(np.float32)
    s5p = np.zeros((128, 4, 67), np.float32)
    for j in range(4):
        for h in range(2):
            g = hh * 8 + 2 * j + h
            ps_ = slice(64 * h, 64 * h + 64)
            s5p[ps_, j, 0] = inp["s5_a_re"][l, g]
            s5p[ps_, j, 1] = inp["s5_a_im"][l, g]
            s5p[ps_, j, 2] = inp["s5_log_dt"][l, g]
            s5p[ps_, j, 3:19] = inp["s5_b_re"][l, g]
            s5p[ps_, j, 19:35] = inp["s5_b_im"][l, g]
            s5p[ps_, j, 35:51] = inp["s5_c_re"][l, g].T
            s5p[ps_, j, 51:67] = inp["s5_c_im"][l, g].T
    s5d = np.ascontiguousarray(inp["s5_d"][l, sl][:, None]).astype(np.float32)
    return {"hgp": hgp, "s5p": s5p.reshape(128, 4 * 67), "s5d": s5d}


def _b_acts(projT_b, hh):
    hg4 = np.stack([projT_b[k * 512 + hh * 128: k * 512 + hh * 128 + 128] for k in range(4)])
    s5u = np.ascontiguousarray(projT_b[2048 + hh * 128: 2048 + hh * 128 + 128])
    att9 = np.stack([projT_b[3584 + i * 1536 + (gi * 4 + hh) * 128: 3584 + i * 1536 + (gi * 4 + hh) * 128 + 128]
                     for gi in range(3) for i in range(3)])
    return {"hg4": np.ascontiguousarray(hg4), "s5u": s5u, "att9": np.ascontiguousarray(att9)}


def _c_params(inp, l):
    gains3 = np.concatenate([_gain_tile(inp["mix_norm"][l]), _gain_tile(inp["ffn2_norm"][l]), _gain_tile(inp["final_norm"])], 1).astype(np.float32)
    convp = np.zeros((128, 4, 34), np.float32)
    for c in range(4):
        sl = slice(c * 128, (c + 1) * 128)
        convp[:, c, 0:31] = inp["conv_w"][l][:, sl].T
        convp[:, c, 31] = inp["conv_b"][l, sl]
        convp[:, c, 32] = inp["conv_ln_g"][l, sl]
        convp[:, c, 33] = inp["conv_ln_b"][l, sl]
    return {"gains3": gains3, "convp": convp.reshape(128, 4 * 34),
            "w_gates": np.ascontiguousarray(inp["w_in"][l][:, NMIX:]), "w_branch": inp["w_branch"][l], "w_out": inp["w_out"][l],
            "w_glu": inp["s5_w_glu"][l], "w_gate": inp["ffn2_w_gate"][l], "w_up": inp["ffn2_w_up"][l], "w_down": inp["ffn2_w_down"][l]}


def _c_conv_halo(conv_b, q):
    t0 = q * TOK
    out = np.zeros((1024, TOK + HALO), conv_b.dtype)
    lo = max(t0 - HALO, 0)
    out[:, HALO - (t0 - lo):] = conv_b[:, lo:t0 + TOK]
    return out


def kernel(**inputs):
    inp = {k: np.asarray(v) for k, v in inputs.items()}
    x = inp["x"]
    progA = build_A()
    progB = {p: build_B((p,)) for p in "hsa"}
    progC = [build_C(False), build_C(True)]
    consts = _consts_B()
    pos = [np.ascontiguousarray(inp["positions"][b][None, :]).astype(np.int32) for b in range(B)]
    xT = [np.ascontiguousarray(x[c // 4, (c % 4) * TOK:(c % 4 + 1) * TOK, :].T) for c in range(NCORES)]
    for l in range(DEPTH):
        wa = {"g_ffn1": _gain_tile(inp["ffn1_norm"][l]), "g_mix": _gain_tile(inp["mix_norm"][l]),
              "w_gate": inp["ffn1_w_gate"][l], "w_up": inp["ffn1_w_up"][l], "w_down": inp["ffn1_w_down"][l],
              "w_in": np.ascontiguousarray(inp["w_in"][l][:, :NMIX])}
        resA = _run(progA, [dict(wa, xT=xT[c]) for c in range(NCORES)])
        x1T = [resA[c]["x1T"] for c in range(NCORES)]
        projT_b = [np.concatenate([resA[4 * b + q]["projT"] for q in range(4)], axis=1) for b in range(B)]
        del resA
        ycat = {}
        for part, big, outk in (("h", "hg4", "ya"), ("s", "s5u", "zb"), ("a", "att9", "yd")):
            in_maps = []
            for c in range(NCORES):
                b, hh = c // 4, c % 4
                m = dict(consts)
                m.update(_b_params(inp, l, hh))
                m[big] = _b_acts(projT_b[b], hh)[big]
                m["pos"] = pos[b]
                in_maps.append(m)
            resB = _run(progB[part], in_maps)
            ycat[outk] = [np.concatenate([resB[4 * b + hh][outk] for hh in range(4)], axis=0) for b in range(B)]
            del resB
        wc = _c_params(inp, l)
        in_maps = []
        for c in range(NCORES):
            b, q = c // 4, c % 4
            tsl = slice(q * TOK, (q + 1) * TOK)
            m = dict(wc)
            m["x1T"] = x1T[c]
            m["yaT"] = np.ascontiguousarray(ycat["ya"][b][:, tsl])
            m["zbT"] = np.ascontiguousarray(ycat["zb"][b][:, tsl])
            m["ydT"] = np.ascontiguousarray(ycat["yd"][b][:, tsl])
            m["convT"] = _c_conv_halo(projT_b[b][2560:3584], q)
            in_maps.append(m)
        resC = _run(progC[1 if l == DEPTH - 1 else 0], in_maps)
        xT = [resC[c]["outT"] for c in range(NCORES)]
        del resC
    out = np.empty((B, L, D), np.float32)
    for c in range(NCORES):
        out[c // 4, (c % 4) * TOK:(c % 4 + 1) * TOK, :] = xT[c].T
    return out
```
